# Optimizing a Trainium2 kernel written in Bass

```python
import functools
import jax, jax.numpy as jnp
from jax import lax
import numpy as np

D_MODEL = 1024
BATCH = 4
SEQ = 4096
DEPTH = 1
DEC_BATCH = 128
DEC_SEQ = 4
PAST_LEN = 8192
PAGE_SIZE = 128

MIX_WIDTH = D_MODEL
HEAD_DIM = 64
RWKV_WIDTH = MIX_WIDTH // 2
RWKV_HEADS = RWKV_WIDTH // HEAD_DIM
ATTN_WIDTH = MIX_WIDTH - RWKV_WIDTH
ATTN_HEADS = ATTN_WIDTH // HEAD_DIM
KV_HEADS = 2
GROUP = ATTN_HEADS // KV_HEADS
DECAY_LORA = 64
AAA_LORA = 64
GATE_LORA = 128
N_RW_COLS = 3 * RWKV_WIDTH + DECAY_LORA + AAA_LORA + GATE_LORA
N_IN_COLS = N_RW_COLS + ATTN_WIDTH + 2 * KV_HEADS * HEAD_DIM
LNX_EPS = 64e-5
WINDOW = 128
ATT_BLOCK = 128
ATTN_SCALE = HEAD_DIM ** -0.5
ROPE_THETA = 500000.0
ROT_DIM = HEAD_DIM // 4
N_META = 16
N_KEYS = 128
N_EXPERTS = N_KEYS * N_KEYS
PEER_HEADS = 8
PEER_TOPK = 16
D_KEY = 256
D_HALF = D_KEY // 2
PEER_BLOCK = 256
NORM_EPS = 1e-5
NEG_INF = -1e30
F32 = jnp.float32

kernel_name = 'hymba_rwkv7_swa_sink_peer_step'


def rmsnorm(x, g):
    xf = x.astype(F32)
    y = xf * lax.rsqrt(jnp.mean(xf * xf, axis=-1, keepdims=True) + NORM_EPS)
    return (y * g.astype(F32)).astype(x.dtype)


def rope_partial(x, pos):
    half = ROT_DIM // 2
    inv_freq = ROPE_THETA ** (-jnp.arange(0, ROT_DIM, 2, dtype=F32) / ROT_DIM)
    ang = pos.astype(F32)[:, None] * inv_freq[None, :]
    cos = jnp.cos(ang)[None, :, None, :]
    sin = jnp.sin(ang)[None, :, None, :]
    xr = x[..., :ROT_DIM].astype(F32)
    x1, x2 = xr[..., :half], xr[..., half:]
    rot = jnp.concatenate([x1 * cos - x2 * sin, x2 * cos + x1 * sin], axis=-1).astype(x.dtype)
    return jnp.concatenate([rot, x[..., ROT_DIM:]], axis=-1)


def sink_probs(scores, mask, sinks):
    s = jnp.where(mask, scores, NEG_INF)
    sk = sinks.astype(F32)[:, :, None, None]
    m = jnp.maximum(jnp.max(s, axis=-1, keepdims=True), sk)
    p = jnp.exp(s - m)
    return p / (jnp.sum(p, axis=-1, keepdims=True) + jnp.exp(sk - m))


def attend_banded(q, k, v, sinks):
    B, L = q.shape[:2]
    front = (-L) % ATT_BLOCK
    n_blk = (L + front) // ATT_BLOCK
    padw = ((0, 0), (front, 0), (0, 0), (0, 0))
    qb = jnp.pad(q, padw).reshape(B, n_blk, ATT_BLOCK, KV_HEADS, GROUP, HEAD_DIM)
    kb = jnp.pad(k, padw).reshape(B, n_blk, ATT_BLOCK, KV_HEADS, HEAD_DIM)
    vb = jnp.pad(v, padw).reshape(B, n_blk, ATT_BLOCK, KV_HEADS, HEAD_DIM)
    prev = lambda t: jnp.pad(t, ((0, 0), (1, 0), (0, 0), (0, 0), (0, 0)))[:, :-1]
    keys = jnp.concatenate([prev(kb), kb], axis=2)
    vals = jnp.concatenate([prev(vb), vb], axis=2)
    qpos = (jnp.arange(L + front) - front).reshape(n_blk, ATT_BLOCK)
    kpos = jnp.concatenate([qpos - ATT_BLOCK, qpos], axis=1)
    dlt = qpos[:, :, None] - kpos[:, None, :]
    mask = (kpos[:, None, :] >= 0) & (dlt >= 0) & (dlt < WINDOW)
    scores = jnp.einsum('bnqkgd,bnskd->bnkgqs', qb, keys).astype(F32) * ATTN_SCALE
    p = sink_probs(scores, mask[None, :, None, None], sinks.reshape(KV_HEADS, GROUP))
    out = jnp.einsum('bnkgqs,bnskd->bnqkgd', p.astype(v.dtype), vals)
    out = out.reshape(B, L + front, ATTN_WIDTH)[:, front:]
    keep = min(WINDOW, L)
    return out, k[:, L - keep:], v[:, L - keep:]


def attend_window(q, k, v, sinks, k_buf, v_buf):
    B, S = q.shape[:2]
    wb = k_buf.shape[1]
    keys = jnp.concatenate([k_buf.astype(k.dtype), k], axis=1)
    vals = jnp.concatenate([v_buf.astype(v.dtype), v], axis=1)
    qpos = PAST_LEN + jnp.arange(S)
    kpos = jnp.concatenate([PAST_LEN - wb + jnp.arange(wb), qpos])
    dlt = qpos[:, None] - kpos[None, :]
    mask = (dlt >= 0) & (dlt < WINDOW)
    qg = q.reshape(B, S, KV_HEADS, GROUP, HEAD_DIM)
    scores = jnp.einsum('bqkgd,bskd->bkgqs', qg, keys).astype(F32) * ATTN_SCALE
    p = sink_probs(scores, mask, sinks.reshape(KV_HEADS, GROUP))
    out = jnp.einsum('bkgqs,bskd->bqkgd', p.astype(v.dtype), vals).reshape(B, S, ATTN_WIDTH)
    return out, keys[:, -wb:], vals[:, -wb:]


def wkv7_scan(r, decay, k, v, a_vec, b_vec, s0):
    def step(S, inp):
        r_t, w_t, k_t, v_t, a_t, b_t = inp
        sa = jnp.einsum('bhij,bhj->bhi', S, a_t)
        S = S * w_t[:, :, None, :] + sa[..., :, None] * b_t[:, :, None, :] + v_t[..., :, None] * k_t[:, :, None, :]
        y = jnp.einsum('bhij,bhj->bhi', S, r_t)
        return S, y
    xs = tuple(jnp.moveaxis(t, 1, 0) for t in (r, decay, k, v, a_vec, b_vec))
    S, ys = lax.scan(step, s0, xs)
    return jnp.moveaxis(ys, 0, 1), S


def rwkv7_time_mix(p, p_prev_row, s0, lp):
    B, T, _ = p.shape
    p_prev = jnp.concatenate([p_prev_row[:, None, :].astype(p.dtype), p[:, :-1]], axis=1)
    m = (p + (p_prev - p) * lp['mu_shift']).astype(F32)
    c = RWKV_WIDTH
    cuts = [c, 2 * c, 3 * c, 3 * c + DECAY_LORA, 3 * c + DECAY_LORA + AAA_LORA]
    xr, xk, xv, xw, xa, xg = jnp.split(m, cuts, axis=-1)
    w = -jax.nn.softplus(-(lp['w0'] + jnp.tanh(xw) @ lp['w_lora_w2'])) - 0.5
    decay = jnp.exp(-jnp.exp(w))
    a = jax.nn.sigmoid(lp['a0'] + xa @ lp['w_lora_a2'])
    g = jax.nn.sigmoid(xg) @ lp['w_lora_g2']
    heads = lambda t: t.reshape(B, T, RWKV_HEADS, HEAD_DIM)
    kk = heads(xk * lp['k_k'])
    kk = kk / jnp.maximum(jnp.sqrt(jnp.sum(kk * kk, axis=-1, keepdims=True)), 1e-12)
    kmod = xk * (1.0 + (a - 1.0) * lp['k_a'])
    r_h, k_h, v_h, a_h = heads(xr), heads(kmod), heads(xv), heads(a)
    y, S = wkv7_scan(r_h, heads(decay), k_h, v_h, -kk, kk * a_h, s0.astype(F32))
    mean = jnp.mean(y, axis=-1, keepdims=True)
    var = jnp.mean(jnp.square(y - mean), axis=-1, keepdims=True)
    y = ((y - mean) * lax.rsqrt(var + LNX_EPS)).reshape(B, T, RWKV_WIDTH) * lp['lnx_w'] + lp['lnx_b']
    y = y + (jnp.sum(r_h * k_h * lp['r_k'], axis=-1, keepdims=True) * v_h).reshape(B, T, RWKV_WIDTH)
    y = y * g
    return y.astype(p.dtype), S


def peer(h, lp):
    shp = h.shape
    t = h.reshape(-1, D_MODEL)
    n = t.shape[0]
    n_blk = -(-n // PEER_BLOCK)
    blocks = jnp.pad(t, ((0, n_blk * PEER_BLOCK - n), (0, 0))).reshape(n_blk, PEER_BLOCK, D_MODEL)
    w_query, sub_keys, eu, ev = lp['w_query'], lp['sub_keys'], lp['expert_u'], lp['expert_v']

    def block(hb):
        q = (hb @ w_query).reshape(PEER_BLOCK, PEER_HEADS, 2, D_HALF).astype(F32)
        s = jnp.einsum('thcd,cnd->thcn', q, sub_keys.astype(F32))
        s1, i1 = lax.top_k(s[:, :, 0], PEER_TOPK)
        s2, i2 = lax.top_k(s[:, :, 1], PEER_TOPK)
        cand = (s1[..., :, None] + s2[..., None, :]).reshape(PEER_BLOCK, PEER_HEADS, PEER_TOPK * PEER_TOPK)
        cidx = (i1[..., :, None] * N_KEYS + i2[..., None, :]).reshape(PEER_BLOCK, PEER_HEADS, PEER_TOPK * PEER_TOPK)
        top, sel = lax.top_k(cand, PEER_TOPK)
        idx = jnp.take_along_axis(cidx, sel, axis=-1).reshape(PEER_BLOCK, PEER_HEADS * PEER_TOPK)
        gate = jax.nn.softmax(top, axis=-1).reshape(PEER_BLOCK, PEER_HEADS * PEER_TOPK)
        pre = jnp.einsum('td,tkd->tk', hb, eu[idx]).astype(F32)
        act = jax.nn.gelu(pre, approximate=False) * gate
        return jnp.einsum('tk,tkd->td', act.astype(hb.dtype), ev[idx])

    out = lax.map(block, blocks).reshape(-1, D_MODEL)[:n]
    return out.reshape(shp)


def trunk_layer(x, pos, prev_h, s0, attend, lp):
    B, T = x.shape[:2]
    h = rmsnorm(x, lp['norm1_g'])
    proj = h @ lp['w_in']
    prev_proj = prev_h.astype(h.dtype) @ lp['w_in'][:, :N_RW_COLS]
    y_rw, s_new = rwkv7_time_mix(proj[..., :N_RW_COLS], prev_proj, s0, lp)
    o = N_RW_COLS
    q = proj[..., o:o + ATTN_WIDTH].reshape(B, T, ATTN_HEADS, HEAD_DIM)
    o += ATTN_WIDTH
    k = proj[..., o:o + KV_HEADS * HEAD_DIM].reshape(B, T, KV_HEADS, HEAD_DIM)
    o += KV_HEADS * HEAD_DIM
    v = proj[..., o:o + KV_HEADS * HEAD_DIM].reshape(B, T, KV_HEADS, HEAD_DIM)
    q = rope_partial(q, pos)
    k = rope_partial(k, pos)
    y_at, k_keep, v_keep = attend(q, k, v, lp['attn_sinks'])
    x = x + jnp.concatenate([y_rw, y_at], axis=-1) @ lp['w_out']
    x = x + peer(rmsnorm(x, lp['norm2_g']), lp)
    return x, (k_keep, v_keep, s_new, h[:, -1])


def setup_inputs(seed: int = 0) -> dict:
    key = jax.random.key(seed)
    ks = jax.random.split(key, 28)
    nrm = lambda k, shp, sc: jax.random.normal(k, shp, F32) * sc
    wb = min(WINDOW, PAST_LEN)
    return {
        'x_prompt': nrm(ks[0], (BATCH, SEQ, D_MODEL), 1.0),
        'x_sample': nrm(ks[1], (DEC_BATCH, DEC_SEQ, D_MODEL), 1.0),
        'cache_k_win': nrm(ks[2], (DEPTH, DEC_BATCH, wb, KV_HEADS, HEAD_DIM), 1.0),
        'cache_v_win': nrm(ks[3], (DEPTH, DEC_BATCH, wb, KV_HEADS, HEAD_DIM), 1.0),
        'state_wkv': nrm(ks[4], (DEPTH, DEC_BATCH, RWKV_HEADS, HEAD_DIM, HEAD_DIM), 0.1),
        'state_shift': nrm(ks[5], (DEPTH, DEC_BATCH, D_MODEL), 1.0),
        'meta_tokens': nrm(ks[6], (N_META, D_MODEL), 1.0),
        'norm1_g': 1.0 + nrm(ks[7], (DEPTH, D_MODEL), 0.02),
        'w_in': nrm(ks[8], (DEPTH, D_MODEL, N_IN_COLS), D_MODEL ** -0.5),
        'mu_shift': jax.random.uniform(ks[9], (DEPTH, N_RW_COLS), F32),
        'w0': jax.random.uniform(ks[10], (DEPTH, RWKV_WIDTH), F32, -6.0, -1.0),
        'w_lora_w2': nrm(ks[11], (DEPTH, DECAY_LORA, RWKV_WIDTH), 0.1),
        'a0': nrm(ks[12], (DEPTH, RWKV_WIDTH), 0.1),
        'w_lora_a2': nrm(ks[13], (DEPTH, AAA_LORA, RWKV_WIDTH), 0.1),
        'w_lora_g2': nrm(ks[14], (DEPTH, GATE_LORA, RWKV_WIDTH), GATE_LORA ** -0.5),
        'k_k': 0.85 + nrm(ks[15], (DEPTH, RWKV_WIDTH), 0.02),
        'k_a': 1.0 + nrm(ks[16], (DEPTH, RWKV_WIDTH), 0.02),
        'r_k': nrm(ks[17], (DEPTH, RWKV_HEADS, HEAD_DIM), 0.1),
        'lnx_w': 1.0 + nrm(ks[18], (DEPTH, RWKV_WIDTH), 0.02),
        'lnx_b': nrm(ks[19], (DEPTH, RWKV_WIDTH), 0.02),
        'attn_sinks': nrm(ks[20], (DEPTH, ATTN_HEADS), 0.5),
        'w_out': nrm(ks[21], (DEPTH, MIX_WIDTH, D_MODEL), MIX_WIDTH ** -0.5),
        'norm2_g': 1.0 + nrm(ks[22], (DEPTH, D_MODEL), 0.02),
        'w_query': nrm(ks[23], (DEPTH, D_MODEL, PEER_HEADS * D_KEY), D_MODEL ** -0.5),
        'sub_keys': nrm(ks[24], (DEPTH, 2, N_KEYS, D_HALF), D_HALF ** -0.5),
        'expert_u': nrm(ks[25], (DEPTH, N_EXPERTS, D_MODEL), D_MODEL ** -0.5),
        'expert_v': nrm(ks[26], (DEPTH, N_EXPERTS, D_MODEL), 0.2),
        'final_norm_g': 1.0 + nrm(ks[27], (D_MODEL,), 0.02),
    }


def reference(x_prompt, x_sample, cache_k_win, cache_v_win, state_wkv, state_shift, meta_tokens, norm1_g, w_in, mu_shift, w0, w_lora_w2, a0, w_lora_a2, w_lora_g2, k_k, k_a, r_k, lnx_w, lnx_b, attn_sinks, w_out, norm2_g, w_query, sub_keys, expert_u, expert_v, final_norm_g):
    B = x_prompt.shape[0]
    meta = jnp.broadcast_to(meta_tokens[None].astype(x_prompt.dtype), (B, N_META, D_MODEL))
    xp = jnp.concatenate([meta, x_prompt], axis=1)
    xs = x_sample
    pos_p = jnp.arange(N_META + x_prompt.shape[1])
    pos_s = PAST_LEN + jnp.arange(x_sample.shape[1])
    new_p = []
    new_s = []
    for l in range(DEPTH):
        lp = dict(norm1_g=norm1_g[l], w_in=w_in[l], mu_shift=mu_shift[l], w0=w0[l], w_lora_w2=w_lora_w2[l],
                  a0=a0[l], w_lora_a2=w_lora_a2[l], w_lora_g2=w_lora_g2[l], k_k=k_k[l], k_a=k_a[l], r_k=r_k[l],
                  lnx_w=lnx_w[l], lnx_b=lnx_b[l], attn_sinks=attn_sinks[l], w_out=w_out[l], norm2_g=norm2_g[l],
                  w_query=w_query[l], sub_keys=sub_keys[l], expert_u=expert_u[l], expert_v=expert_v[l])
        xp, st_p = trunk_layer(xp, pos_p, jnp.zeros((B, D_MODEL), xp.dtype),
                               jnp.zeros((B, RWKV_HEADS, HEAD_DIM, HEAD_DIM), F32), attend_banded, lp)
        xs, st_s = trunk_layer(xs, pos_s, state_shift[l], state_wkv[l],
                               functools.partial(attend_window, k_buf=cache_k_win[l], v_buf=cache_v_win[l]), lp)
        new_p.append(st_p)
        new_s.append(st_s)
    stk = lambda sts, i: jnp.stack([st[i] for st in sts], axis=0)
    y_prompt = rmsnorm(xp, final_norm_g)[:, N_META:]
    y_sample = rmsnorm(xs, final_norm_g)
    return (y_prompt, y_sample,
            stk(new_p, 0), stk(new_p, 1), stk(new_p, 2).astype(state_wkv.dtype), stk(new_p, 3),
            stk(new_s, 0), stk(new_s, 1), stk(new_s, 2).astype(state_wkv.dtype), stk(new_s, 3))
```

```python
import numpy as np
from contextlib import ExitStack
import concourse.bass as bass
import concourse.mybir as mybir
from concourse.alu_op_type import AluOpType as ALU
from concourse.bass_utils import run_bass_kernel_spmd

F32 = mybir.dt.float32
BF16 = mybir.dt.bfloat16
U32 = mybir.dt.uint32
AF = mybir.ActivationFunctionType
AX = mybir.AxisListType

D = 1024
NRW = 1792
NCOL = 2560
NCORES = 8
NSEQ_S = 16
PAST = 8192
NPT = 33
NEXP = 16384
OPTS = {'limit': 10**9, 'dump_only': '', 'dumps': False, 'nexp': 16384, 'dbg_tile': 1, 'stage': 99, 'peer': True, 'win_copy': True, 'samp_state': True}


class Sched:
    def __init__(self, nc):
        self.nc = nc
        self.eng = {'pe': nc.tensor, 'act': nc.scalar, 'dve': nc.vector, 'pool': nc.gpsimd, 'sp': nc.sync}
        self.sem = {e: nc.alloc_semaphore("sem_" + e) for e in ['pe', 'act', 'dve', 'pool']}
        self.cnt = {e: 0 for e in self.sem}
        self.seen = {e: {} for e in self.eng}
        self.last_w = {}
        self.readers = {}
        self.dslots = {}
        for q in ['sp', 'pool', 'act']:
            self.dslots[q] = [[nc.alloc_semaphore("dq_%s_%d" % (q, i)), 0] for i in range(8)]
        self.dnext = {q: 0 for q in self.dslots}
        self.tokens = {}

    def _wait(self, e, toks):
        need = {}
        for t in toks:
            if t is None:
                continue
            sid, val, we = t
            if we == e and e == 'pe':
                continue
            if self.seen[e].get(sid, 0) >= val:
                continue
            if need.get(sid, (None, 0))[1] < val:
                need[sid] = (t, val)
        for sid, (t, val) in need.items():
            self.eng[e].wait_ge(self.tokens[sid], val)
            self.seen[e][sid] = val

    def _deps(self, e, reads, writes):
        toks = []
        for k in reads:
            toks.append(self.last_w.get(k))
        for k in writes:
            toks.append(self.last_w.get(k))
            for t in self.readers.get(k, []):
                toks.append(t[:3])
        return toks

    def _mark(self, tok, reads, writes, is_dma):
        for k in reads:
            self.readers.setdefault(k, []).append(tok + (is_dma,))
        for k in writes:
            self.last_w[k] = tok
            self.readers[k] = []

    def op(self, e, reads, writes, fn):
        self.n = getattr(self, 'n', 0) + 1
        if self.n > OPTS['limit']:
            return None
        self._wait(e, self._deps(e, reads, writes))
        inst = fn(self.eng[e])
        self.cnt[e] += 1
        sem = self.sem[e]
        inst.then_inc(sem, 1)
        sid = id(sem)
        self.tokens[sid] = sem
        self._mark((sid, self.cnt[e], e), reads, writes, False)
        return inst

    def dma(self, q, out, in_, reads, writes, fn=None):
        self.n = getattr(self, 'n', 0) + 1
        if self.n > OPTS['limit']:
            return None
        slots = self.dslots[q]
        i = self.dnext[q]
        self.dnext[q] = (i + 1) % len(slots)
        sem, val = slots[i]
        sid = id(sem)
        self.tokens[sid] = sem
        toks = self._deps(q, reads, writes)
        if val > 0:
            toks.append((sid, val, 'dma'))
        self._wait(q, toks)
        if fn is None:
            inst = self.eng[q].dma_start(out=out, in_=in_)
        else:
            inst = fn(self.eng[q])
        inst.then_inc(sem, 16)
        slots[i][1] = val + 16
        self._mark((sid, val + 16, 'dma'), reads, writes, True)

    def finish(self):
        for q, slots in self.dslots.items():
            toks = [(id(s), v, 'dma') for s, v in slots if v > 0]
            self._wait(q, toks)


def build(NT_P, NT_S, dbg=False, dbg_tile=1):
    nc = bass.Bass("TRN2", target_bir_lowering=False)
    S = Sched(nc)
    NTT = NT_P + NT_S
    NPE = NT_P + (1 if NT_S else 0)

    def din(name, shape, dt=F32):
        return nc.dram_tensor(name, list(shape), dt, kind="ExternalInput").ap()

    def dout(name, shape, dt=F32):
        return nc.dram_tensor(name, list(shape), dt, kind="ExternalOutput").ap()

    xp = din("xp", [max(NT_P, 1) * 128, D])
    xs = din("xs", [NSEQ_S * 4, D])
    ck = din("ck", [NSEQ_S, 128, 128])
    cv = din("cv", [NSEQ_S, 128, 128])
    swkv = din("swkv", [NSEQ_S, 8, 64, 64])
    sshift = din("sshift", [NSEQ_S, D])
    w_in = din("w_in", [D, NCOL])
    w_out = din("w_out", [D, D])
    w_q = din("w_q", [D, 2048])
    subk = din("subk", [2, 128, 128])
    eu = din("eu", [OPTS['nexp'], D])
    ev = din("ev", [OPTS['nexp'], D])
    lw2 = din("lw2", [64, 512])
    la2 = din("la2", [64, 512])
    lg2 = din("lg2", [128, 512])
    vec512 = din("vec512", [7, 512])
    mu = din("mu", [1, NRW])
    g1 = din("g1", [1, D])
    g2 = din("g2", [1, D])
    gF = din("gF", [1, D])
    sinks = din("sinks", [1, 8])
    rope = din("rope", [128, NPT + 1, 16])
    cmask = din("cmask", [128, 3, 256])
    cst = din("cst", [128, 6, 128])
    vmask = din("vmask", [128, 2])
    iota_in = din("iota", [128, 256])

    y_p = dout("y_p", [max(NT_P, 1) * 128, D])
    y_s = dout("y_s", [NSEQ_S * 4, D])
    kw_p = dout("kw_p", [128, 128])
    vw_p = dout("vw_p", [128, 128])
    wkv_p = dout("wkv_p", [8, 64, 64])
    sh_p = dout("sh_p", [1, D])
    kw_s = dout("kw_s", [NSEQ_S, 128, 128])
    vw_s = dout("vw_s", [NSEQ_S, 128, 128])
    wkv_s = dout("wkv_s", [NSEQ_S, 8, 64, 64])
    sh_s = dout("sh_s", [NSEQ_S, D])
    xmid = nc.dram_tensor("xmid", [max(NPE, 1) * 128, D], F32, kind="Internal").ap()
    dbg_t = {}
    if dbg:
        for nm, shp in [("d_m", [128, NRW]), ("d_y", [128, 512]), ("d_at", [128, 512]), ("d_S", [64, 512]),
                        ("d_pre", [128, 128]), ("d_idx", [128, 128]), ("d_gate", [128, 128])]:
            dbg_t[nm] = dout(nm, shp)

    stk = {'cur': ExitStack()}
    glob_stack = stk['cur']

    dumped = {}

    def DUMP(name, t, key, ti=None):
        if not OPTS['dumps'] or (ti is not None and ti != OPTS['dbg_tile']) or name in dumped:
            return
        if OPTS['dump_only'] and name not in str(OPTS['dump_only']).split(','):
            return
        src = t if isinstance(t, bass.AP) else t[:]
        o = nc.dram_tensor("z_" + name, list(src.shape), src.dtype, kind="ExternalOutput").ap()
        dumped[name] = 1
        S.dma('sp', o, src, [key], [])

    def sb(name, shape, dt=F32):
        return stk['cur'].enter_context(nc.sbuf_tensor(name, list(shape), dt))

    def ps(name, shape, dt=F32):
        return nc.alloc_psum_tensor(name, list(shape), dt)

    c_cst = sb("c_cst", [128, 6, 128])
    S.dma('sp', c_cst[:], cst, [], ['c_cst'])
    c_cstb = sb("c_cstb", [128, 6, 128], BF16)
    S.op('dve', ['c_cst'], ['c_cstb'], lambda e: e.tensor_copy(out=c_cstb[:], in_=c_cst[:]))
    ident_f = c_cst[:, 0, :]
    tri_f = c_cst[:, 1, :]
    ones_f = c_cst[:, 2, :]
    ident_b = c_cstb[:, 0, :]
    c_m4 = sb("c_m4", [128, 4, 128])
    for i, j in enumerate([3, 4, 3, 4]):
        S.op('dve', ['c_cst'], ['c_m4'], lambda e, i=i, j=j: e.tensor_copy(out=c_m4[:, i, :], in_=c_cst[:, j, :]))
    c_low = c_cst[:, 5, :]
    c_amask = sb("c_amask", [128, 3, 256])
    S.dma('sp', c_amask[:], cmask, [], ['c_amask'])
    c_rope = sb("c_rope", [128, NPT + 1, 16])
    S.dma('sp', c_rope[:], rope, [], ['c_rope'])
    c_vm = sb("c_vm", [128, 2])
    S.dma('sp', c_vm[:], vmask, [], ['c_vm'])
    c_v512 = sb("c_v512", [128, 7, 512])
    S.dma('sp', c_v512[:], vec512.partition_broadcast(128), [], ['c_v512'])
    W0, A0, KK, KA, RK, LW, LB = range(7)
    c_g1bc = sb("c_g1bc", [128, D])
    S.dma('sp', c_g1bc[:], g1.partition_broadcast(128)[:, 0, :], [], ['c_g1bc'])
    c_sink = sb("c_sink", [128, 8])
    S.dma('sp', c_sink[:], sinks.partition_broadcast(128)[:, 0, :], [], ['c_sink'])
    c_g1col = sb("c_g1col", [128, 8])
    with nc.allow_non_contiguous_dma(reason="tiny param column load"):
        S.dma('sp', c_g1col[:], g1.rearrange("o (kc p) -> p (o kc)", p=128), [], ['c_g1col'])

    def rsq(key, out, in_, c, op0):
        S.op('dve', [key], [key], lambda e: e.tensor_scalar(out=out, in0=in_, scalar1=c, scalar2=None, op0=op0))
        S.op('act', [key], [key], lambda e: e.activation(out=out, in_=out, func=AF.Sqrt))
        S.op('dve', [key], [key], lambda e: e.reciprocal(out=out, in_=out))

    P = [ps("P%d" % i, [128, 512]) for i in range(8)]
    PK = ["P%d" % i for i in range(8)]

    def barrier():
        toks = []
        for e, sem in S.sem.items():
            if S.cnt[e] > 0:
                S.tokens[id(sem)] = sem
                toks.append((id(sem), S.cnt[e], 'x'))
        for q, slots in S.dslots.items():
            for s, v in slots:
                if v > 0:
                    S.tokens[id(s)] = s
                    toks.append((id(s), v, 'dma'))
        for e in ['pe', 'act', 'dve', 'pool', 'sp']:
            S._wait(e, toks)

    ph1 = ExitStack()
    stk['cur'] = ph1
    W1 = sb("W1", [128, 8, NRW], BF16)
    W2 = sb("W2", [128, 8, NRW], BF16)
    Wat = sb("Wat", [128, 8, 768], BF16)
    Wo = sb("Wo", [128, 8, D], BF16)
    L_w2 = sb("L_w2", [128, 512], BF16)
    L_g2 = sb("L_g2", [128, 512], BF16)
    with nc.sbuf_tensor("stg", [128, NCOL], F32) as stg, nc.sbuf_tensor("mub", [128, NRW], F32) as mub, \
            nc.sbuf_tensor("omu", [128, NRW], F32) as omu:
        S.dma('sp', mub[:], mu.partition_broadcast(128)[:, 0, :], [], ['mub'])
        S.op('dve', ['mub'], ['omu'], lambda e: e.tensor_scalar(out=omu[:], in0=mub[:], scalar1=-1.0, scalar2=1.0,
                                                                op0=ALU.mult, op1=ALU.add))
        for kc in range(8):
            S.dma('sp', stg[:], w_in[kc * 128:(kc + 1) * 128, :], [], ['stg'])
            S.op('dve', ['stg', 'omu'], ['W1'], lambda e, kc=kc: e.tensor_tensor(out=W1[:, kc, :], in0=stg[:, 0:NRW], in1=omu[:], op=ALU.mult))
            S.op('pool', ['stg', 'mub'], ['W2'], lambda e, kc=kc: e.tensor_tensor(out=W2[:, kc, :], in0=stg[:, 0:NRW], in1=mub[:], op=ALU.mult))
            S.op('act', ['stg'], ['Wat'], lambda e, kc=kc: e.copy(out=Wat[:, kc, :], in_=stg[:, NRW:NCOL]))
        for kc in range(8):
            S.dma('sp', stg[:, 0:D], w_out[kc * 128:(kc + 1) * 128, :], [], ['stg'])
            S.op('act', ['stg'], ['Wo'], lambda e, kc=kc: e.copy(out=Wo[:, kc, :], in_=stg[:, 0:D]))
        S.dma('sp', stg[0:64, 0:512], lw2, [], ['stg'])
        S.dma('sp', stg[64:128, 0:512], la2, [], ['stg'])
        S.dma('sp', stg[:, 512:1024], lg2, [], ['stg'])
        S.op('act', ['stg'], ['L_w2'], lambda e: e.copy(out=L_w2[:], in_=stg[:, 0:512]))
        S.op('act', ['stg'], ['L_g2'], lambda e: e.copy(out=L_g2[:], in_=stg[:, 512:1024]))
        barrier()

    xt0 = sb("xt0", [128, D])
    xt = [xt0, xt0]
    xts = xt0
    xn = sb("xn", [128, D], BF16)
    ssq = sb("ssq", [128, 4])
    hT = sb("hT", [128, 8, 128], BF16)
    hTs = sb("hTs", [128, 8, 128], BF16)
    S.op('pool', [], ['hT'], lambda e: e.memset(hT[:], 0.0))
    S.op('pool', [], ['hTs'], lambda e: e.memset(hTs[:], 0.0))
    A = {}
    for nm in ["r", "k", "v", "ld", "asig", "g", "kk", "b", "kmod", "c", "t1", "t2", "t3"]:
        A[nm] = sb("a_" + nm, [128, 512])
    Bt = {}
    for nm in ["rt", "at", "bt", "kt", "bbar", "kbar", "vb", "W0T", "UT"]:
        Bt[nm] = sb("b_" + nm, [128, 512], BF16)
    lin = sb("lin", [128, 2, 128], BF16)
    st8 = sb("st8", [128, 8, 4])
    AR_fm = sb("AR_fm", [64, 8, 256], BF16)
    B_fm = sb("B_fm", [64, 8, 128], BF16)
    K_fm = sb("K_fm", [64, 8, 128], BF16)
    MATS = sb("MATS", [128, 8, 512], BF16)
    _m = sb("Mb", [128, 8, 128], BF16)
    _mt = sb("MTb", [128, 8, 128], BF16)
    _t = sb("Tb", [128, 8, 128], BF16)
    Mb, MTb, Tb = [_m, _m], [_mt, _mt], [_t, _t]
    S32 = sb("S32", [64, 8, 64])
    Sb = sb("Sb", [64, 8, 64], BF16)
    ecl_fm = sb("ecl_fm", [64, 8])
    sti = sb("sti", [64, 8, 64])
    qkv = sb("qkv", [128, 768])
    rtmp = sb("rtmp", [128, 10, 8])
    qb = sb("qb", [128, 768], BF16)
    QT = sb("QT", [64, 8, 128], BF16)
    KT = [sb("KT%d" % i, [64, 2, 128], BF16) for i in range(2)]
    Vb = [sb("Vb%d" % i, [128, 128], BF16) for i in range(2)]
    sm = sb("sm", [128, 256])
    for i in range(2):
        S.op('pool', [], ['KT%d' % i], lambda e, i=i: e.memset(KT[i][:], 0.0))
        S.op('pool', [], ['Vb%d' % i], lambda e, i=i: e.memset(Vb[i][:], 0.0))
    eb = sb("eb", [128, 256], BF16)
    eT = sb("eT", [128, 2, 128], BF16)
    ast = sb("ast", [128, 8])
    ycat = sb("ycat", [128, D], BF16)
    ycatT = sb("ycatT", [128, 8, 128], BF16)
    junk = ycatT[:].rearrange("p k t -> p (k t)")
    xm = sb("xm", [128, D])
    hrow = xm
    ckb = sm

    def v3(ap, h=8):
        return ap.rearrange("p (h j) -> p h j", h=h)

    def bc_last(ap, n):
        return ap.broadcast_to([ap.shape[0], ap.shape[1], n])

    def V512(i):
        return c_v512[:, i, :]

    def load_x(ti):
        if ti < NT_P:
            b = xt[ti % 2]
            S.dma('sp', b[:], xp[ti * 128:(ti + 1) * 128, :], [], ['xt0'])
        else:
            s = ti - NT_P
            if s == 0:
                S.op('pool', [], ['xt0'], lambda e: e.memset(xt0[:], 0.0))
                S.dma('sp', xmid[NT_P * 128:(NT_P + 1) * 128, :], xt0[:], ['xt0'], ['xmid'])
            S.dma('sp', xts[0:4, :], xs[s * 4:(s + 1) * 4, :], [], ['xt0'])

    def mix_tile(ti):
        samp = ti >= NT_P
        sq = ti - NT_P
        xk = 'xt0'
        xb = xts if samp else xt[ti % 2]
        par = ti % 2
        vm = c_vm[:, 1:2] if samp else c_vm[:, 0:1]
        if OPTS['stage'] <= 0:
            return
        S.op('act', [xk], ['ycatT', 'ssq'], lambda e: e.activation(out=junk[:], in_=xb[:], func=AF.Square, accum_out=ssq[:, 0:1]))
        rsq('ssq', ssq[:, 1:2], ssq[:, 0:1], D * 1e-5, ALU.add)
        S.op('dve', [xk, 'ssq'], ['xn'], lambda e: e.tensor_scalar(out=xn[:], in0=xb[:], scalar1=ssq[:, 1:2], scalar2=32.0,
                                                                   op0=ALU.mult, op1=ALU.mult))
        if samp:
            S.dma('sp', hrow[0:1, :], sshift[sq:sq + 1, :], [], ['xm'])
            S.op('act', ['xm'], ['ycatT'], lambda e: e.copy(out=junk[0:1, :], in_=hrow[0:1, :]))
        pT = P[0][:].bitcast(BF16)
        if samp:
            for kc in range(8):
                S.op('pe', ['ycatT', 'c_cstb'], ['P1'], lambda e, kc=kc: e.transpose(out=P[1][:].bitcast(BF16)[:, 2 * kc:2 * kc + 1], in_=junk[0:1, kc * 128:(kc + 1) * 128], identity=ident_b[0:1, 0:1]))
            S.op('dve', ['P1'], ['hTs'], lambda e: e.tensor_copy(out=hTs[:, :, 0], in_=P[1][:].bitcast(BF16)[:, 0:16].rearrange("p (k two) -> p k two", two=2)[:, :, 0]))
        else:
            S.op('dve', ['hT'], ['hTs'], lambda e: e.tensor_copy(out=hTs[:, :, 0], in_=hT[:, :, 127]))
        for kc in range(8):
            S.op('pe', ['xn', 'c_cstb'], ['P0'], lambda e, kc=kc: e.transpose(out=pT[:, kc * 128:(kc + 1) * 128], in_=xn[:, kc * 128:(kc + 1) * 128], identity=ident_b))
        S.op('dve', ['P0', 'c_g1col'], ['hT'], lambda e: e.tensor_tensor(
            out=hT[:], in0=pT.rearrange("p (k t) -> p k t", k=8),
            in1=c_g1col[:].unsqueeze(2).broadcast_to([128, 8, 128]), op=ALU.mult))
        S.op('pool', ['hT'], ['hTs'], lambda e: e.tensor_copy(out=hTs[:, :, 1:128], in_=hT[:, :, 0:127]))
        if samp or ti == NT_P - 1:
            S.op('pool', ['xn', 'c_g1bc'], ['xm'], lambda e: e.tensor_tensor(out=hrow[:], in0=xn[:], in1=c_g1bc[:], op=ALU.mult))
            if samp:
                S.dma('sp', sh_s[sq:sq + 1, :], hrow[3:4, :], ['xm'], [])
            else:
                S.dma('sp', sh_p[0:1, :], hrow[127:128, :], ['xm'], [])
        DUMP("xn", xn, 'xn', ti); DUMP("hT", hT, 'hT', ti); DUMP("hTs", hTs, 'hTs', ti)
        if OPTS['stage'] <= 1:
            return
        def proj(bank, c0, n, dst_reads=()):
            for kc in range(8):
                S.op('pe', ['hT', 'W1'], [PK[bank]], lambda e, kc=kc: e.matmul(out=P[bank][:, 0:n], lhsT=hT[:, kc, :], rhs=W1[:, kc, c0:c0 + n], start=(kc == 0), stop=False))
            for kc in range(8):
                S.op('pe', ['hTs', 'W2'], [PK[bank]], lambda e, kc=kc: e.matmul(out=P[bank][:, 0:n], lhsT=hTs[:, kc, :], rhs=W2[:, kc, c0:c0 + n], start=False, stop=(kc == 7)))
        proj(1, 0, 512)
        S.op('act', ['P1'], ['a_r'], lambda e: e.copy(out=A["r"][:], in_=P[1][:]))
        proj(2, 512, 512)
        S.op('act', ['P2'], ['a_k'], lambda e: e.copy(out=A["k"][:], in_=P[2][:]))
        proj(3, 1024, 512)
        S.op('act', ['P3'], ['a_v'], lambda e: e.copy(out=A["v"][:], in_=P[3][:]))
        S.op('dve', ['a_v', 'c_vm'], ['b_vb'], lambda e: e.tensor_scalar(out=Bt["vb"][:], in0=A["v"][:], scalar1=vm, scalar2=None, op0=ALU.mult))
        for j, c0 in enumerate([1536, 1664]):
            for kc in range(8):
                S.op('pe', ['hT', 'W1'], ['P4'], lambda e, kc=kc, j=j, c0=c0: e.matmul(out=P[4][:, j * 128:(j + 1) * 128], lhsT=W1[:, kc, c0:c0 + 128], rhs=hT[:, kc, :], start=(kc == 0), stop=False))
            for kc in range(8):
                S.op('pe', ['hTs', 'W2'], ['P4'], lambda e, kc=kc, j=j, c0=c0: e.matmul(out=P[4][:, j * 128:(j + 1) * 128], lhsT=W2[:, kc, c0:c0 + 128], rhs=hTs[:, kc, :], start=False, stop=(kc == 7)))
        S.op('act', ['P4'], ['lin'], lambda e: e.activation(out=lin[0:64, 0, :], in_=P[4][0:64, 0:128], func=AF.Tanh))
        S.op('act', ['P4'], ['lin'], lambda e: e.copy(out=lin[64:128, 0, :], in_=P[4][64:128, 0:128]))
        S.op('act', ['P4'], ['lin'], lambda e: e.activation(out=lin[:, 1, :], in_=P[4][:, 128:256], func=AF.Sigmoid))
        for kc in range(8):
            S.op('pe', ['hT', 'Wat'], ['P5'], lambda e, kc=kc: e.matmul(out=P[5][:], lhsT=hT[:, kc, :], rhs=Wat[:, kc, 0:512], start=(kc == 0), stop=(kc == 7)))
        for kc in range(8):
            S.op('pe', ['hT', 'Wat'], ['P6'], lambda e, kc=kc: e.matmul(out=P[6][:, 0:256], lhsT=hT[:, kc, :], rhs=Wat[:, kc, 512:768], start=(kc == 0), stop=(kc == 7)))
        S.op('act', ['P5'], ['qkv'], lambda e: e.copy(out=qkv[:, 0:512], in_=P[5][:]))
        S.op('act', ['P6'], ['qkv'], lambda e: e.copy(out=qkv[:, 512:768], in_=P[6][:, 0:256]))
        DUMP("a_r", A["r"], 'a_r', ti); DUMP("a_v", A["v"], 'a_v', ti); DUMP("lin", lin, 'lin', ti); DUMP("qkv0", qkv, 'qkv', ti)
        if OPTS['stage'] <= 2:
            return
        S.op('pe', ['lin', 'L_w2'], ['P1'], lambda e: e.matmul(out=P[1][:], lhsT=lin[0:64, 0, :], rhs=L_w2[0:64, :], start=True, stop=True))
        S.op('pe', ['lin', 'L_w2'], ['P2'], lambda e: e.matmul(out=P[2][:], lhsT=lin[64:128, 0, :], rhs=L_w2[64:128, :], start=True, stop=True))
        S.op('pe', ['lin', 'L_g2'], ['P3'], lambda e: e.matmul(out=P[3][:], lhsT=lin[:, 1, :], rhs=L_g2[:], start=True, stop=True))
        S.op('dve', ['P1', 'c_v512'], ['a_t1'], lambda e: e.tensor_tensor(out=A["t1"][:], in0=P[1][:], in1=V512(W0), op=ALU.add))
        S.op('act', ['a_t1'], ['a_t1'], lambda e: e.activation(out=A["t1"][:], in_=A["t1"][:], func=AF.Sigmoid))
        S.op('dve', ['a_t1', 'c_vm'], ['a_ld'], lambda e: e.tensor_scalar(out=A["ld"][:], in0=A["t1"][:], scalar1=vm, scalar2=-0.6065306597,
                                                                           op0=ALU.mult, op1=ALU.mult))
        S.op('dve', ['P2', 'c_v512'], ['a_t2'], lambda e: e.tensor_tensor(out=A["t2"][:], in0=P[2][:], in1=V512(A0), op=ALU.add))
        S.op('act', ['a_t2'], ['a_asig'], lambda e: e.activation(out=A["asig"][:], in_=A["t2"][:], func=AF.Sigmoid))
        S.op('act', ['P3'], ['a_g'], lambda e: e.copy(out=A["g"][:], in_=P[3][:]))
        S.op('pool', ['a_k', 'c_v512'], ['a_kk'], lambda e: e.tensor_tensor(out=A["kk"][:], in0=A["k"][:], in1=V512(KK), op=ALU.mult))
        S.op('pool', ['a_kk'], ['a_t3'], lambda e: e.tensor_tensor(out=A["t3"][:], in0=A["kk"][:], in1=A["kk"][:], op=ALU.mult))
        S.op('dve', ['a_t3'], ['st8'], lambda e: e.tensor_reduce(out=st8[:, :, 0], in_=v3(A["t3"][:]), axis=AX.X, op=ALU.add))
        rsq('st8', st8[:, :, 1], st8[:, :, 0], 1e-24, ALU.max)
        S.op('dve', ['a_kk', 'st8'], ['a_kk'], lambda e: e.tensor_tensor(out=v3(A["kk"][:]), in0=v3(A["kk"][:]), in1=bc_last(st8[:, :, 1:2], 64), op=ALU.mult))
        S.op('dve', ['a_kk', 'a_asig', 'c_vm'], ['a_b'], lambda e: e.scalar_tensor_tensor(out=A["b"][:], in0=A["kk"][:], scalar=vm, in1=A["asig"][:], op0=ALU.mult, op1=ALU.mult))
        S.op('dve', ['a_asig', 'c_v512'], ['a_t2'], lambda e: e.scalar_tensor_tensor(out=A["t2"][:], in0=A["asig"][:], scalar=-1.0, in1=V512(KA), op0=ALU.add, op1=ALU.mult))
        S.op('dve', ['a_t2', 'a_k'], ['a_kmod'], lambda e: e.scalar_tensor_tensor(out=A["kmod"][:], in0=A["t2"][:], scalar=1.0, in1=A["k"][:], op0=ALU.add, op1=ALU.mult))
        S.op('pe', ['a_ld', 'c_cst'], ['P1'], lambda e: e.matmul(out=P[1][:], lhsT=tri_f, rhs=A["ld"][:], start=True, stop=True))
        S.op('pe', ['a_ld', 'c_cst'], ['P2'], lambda e: e.matmul(out=P[2][:], lhsT=ones_f, rhs=A["ld"][:], start=True, stop=True))
        for h in range(8):
            S.op('pe', ['a_ld', 'c_cst'], ['P3'], lambda e, h=h: e.matmul(out=P[3][0:64, h:h + 1], lhsT=A["ld"][:, h * 64:(h + 1) * 64], rhs=ones_f[:, 0:1], start=True, stop=True))
        S.op('act', ['P3'], ['ecl_fm'], lambda e: e.activation(out=ecl_fm[:], in_=P[3][0:64, 0:8], func=AF.Exp))
        S.op('act', ['P1'], ['a_c'], lambda e: e.copy(out=A["c"][:], in_=P[1][:]))
        S.op('act', ['a_c'], ['a_t1'], lambda e: e.activation(out=A["t1"][:], in_=A["c"][:], func=AF.Exp))
        S.op('dve', ['a_t1', 'a_r'], ['b_rt'], lambda e: e.tensor_tensor(out=Bt["rt"][:], in0=A["r"][:], in1=A["t1"][:], op=ALU.mult))
        S.op('pool', ['a_c', 'a_ld'], ['a_t2'], lambda e: e.tensor_tensor(out=A["t2"][:], in0=A["c"][:], in1=A["ld"][:], op=ALU.subtract))
        S.op('act', ['a_t2'], ['a_t2'], lambda e: e.activation(out=A["t2"][:], in_=A["t2"][:], func=AF.Exp))
        S.op('dve', ['a_t2', 'a_kk'], ['b_at'], lambda e: e.scalar_tensor_tensor(out=Bt["at"][:], in0=A["kk"][:], scalar=-1.0, in1=A["t2"][:], op0=ALU.mult, op1=ALU.mult))
        S.op('act', ['a_c'], ['a_t3'], lambda e: e.activation(out=A["t3"][:], in_=A["c"][:], func=AF.Exp, scale=-1.0))
        S.op('dve', ['a_t3', 'a_b'], ['b_bt'], lambda e: e.tensor_tensor(out=Bt["bt"][:], in0=A["b"][:], in1=A["t3"][:], op=ALU.mult))
        S.op('pool', ['a_t3', 'a_kmod'], ['b_kt'], lambda e: e.tensor_tensor(out=Bt["kt"][:], in0=A["kmod"][:], in1=A["t3"][:], op=ALU.mult))
        S.op('dve', ['P2', 'a_c'], ['a_t1'], lambda e: e.tensor_tensor(out=A["t1"][:], in0=P[2][:], in1=A["c"][:], op=ALU.subtract))
        S.op('act', ['a_t1'], ['a_t1'], lambda e: e.activation(out=A["t1"][:], in_=A["t1"][:], func=AF.Exp))
        S.op('dve', ['a_t1', 'a_b'], ['b_bbar'], lambda e: e.tensor_tensor(out=Bt["bbar"][:], in0=A["b"][:], in1=A["t1"][:], op=ALU.mult))
        S.op('pool', ['a_t1', 'a_kmod'], ['b_kbar'], lambda e: e.tensor_tensor(out=Bt["kbar"][:], in0=A["kmod"][:], in1=A["t1"][:], op=ALU.mult))
        S.op('pool', ['a_r', 'c_v512'], ['a_t2'], lambda e: e.tensor_tensor(out=A["t2"][:], in0=A["r"][:], in1=V512(RK), op=ALU.mult))
        S.op('pool', ['a_t2', 'a_kmod'], ['a_t2'], lambda e: e.tensor_tensor(out=A["t2"][:], in0=A["t2"][:], in1=A["kmod"][:], op=ALU.mult))
        S.op('dve', ['a_t2'], ['st8'], lambda e: e.tensor_reduce(out=st8[:, :, 2], in_=v3(A["t2"][:]), axis=AX.X, op=ALU.add))
        DUMP("a_ld", A["ld"], 'a_ld', ti); DUMP("a_c", A["c"], 'a_c', ti); DUMP("a_kk", A["kk"], 'a_kk', ti); DUMP("b_rt", Bt["rt"], 'b_rt', ti); DUMP("b_at", Bt["at"], 'b_at', ti); DUMP("b_bt", Bt["bt"], 'b_bt', ti); DUMP("b_kbar", Bt["kbar"], 'b_kbar', ti); DUMP("ecl_fm", ecl_fm, 'ecl_fm', ti)
        if OPTS['stage'] <= 3:
            return
        pTb = [P[i][:].bitcast(BF16) for i in range(8)]
        for qi, (nm, bank) in enumerate([("at", 4), ("rt", 5), ("bt", 6), ("kt", 7)]):
            for h in range(8):
                S.op('pe', ['b_' + nm, 'c_cstb'], [PK[bank]], lambda e, h=h, nm=nm, bank=bank: e.transpose(
                    out=pTb[bank][0:64, h * 128:(h + 1) * 128], in_=Bt[nm][:, h * 64:(h + 1) * 64], identity=ident_b))
        S.op('act', ['P4'], ['AR_fm'], lambda e: e.copy(out=AR_fm[:, :, 0:128], in_=pTb[4][0:64, 0:1024].rearrange("p (h t) -> p h t", h=8)))
        S.op('dve', ['P5'], ['AR_fm'], lambda e: e.tensor_copy(out=AR_fm[:, :, 128:256], in_=pTb[5][0:64, 0:1024].rearrange("p (h t) -> p h t", h=8)))
        S.op('act', ['P6'], ['B_fm'], lambda e: e.copy(out=B_fm[:], in_=pTb[6][0:64, 0:1024].rearrange("p (h t) -> p h t", h=8)))
        S.op('dve', ['P7'], ['K_fm'], lambda e: e.tensor_copy(out=K_fm[:], in_=pTb[7][0:64, 0:1024].rearrange("p (h t) -> p h t", h=8)))
        for h in range(8):
            bank = h % 2
            S.op('pe', ['B_fm', 'AR_fm'], [PK[bank]], lambda e, h=h, bank=bank: e.matmul(out=P[bank][:, 0:256], lhsT=B_fm[:, h, :], rhs=AR_fm[:, h, :], start=True, stop=True))
            S.op('pe', ['K_fm', 'AR_fm'], [PK[bank]], lambda e, h=h, bank=bank: e.matmul(out=P[bank][:, 256:512], lhsT=K_fm[:, h, :], rhs=AR_fm[:, h, :], start=True, stop=True))
            S.op('dve', [PK[bank], 'c_m4'], ['MATS'], lambda e, h=h, bank=bank: e.tensor_tensor(out=MATS[:, h, :], in0=P[bank][:], in1=c_m4[:].rearrange("p a b -> p (a b)"), op=ALU.mult))
        for hh in range(2):
            bank = 2 + hh
            for h4 in range(4):
                h = hh * 4 + h4
                S.op('pe', ['B_fm', 'AR_fm'], [PK[bank]], lambda e, h=h, h4=h4, bank=bank: e.matmul(out=P[bank][:, h4 * 128:(h4 + 1) * 128], lhsT=AR_fm[:, h, 0:128], rhs=B_fm[:, h, :], start=True, stop=True))
            S.op('dve', [PK[bank], 'c_cst'], ['MTb'], lambda e, hh=hh, bank=bank: e.tensor_tensor(
                out=MTb[0][:, hh * 4:(hh + 1) * 4, :], in0=P[bank][:].rearrange("p (h t) -> p h t", h=4),
                in1=c_low.unsqueeze(1).broadcast_to([128, 4, 128]), op=ALU.mult))
        S.op('act', ['MATS'], ['Mb'], lambda e: e.copy(out=Mb[0][:], in_=MATS[:, :, 0:128]))
        S.op('pool', ['MATS', 'c_cstb'], ['Tb'], lambda e: e.tensor_tensor(out=Tb[0][:], in0=MATS[:, :, 0:128], in1=ident_b.unsqueeze(1).broadcast_to([128, 8, 128]), op=ALU.add))
        cur = 0
        for lvl in range(1, 7):
            nxt = 1 - cur
            for hh in range(2):
                hs = slice(hh * 4, hh * 4 + 4)
                bM, bMT, bT = 2 + hh * 3, 3 + hh * 3, 4 + hh * 3
                for h4 in range(4):
                    h = hh * 4 + h4
                    cs = slice(h4 * 128, (h4 + 1) * 128)
                    if lvl < 6:
                        S.op('pe', ['Mb', 'MTb'], [PK[bM]], lambda e, h=h, cs=cs, bM=bM, cur=cur: e.matmul(out=P[bM][:, cs], lhsT=MTb[cur][:, h, :], rhs=Mb[cur][:, h, :], start=True, stop=True))
                    S.op('pe', ['Mb', 'MTb'], [PK[bMT]], lambda e, h=h, cs=cs, bMT=bMT, cur=cur: e.matmul(out=P[bMT][:, cs], lhsT=Mb[cur][:, h, :], rhs=MTb[cur][:, h, :], start=True, stop=True))
                if lvl < 6:
                    S.op('act', [PK[bM]], ['Mb'], lambda e, hs=hs, bM=bM, nxt=nxt: e.copy(out=Mb[nxt][:, hs, :], in_=P[bM][:].rearrange("p (h t) -> p h t", h=4)))
                S.op('dve', [PK[bMT]], ['MTb'], lambda e, hs=hs, bMT=bMT, nxt=nxt: e.tensor_copy(out=MTb[nxt][:, hs, :], in_=P[bMT][:].rearrange("p (h t) -> p h t", h=4)))
                for h4 in range(4):
                    h = hh * 4 + h4
                    cs = slice(h4 * 128, (h4 + 1) * 128)
                    S.op('pe', ['MTb', 'Tb'], [PK[bT]], lambda e, h=h, cs=cs, bT=bT, cur=cur, nxt=nxt: e.matmul(out=P[bT][:, cs], lhsT=MTb[nxt][:, h, :], rhs=Tb[cur][:, h, :], start=True, stop=True))
                S.op('dve', [PK[bT], 'Tb'], ['Tb'], lambda e, hs=hs, bT=bT, cur=cur, nxt=nxt: e.tensor_tensor(
                    out=Tb[nxt][:, hs, :], in0=P[bT][:].rearrange("p (h t) -> p h t", h=4), in1=Tb[cur][:, hs, :], op=ALU.add))
            cur = nxt
        Tf = Tb[cur]
        TfK = 'Tb'
        DUMP("AR_fm", AR_fm, 'AR_fm', ti); DUMP("K_fm", K_fm, 'K_fm', ti); DUMP("MATS", MATS, 'MATS', ti); DUMP("Tb", Tb[0], 'Tb', ti); DUMP("MTb", MTb[0], 'MTb', ti)
        if OPTS['stage'] <= 4:
            return
        if samp or ti == 0:
            if samp:
                S.dma('sp', sti[:], swkv[sq].rearrange("h i j -> i h j"), [], ['sti'])
                for h in range(8):
                    S.op('pe', ['sti', 'c_cst'], ['P0'], lambda e, h=h: e.transpose(out=P[0][0:64, h * 64:(h + 1) * 64], in_=sti[:, h, :], identity=ident_f[0:64, 0:64]))
                S.op('dve', ['P0'], ['S32'], lambda e: e.tensor_copy(out=S32[:], in_=P[0][0:64, :].rearrange("p (h i) -> p h i", h=8)))
            else:
                S.op('dve', [], ['S32'], lambda e: e.memset(S32[:], 0.0))
            S.op('act', ['S32'], ['Sb'], lambda e: e.copy(out=Sb[:], in_=S32[:]))
        for h in range(8):
            cs = slice(h * 64, (h + 1) * 64)
            S.op('pe', ['AR_fm', 'Sb'], ['P0'], lambda e, h=h, cs=cs: e.matmul(out=P[0][:, cs], lhsT=AR_fm[:, h, 0:128], rhs=Sb[:, h, :], start=True, stop=False))
            S.op('pe', ['MATS', 'b_vb'], ['P0'], lambda e, h=h, cs=cs: e.matmul(out=P[0][:, cs], lhsT=MATS[:, h, 256:384], rhs=Bt["vb"][:, cs], start=False, stop=True))
        S.op('act', ['P0'], ['b_W0T'], lambda e: e.copy(out=Bt["W0T"][:], in_=P[0][:]))
        for h in range(8):
            cs = slice(h * 64, (h + 1) * 64)
            S.op('pe', [TfK, 'b_W0T'], ['P1'], lambda e, h=h, cs=cs: e.matmul(out=P[1][:, cs], lhsT=Tf[:, h, :], rhs=Bt["W0T"][:, cs], start=True, stop=True))
        S.op('act', ['P1'], ['b_UT'], lambda e: e.copy(out=Bt["UT"][:], in_=P[1][:]))
        for h in range(8):
            cs = slice(h * 64, (h + 1) * 64)
            S.op('pe', ['AR_fm', 'Sb'], ['P0'], lambda e, h=h, cs=cs: e.matmul(out=P[0][:, cs], lhsT=AR_fm[:, h, 128:256], rhs=Sb[:, h, :], start=True, stop=False))
            S.op('pe', ['MATS', 'b_UT'], ['P0'], lambda e, h=h, cs=cs: e.matmul(out=P[0][:, cs], lhsT=MATS[:, h, 128:256], rhs=Bt["UT"][:, cs], start=False, stop=False))
            S.op('pe', ['MATS', 'b_vb'], ['P0'], lambda e, h=h, cs=cs: e.matmul(out=P[0][:, cs], lhsT=MATS[:, h, 384:512], rhs=Bt["vb"][:, cs], start=False, stop=True))
        for h in range(8):
            cs = slice(h * 64, (h + 1) * 64)
            S.op('pe', ['b_bbar', 'b_UT'], ['P1'], lambda e, h=h, cs=cs: e.matmul(out=P[1][0:64, cs], lhsT=Bt["bbar"][:, cs], rhs=Bt["UT"][:, cs], start=True, stop=False))
            S.op('pe', ['b_kbar', 'b_vb'], ['P1'], lambda e, h=h, cs=cs: e.matmul(out=P[1][0:64, cs], lhsT=Bt["kbar"][:, cs], rhs=Bt["vb"][:, cs], start=False, stop=True))
        S.op('dve', ['S32', 'ecl_fm'], ['S32'], lambda e: e.tensor_tensor(out=S32[:], in0=S32[:], in1=ecl_fm[:].unsqueeze(2).broadcast_to([64, 8, 64]), op=ALU.mult))
        S.op('dve', ['S32', 'P1'], ['S32'], lambda e: e.tensor_tensor(out=S32[:], in0=S32[:], in1=P[1][0:64, :].rearrange("p (h i) -> p h i", h=8), op=ALU.add))
        S.op('act', ['S32'], ['Sb'], lambda e: e.copy(out=Sb[:], in_=S32[:]))
        if samp or ti == NT_P - 1:
            for h in range(8):
                S.op('pe', ['S32', 'c_cst'], ['P2'], lambda e, h=h: e.transpose(out=P[2][0:64, h * 64:(h + 1) * 64], in_=S32[:, h, :], identity=ident_f[0:64, 0:64]))
            S.op('act', ['P2'], ['sti'], lambda e: e.copy(out=sti[:], in_=P[2][0:64, :].rearrange("p (h j) -> p h j", h=8)))
            dst = wkv_s[sq] if samp else wkv_p
            S.dma('sp', dst.rearrange("h i j -> i h j"), sti[:], ['sti'], [])
        DUMP("S32", S32, 'S32', ti); DUMP("b_UT", Bt["UT"], 'b_UT', ti); DUMP("b_W0T", Bt["W0T"], 'b_W0T', ti)
        if OPTS['stage'] <= 5:
            return
        Y3 = v3(P[0][:])
        S.op('dve', ['P0'], ['st8'], lambda e: e.tensor_reduce(out=st8[:, :, 0], in_=Y3, axis=AX.X, op=ALU.add))
        S.op('dve', ['st8'], ['st8'], lambda e: e.tensor_scalar(out=st8[:, :, 0], in0=st8[:, :, 0], scalar1=1.0 / 64, scalar2=None, op0=ALU.mult))
        S.op('dve', ['P0', 'st8'], ['a_t1'], lambda e: e.tensor_tensor(out=v3(A["t1"][:]), in0=Y3, in1=bc_last(st8[:, :, 0:1], 64), op=ALU.subtract))
        S.op('pool', ['a_t1'], ['a_t2'], lambda e: e.tensor_tensor(out=A["t2"][:], in0=A["t1"][:], in1=A["t1"][:], op=ALU.mult))
        S.op('dve', ['a_t2'], ['st8'], lambda e: e.tensor_reduce(out=st8[:, :, 1], in_=v3(A["t2"][:]), axis=AX.X, op=ALU.add))
        S.op('dve', ['st8'], ['st8'], lambda e: e.tensor_scalar(out=st8[:, :, 1], in0=st8[:, :, 1], scalar1=1.0 / 64, scalar2=64e-5, op0=ALU.mult, op1=ALU.add))
        rsq('st8', st8[:, :, 1], st8[:, :, 1], 0.0, ALU.add)
        S.op('dve', ['a_t1', 'st8'], ['a_t1'], lambda e: e.tensor_tensor(out=v3(A["t1"][:]), in0=v3(A["t1"][:]), in1=bc_last(st8[:, :, 1:2], 64), op=ALU.mult))
        S.op('pool', ['a_t1', 'c_v512'], ['a_t1'], lambda e: e.tensor_tensor(out=A["t1"][:], in0=A["t1"][:], in1=V512(LW), op=ALU.mult))
        S.op('pool', ['a_t1', 'c_v512'], ['a_t1'], lambda e: e.tensor_tensor(out=A["t1"][:], in0=A["t1"][:], in1=V512(LB), op=ALU.add))
        S.op('dve', ['a_v', 'st8'], ['a_t2'], lambda e: e.tensor_tensor(out=v3(A["t2"][:]), in0=v3(A["v"][:]), in1=bc_last(st8[:, :, 2:3], 64), op=ALU.mult))
        S.op('pool', ['a_t1', 'a_t2'], ['a_t1'], lambda e: e.tensor_tensor(out=A["t1"][:], in0=A["t1"][:], in1=A["t2"][:], op=ALU.add))
        S.op('dve', ['a_t1', 'a_g'], ['ycat'], lambda e: e.tensor_tensor(out=ycat[:, 0:512], in0=A["t1"][:], in1=A["g"][:], op=ALU.mult))
        DUMP("ycat_rw", ycat[:, 0:512], 'ycat', ti)
        if OPTS['stage'] <= 6:
            return
        ri = (NPT if samp else (ti + (NPT - NT_P) * 0))
        cosb = c_rope[:, ri, 0:8].unsqueeze(1)
        sinb = c_rope[:, ri, 8:16].unsqueeze(1)
        for (c0, nh) in [(0, 8), (512, 2)]:
            X = qkv[:, c0:c0 + nh * 64].rearrange("p (h j) -> p h j", h=nh)
            x1, x2 = X[:, :, 0:8], X[:, :, 8:16]
            cb = cosb.broadcast_to([128, nh, 8])
            sbb = sinb.broadcast_to([128, nh, 8])
            R = rtmp[:, 0:nh, :]
            T1 = rtmp[:, 0:nh, :]
            ra = rtmp[:].rearrange("p a b -> p (a b)")
            t_a = ra[:, 0:nh * 8].rearrange("p (h j) -> p h j", h=nh)
            t_b = ra[:, 80 - 0:80].rearrange("p (h j) -> p h j", h=1) if False else None
            S.op('dve', ['qkv', 'c_rope'], ['rtmp'], lambda e, x1=x1, cb=cb, t_a=t_a: e.tensor_tensor(out=t_a, in0=x1, in1=cb, op=ALU.mult))
            S.op('dve', ['qkv', 'c_rope'], ['sm'], lambda e, x2=x2, sbb=sbb, nh=nh: e.tensor_tensor(out=sm[:, 0:nh * 8].rearrange("p (h j) -> p h j", h=nh), in0=x2, in1=sbb, op=ALU.mult))
            S.op('dve', ['qkv', 'c_rope'], ['sm'], lambda e, x2=x2, cb=cb, nh=nh: e.tensor_tensor(out=sm[:, 64:64 + nh * 8].rearrange("p (h j) -> p h j", h=nh), in0=x2, in1=cb, op=ALU.mult))
            S.op('dve', ['qkv', 'c_rope'], ['sm'], lambda e, x1=x1, sbb=sbb, nh=nh: e.tensor_tensor(out=sm[:, 128:128 + nh * 8].rearrange("p (h j) -> p h j", h=nh), in0=x1, in1=sbb, op=ALU.mult))
            S.op('dve', ['rtmp', 'sm'], ['qkv'], lambda e, x1=x1, t_a=t_a, nh=nh: e.tensor_tensor(out=x1, in0=t_a, in1=sm[:, 0:nh * 8].rearrange("p (h j) -> p h j", h=nh), op=ALU.subtract))
            S.op('dve', ['sm'], ['qkv'], lambda e, x2=x2, nh=nh: e.tensor_tensor(out=x2, in0=sm[:, 64:64 + nh * 8].rearrange("p (h j) -> p h j", h=nh), in1=sm[:, 128:128 + nh * 8].rearrange("p (h j) -> p h j", h=nh), op=ALU.add))
        S.op('act', ['qkv'], ['qb'], lambda e: e.copy(out=qb[:], in_=qkv[:]))
        S.op('pool', ['qkv'], ['Vb%d' % par], lambda e: e.tensor_copy(out=Vb[par][:], in_=qkv[:, 640:768]))
        for h in range(8):
            S.op('pe', ['qb', 'c_cstb'], ['P2'], lambda e, h=h: e.transpose(out=pTb[2][0:64, h * 128:(h + 1) * 128], in_=qb[:, h * 64:(h + 1) * 64], identity=ident_b))
        for kv in range(2):
            S.op('pe', ['qb', 'c_cstb'], ['P3'], lambda e, kv=kv: e.transpose(out=pTb[3][0:64, kv * 128:(kv + 1) * 128], in_=qb[:, 512 + kv * 64:512 + (kv + 1) * 64], identity=ident_b))
        S.op('act', ['P2'], ['QT'], lambda e: e.copy(out=QT[:], in_=pTb[2][0:64, 0:1024].rearrange("p (h t) -> p h t", h=8)))
        S.op('dve', ['P3'], ['KT%d' % par], lambda e: e.tensor_copy(out=KT[par][:], in_=pTb[3][0:64, 0:256].rearrange("p (h t) -> p h t", h=2)))
        pp = 1 - par
        if samp:
            S.dma('sp', ckb[:, 0:128], ck[sq], [], ['sm'])
            S.dma('sp', ckb[:, 128:256], cv[sq], [], ['sm'])
            S.op('act', ['sm'], ['ycatT'], lambda e: e.copy(out=junk[:, 0:256], in_=ckb[:]))
            for kv in range(2):
                S.op('pe', ['ycatT', 'c_cstb'], ['P3'], lambda e, kv=kv: e.transpose(out=pTb[3][0:64, 256 + kv * 128:256 + (kv + 1) * 128], in_=junk[:, kv * 64:(kv + 1) * 64], identity=ident_b))
            S.op('dve', ['P3'], ['KT%d' % pp], lambda e: e.tensor_copy(out=KT[pp][:], in_=pTb[3][0:64, 256:512].rearrange("p (h t) -> p h t", h=2)))
            S.op('pool', ['ycatT'], ['Vb%d' % pp], lambda e: e.tensor_copy(out=Vb[pp][:], in_=junk[:, 128:256]))
            S.dma('sp', kw_s[sq, 0:124, :], ck[sq, 4:128, :], [], [])
            S.dma('sp', vw_s[sq, 0:124, :], cv[sq, 4:128, :], [], [])
            S.dma('sp', kw_s[sq, 124:128, :], qkv[0:4, 512:640], ['qkv'], [])
            S.dma('sp', vw_s[sq, 124:128, :], qkv[0:4, 640:768], ['qkv'], [])
        elif ti == NT_P - 1:
            S.dma('sp', kw_p, qkv[:, 512:640], ['qkv'], [])
            S.dma('sp', vw_p, qkv[:, 640:768], ['qkv'], [])
        mi = 0 if (samp or ti > 1) else (1 if ti == 0 else 2)
        for h in range(8):
            kv = h // 4
            bank = 4 + (h % 2)
            S.op('pe', ['QT', 'KT%d' % pp], [PK[bank]], lambda e, h=h, kv=kv, bank=bank: e.matmul(out=P[bank][:, 0:128], lhsT=QT[:, h, :], rhs=KT[pp][:, kv, :], start=True, stop=True))
            S.op('pe', ['QT', 'KT%d' % par], [PK[bank]], lambda e, h=h, kv=kv, bank=bank: e.matmul(out=P[bank][:, 128:256], lhsT=QT[:, h, :], rhs=KT[par][:, kv, :], start=True, stop=True))
            S.op('dve', [PK[bank], 'c_amask'], ['sm'], lambda e, bank=bank: e.scalar_tensor_tensor(out=sm[:], in0=P[bank][:, 0:256], scalar=0.125, in1=c_amask[:, mi, :], op0=ALU.mult, op1=ALU.add))
            S.op('dve', ['sm'], ['ast'], lambda e: e.tensor_reduce(out=ast[:, 0:1], in_=sm[:], axis=AX.X, op=ALU.max))
            S.op('dve', ['ast', 'c_sink'], ['ast'], lambda e, h=h: e.tensor_scalar(out=ast[:, 1:2], in0=ast[:, 0:1], scalar1=c_sink[:, h:h + 1], scalar2=-1.0, op0=ALU.max, op1=ALU.mult))
            S.op('act', ['sm', 'ast'], ['eb', 'ast'], lambda e: e.activation(out=eb[:], in_=sm[:], func=AF.Exp, bias=ast[:, 1:2], scale=1.0, accum_out=ast[:, 2:3]))
            S.op('act', ['ast', 'c_sink'], ['ast'], lambda e, h=h: e.activation(out=ast[:, 3:4], in_=c_sink[:, h:h + 1], func=AF.Exp, bias=ast[:, 1:2], scale=1.0))
            S.op('dve', ['ast'], ['ast'], lambda e: e.tensor_tensor(out=ast[:, 4:5], in0=ast[:, 2:3], in1=ast[:, 3:4], op=ALU.add))
            S.op('dve', ['ast'], ['ast'], lambda e: e.reciprocal(out=ast[:, 5:6], in_=ast[:, 4:5]))
            for half in range(2):
                S.op('pe', ['eb', 'c_cstb'], ['P6'], lambda e, half=half: e.transpose(out=pTb[6][:, half * 128:(half + 1) * 128], in_=eb[:, half * 128:(half + 1) * 128], identity=ident_b))
            S.op('act', ['P6'], ['eT'], lambda e: e.copy(out=eT[:], in_=pTb[6][:, 0:256].rearrange("p (a t) -> p a t", a=2)))
            S.op('pe', ['eT', 'Vb%d' % pp], ['P7'], lambda e, kv=kv: e.matmul(out=P[7][:, 0:64], lhsT=eT[:, 0, :], rhs=Vb[pp][:, kv * 64:(kv + 1) * 64], start=True, stop=False))
            S.op('pe', ['eT', 'Vb%d' % par], ['P7'], lambda e, kv=kv: e.matmul(out=P[7][:, 0:64], lhsT=eT[:, 1, :], rhs=Vb[par][:, kv * 64:(kv + 1) * 64], start=False, stop=True))
            S.op('dve', ['P7', 'ast'], ['ycat'], lambda e, h=h: e.tensor_scalar(out=ycat[:, 512 + h * 64:512 + (h + 1) * 64], in0=P[7][:, 0:64], scalar1=ast[:, 5:6], scalar2=None, op0=ALU.mult))
        DUMP("ycat", ycat, 'ycat', ti); DUMP("qkv", qkv, 'qkv', ti); DUMP("QT", QT, 'QT', ti)
        if OPTS['stage'] <= 7:
            return
        for kc in range(8):
            S.op('pe', ['ycat', 'c_cstb'], ['P2'], lambda e, kc=kc: e.transpose(out=pTb[2][:, kc * 128:(kc + 1) * 128], in_=ycat[:, kc * 128:(kc + 1) * 128], identity=ident_b))
        S.op('act', ['P2'], ['ycatT'], lambda e: e.copy(out=ycatT[:], in_=pTb[2][:, 0:1024].rearrange("p (k t) -> p k t", k=8)))
        for half in range(2):
            bank = 3 + half
            for kc in range(8):
                S.op('pe', ['ycatT', 'Wo'], [PK[bank]], lambda e, kc=kc, half=half, bank=bank: e.matmul(out=P[bank][:], lhsT=ycatT[:, kc, :], rhs=Wo[:, kc, half * 512:(half + 1) * 512], start=(kc == 0), stop=(kc == 7)))
            S.op('dve', [PK[bank], xk], ['xm'], lambda e, half=half, bank=bank: e.tensor_tensor(out=xm[:, half * 512:(half + 1) * 512], in0=P[bank][:], in1=xb[:, half * 512:(half + 1) * 512], op=ALU.add))
        if dbg and ti == OPTS['dbg_tile']:
            S.op('dve', ['ycat'], ['a_t1'], lambda e: e.tensor_copy(out=A["t1"][:], in_=ycat[:, 0:512]))
            S.op('dve', ['ycat'], ['a_t2'], lambda e: e.tensor_copy(out=A["t2"][:], in_=ycat[:, 512:1024]))
            S.dma('sp', dbg_t["d_y"], A["t1"][:], ['a_t1'], [])
            S.dma('sp', dbg_t["d_at"], A["t2"][:], ['a_t2'], [])
            S.dma('sp', dbg_t["d_S"], S32[:].rearrange("p h i -> p (h i)"), ['S32'], [])
        DUMP("xm", xm, 'xm', ti)
        if samp:
            S.dma('sp', xmid[NT_P * 128 + sq * 4:NT_P * 128 + sq * 4 + 4, :], xm[0:4, :], ['xm'], ['xmid'])
        else:
            S.dma('sp', xmid[ti * 128:(ti + 1) * 128, :], xm[:], ['xm'], ['xmid'])


    def peer_phase():
        c_g2bc = sb("c_g2bc", [128, D])
        S.dma('sp', c_g2bc[:], g2.partition_broadcast(128)[:, 0, :], [], ['c_g2bc'])
        c_gFbc = sb("c_gFbc", [128, D])
        S.dma('sp', c_gFbc[:], gF.partition_broadcast(128)[:, 0, :], [], ['c_gFbc'])
        c_iota = sb("c_iota", [128, 256])
        S.dma('sp', c_iota[:], iota_in, [], ['c_iota'])
        Wq = sb("Wq", [128, 8, 2048], BF16)
        skT = sb("skT", [128, 2, 128], BF16)
        xm2 = sb("xm2", [128, D])
        hn32 = sb("hn32", [128, D])
        hnb = sb("hnb", [128, D], BF16)
        hn2T = sb("hn2T", [128, 8, 128], BF16)
        qT = sb("qT", [128, 16, 128], BF16)
        s_sb = sb("s_sb", [128, 16, 128])
        s2 = sb("s2", [128, 16, 128])
        tv = sb("tv", [128, 16, 16])
        tiu = sb("tiu", [128, 16, 16], U32)
        tif = sb("tif", [128, 16, 16])
        cand = sb("cand", [128, 8, 256])
        cand2 = sb("cand2", [128, 8, 256])
        cidx = sb("cidx", [128, 8, 256])
        top = sb("top", [128, 8, 16])
        selu = sb("selu", [128, 8, 16], U32)
        self_ = sb("self", [128, 8, 16])
        idxf = sb("idxf", [128, 8, 16])
        idxu = sb("idxu", [128, 128], U32)
        gate = sb("gate", [128, 8, 16])
        gst = sb("gst", [128, 8, 2])
        pre = sb("pre", [128, 128])
        wgt = sb("wgt", [128, 128])
        acc = sb("acc", [128, D])
        ss2 = sb("ss2", [128, 4])
        NG = 4
        gb = [sb("gb%d" % i, [128, D]) for i in range(NG)]
        with nc.sbuf_tensor("stg2", [128, 2048], F32) as stg2:
            for kc in range(8):
                S.dma('sp', stg2[:], w_q[kc * 128:(kc + 1) * 128, :], [], ['stg2'])
                S.op('act', ['stg2'], ['Wq'], lambda e, kc=kc: e.copy(out=Wq[:, kc, :], in_=stg2[:]))
            S.dma('sp', stg2[:, 0:256].rearrange("p (c d) -> p c d", c=2), subk.rearrange("c n d -> n c d"), [], ['stg2'])
            S.op('act', ['stg2'], ['hnb'], lambda e: e.copy(out=hnb[:, 0:256], in_=stg2[:, 0:256]))
            for c in range(2):
                S.op('pe', ['hnb', 'c_cstb'], ['P0'], lambda e, c=c: e.transpose(out=P[0][:].bitcast(BF16)[:, c * 128:(c + 1) * 128], in_=hnb[:, c * 128:(c + 1) * 128], identity=ident_b))
            S.op('act', ['P0'], ['skT'], lambda e: e.copy(out=skT[:], in_=P[0][:].bitcast(BF16)[:, 0:256].rearrange("p (c n) -> p c n", c=2)))
            barrier()
        pTb = [P[i][:].bitcast(BF16) for i in range(8)]
        for pt in range(NPE):
            samp = pt >= NT_P
            S.dma('sp', xm2[:], xmid[pt * 128:(pt + 1) * 128, :], ['xmid'], ['xm2'])
            S.op('act', ['xm2'], ['hnb', 'ss2'], lambda e: e.activation(out=hnb[:], in_=xm2[:], func=AF.Square, accum_out=ss2[:, 0:1]))
            rsq('ss2', ss2[:, 1:2], ss2[:, 0:1], D * 1e-5, ALU.add)
            S.op('dve', ['xm2', 'ss2'], ['hn32'], lambda e: e.tensor_scalar(out=hn32[:], in0=xm2[:], scalar1=ss2[:, 1:2], scalar2=32.0, op0=ALU.mult, op1=ALU.mult))
            S.op('pool', ['hn32', 'c_g2bc'], ['hn32'], lambda e: e.tensor_tensor(out=hn32[:], in0=hn32[:], in1=c_g2bc[:], op=ALU.mult))
            S.op('act', ['hn32'], ['hnb'], lambda e: e.copy(out=hnb[:], in_=hn32[:]))
            for kc in range(8):
                S.op('pe', ['hnb', 'c_cstb'], ['P0'], lambda e, kc=kc: e.transpose(out=pTb[0][:, kc * 128:(kc + 1) * 128], in_=hnb[:, kc * 128:(kc + 1) * 128], identity=ident_b))
            S.op('act', ['P0'], ['hn2T'], lambda e: e.copy(out=hn2T[:], in_=pTb[0][:, 0:1024].rearrange("p (k t) -> p k t", k=8)))
            for hc in range(16):
                bank = 1 + hc // 4
                cs = slice((hc % 4) * 128, (hc % 4 + 1) * 128)
                for kc in range(8):
                    S.op('pe', ['Wq', 'hn2T'], [PK[bank]], lambda e, hc=hc, kc=kc, bank=bank, cs=cs: e.matmul(out=P[bank][:, cs], lhsT=Wq[:, kc, hc * 128:(hc + 1) * 128], rhs=hn2T[:, kc, :], start=(kc == 0), stop=(kc == 7)))
            for b in range(4):
                S.op('act' if b % 2 else 'dve', [PK[1 + b]], ['qT'], lambda e, b=b: (e.copy if b % 2 else e.tensor_copy)(out=qT[:, b * 4:(b + 1) * 4, :], in_=P[1 + b][:].rearrange("p (a t) -> p a t", a=4)))
            sbanks = [5, 6, 7, 0]
            for hc in range(16):
                bank = sbanks[hc // 4]
                cs = slice((hc % 4) * 128, (hc % 4 + 1) * 128)
                S.op('pe', ['qT', 'skT'], [PK[bank]], lambda e, hc=hc, bank=bank, cs=cs: e.matmul(out=P[bank][:, cs], lhsT=qT[:, hc, :], rhs=skT[:, hc % 2, :], start=True, stop=True))
            for b in range(4):
                S.op('act' if b % 2 else 'dve', [PK[sbanks[b]]], ['s_sb'], lambda e, b=b: (e.copy if b % 2 else e.tensor_copy)(out=s_sb[:, b * 4:(b + 1) * 4, :], in_=P[sbanks[b]][:].rearrange("p (a t) -> p a t", a=4)))
            for hc in range(16):
                S.op('dve', ['s_sb'], ['tv'], lambda e, hc=hc: e.max(out=tv[:, hc, 0:8], in_=s_sb[:, hc, :]))
                S.op('dve', ['s_sb', 'tv'], ['tiu'], lambda e, hc=hc: e.max_index(out=tiu[:, hc, 0:8], in_max=tv[:, hc, 0:8], in_values=s_sb[:, hc, :]))
                S.op('dve', ['s_sb', 'tv'], ['s2'], lambda e, hc=hc: e.match_replace(out=s2[:, hc, :], in_to_replace=tv[:, hc, 0:8], in_values=s_sb[:, hc, :], imm_value=-1e30))
                S.op('dve', ['s2'], ['tv'], lambda e, hc=hc: e.max(out=tv[:, hc, 8:16], in_=s2[:, hc, :]))
                S.op('dve', ['s2', 'tv'], ['tiu'], lambda e, hc=hc: e.max_index(out=tiu[:, hc, 8:16], in_max=tv[:, hc, 8:16], in_values=s2[:, hc, :]))
            S.op('dve', ['tiu'], ['tif'], lambda e: e.tensor_copy(out=tif[:], in_=tiu[:]))
            tv4 = tv[:].rearrange("p (h c) k -> p h c k", c=2)
            tf4 = tif[:].rearrange("p (h c) k -> p h c k", c=2)
            c4 = lambda t: t[:].rearrange("p h (a b) -> p h a b", a=16)
            S.op('dve', ['tv'], ['cand'], lambda e: e.tensor_tensor(out=c4(cand), in0=tv4[:, :, 0, :].unsqueeze(3).broadcast_to([128, 8, 16, 16]),
                                                                    in1=tv4[:, :, 1, :].unsqueeze(2).broadcast_to([128, 8, 16, 16]), op=ALU.add))
            S.op('dve', ['tif'], ['tif'], lambda e: e.tensor_scalar(out=tf4[:, :, 0, :], in0=tf4[:, :, 0, :], scalar1=128.0, scalar2=None, op0=ALU.mult))
            S.op('dve', ['tif'], ['cidx'], lambda e: e.tensor_tensor(out=c4(cidx), in0=tf4[:, :, 0, :].unsqueeze(3).broadcast_to([128, 8, 16, 16]),
                                                                     in1=tf4[:, :, 1, :].unsqueeze(2).broadcast_to([128, 8, 16, 16]), op=ALU.add))
            for h in range(8):
                S.op('dve', ['cand'], ['top'], lambda e, h=h: e.max(out=top[:, h, 0:8], in_=cand[:, h, :]))
                S.op('dve', ['cand', 'top'], ['selu'], lambda e, h=h: e.max_index(out=selu[:, h, 0:8], in_max=top[:, h, 0:8], in_values=cand[:, h, :]))
                S.op('dve', ['cand', 'top'], ['cand2'], lambda e, h=h: e.match_replace(out=cand2[:, h, :], in_to_replace=top[:, h, 0:8], in_values=cand[:, h, :], imm_value=-1e30))
                S.op('dve', ['cand2'], ['top'], lambda e, h=h: e.max(out=top[:, h, 8:16], in_=cand2[:, h, :]))
                S.op('dve', ['cand2', 'top'], ['selu'], lambda e, h=h: e.max_index(out=selu[:, h, 8:16], in_max=top[:, h, 8:16], in_values=cand2[:, h, :]))
            S.op('dve', ['selu'], ['self'], lambda e: e.tensor_copy(out=self_[:], in_=selu[:]))
            for k in range(16):
                S.op('dve', ['self', 'c_iota'], ['cand2'], lambda e, k=k: e.tensor_tensor(out=cand2[:], in0=c_iota[:].unsqueeze(1).broadcast_to([128, 8, 256]),
                                                                                          in1=self_[:, :, k:k + 1].broadcast_to([128, 8, 256]), op=ALU.is_equal))
                S.op('pool', ['cand2', 'cidx'], ['cand2'], lambda e: e.tensor_tensor(out=cand2[:], in0=cand2[:], in1=cidx[:], op=ALU.mult))
                S.op('dve', ['cand2'], ['idxf'], lambda e, k=k: e.tensor_reduce(out=idxf[:, :, k], in_=cand2[:], axis=AX.X, op=ALU.add))
            S.op('dve', ['idxf'], ['idxf'], lambda e: e.tensor_scalar(out=idxf[:], in0=idxf[:], scalar1=0.0, scalar2=float(NEXP - 1), op0=ALU.max, op1=ALU.min))
            S.op('dve', ['idxf'], ['idxu'], lambda e: e.tensor_copy(out=idxu[:], in_=idxf[:].rearrange("p h k -> p (h k)")))
            S.op('dve', ['top'], ['gate'], lambda e: e.tensor_tensor(out=gate[:], in0=top[:], in1=top[:, :, 0:1].broadcast_to([128, 8, 16]), op=ALU.subtract))
            S.op('act', ['gate'], ['gate'], lambda e: e.activation(out=gate[:], in_=gate[:], func=AF.Exp))
            S.op('dve', ['gate'], ['gst'], lambda e: e.tensor_reduce(out=gst[:, :, 0], in_=gate[:], axis=AX.X, op=ALU.add))
            S.op('dve', ['gst'], ['gst'], lambda e: e.reciprocal(out=gst[:, :, 1], in_=gst[:, :, 0]))
            S.op('dve', ['gate', 'gst'], ['gate'], lambda e: e.tensor_tensor(out=gate[:], in0=gate[:], in1=gst[:, :, 1:2].broadcast_to([128, 8, 16]), op=ALU.mult))
            for sl in range(128):
                g = sl % NG
                S.dma('pool', None, None, ['idxu'], ['gb%d' % g], fn=lambda e, sl=sl, g=g: e.indirect_dma_start(
                    out=gb[g][:], out_offset=None, in_=eu, in_offset=bass.IndirectOffsetOnAxis(ap=idxu[:, sl:sl + 1], axis=0)))
                S.op('dve', ['gb%d' % g, 'hn32'], ['gb%d' % g], lambda e, sl=sl, g=g: e.tensor_tensor(out=gb[g][:], in0=gb[g][:], in1=hn32[:], op=ALU.mult))
                S.op('dve', ['gb%d' % g], ['pre'], lambda e, sl=sl, g=g: e.tensor_reduce(out=pre[:, sl:sl + 1], in_=gb[g][:], axis=AX.X, op=ALU.add))
            S.op('act', ['pre'], ['wgt'], lambda e: e.activation(out=wgt[:], in_=pre[:], func=AF.Gelu))
            S.op('dve', ['wgt', 'gate'], ['wgt'], lambda e: e.tensor_tensor(out=wgt[:], in0=wgt[:], in1=gate[:].rearrange("p h k -> p (h k)"), op=ALU.mult))
            S.op('dve', ['xm2'], ['acc'], lambda e: e.tensor_copy(out=acc[:], in_=xm2[:]))
            for sl in range(128):
                g = sl % NG
                S.dma('pool', None, None, ['idxu'], ['gb%d' % g], fn=lambda e, sl=sl, g=g: e.indirect_dma_start(
                    out=gb[g][:], out_offset=None, in_=ev, in_offset=bass.IndirectOffsetOnAxis(ap=idxu[:, sl:sl + 1], axis=0)))
                S.op('dve', ['gb%d' % g, 'wgt', 'acc'], ['acc'], lambda e, sl=sl, g=g: e.scalar_tensor_tensor(
                    out=acc[:], in0=gb[g][:], scalar=wgt[:, sl:sl + 1], in1=acc[:], op0=ALU.mult, op1=ALU.add))
            S.op('act', ['acc'], ['hnb', 'ss2'], lambda e: e.activation(out=hnb[:], in_=acc[:], func=AF.Square, accum_out=ss2[:, 2:3]))
            rsq('ss2', ss2[:, 3:4], ss2[:, 2:3], D * 1e-5, ALU.add)
            S.op('dve', ['acc', 'ss2'], ['acc'], lambda e: e.tensor_scalar(out=acc[:], in0=acc[:], scalar1=ss2[:, 3:4], scalar2=32.0, op0=ALU.mult, op1=ALU.mult))
            S.op('pool', ['acc', 'c_gFbc'], ['acc'], lambda e: e.tensor_tensor(out=acc[:], in0=acc[:], in1=c_gFbc[:], op=ALU.mult))
            if samp:
                S.dma('sp', y_s[:, :], acc[0:NSEQ_S * 4, :], ['acc'], [])
            else:
                S.dma('sp', y_p[pt * 128:(pt + 1) * 128, :], acc[:], ['acc'], [])

    for ti in range(NTT):
        load_x(ti)
        mix_tile(ti)
    print("sbuf left", nc.sbuf_bytes_remaining() if callable(nc.sbuf_bytes_remaining) else nc.sbuf_bytes_remaining)
    print("total ops", getattr(S, 'n', 0))
    barrier()
    ph1.close()
    stk['cur'] = glob_stack
    if OPTS['peer']:
        peer_phase()
    barrier()
    S.finish()
    return nc


_CACHE = {}


def _consts():
    ar = np.arange(128)
    ident = np.eye(128, dtype=np.float32)
    tri = (ar[:, None] <= ar[None, :]).astype(np.float32)
    ones = np.ones((128, 128), np.float32)
    su = (ar[:, None] < ar[None, :]).astype(np.float32)
    lo = (ar[:, None] > ar[None, :]).astype(np.float32)
    cst = np.stack([ident, tri, ones, su, tri, lo], axis=1).astype(np.float32)
    q = ar[:, None]
    c = np.arange(256)[None, :]
    ok = (c > q) & (c <= q + 128)
    m_std = np.where(ok, 0.0, -30000.0).astype(np.float32)
    m_t0 = np.where(ok & (c >= 240), 0.0, -30000.0).astype(np.float32)
    m_t1 = np.where(ok & (c >= 112), 0.0, -30000.0).astype(np.float32)
    cmask = np.stack([m_std, m_t0, m_t1], axis=1)
    inv = (np.float32(500000.0) ** (-np.arange(0, 16, 2, dtype=np.float32) / np.float32(16))).astype(np.float32)
    rope = np.zeros((128, NPT + 1, 16), np.float32)
    for i in range(NPT + 1):
        pos = (i * 128 - 112 + ar) if i < NPT else (PAST + ar)
        ang = pos.astype(np.float32)[:, None] * inv[None, :]
        rope[:, i, 0:8] = np.cos(ang)
        rope[:, i, 8:16] = np.sin(ang)
    vmask = np.zeros((128, 2), np.float32)
    vmask[:, 0] = 1.0
    vmask[0:4, 1] = 1.0
    iota = np.tile(np.arange(256, dtype=np.float32)[None, :], (128, 1))
    return dict(cst=cst, cmask=cmask, rope=rope, vmask=vmask, iota=iota)


def kernel(x_prompt, x_sample, cache_k_win, cache_v_win, state_wkv, state_shift, meta_tokens, norm1_g, w_in, mu_shift,
           w0, w_lora_w2, a0, w_lora_a2, w_lora_g2, k_k, k_a, r_k, lnx_w, lnx_b, attn_sinks, w_out, norm2_g, w_query,
           sub_keys, expert_u, expert_v, final_norm_g, _ntp=NPT, _nts=NSEQ_S, _dbg=False):
    f = lambda a: np.ascontiguousarray(np.asarray(a), dtype=np.float32)
    key = (_ntp, _nts)
    if key not in _CACHE:
        _CACHE[key] = build(_ntp, _nts, dbg=_dbg)
    nc = _CACHE[key]
    cs = _consts()
    x_prompt, x_sample = f(x_prompt), f(x_sample)
    B = x_prompt.shape[0]
    shared = dict(
        w_in=f(w_in)[0], w_out=f(w_out)[0], w_q=f(w_query)[0], subk=f(sub_keys)[0], eu=f(expert_u)[0][:OPTS['nexp']], ev=f(expert_v)[0][:OPTS['nexp']],
        lw2=f(w_lora_w2)[0], la2=f(w_lora_a2)[0], lg2=f(w_lora_g2)[0],
        vec512=np.stack([f(w0)[0], f(a0)[0], f(k_k)[0], f(k_a)[0], f(r_k)[0].reshape(512), f(lnx_w)[0], f(lnx_b)[0]]),
        mu=f(mu_shift), g1=f(norm1_g), g2=f(norm2_g), gF=f(final_norm_g)[None, :], sinks=f(attn_sinks), **cs)
    in_maps = []
    for c in range(NCORES):
        xp = np.zeros((NPT * 128, D), np.float32)
        if c < B:
            xp[112:128] = f(meta_tokens)
            xp[128:] = x_prompt[c]
        sl = slice(c * NSEQ_S, (c + 1) * NSEQ_S)
        m = dict(shared)
        m.update(xp=xp[:max(_ntp, 1) * 128], xs=x_sample[sl].reshape(NSEQ_S * 4, D),
                 ck=f(cache_k_win)[0, sl].reshape(NSEQ_S, 128, 128), cv=f(cache_v_win)[0, sl].reshape(NSEQ_S, 128, 128),
                 swkv=f(state_wkv)[0, sl], sshift=f(state_shift)[0, sl])
        in_maps.append(m)
    res = run_bass_kernel_spmd(nc, in_maps, core_ids=list(range(NCORES))).results
    if _ntp != NPT or _nts != NSEQ_S:
        return res
    y_prompt = np.stack([res[b]["y_p"][128:] for b in range(B)])
    y_sample = np.concatenate([res[c]["y_s"].reshape(NSEQ_S, 4, D) for c in range(NCORES)])
    kwp = np.stack([res[b]["kw_p"].reshape(128, 2, 64) for b in range(B)])[None]
    vwp = np.stack([res[b]["vw_p"].reshape(128, 2, 64) for b in range(B)])[None]
    wkvp = np.stack([res[b]["wkv_p"] for b in range(B)])[None]
    shp = np.stack([res[b]["sh_p"][0] for b in range(B)])[None]
    kws = np.concatenate([res[c]["kw_s"].reshape(NSEQ_S, 128, 2, 64) for c in range(NCORES)])[None]
    vws = np.concatenate([res[c]["vw_s"].reshape(NSEQ_S, 128, 2, 64) for c in range(NCORES)])[None]
    wkvs = np.concatenate([res[c]["wkv_s"] for c in range(NCORES)])[None]
    shs = np.concatenate([res[c]["sh_s"] for c in range(NCORES)])[None]
    return (y_prompt, y_sample, kwp, vwp, wkvp, shp, kws, vws, wkvs, shs)
```

```python
import numpy as np
from contextlib import ExitStack
import concourse.bass as bass
import concourse.mybir as mybir
from concourse.alu_op_type import AluOpType as ALU
from concourse.bass_utils import run_bass_kernel_spmd

F32 = mybir.dt.float32
BF16 = mybir.dt.bfloat16
U32 = mybir.dt.uint32
AF = mybir.ActivationFunctionType
AX = mybir.AxisListType

D = 1024
NRW = 1792
NCOL = 2560
NCORES = 8
NSEQ_S = 16
PAST = 8192
NPT = 33
NEXP = 16384
OPTS = {'limit': 10**9, 'dump_only': '', 'dumps': False, 'nexp': 16384, 'dbg_tile': 1, 'stage': 99, 'peer': True, 'win_copy': True, 'samp_state': True}


class Sched:
    def __init__(self, nc):
        self.nc = nc
        self.eng = {'pe': nc.tensor, 'act': nc.scalar, 'dve': nc.vector, 'pool': nc.gpsimd, 'sp': nc.sync}
        self.sem = {e: nc.alloc_semaphore("sem_" + e) for e in ['pe', 'act', 'dve', 'pool']}
        self.cnt = {e: 0 for e in self.sem}
        self.seen = {e: {} for e in self.eng}
        self.last_w = {}
        self.readers = {}
        self.dslots = {}
        for q in ['sp', 'pool', 'act']:
            self.dslots[q] = [[nc.alloc_semaphore("dq_%s_%d" % (q, i)), 0] for i in range(8)]
        self.dnext = {q: 0 for q in self.dslots}
        self.tokens = {}

    def _wait(self, e, toks):
        need = {}
        for t in toks:
            if t is None:
                continue
            sid, val, we = t
            if we == e and e == 'pe':
                continue
            if self.seen[e].get(sid, 0) >= val:
                continue
            if need.get(sid, (None, 0))[1] < val:
                need[sid] = (t, val)
        for sid, (t, val) in need.items():
            self.eng[e].wait_ge(self.tokens[sid], val)
            self.seen[e][sid] = val

    def _deps(self, e, reads, writes):
        toks = []
        for k in reads:
            toks.append(self.last_w.get(k))
        for k in writes:
            toks.append(self.last_w.get(k))
            for t in self.readers.get(k, []):
                toks.append(t[:3])
        return toks

    def _mark(self, tok, reads, writes, is_dma):
        for k in reads:
            self.readers.setdefault(k, []).append(tok + (is_dma,))
        for k in writes:
            self.last_w[k] = tok
            self.readers[k] = []

    def op(self, e, reads, writes, fn):
        self.n = getattr(self, 'n', 0) + 1
        if self.n > OPTS['limit']:
            return None
        self._wait(e, self._deps(e, reads, writes))
        inst = fn(self.eng[e])
        self.cnt[e] += 1
        sem = self.sem[e]
        inst.then_inc(sem, 1)
        sid = id(sem)
        self.tokens[sid] = sem
        self._mark((sid, self.cnt[e], e), reads, writes, False)
        return inst

    def dma(self, q, out, in_, reads, writes, fn=None):
        self.n = getattr(self, 'n', 0) + 1
        if self.n > OPTS['limit']:
            return None
        slots = self.dslots[q]
        i = self.dnext[q]
        self.dnext[q] = (i + 1) % len(slots)
        sem, val = slots[i]
        sid = id(sem)
        self.tokens[sid] = sem
        toks = self._deps(q, reads, writes)
        if val > 0:
            toks.append((sid, val, 'dma'))
        self._wait(q, toks)
        if fn is None:
            inst = self.eng[q].dma_start(out=out, in_=in_)
        else:
            inst = fn(self.eng[q])
        inst.then_inc(sem, 16)
        slots[i][1] = val + 16
        self._mark((sid, val + 16, 'dma'), reads, writes, True)

    def finish(self):
        for q, slots in self.dslots.items():
            toks = [(id(s), v, 'dma') for s, v in slots if v > 0]
            self._wait(q, toks)


def build(NT_A, NT_B, NT_S, dbg=False, dbg_tile=1):
    nc = bass.Bass("TRN2", target_bir_lowering=False)
    S = Sched(nc)
    NT_P = NT_A + NT_B
    NTT = NT_P + NT_S
    NPE = NT_B + (1 if NT_S else 0)
    OUT_T = NT_P - 2 if NT_A > 0 else NT_P - 1

    def din(name, shape, dt=F32):
        return nc.dram_tensor(name, list(shape), dt, kind="ExternalInput").ap()

    def dout(name, shape, dt=F32):
        return nc.dram_tensor(name, list(shape), dt, kind="ExternalOutput").ap()

    xp = din("xp", [max(NT_P, 1) * 128, D])
    xs = din("xs", [NSEQ_S * 4, D])
    ck = din("ck", [NSEQ_S, 128, 128])
    cv = din("cv", [NSEQ_S, 128, 128])
    swkv = din("swkv", [NSEQ_S, 8, 64, 64])
    sshift = din("sshift", [NSEQ_S, D])
    w_in = din("w_in", [D, NCOL])
    w_out = din("w_out", [D, D])
    w_q = din("w_q", [D, 2048])
    subk = din("subk", [2, 128, 128])
    eu = din("eu", [OPTS['nexp'], D])
    ev = din("ev", [OPTS['nexp'], D])
    lw2 = din("lw2", [64, 512])
    la2 = din("la2", [64, 512])
    lg2 = din("lg2", [128, 512])
    vec512 = din("vec512", [7, 512])
    mu = din("mu", [1, NRW])
    g1 = din("g1", [1, D])
    g2 = din("g2", [1, D])
    gF = din("gF", [1, D])
    sinks = din("sinks", [1, 8])
    rope = din("rope", [128, NT_P + 1, 16])
    cmask = din("cmask", [128, 3, 256])
    cst = din("cst", [128, 6, 128])
    vmask = din("vmask", [128, 2])
    iota_in = din("iota", [128, 256])

    y_p = dout("y_p", [max(NT_B, 1) * 128, D])
    y_s = dout("y_s", [NSEQ_S * 4, D])
    kw_p = dout("kw_p", [128, 128])
    vw_p = dout("vw_p", [128, 128])
    wkv_p = dout("wkv_p", [8, 64, 64])
    sh_p = dout("sh_p", [1, D])
    kw_s = dout("kw_s", [NSEQ_S, 128, 128])
    vw_s = dout("vw_s", [NSEQ_S, 128, 128])
    wkv_s = dout("wkv_s", [NSEQ_S, 8, 64, 64])
    sh_s = dout("sh_s", [NSEQ_S, D])
    xmid = nc.dram_tensor("xmid", [max(NPE, 1) * 128, D], F32, kind="Internal").ap()
    dbg_t = {}
    if dbg:
        for nm, shp in [("d_m", [128, NRW]), ("d_y", [128, 512]), ("d_at", [128, 512]), ("d_S", [64, 512]),
                        ("d_pre", [128, 128]), ("d_idx", [128, 128]), ("d_gate", [128, 128])]:
            dbg_t[nm] = dout(nm, shp)

    stk = {'cur': ExitStack()}
    glob_stack = stk['cur']

    dumped = {}

    def DUMP(name, t, key, ti=None):
        if not OPTS['dumps'] or (ti is not None and ti != OPTS['dbg_tile']) or name in dumped:
            return
        if OPTS['dump_only'] and name not in str(OPTS['dump_only']).split(','):
            return
        src = t if isinstance(t, bass.AP) else t[:]
        o = nc.dram_tensor("z_" + name, list(src.shape), src.dtype, kind="ExternalOutput").ap()
        dumped[name] = 1
        S.dma('sp', o, src, [key], [])

    def sb(name, shape, dt=F32):
        return stk['cur'].enter_context(nc.sbuf_tensor(name, list(shape), dt))

    def ps(name, shape, dt=F32):
        return nc.alloc_psum_tensor(name, list(shape), dt)

    c_cst = sb("c_cst", [128, 6, 128])
    S.dma('sp', c_cst[:], cst, [], ['c_cst'])
    c_cstb = sb("c_cstb", [128, 6, 128], BF16)
    S.op('dve', ['c_cst'], ['c_cstb'], lambda e: e.tensor_copy(out=c_cstb[:], in_=c_cst[:]))
    ident_f = c_cst[:, 0, :]
    tri_f = c_cst[:, 1, :]
    ones_f = c_cst[:, 2, :]
    ident_b = c_cstb[:, 0, :]
    c_m4 = sb("c_m4", [128, 4, 128])
    for i, j in enumerate([3, 4, 3, 4]):
        S.op('dve', ['c_cst'], ['c_m4'], lambda e, i=i, j=j: e.tensor_copy(out=c_m4[:, i, :], in_=c_cst[:, j, :]))
    c_low = c_cst[:, 5, :]
    c_amask = sb("c_amask", [128, 3, 256])
    S.dma('sp', c_amask[:], cmask, [], ['c_amask'])
    c_rope = sb("c_rope", [128, NT_P + 1, 16])
    S.dma('sp', c_rope[:], rope, [], ['c_rope'])
    c_vm = sb("c_vm", [128, 2])
    S.dma('sp', c_vm[:], vmask, [], ['c_vm'])
    c_v512 = sb("c_v512", [128, 7, 512])
    S.dma('sp', c_v512[:], vec512.partition_broadcast(128), [], ['c_v512'])
    W0, A0, KK, KA, RK, LW, LB = range(7)
    c_g1bc = sb("c_g1bc", [128, D])
    S.dma('sp', c_g1bc[:], g1.partition_broadcast(128)[:, 0, :], [], ['c_g1bc'])
    c_sink = sb("c_sink", [128, 8])
    S.dma('sp', c_sink[:], sinks.partition_broadcast(128)[:, 0, :], [], ['c_sink'])
    c_g1col = sb("c_g1col", [128, 8])
    with nc.allow_non_contiguous_dma(reason="tiny param column load"):
        S.dma('sp', c_g1col[:], g1.rearrange("o (kc p) -> p (o kc)", p=128), [], ['c_g1col'])

    def rsq(key, out, in_, c, op0):
        S.op('dve', [key], [key], lambda e: e.tensor_scalar(out=out, in0=in_, scalar1=c, scalar2=None, op0=op0))
        S.op('act', [key], [key], lambda e: e.activation(out=out, in_=out, func=AF.Sqrt))
        S.op('dve', [key], [key], lambda e: e.reciprocal(out=out, in_=out))

    P = [ps("P%d" % i, [128, 512]) for i in range(8)]
    PK = ["P%d" % i for i in range(8)]

    def barrier():
        toks = []
        for e, sem in S.sem.items():
            if S.cnt[e] > 0:
                S.tokens[id(sem)] = sem
                toks.append((id(sem), S.cnt[e], 'x'))
        for q, slots in S.dslots.items():
            for s, v in slots:
                if v > 0:
                    S.tokens[id(s)] = s
                    toks.append((id(s), v, 'dma'))
        for e in ['pe', 'act', 'dve', 'pool', 'sp']:
            S._wait(e, toks)

    ph1 = ExitStack()
    stk['cur'] = ph1
    W1 = sb("W1", [128, 8, NRW], BF16)
    W2 = sb("W2", [128, 8, NRW], BF16)
    Wat = sb("Wat", [128, 8, 768], BF16)
    Wo = sb("Wo", [128, 8, D], BF16)
    L_w2 = sb("L_w2", [128, 512], BF16)
    L_g2 = sb("L_g2", [128, 512], BF16)
    with nc.sbuf_tensor("stg", [128, NCOL], F32) as stg, nc.sbuf_tensor("mub", [128, NRW], F32) as mub, \
            nc.sbuf_tensor("omu", [128, NRW], F32) as omu:
        S.dma('sp', mub[:], mu.partition_broadcast(128)[:, 0, :], [], ['mub'])
        S.op('dve', ['mub'], ['omu'], lambda e: e.tensor_scalar(out=omu[:], in0=mub[:], scalar1=-1.0, scalar2=1.0,
                                                                op0=ALU.mult, op1=ALU.add))
        for kc in range(8):
            S.dma('sp', stg[:], w_in[kc * 128:(kc + 1) * 128, :], [], ['stg'])
            S.op('dve', ['stg', 'omu'], ['W1'], lambda e, kc=kc: e.tensor_tensor(out=W1[:, kc, :], in0=stg[:, 0:NRW], in1=omu[:], op=ALU.mult))
            S.op('pool', ['stg', 'mub'], ['W2'], lambda e, kc=kc: e.tensor_tensor(out=W2[:, kc, :], in0=stg[:, 0:NRW], in1=mub[:], op=ALU.mult))
            S.op('act', ['stg'], ['Wat'], lambda e, kc=kc: e.copy(out=Wat[:, kc, :], in_=stg[:, NRW:NCOL]))
        for kc in range(8):
            S.dma('sp', stg[:, 0:D], w_out[kc * 128:(kc + 1) * 128, :], [], ['stg'])
            S.op('act', ['stg'], ['Wo'], lambda e, kc=kc: e.copy(out=Wo[:, kc, :], in_=stg[:, 0:D]))
        S.dma('sp', stg[0:64, 0:512], lw2, [], ['stg'])
        S.dma('sp', stg[64:128, 0:512], la2, [], ['stg'])
        S.dma('sp', stg[:, 512:1024], lg2, [], ['stg'])
        S.op('act', ['stg'], ['L_w2'], lambda e: e.copy(out=L_w2[:], in_=stg[:, 0:512]))
        S.op('act', ['stg'], ['L_g2'], lambda e: e.copy(out=L_g2[:], in_=stg[:, 512:1024]))
        barrier()

    xt0 = sb("xt0", [128, D])
    xt = [xt0, xt0]
    xts = xt0
    xn = sb("xn", [128, D], BF16)
    ssq = sb("ssq", [128, 4])
    hT = sb("hT", [128, 8, 128], BF16)
    hTs = sb("hTs", [128, 8, 128], BF16)
    S.op('pool', [], ['hT'], lambda e: e.memset(hT[:], 0.0))
    S.op('pool', [], ['hTs'], lambda e: e.memset(hTs[:], 0.0))
    A = {}
    for nm in ["r", "k", "v", "ld", "asig", "g", "kk", "b", "kmod", "c", "t1", "t2", "t3"]:
        A[nm] = sb("a_" + nm, [128, 512])
    Bt = {}
    for nm in ["rt", "at", "bt", "kt", "bbar", "kbar", "vb", "W0T", "UT"]:
        Bt[nm] = sb("b_" + nm, [128, 512], BF16)
    lin = sb("lin", [128, 2, 128], BF16)
    st8 = sb("st8", [128, 8, 4])
    AR_fm = sb("AR_fm", [64, 8, 256], BF16)
    B_fm = sb("B_fm", [64, 8, 128], BF16)
    K_fm = sb("K_fm", [64, 8, 128], BF16)
    MATS = sb("MATS", [128, 8, 512], BF16)
    _m = sb("Mb", [128, 8, 128], BF16)
    _mt = sb("MTb", [128, 8, 128], BF16)
    _t = sb("Tb", [128, 8, 128], BF16)
    Mb, MTb, Tb = [_m, _m], [_mt, _mt], [_t, _t]
    S32 = sb("S32", [64, 8, 64])
    Sb = sb("Sb", [64, 8, 64], BF16)
    ecl_fm = sb("ecl_fm", [64, 8])
    sti = sb("sti", [64, 8, 64])
    qkv = sb("qkv", [128, 768])
    rtmp = sb("rtmp", [128, 10, 8])
    qb = sb("qb", [128, 768], BF16)
    QT = sb("QT", [64, 8, 128], BF16)
    KT = [sb("KT%d" % i, [64, 2, 128], BF16) for i in range(2)]
    Vb = [sb("Vb%d" % i, [128, 128], BF16) for i in range(2)]
    sm = sb("sm", [128, 256])
    for i in range(2):
        S.op('pool', [], ['KT%d' % i], lambda e, i=i: e.memset(KT[i][:], 0.0))
        S.op('pool', [], ['Vb%d' % i], lambda e, i=i: e.memset(Vb[i][:], 0.0))
    eb = sb("eb", [128, 256], BF16)
    eT = sb("eT", [128, 2, 128], BF16)
    ast = sb("ast", [128, 8])
    ycat = sb("ycat", [128, D], BF16)
    ycatT = sb("ycatT", [128, 8, 128], BF16)
    junk = ycatT[:].rearrange("p k t -> p (k t)")
    xm = sb("xm", [128, D])
    hrow = xm
    ckb = sm

    def v3(ap, h=8):
        return ap.rearrange("p (h j) -> p h j", h=h)

    def bc_last(ap, n):
        return ap.broadcast_to([ap.shape[0], ap.shape[1], n])

    def V512(i):
        return c_v512[:, i, :]

    def load_x(ti):
        if ti < NT_P:
            b = xt[ti % 2]
            S.dma('sp', b[:], xp[ti * 128:(ti + 1) * 128, :], [], ['xt0'])
        else:
            s = ti - NT_P
            if s == 0:
                S.op('pool', [], ['xt0'], lambda e: e.memset(xt0[:], 0.0))
                S.dma('sp', xmid[NT_B * 128:(NT_B + 1) * 128, :], xt0[:], ['xt0'], ['xmid'])
            S.dma('sp', xts[0:4, :], xs[s * 4:(s + 1) * 4, :], [], ['xt0'])

    def mix_tile(ti):
        samp = ti >= NT_P
        sq = ti - NT_P
        xk = 'xt0'
        xb = xts if samp else xt[ti % 2]
        par = ti % 2
        vm = c_vm[:, 1:2] if samp else c_vm[:, 0:1]
        if OPTS['stage'] <= 0:
            return
        S.op('act', [xk], ['ycatT', 'ssq'], lambda e: e.activation(out=junk[:], in_=xb[:], func=AF.Square, accum_out=ssq[:, 0:1]))
        rsq('ssq', ssq[:, 1:2], ssq[:, 0:1], D * 1e-5, ALU.add)
        S.op('dve', [xk, 'ssq'], ['xn'], lambda e: e.tensor_scalar(out=xn[:], in0=xb[:], scalar1=ssq[:, 1:2], scalar2=32.0,
                                                                   op0=ALU.mult, op1=ALU.mult))
        if samp:
            S.dma('sp', hrow[0:1, :], sshift[sq:sq + 1, :], [], ['xm'])
            S.op('act', ['xm'], ['ycatT'], lambda e: e.copy(out=junk[0:1, :], in_=hrow[0:1, :]))
        pT = P[0][:].bitcast(BF16)
        if samp:
            for kc in range(8):
                S.op('pe', ['ycatT', 'c_cstb'], ['P1'], lambda e, kc=kc: e.transpose(out=P[1][:].bitcast(BF16)[:, 2 * kc:2 * kc + 1], in_=junk[0:1, kc * 128:(kc + 1) * 128], identity=ident_b[0:1, 0:1]))
            S.op('dve', ['P1'], ['hTs'], lambda e: e.tensor_copy(out=hTs[:, :, 0], in_=P[1][:].bitcast(BF16)[:, 0:16].rearrange("p (k two) -> p k two", two=2)[:, :, 0]))
        else:
            S.op('dve', ['hT'], ['hTs'], lambda e: e.tensor_copy(out=hTs[:, :, 0], in_=hT[:, :, 127]))
        for kc in range(8):
            S.op('pe', ['xn', 'c_cstb'], ['P0'], lambda e, kc=kc: e.transpose(out=pT[:, kc * 128:(kc + 1) * 128], in_=xn[:, kc * 128:(kc + 1) * 128], identity=ident_b))
        S.op('dve', ['P0', 'c_g1col'], ['hT'], lambda e: e.tensor_tensor(
            out=hT[:], in0=pT.rearrange("p (k t) -> p k t", k=8),
            in1=c_g1col[:].unsqueeze(2).broadcast_to([128, 8, 128]), op=ALU.mult))
        S.op('pool', ['hT'], ['hTs'], lambda e: e.tensor_copy(out=hTs[:, :, 1:128], in_=hT[:, :, 0:127]))
        if samp or ti == OUT_T:
            S.op('pool', ['xn', 'c_g1bc'], ['xm'], lambda e: e.tensor_tensor(out=hrow[:], in0=xn[:], in1=c_g1bc[:], op=ALU.mult))
            if samp:
                S.dma('sp', sh_s[sq:sq + 1, :], hrow[3:4, :], ['xm'], [])
            else:
                S.dma('sp', sh_p[0:1, :], hrow[127:128, :], ['xm'], [])
        DUMP("xn", xn, 'xn', ti); DUMP("hT", hT, 'hT', ti); DUMP("hTs", hTs, 'hTs', ti)
        if OPTS['stage'] <= 1:
            return
        def proj(bank, c0, n, dst_reads=()):
            for kc in range(8):
                S.op('pe', ['hT', 'W1'], [PK[bank]], lambda e, kc=kc: e.matmul(out=P[bank][:, 0:n], lhsT=hT[:, kc, :], rhs=W1[:, kc, c0:c0 + n], start=(kc == 0), stop=False))
            for kc in range(8):
                S.op('pe', ['hTs', 'W2'], [PK[bank]], lambda e, kc=kc: e.matmul(out=P[bank][:, 0:n], lhsT=hTs[:, kc, :], rhs=W2[:, kc, c0:c0 + n], start=False, stop=(kc == 7)))
        proj(1, 0, 512)
        S.op('act', ['P1'], ['a_r'], lambda e: e.copy(out=A["r"][:], in_=P[1][:]))
        proj(2, 512, 512)
        S.op('act', ['P2'], ['a_k'], lambda e: e.copy(out=A["k"][:], in_=P[2][:]))
        proj(3, 1024, 512)
        S.op('act', ['P3'], ['a_v'], lambda e: e.copy(out=A["v"][:], in_=P[3][:]))
        S.op('dve', ['a_v', 'c_vm'], ['b_vb'], lambda e: e.tensor_scalar(out=Bt["vb"][:], in0=A["v"][:], scalar1=vm, scalar2=None, op0=ALU.mult))
        for j, c0 in enumerate([1536, 1664]):
            for kc in range(8):
                S.op('pe', ['hT', 'W1'], ['P4'], lambda e, kc=kc, j=j, c0=c0: e.matmul(out=P[4][:, j * 128:(j + 1) * 128], lhsT=W1[:, kc, c0:c0 + 128], rhs=hT[:, kc, :], start=(kc == 0), stop=False))
            for kc in range(8):
                S.op('pe', ['hTs', 'W2'], ['P4'], lambda e, kc=kc, j=j, c0=c0: e.matmul(out=P[4][:, j * 128:(j + 1) * 128], lhsT=W2[:, kc, c0:c0 + 128], rhs=hTs[:, kc, :], start=False, stop=(kc == 7)))
        S.op('act', ['P4'], ['lin'], lambda e: e.activation(out=lin[0:64, 0, :], in_=P[4][0:64, 0:128], func=AF.Tanh))
        S.op('act', ['P4'], ['lin'], lambda e: e.copy(out=lin[64:128, 0, :], in_=P[4][64:128, 0:128]))
        S.op('act', ['P4'], ['lin'], lambda e: e.activation(out=lin[:, 1, :], in_=P[4][:, 128:256], func=AF.Sigmoid))
        for kc in range(8):
            S.op('pe', ['hT', 'Wat'], ['P5'], lambda e, kc=kc: e.matmul(out=P[5][:], lhsT=hT[:, kc, :], rhs=Wat[:, kc, 0:512], start=(kc == 0), stop=(kc == 7)))
        for kc in range(8):
            S.op('pe', ['hT', 'Wat'], ['P6'], lambda e, kc=kc: e.matmul(out=P[6][:, 0:256], lhsT=hT[:, kc, :], rhs=Wat[:, kc, 512:768], start=(kc == 0), stop=(kc == 7)))
        S.op('act', ['P5'], ['qkv'], lambda e: e.copy(out=qkv[:, 0:512], in_=P[5][:]))
        S.op('act', ['P6'], ['qkv'], lambda e: e.copy(out=qkv[:, 512:768], in_=P[6][:, 0:256]))
        DUMP("a_r", A["r"], 'a_r', ti); DUMP("a_v", A["v"], 'a_v', ti); DUMP("lin", lin, 'lin', ti); DUMP("qkv0", qkv, 'qkv', ti)
        if OPTS['stage'] <= 2:
            return
        S.op('pe', ['lin', 'L_w2'], ['P1'], lambda e: e.matmul(out=P[1][:], lhsT=lin[0:64, 0, :], rhs=L_w2[0:64, :], start=True, stop=True))
        S.op('pe', ['lin', 'L_w2'], ['P2'], lambda e: e.matmul(out=P[2][:], lhsT=lin[64:128, 0, :], rhs=L_w2[64:128, :], start=True, stop=True))
        S.op('pe', ['lin', 'L_g2'], ['P3'], lambda e: e.matmul(out=P[3][:], lhsT=lin[:, 1, :], rhs=L_g2[:], start=True, stop=True))
        S.op('dve', ['P1', 'c_v512'], ['a_t1'], lambda e: e.tensor_tensor(out=A["t1"][:], in0=P[1][:], in1=V512(W0), op=ALU.add))
        S.op('act', ['a_t1'], ['a_t1'], lambda e: e.activation(out=A["t1"][:], in_=A["t1"][:], func=AF.Sigmoid))
        S.op('dve', ['a_t1', 'c_vm'], ['a_ld'], lambda e: e.tensor_scalar(out=A["ld"][:], in0=A["t1"][:], scalar1=vm, scalar2=-0.6065306597,
                                                                           op0=ALU.mult, op1=ALU.mult))
        S.op('dve', ['P2', 'c_v512'], ['a_t2'], lambda e: e.tensor_tensor(out=A["t2"][:], in0=P[2][:], in1=V512(A0), op=ALU.add))
        S.op('act', ['a_t2'], ['a_asig'], lambda e: e.activation(out=A["asig"][:], in_=A["t2"][:], func=AF.Sigmoid))
        S.op('act', ['P3'], ['a_g'], lambda e: e.copy(out=A["g"][:], in_=P[3][:]))
        S.op('pool', ['a_k', 'c_v512'], ['a_kk'], lambda e: e.tensor_tensor(out=A["kk"][:], in0=A["k"][:], in1=V512(KK), op=ALU.mult))
        S.op('pool', ['a_kk'], ['a_t3'], lambda e: e.tensor_tensor(out=A["t3"][:], in0=A["kk"][:], in1=A["kk"][:], op=ALU.mult))
        S.op('dve', ['a_t3'], ['st8'], lambda e: e.tensor_reduce(out=st8[:, :, 0], in_=v3(A["t3"][:]), axis=AX.X, op=ALU.add))
        rsq('st8', st8[:, :, 1], st8[:, :, 0], 1e-24, ALU.max)
        S.op('dve', ['a_kk', 'st8'], ['a_kk'], lambda e: e.tensor_tensor(out=v3(A["kk"][:]), in0=v3(A["kk"][:]), in1=bc_last(st8[:, :, 1:2], 64), op=ALU.mult))
        S.op('dve', ['a_kk', 'a_asig', 'c_vm'], ['a_b'], lambda e: e.scalar_tensor_tensor(out=A["b"][:], in0=A["kk"][:], scalar=vm, in1=A["asig"][:], op0=ALU.mult, op1=ALU.mult))
        S.op('dve', ['a_asig', 'c_v512'], ['a_t2'], lambda e: e.scalar_tensor_tensor(out=A["t2"][:], in0=A["asig"][:], scalar=-1.0, in1=V512(KA), op0=ALU.add, op1=ALU.mult))
        S.op('dve', ['a_t2', 'a_k'], ['a_kmod'], lambda e: e.scalar_tensor_tensor(out=A["kmod"][:], in0=A["t2"][:], scalar=1.0, in1=A["k"][:], op0=ALU.add, op1=ALU.mult))
        S.op('pe', ['a_ld', 'c_cst'], ['P1'], lambda e: e.matmul(out=P[1][:], lhsT=tri_f, rhs=A["ld"][:], start=True, stop=True))
        S.op('pe', ['a_ld', 'c_cst'], ['P2'], lambda e: e.matmul(out=P[2][:], lhsT=ones_f, rhs=A["ld"][:], start=True, stop=True))
        for h in range(8):
            S.op('pe', ['a_ld', 'c_cst'], ['P3'], lambda e, h=h: e.matmul(out=P[3][0:64, h:h + 1], lhsT=A["ld"][:, h * 64:(h + 1) * 64], rhs=ones_f[:, 0:1], start=True, stop=True))
        S.op('act', ['P3'], ['ecl_fm'], lambda e: e.activation(out=ecl_fm[:], in_=P[3][0:64, 0:8], func=AF.Exp))
        S.op('act', ['P1'], ['a_c'], lambda e: e.copy(out=A["c"][:], in_=P[1][:]))
        S.op('act', ['a_c'], ['a_t1'], lambda e: e.activation(out=A["t1"][:], in_=A["c"][:], func=AF.Exp))
        S.op('dve', ['a_t1', 'a_r'], ['b_rt'], lambda e: e.tensor_tensor(out=Bt["rt"][:], in0=A["r"][:], in1=A["t1"][:], op=ALU.mult))
        S.op('pool', ['a_c', 'a_ld'], ['a_t2'], lambda e: e.tensor_tensor(out=A["t2"][:], in0=A["c"][:], in1=A["ld"][:], op=ALU.subtract))
        S.op('act', ['a_t2'], ['a_t2'], lambda e: e.activation(out=A["t2"][:], in_=A["t2"][:], func=AF.Exp))
        S.op('dve', ['a_t2', 'a_kk'], ['b_at'], lambda e: e.scalar_tensor_tensor(out=Bt["at"][:], in0=A["kk"][:], scalar=-1.0, in1=A["t2"][:], op0=ALU.mult, op1=ALU.mult))
        S.op('act', ['a_c'], ['a_t3'], lambda e: e.activation(out=A["t3"][:], in_=A["c"][:], func=AF.Exp, scale=-1.0))
        S.op('dve', ['a_t3', 'a_b'], ['b_bt'], lambda e: e.tensor_tensor(out=Bt["bt"][:], in0=A["b"][:], in1=A["t3"][:], op=ALU.mult))
        S.op('pool', ['a_t3', 'a_kmod'], ['b_kt'], lambda e: e.tensor_tensor(out=Bt["kt"][:], in0=A["kmod"][:], in1=A["t3"][:], op=ALU.mult))
        S.op('dve', ['P2', 'a_c'], ['a_t1'], lambda e: e.tensor_tensor(out=A["t1"][:], in0=P[2][:], in1=A["c"][:], op=ALU.subtract))
        S.op('act', ['a_t1'], ['a_t1'], lambda e: e.activation(out=A["t1"][:], in_=A["t1"][:], func=AF.Exp))
        S.op('dve', ['a_t1', 'a_b'], ['b_bbar'], lambda e: e.tensor_tensor(out=Bt["bbar"][:], in0=A["b"][:], in1=A["t1"][:], op=ALU.mult))
        S.op('pool', ['a_t1', 'a_kmod'], ['b_kbar'], lambda e: e.tensor_tensor(out=Bt["kbar"][:], in0=A["kmod"][:], in1=A["t1"][:], op=ALU.mult))
        S.op('pool', ['a_r', 'c_v512'], ['a_t2'], lambda e: e.tensor_tensor(out=A["t2"][:], in0=A["r"][:], in1=V512(RK), op=ALU.mult))
        S.op('pool', ['a_t2', 'a_kmod'], ['a_t2'], lambda e: e.tensor_tensor(out=A["t2"][:], in0=A["t2"][:], in1=A["kmod"][:], op=ALU.mult))
        S.op('dve', ['a_t2'], ['st8'], lambda e: e.tensor_reduce(out=st8[:, :, 2], in_=v3(A["t2"][:]), axis=AX.X, op=ALU.add))
        DUMP("a_ld", A["ld"], 'a_ld', ti); DUMP("a_c", A["c"], 'a_c', ti); DUMP("a_kk", A["kk"], 'a_kk', ti); DUMP("b_rt", Bt["rt"], 'b_rt', ti); DUMP("b_at", Bt["at"], 'b_at', ti); DUMP("b_bt", Bt["bt"], 'b_bt', ti); DUMP("b_kbar", Bt["kbar"], 'b_kbar', ti); DUMP("ecl_fm", ecl_fm, 'ecl_fm', ti)
        if OPTS['stage'] <= 3:
            return
        pTb = [P[i][:].bitcast(BF16) for i in range(8)]
        for qi, (nm, bank) in enumerate([("at", 4), ("rt", 5), ("bt", 6), ("kt", 7)]):
            for h in range(8):
                S.op('pe', ['b_' + nm, 'c_cstb'], [PK[bank]], lambda e, h=h, nm=nm, bank=bank: e.transpose(
                    out=pTb[bank][0:64, h * 128:(h + 1) * 128], in_=Bt[nm][:, h * 64:(h + 1) * 64], identity=ident_b))
        S.op('act', ['P4'], ['AR_fm'], lambda e: e.copy(out=AR_fm[:, :, 0:128], in_=pTb[4][0:64, 0:1024].rearrange("p (h t) -> p h t", h=8)))
        S.op('dve', ['P5'], ['AR_fm'], lambda e: e.tensor_copy(out=AR_fm[:, :, 128:256], in_=pTb[5][0:64, 0:1024].rearrange("p (h t) -> p h t", h=8)))
        S.op('act', ['P6'], ['B_fm'], lambda e: e.copy(out=B_fm[:], in_=pTb[6][0:64, 0:1024].rearrange("p (h t) -> p h t", h=8)))
        S.op('dve', ['P7'], ['K_fm'], lambda e: e.tensor_copy(out=K_fm[:], in_=pTb[7][0:64, 0:1024].rearrange("p (h t) -> p h t", h=8)))
        for h in range(8):
            bank = h % 2
            S.op('pe', ['B_fm', 'AR_fm'], [PK[bank]], lambda e, h=h, bank=bank: e.matmul(out=P[bank][:, 0:256], lhsT=B_fm[:, h, :], rhs=AR_fm[:, h, :], start=True, stop=True))
            S.op('pe', ['K_fm', 'AR_fm'], [PK[bank]], lambda e, h=h, bank=bank: e.matmul(out=P[bank][:, 256:512], lhsT=K_fm[:, h, :], rhs=AR_fm[:, h, :], start=True, stop=True))
            S.op('dve', [PK[bank], 'c_m4'], ['MATS'], lambda e, h=h, bank=bank: e.tensor_tensor(out=MATS[:, h, :], in0=P[bank][:], in1=c_m4[:].rearrange("p a b -> p (a b)"), op=ALU.mult))
        for hh in range(2):
            bank = 2 + hh
            for h4 in range(4):
                h = hh * 4 + h4
                S.op('pe', ['B_fm', 'AR_fm'], [PK[bank]], lambda e, h=h, h4=h4, bank=bank: e.matmul(out=P[bank][:, h4 * 128:(h4 + 1) * 128], lhsT=AR_fm[:, h, 0:128], rhs=B_fm[:, h, :], start=True, stop=True))
            S.op('dve', [PK[bank], 'c_cst'], ['MTb'], lambda e, hh=hh, bank=bank: e.tensor_tensor(
                out=MTb[0][:, hh * 4:(hh + 1) * 4, :], in0=P[bank][:].rearrange("p (h t) -> p h t", h=4),
                in1=c_low.unsqueeze(1).broadcast_to([128, 4, 128]), op=ALU.mult))
        S.op('act', ['MATS'], ['Mb'], lambda e: e.copy(out=Mb[0][:], in_=MATS[:, :, 0:128]))
        S.op('pool', ['MATS', 'c_cstb'], ['Tb'], lambda e: e.tensor_tensor(out=Tb[0][:], in0=MATS[:, :, 0:128], in1=ident_b.unsqueeze(1).broadcast_to([128, 8, 128]), op=ALU.add))
        cur = 0
        for lvl in range(1, 7):
            nxt = 1 - cur
            for hh in range(2):
                hs = slice(hh * 4, hh * 4 + 4)
                bM, bMT, bT = 2 + hh * 3, 3 + hh * 3, 4 + hh * 3
                for h4 in range(4):
                    h = hh * 4 + h4
                    cs = slice(h4 * 128, (h4 + 1) * 128)
                    if lvl < 6:
                        S.op('pe', ['Mb', 'MTb'], [PK[bM]], lambda e, h=h, cs=cs, bM=bM, cur=cur: e.matmul(out=P[bM][:, cs], lhsT=MTb[cur][:, h, :], rhs=Mb[cur][:, h, :], start=True, stop=True))
                    S.op('pe', ['Mb', 'MTb'], [PK[bMT]], lambda e, h=h, cs=cs, bMT=bMT, cur=cur: e.matmul(out=P[bMT][:, cs], lhsT=Mb[cur][:, h, :], rhs=MTb[cur][:, h, :], start=True, stop=True))
                if lvl < 6:
                    S.op('act', [PK[bM]], ['Mb'], lambda e, hs=hs, bM=bM, nxt=nxt: e.copy(out=Mb[nxt][:, hs, :], in_=P[bM][:].rearrange("p (h t) -> p h t", h=4)))
                S.op('dve', [PK[bMT]], ['MTb'], lambda e, hs=hs, bMT=bMT, nxt=nxt: e.tensor_copy(out=MTb[nxt][:, hs, :], in_=P[bMT][:].rearrange("p (h t) -> p h t", h=4)))
                for h4 in range(4):
                    h = hh * 4 + h4
                    cs = slice(h4 * 128, (h4 + 1) * 128)
                    S.op('pe', ['MTb', 'Tb'], [PK[bT]], lambda e, h=h, cs=cs, bT=bT, cur=cur, nxt=nxt: e.matmul(out=P[bT][:, cs], lhsT=MTb[nxt][:, h, :], rhs=Tb[cur][:, h, :], start=True, stop=True))
                S.op('dve', [PK[bT], 'Tb'], ['Tb'], lambda e, hs=hs, bT=bT, cur=cur, nxt=nxt: e.tensor_tensor(
                    out=Tb[nxt][:, hs, :], in0=P[bT][:].rearrange("p (h t) -> p h t", h=4), in1=Tb[cur][:, hs, :], op=ALU.add))
            cur = nxt
        Tf = Tb[cur]
        TfK = 'Tb'
        DUMP("AR_fm", AR_fm, 'AR_fm', ti); DUMP("K_fm", K_fm, 'K_fm', ti); DUMP("MATS", MATS, 'MATS', ti); DUMP("Tb", Tb[0], 'Tb', ti); DUMP("MTb", MTb[0], 'MTb', ti)
        if OPTS['stage'] <= 4:
            return
        if samp or ti == 0:
            if samp:
                S.dma('sp', sti[:], swkv[sq].rearrange("h i j -> i h j"), [], ['sti'])
                for h in range(8):
                    S.op('pe', ['sti', 'c_cst'], ['P0'], lambda e, h=h: e.transpose(out=P[0][0:64, h * 64:(h + 1) * 64], in_=sti[:, h, :], identity=ident_f[0:64, 0:64]))
                S.op('dve', ['P0'], ['S32'], lambda e: e.tensor_copy(out=S32[:], in_=P[0][0:64, :].rearrange("p (h i) -> p h i", h=8)))
            else:
                S.op('dve', [], ['S32'], lambda e: e.memset(S32[:], 0.0))
            S.op('act', ['S32'], ['Sb'], lambda e: e.copy(out=Sb[:], in_=S32[:]))
        for h in range(8):
            cs = slice(h * 64, (h + 1) * 64)
            S.op('pe', ['AR_fm', 'Sb'], ['P0'], lambda e, h=h, cs=cs: e.matmul(out=P[0][:, cs], lhsT=AR_fm[:, h, 0:128], rhs=Sb[:, h, :], start=True, stop=False))
            S.op('pe', ['MATS', 'b_vb'], ['P0'], lambda e, h=h, cs=cs: e.matmul(out=P[0][:, cs], lhsT=MATS[:, h, 256:384], rhs=Bt["vb"][:, cs], start=False, stop=True))
        S.op('act', ['P0'], ['b_W0T'], lambda e: e.copy(out=Bt["W0T"][:], in_=P[0][:]))
        for h in range(8):
            cs = slice(h * 64, (h + 1) * 64)
            S.op('pe', [TfK, 'b_W0T'], ['P1'], lambda e, h=h, cs=cs: e.matmul(out=P[1][:, cs], lhsT=Tf[:, h, :], rhs=Bt["W0T"][:, cs], start=True, stop=True))
        S.op('act', ['P1'], ['b_UT'], lambda e: e.copy(out=Bt["UT"][:], in_=P[1][:]))
        for h in range(8):
            cs = slice(h * 64, (h + 1) * 64)
            S.op('pe', ['AR_fm', 'Sb'], ['P0'], lambda e, h=h, cs=cs: e.matmul(out=P[0][:, cs], lhsT=AR_fm[:, h, 128:256], rhs=Sb[:, h, :], start=True, stop=False))
            S.op('pe', ['MATS', 'b_UT'], ['P0'], lambda e, h=h, cs=cs: e.matmul(out=P[0][:, cs], lhsT=MATS[:, h, 128:256], rhs=Bt["UT"][:, cs], start=False, stop=False))
            S.op('pe', ['MATS', 'b_vb'], ['P0'], lambda e, h=h, cs=cs: e.matmul(out=P[0][:, cs], lhsT=MATS[:, h, 384:512], rhs=Bt["vb"][:, cs], start=False, stop=True))
        for h in range(8):
            cs = slice(h * 64, (h + 1) * 64)
            S.op('pe', ['b_bbar', 'b_UT'], ['P1'], lambda e, h=h, cs=cs: e.matmul(out=P[1][0:64, cs], lhsT=Bt["bbar"][:, cs], rhs=Bt["UT"][:, cs], start=True, stop=False))
            S.op('pe', ['b_kbar', 'b_vb'], ['P1'], lambda e, h=h, cs=cs: e.matmul(out=P[1][0:64, cs], lhsT=Bt["kbar"][:, cs], rhs=Bt["vb"][:, cs], start=False, stop=True))
        S.op('dve', ['S32', 'ecl_fm'], ['S32'], lambda e: e.tensor_tensor(out=S32[:], in0=S32[:], in1=ecl_fm[:].unsqueeze(2).broadcast_to([64, 8, 64]), op=ALU.mult))
        S.op('dve', ['S32', 'P1'], ['S32'], lambda e: e.tensor_tensor(out=S32[:], in0=S32[:], in1=P[1][0:64, :].rearrange("p (h i) -> p h i", h=8), op=ALU.add))
        S.op('act', ['S32'], ['Sb'], lambda e: e.copy(out=Sb[:], in_=S32[:]))
        if samp or ti == OUT_T:
            for h in range(8):
                S.op('pe', ['S32', 'c_cst'], ['P2'], lambda e, h=h: e.transpose(out=P[2][0:64, h * 64:(h + 1) * 64], in_=S32[:, h, :], identity=ident_f[0:64, 0:64]))
            S.op('act', ['P2'], ['sti'], lambda e: e.copy(out=sti[:], in_=P[2][0:64, :].rearrange("p (h j) -> p h j", h=8)))
            dst = wkv_s[sq] if samp else wkv_p
            S.dma('sp', dst.rearrange("h i j -> i h j"), sti[:], ['sti'], [])
        DUMP("S32", S32, 'S32', ti); DUMP("b_UT", Bt["UT"], 'b_UT', ti); DUMP("b_W0T", Bt["W0T"], 'b_W0T', ti)
        if OPTS['stage'] <= 5:
            return
        state_only = ti < NT_A
        if state_only and ti != NT_A - 1:
            return
        if not state_only:
            rwkv_post(ti)
        attn_and_out(ti, samp, sq, xk, xb, par, state_only)

    def rwkv_post(ti):
        if True:
            pass
        Y3 = v3(P[0][:])
        S.op('dve', ['P0'], ['st8'], lambda e: e.tensor_reduce(out=st8[:, :, 0], in_=Y3, axis=AX.X, op=ALU.add))
        S.op('dve', ['st8'], ['st8'], lambda e: e.tensor_scalar(out=st8[:, :, 0], in0=st8[:, :, 0], scalar1=1.0 / 64, scalar2=None, op0=ALU.mult))
        S.op('dve', ['P0', 'st8'], ['a_t1'], lambda e: e.tensor_tensor(out=v3(A["t1"][:]), in0=Y3, in1=bc_last(st8[:, :, 0:1], 64), op=ALU.subtract))
        S.op('pool', ['a_t1'], ['a_t2'], lambda e: e.tensor_tensor(out=A["t2"][:], in0=A["t1"][:], in1=A["t1"][:], op=ALU.mult))
        S.op('dve', ['a_t2'], ['st8'], lambda e: e.tensor_reduce(out=st8[:, :, 1], in_=v3(A["t2"][:]), axis=AX.X, op=ALU.add))
        S.op('dve', ['st8'], ['st8'], lambda e: e.tensor_scalar(out=st8[:, :, 1], in0=st8[:, :, 1], scalar1=1.0 / 64, scalar2=64e-5, op0=ALU.mult, op1=ALU.add))
        rsq('st8', st8[:, :, 1], st8[:, :, 1], 0.0, ALU.add)
        S.op('dve', ['a_t1', 'st8'], ['a_t1'], lambda e: e.tensor_tensor(out=v3(A["t1"][:]), in0=v3(A["t1"][:]), in1=bc_last(st8[:, :, 1:2], 64), op=ALU.mult))
        S.op('pool', ['a_t1', 'c_v512'], ['a_t1'], lambda e: e.tensor_tensor(out=A["t1"][:], in0=A["t1"][:], in1=V512(LW), op=ALU.mult))
        S.op('pool', ['a_t1', 'c_v512'], ['a_t1'], lambda e: e.tensor_tensor(out=A["t1"][:], in0=A["t1"][:], in1=V512(LB), op=ALU.add))
        S.op('dve', ['a_v', 'st8'], ['a_t2'], lambda e: e.tensor_tensor(out=v3(A["t2"][:]), in0=v3(A["v"][:]), in1=bc_last(st8[:, :, 2:3], 64), op=ALU.mult))
        S.op('pool', ['a_t1', 'a_t2'], ['a_t1'], lambda e: e.tensor_tensor(out=A["t1"][:], in0=A["t1"][:], in1=A["t2"][:], op=ALU.add))
        S.op('dve', ['a_t1', 'a_g'], ['ycat'], lambda e: e.tensor_tensor(out=ycat[:, 0:512], in0=A["t1"][:], in1=A["g"][:], op=ALU.mult))
        DUMP("ycat_rw", ycat[:, 0:512], 'ycat', ti)

    def attn_and_out(ti, samp, sq, xk, xb, par, state_only):
        pTb = [P[i][:].bitcast(BF16) for i in range(8)]
        if OPTS['stage'] <= 6:
            return
        ri = NT_P if samp else ti
        cosb = c_rope[:, ri, 0:8].unsqueeze(1)
        sinb = c_rope[:, ri, 8:16].unsqueeze(1)
        for (c0, nh) in [(0, 8), (512, 2)]:
            X = qkv[:, c0:c0 + nh * 64].rearrange("p (h j) -> p h j", h=nh)
            x1, x2 = X[:, :, 0:8], X[:, :, 8:16]
            cb = cosb.broadcast_to([128, nh, 8])
            sbb = sinb.broadcast_to([128, nh, 8])
            R = rtmp[:, 0:nh, :]
            T1 = rtmp[:, 0:nh, :]
            ra = rtmp[:].rearrange("p a b -> p (a b)")
            t_a = ra[:, 0:nh * 8].rearrange("p (h j) -> p h j", h=nh)
            t_b = ra[:, 80 - 0:80].rearrange("p (h j) -> p h j", h=1) if False else None
            S.op('dve', ['qkv', 'c_rope'], ['rtmp'], lambda e, x1=x1, cb=cb, t_a=t_a: e.tensor_tensor(out=t_a, in0=x1, in1=cb, op=ALU.mult))
            S.op('dve', ['qkv', 'c_rope'], ['sm'], lambda e, x2=x2, sbb=sbb, nh=nh: e.tensor_tensor(out=sm[:, 0:nh * 8].rearrange("p (h j) -> p h j", h=nh), in0=x2, in1=sbb, op=ALU.mult))
            S.op('dve', ['qkv', 'c_rope'], ['sm'], lambda e, x2=x2, cb=cb, nh=nh: e.tensor_tensor(out=sm[:, 64:64 + nh * 8].rearrange("p (h j) -> p h j", h=nh), in0=x2, in1=cb, op=ALU.mult))
            S.op('dve', ['qkv', 'c_rope'], ['sm'], lambda e, x1=x1, sbb=sbb, nh=nh: e.tensor_tensor(out=sm[:, 128:128 + nh * 8].rearrange("p (h j) -> p h j", h=nh), in0=x1, in1=sbb, op=ALU.mult))
            S.op('dve', ['rtmp', 'sm'], ['qkv'], lambda e, x1=x1, t_a=t_a, nh=nh: e.tensor_tensor(out=x1, in0=t_a, in1=sm[:, 0:nh * 8].rearrange("p (h j) -> p h j", h=nh), op=ALU.subtract))
            S.op('dve', ['sm'], ['qkv'], lambda e, x2=x2, nh=nh: e.tensor_tensor(out=x2, in0=sm[:, 64:64 + nh * 8].rearrange("p (h j) -> p h j", h=nh), in1=sm[:, 128:128 + nh * 8].rearrange("p (h j) -> p h j", h=nh), op=ALU.add))
        S.op('act', ['qkv'], ['qb'], lambda e: e.copy(out=qb[:], in_=qkv[:]))
        S.op('pool', ['qkv'], ['Vb%d' % par], lambda e: e.tensor_copy(out=Vb[par][:], in_=qkv[:, 640:768]))
        for h in range(8):
            S.op('pe', ['qb', 'c_cstb'], ['P2'], lambda e, h=h: e.transpose(out=pTb[2][0:64, h * 128:(h + 1) * 128], in_=qb[:, h * 64:(h + 1) * 64], identity=ident_b))
        for kv in range(2):
            S.op('pe', ['qb', 'c_cstb'], ['P3'], lambda e, kv=kv: e.transpose(out=pTb[3][0:64, kv * 128:(kv + 1) * 128], in_=qb[:, 512 + kv * 64:512 + (kv + 1) * 64], identity=ident_b))
        S.op('act', ['P2'], ['QT'], lambda e: e.copy(out=QT[:], in_=pTb[2][0:64, 0:1024].rearrange("p (h t) -> p h t", h=8)))
        S.op('dve', ['P3'], ['KT%d' % par], lambda e: e.tensor_copy(out=KT[par][:], in_=pTb[3][0:64, 0:256].rearrange("p (h t) -> p h t", h=2)))
        pp = 1 - par
        if samp:
            S.dma('sp', ckb[:, 0:128], ck[sq], [], ['sm'])
            S.dma('sp', ckb[:, 128:256], cv[sq], [], ['sm'])
            S.op('act', ['sm'], ['ycatT'], lambda e: e.copy(out=junk[:, 0:256], in_=ckb[:]))
            for kv in range(2):
                S.op('pe', ['ycatT', 'c_cstb'], ['P3'], lambda e, kv=kv: e.transpose(out=pTb[3][0:64, 256 + kv * 128:256 + (kv + 1) * 128], in_=junk[:, kv * 64:(kv + 1) * 64], identity=ident_b))
            S.op('dve', ['P3'], ['KT%d' % pp], lambda e: e.tensor_copy(out=KT[pp][:], in_=pTb[3][0:64, 256:512].rearrange("p (h t) -> p h t", h=2)))
            S.op('pool', ['ycatT'], ['Vb%d' % pp], lambda e: e.tensor_copy(out=Vb[pp][:], in_=junk[:, 128:256]))
            S.dma('sp', kw_s[sq, 0:124, :], ck[sq, 4:128, :], [], [])
            S.dma('sp', vw_s[sq, 0:124, :], cv[sq, 4:128, :], [], [])
            S.dma('sp', kw_s[sq, 124:128, :], qkv[0:4, 512:640], ['qkv'], [])
            S.dma('sp', vw_s[sq, 124:128, :], qkv[0:4, 640:768], ['qkv'], [])
        elif ti == OUT_T:
            S.dma('sp', kw_p, qkv[:, 512:640], ['qkv'], [])
            S.dma('sp', vw_p, qkv[:, 640:768], ['qkv'], [])
        if state_only:
            return
        mi = 2 if (samp or ti - NT_A > 1) else (ti - NT_A)
        for h in range(8):
            kv = h // 4
            bank = 4 + (h % 2)
            S.op('pe', ['QT', 'KT%d' % pp], [PK[bank]], lambda e, h=h, kv=kv, bank=bank: e.matmul(out=P[bank][:, 0:128], lhsT=QT[:, h, :], rhs=KT[pp][:, kv, :], start=True, stop=True))
            S.op('pe', ['QT', 'KT%d' % par], [PK[bank]], lambda e, h=h, kv=kv, bank=bank: e.matmul(out=P[bank][:, 128:256], lhsT=QT[:, h, :], rhs=KT[par][:, kv, :], start=True, stop=True))
            S.op('dve', [PK[bank], 'c_amask'], ['sm'], lambda e, bank=bank: e.scalar_tensor_tensor(out=sm[:], in0=P[bank][:, 0:256], scalar=0.125, in1=c_amask[:, mi, :], op0=ALU.mult, op1=ALU.add))
            S.op('dve', ['sm'], ['ast'], lambda e: e.tensor_reduce(out=ast[:, 0:1], in_=sm[:], axis=AX.X, op=ALU.max))
            S.op('dve', ['ast', 'c_sink'], ['ast'], lambda e, h=h: e.tensor_scalar(out=ast[:, 1:2], in0=ast[:, 0:1], scalar1=c_sink[:, h:h + 1], scalar2=-1.0, op0=ALU.max, op1=ALU.mult))
            S.op('act', ['sm', 'ast'], ['eb', 'ast'], lambda e: e.activation(out=eb[:], in_=sm[:], func=AF.Exp, bias=ast[:, 1:2], scale=1.0, accum_out=ast[:, 2:3]))
            S.op('act', ['ast', 'c_sink'], ['ast'], lambda e, h=h: e.activation(out=ast[:, 3:4], in_=c_sink[:, h:h + 1], func=AF.Exp, bias=ast[:, 1:2], scale=1.0))
            S.op('dve', ['ast'], ['ast'], lambda e: e.tensor_tensor(out=ast[:, 4:5], in0=ast[:, 2:3], in1=ast[:, 3:4], op=ALU.add))
            S.op('dve', ['ast'], ['ast'], lambda e: e.reciprocal(out=ast[:, 5:6], in_=ast[:, 4:5]))
            for half in range(2):
                S.op('pe', ['eb', 'c_cstb'], ['P6'], lambda e, half=half: e.transpose(out=pTb[6][:, half * 128:(half + 1) * 128], in_=eb[:, half * 128:(half + 1) * 128], identity=ident_b))
            S.op('act', ['P6'], ['eT'], lambda e: e.copy(out=eT[:], in_=pTb[6][:, 0:256].rearrange("p (a t) -> p a t", a=2)))
            S.op('pe', ['eT', 'Vb%d' % pp], ['P7'], lambda e, kv=kv: e.matmul(out=P[7][:, 0:64], lhsT=eT[:, 0, :], rhs=Vb[pp][:, kv * 64:(kv + 1) * 64], start=True, stop=False))
            S.op('pe', ['eT', 'Vb%d' % par], ['P7'], lambda e, kv=kv: e.matmul(out=P[7][:, 0:64], lhsT=eT[:, 1, :], rhs=Vb[par][:, kv * 64:(kv + 1) * 64], start=False, stop=True))
            S.op('dve', ['P7', 'ast'], ['ycat'], lambda e, h=h: e.tensor_scalar(out=ycat[:, 512 + h * 64:512 + (h + 1) * 64], in0=P[7][:, 0:64], scalar1=ast[:, 5:6], scalar2=None, op0=ALU.mult))
        DUMP("ycat", ycat, 'ycat', ti); DUMP("qkv", qkv, 'qkv', ti); DUMP("QT", QT, 'QT', ti)
        if OPTS['stage'] <= 7:
            return
        for kc in range(8):
            S.op('pe', ['ycat', 'c_cstb'], ['P2'], lambda e, kc=kc: e.transpose(out=pTb[2][:, kc * 128:(kc + 1) * 128], in_=ycat[:, kc * 128:(kc + 1) * 128], identity=ident_b))
        S.op('act', ['P2'], ['ycatT'], lambda e: e.copy(out=ycatT[:], in_=pTb[2][:, 0:1024].rearrange("p (k t) -> p k t", k=8)))
        for half in range(2):
            bank = 3 + half
            for kc in range(8):
                S.op('pe', ['ycatT', 'Wo'], [PK[bank]], lambda e, kc=kc, half=half, bank=bank: e.matmul(out=P[bank][:], lhsT=ycatT[:, kc, :], rhs=Wo[:, kc, half * 512:(half + 1) * 512], start=(kc == 0), stop=(kc == 7)))
            S.op('dve', [PK[bank], xk], ['xm'], lambda e, half=half, bank=bank: e.tensor_tensor(out=xm[:, half * 512:(half + 1) * 512], in0=P[bank][:], in1=xb[:, half * 512:(half + 1) * 512], op=ALU.add))
        if dbg and ti == OPTS['dbg_tile']:
            S.op('dve', ['ycat'], ['a_t1'], lambda e: e.tensor_copy(out=A["t1"][:], in_=ycat[:, 0:512]))
            S.op('dve', ['ycat'], ['a_t2'], lambda e: e.tensor_copy(out=A["t2"][:], in_=ycat[:, 512:1024]))
            S.dma('sp', dbg_t["d_y"], A["t1"][:], ['a_t1'], [])
            S.dma('sp', dbg_t["d_at"], A["t2"][:], ['a_t2'], [])
            S.dma('sp', dbg_t["d_S"], S32[:].rearrange("p h i -> p (h i)"), ['S32'], [])
        DUMP("xm", xm, 'xm', ti)
        if samp:
            S.dma('sp', xmid[NT_B * 128 + sq * 4:NT_B * 128 + sq * 4 + 4, :], xm[0:4, :], ['xm'], ['xmid'])
        else:
            S.dma('sp', xmid[(ti - NT_A) * 128:(ti - NT_A + 1) * 128, :], xm[:], ['xm'], ['xmid'])


    def peer_phase():
        c_g2bc = sb("c_g2bc", [128, D])
        S.dma('sp', c_g2bc[:], g2.partition_broadcast(128)[:, 0, :], [], ['c_g2bc'])
        c_gFbc = sb("c_gFbc", [128, D])
        S.dma('sp', c_gFbc[:], gF.partition_broadcast(128)[:, 0, :], [], ['c_gFbc'])
        c_iota = sb("c_iota", [128, 256])
        S.dma('sp', c_iota[:], iota_in, [], ['c_iota'])
        Wq = sb("Wq", [128, 8, 2048], BF16)
        skT = sb("skT", [128, 2, 128], BF16)
        xm2 = sb("xm2", [128, D])
        hn32 = sb("hn32", [128, D])
        hnb = sb("hnb", [128, D], BF16)
        hn2T = sb("hn2T", [128, 8, 128], BF16)
        qT = sb("qT", [128, 16, 128], BF16)
        s_sb = sb("s_sb", [128, 16, 128])
        s2 = sb("s2", [128, 16, 128])
        tv = sb("tv", [128, 16, 16])
        tiu = sb("tiu", [128, 16, 16], U32)
        tif = sb("tif", [128, 16, 16])
        cand = sb("cand", [128, 8, 256])
        cand2 = sb("cand2", [128, 8, 256])
        cidx = sb("cidx", [128, 8, 256])
        top = sb("top", [128, 8, 16])
        selu = sb("selu", [128, 8, 16], U32)
        self_ = sb("self", [128, 8, 16])
        idxf = sb("idxf", [128, 8, 16])
        idxu = sb("idxu", [128, 128], U32)
        gate = sb("gate", [128, 8, 16])
        gst = sb("gst", [128, 8, 2])
        pre = sb("pre", [128, 128])
        wgt = sb("wgt", [128, 128])
        acc = sb("acc", [128, D])
        ss2 = sb("ss2", [128, 4])
        NG = 4
        gb = [sb("gb%d" % i, [128, D]) for i in range(NG)]
        with nc.sbuf_tensor("stg2", [128, 2048], F32) as stg2:
            for kc in range(8):
                S.dma('sp', stg2[:], w_q[kc * 128:(kc + 1) * 128, :], [], ['stg2'])
                S.op('act', ['stg2'], ['Wq'], lambda e, kc=kc: e.copy(out=Wq[:, kc, :], in_=stg2[:]))
            S.dma('sp', stg2[:, 0:256].rearrange("p (c d) -> p c d", c=2), subk.rearrange("c n d -> n c d"), [], ['stg2'])
            S.op('act', ['stg2'], ['hnb'], lambda e: e.copy(out=hnb[:, 0:256], in_=stg2[:, 0:256]))
            for c in range(2):
                S.op('pe', ['hnb', 'c_cstb'], ['P0'], lambda e, c=c: e.transpose(out=P[0][:].bitcast(BF16)[:, c * 128:(c + 1) * 128], in_=hnb[:, c * 128:(c + 1) * 128], identity=ident_b))
            S.op('act', ['P0'], ['skT'], lambda e: e.copy(out=skT[:], in_=P[0][:].bitcast(BF16)[:, 0:256].rearrange("p (c n) -> p c n", c=2)))
            barrier()
        pTb = [P[i][:].bitcast(BF16) for i in range(8)]
        for pt in range(NPE):
            samp = pt >= NT_B
            S.dma('sp', xm2[:], xmid[pt * 128:(pt + 1) * 128, :], ['xmid'], ['xm2'])
            S.op('act', ['xm2'], ['hnb', 'ss2'], lambda e: e.activation(out=hnb[:], in_=xm2[:], func=AF.Square, accum_out=ss2[:, 0:1]))
            rsq('ss2', ss2[:, 1:2], ss2[:, 0:1], D * 1e-5, ALU.add)
            S.op('dve', ['xm2', 'ss2'], ['hn32'], lambda e: e.tensor_scalar(out=hn32[:], in0=xm2[:], scalar1=ss2[:, 1:2], scalar2=32.0, op0=ALU.mult, op1=ALU.mult))
            S.op('pool', ['hn32', 'c_g2bc'], ['hn32'], lambda e: e.tensor_tensor(out=hn32[:], in0=hn32[:], in1=c_g2bc[:], op=ALU.mult))
            S.op('act', ['hn32'], ['hnb'], lambda e: e.copy(out=hnb[:], in_=hn32[:]))
            for kc in range(8):
                S.op('pe', ['hnb', 'c_cstb'], ['P0'], lambda e, kc=kc: e.transpose(out=pTb[0][:, kc * 128:(kc + 1) * 128], in_=hnb[:, kc * 128:(kc + 1) * 128], identity=ident_b))
            S.op('act', ['P0'], ['hn2T'], lambda e: e.copy(out=hn2T[:], in_=pTb[0][:, 0:1024].rearrange("p (k t) -> p k t", k=8)))
            for hc in range(16):
                bank = 1 + hc // 4
                cs = slice((hc % 4) * 128, (hc % 4 + 1) * 128)
                for kc in range(8):
                    S.op('pe', ['Wq', 'hn2T'], [PK[bank]], lambda e, hc=hc, kc=kc, bank=bank, cs=cs: e.matmul(out=P[bank][:, cs], lhsT=Wq[:, kc, hc * 128:(hc + 1) * 128], rhs=hn2T[:, kc, :], start=(kc == 0), stop=(kc == 7)))
            for b in range(4):
                S.op('act' if b % 2 else 'dve', [PK[1 + b]], ['qT'], lambda e, b=b: (e.copy if b % 2 else e.tensor_copy)(out=qT[:, b * 4:(b + 1) * 4, :], in_=P[1 + b][:].rearrange("p (a t) -> p a t", a=4)))
            sbanks = [5, 6, 7, 0]
            for hc in range(16):
                bank = sbanks[hc // 4]
                cs = slice((hc % 4) * 128, (hc % 4 + 1) * 128)
                S.op('pe', ['qT', 'skT'], [PK[bank]], lambda e, hc=hc, bank=bank, cs=cs: e.matmul(out=P[bank][:, cs], lhsT=qT[:, hc, :], rhs=skT[:, hc % 2, :], start=True, stop=True))
            for b in range(4):
                S.op('act' if b % 2 else 'dve', [PK[sbanks[b]]], ['s_sb'], lambda e, b=b: (e.copy if b % 2 else e.tensor_copy)(out=s_sb[:, b * 4:(b + 1) * 4, :], in_=P[sbanks[b]][:].rearrange("p (a t) -> p a t", a=4)))
            for hc in range(16):
                S.op('dve', ['s_sb'], ['tv'], lambda e, hc=hc: e.max(out=tv[:, hc, 0:8], in_=s_sb[:, hc, :]))
                S.op('dve', ['s_sb', 'tv'], ['tiu'], lambda e, hc=hc: e.max_index(out=tiu[:, hc, 0:8], in_max=tv[:, hc, 0:8], in_values=s_sb[:, hc, :]))
                S.op('dve', ['s_sb', 'tv'], ['s2'], lambda e, hc=hc: e.match_replace(out=s2[:, hc, :], in_to_replace=tv[:, hc, 0:8], in_values=s_sb[:, hc, :], imm_value=-1e30))
                S.op('dve', ['s2'], ['tv'], lambda e, hc=hc: e.max(out=tv[:, hc, 8:16], in_=s2[:, hc, :]))
                S.op('dve', ['s2', 'tv'], ['tiu'], lambda e, hc=hc: e.max_index(out=tiu[:, hc, 8:16], in_max=tv[:, hc, 8:16], in_values=s2[:, hc, :]))
            S.op('dve', ['tiu'], ['tif'], lambda e: e.tensor_copy(out=tif[:], in_=tiu[:]))
            tv4 = tv[:].rearrange("p (h c) k -> p h c k", c=2)
            tf4 = tif[:].rearrange("p (h c) k -> p h c k", c=2)
            c4 = lambda t: t[:].rearrange("p h (a b) -> p h a b", a=16)
            S.op('dve', ['tv'], ['cand'], lambda e: e.tensor_tensor(out=c4(cand), in0=tv4[:, :, 0, :].unsqueeze(3).broadcast_to([128, 8, 16, 16]),
                                                                    in1=tv4[:, :, 1, :].unsqueeze(2).broadcast_to([128, 8, 16, 16]), op=ALU.add))
            S.op('dve', ['tif'], ['tif'], lambda e: e.tensor_scalar(out=tf4[:, :, 0, :], in0=tf4[:, :, 0, :], scalar1=128.0, scalar2=None, op0=ALU.mult))
            S.op('dve', ['tif'], ['cidx'], lambda e: e.tensor_tensor(out=c4(cidx), in0=tf4[:, :, 0, :].unsqueeze(3).broadcast_to([128, 8, 16, 16]),
                                                                     in1=tf4[:, :, 1, :].unsqueeze(2).broadcast_to([128, 8, 16, 16]), op=ALU.add))
            for h in range(8):
                S.op('dve', ['cand'], ['top'], lambda e, h=h: e.max(out=top[:, h, 0:8], in_=cand[:, h, :]))
                S.op('dve', ['cand', 'top'], ['selu'], lambda e, h=h: e.max_index(out=selu[:, h, 0:8], in_max=top[:, h, 0:8], in_values=cand[:, h, :]))
                S.op('dve', ['cand', 'top'], ['cand2'], lambda e, h=h: e.match_replace(out=cand2[:, h, :], in_to_replace=top[:, h, 0:8], in_values=cand[:, h, :], imm_value=-1e30))
                S.op('dve', ['cand2'], ['top'], lambda e, h=h: e.max(out=top[:, h, 8:16], in_=cand2[:, h, :]))
                S.op('dve', ['cand2', 'top'], ['selu'], lambda e, h=h: e.max_index(out=selu[:, h, 8:16], in_max=top[:, h, 8:16], in_values=cand2[:, h, :]))
            S.op('dve', ['selu'], ['self'], lambda e: e.tensor_copy(out=self_[:], in_=selu[:]))
            for k in range(16):
                S.op('dve', ['self', 'c_iota'], ['cand2'], lambda e, k=k: e.tensor_tensor(out=cand2[:], in0=c_iota[:].unsqueeze(1).broadcast_to([128, 8, 256]),
                                                                                          in1=self_[:, :, k:k + 1].broadcast_to([128, 8, 256]), op=ALU.is_equal))
                S.op('pool', ['cand2', 'cidx'], ['cand2'], lambda e: e.tensor_tensor(out=cand2[:], in0=cand2[:], in1=cidx[:], op=ALU.mult))
                S.op('dve', ['cand2'], ['idxf'], lambda e, k=k: e.tensor_reduce(out=idxf[:, :, k], in_=cand2[:], axis=AX.X, op=ALU.add))
            S.op('dve', ['idxf'], ['idxf'], lambda e: e.tensor_scalar(out=idxf[:], in0=idxf[:], scalar1=0.0, scalar2=float(NEXP - 1), op0=ALU.max, op1=ALU.min))
            S.op('dve', ['idxf'], ['idxu'], lambda e: e.tensor_copy(out=idxu[:], in_=idxf[:].rearrange("p h k -> p (h k)")))
            S.op('dve', ['top'], ['gate'], lambda e: e.tensor_tensor(out=gate[:], in0=top[:], in1=top[:, :, 0:1].broadcast_to([128, 8, 16]), op=ALU.subtract))
            S.op('act', ['gate'], ['gate'], lambda e: e.activation(out=gate[:], in_=gate[:], func=AF.Exp))
            S.op('dve', ['gate'], ['gst'], lambda e: e.tensor_reduce(out=gst[:, :, 0], in_=gate[:], axis=AX.X, op=ALU.add))
            S.op('dve', ['gst'], ['gst'], lambda e: e.reciprocal(out=gst[:, :, 1], in_=gst[:, :, 0]))
            S.op('dve', ['gate', 'gst'], ['gate'], lambda e: e.tensor_tensor(out=gate[:], in0=gate[:], in1=gst[:, :, 1:2].broadcast_to([128, 8, 16]), op=ALU.mult))
            for sl in range(128):
                g = sl % NG
                S.dma('pool', None, None, ['idxu'], ['gb%d' % g], fn=lambda e, sl=sl, g=g: e.indirect_dma_start(
                    out=gb[g][:], out_offset=None, in_=eu, in_offset=bass.IndirectOffsetOnAxis(ap=idxu[:, sl:sl + 1], axis=0)))
                S.op('dve', ['gb%d' % g, 'hn32'], ['gb%d' % g], lambda e, sl=sl, g=g: e.tensor_tensor(out=gb[g][:], in0=gb[g][:], in1=hn32[:], op=ALU.mult))
                S.op('dve', ['gb%d' % g], ['pre'], lambda e, sl=sl, g=g: e.tensor_reduce(out=pre[:, sl:sl + 1], in_=gb[g][:], axis=AX.X, op=ALU.add))
            S.op('act', ['pre'], ['wgt'], lambda e: e.activation(out=wgt[:], in_=pre[:], func=AF.Gelu))
            S.op('dve', ['wgt', 'gate'], ['wgt'], lambda e: e.tensor_tensor(out=wgt[:], in0=wgt[:], in1=gate[:].rearrange("p h k -> p (h k)"), op=ALU.mult))
            S.op('dve', ['xm2'], ['acc'], lambda e: e.tensor_copy(out=acc[:], in_=xm2[:]))
            for sl in range(128):
                g = sl % NG
                S.dma('pool', None, None, ['idxu'], ['gb%d' % g], fn=lambda e, sl=sl, g=g: e.indirect_dma_start(
                    out=gb[g][:], out_offset=None, in_=ev, in_offset=bass.IndirectOffsetOnAxis(ap=idxu[:, sl:sl + 1], axis=0)))
                S.op('dve', ['gb%d' % g, 'wgt', 'acc'], ['acc'], lambda e, sl=sl, g=g: e.scalar_tensor_tensor(
                    out=acc[:], in0=gb[g][:], scalar=wgt[:, sl:sl + 1], in1=acc[:], op0=ALU.mult, op1=ALU.add))
            S.op('act', ['acc'], ['hnb', 'ss2'], lambda e: e.activation(out=hnb[:], in_=acc[:], func=AF.Square, accum_out=ss2[:, 2:3]))
            rsq('ss2', ss2[:, 3:4], ss2[:, 2:3], D * 1e-5, ALU.add)
            S.op('dve', ['acc', 'ss2'], ['acc'], lambda e: e.tensor_scalar(out=acc[:], in0=acc[:], scalar1=ss2[:, 3:4], scalar2=32.0, op0=ALU.mult, op1=ALU.mult))
            S.op('pool', ['acc', 'c_gFbc'], ['acc'], lambda e: e.tensor_tensor(out=acc[:], in0=acc[:], in1=c_gFbc[:], op=ALU.mult))
            if samp:
                S.dma('sp', y_s[:, :], acc[0:NSEQ_S * 4, :], ['acc'], [])
            else:
                S.dma('sp', y_p[pt * 128:(pt + 1) * 128, :], acc[:], ['acc'], [])

    for ti in range(NTT):
        load_x(ti)
        mix_tile(ti)
    print("sbuf left", nc.sbuf_bytes_remaining() if callable(nc.sbuf_bytes_remaining) else nc.sbuf_bytes_remaining)
    print("total ops", getattr(S, 'n', 0))
    barrier()
    ph1.close()
    stk['cur'] = glob_stack
    if OPTS['peer']:
        peer_phase()
    barrier()
    S.finish()
    return nc


_CACHE = {}
NTA_FULL, NTB_FULL = 17, 17


def _consts(nta, ntb, hf):
    ar = np.arange(128)
    ident = np.eye(128, dtype=np.float32)
    tri = (ar[:, None] <= ar[None, :]).astype(np.float32)
    ones = np.ones((128, 128), np.float32)
    su = (ar[:, None] < ar[None, :]).astype(np.float32)
    lo = (ar[:, None] > ar[None, :]).astype(np.float32)
    cst = np.stack([ident, tri, ones, su, tri, lo], axis=1).astype(np.float32)
    q = ar[:, None]
    c = np.arange(256)[None, :]
    ok = (c > q) & (c <= q + 128)
    m_std = np.where(ok, 0.0, -30000.0).astype(np.float32)
    m_t0 = np.where(ok & (c >= 240), 0.0, -30000.0).astype(np.float32)
    m_t1 = np.where(ok & (c >= 112), 0.0, -30000.0).astype(np.float32)
    first = (hf == 0) or (nta == 0)
    cmask = np.stack([m_t0 if first else m_std, m_t1 if first else m_std, m_std], axis=1)
    inv = (np.float32(500000.0) ** (-np.arange(0, 16, 2, dtype=np.float32) / np.float32(16))).astype(np.float32)
    ntp = nta + ntb
    rope = np.zeros((128, ntp + 1, 16), np.float32)
    for i in range(ntp + 1):
        if i < ntp:
            st = i if (hf == 1 or nta == 0) else (i - nta if i >= nta else i)
            pos = st * 128 - 112 + ar
        else:
            pos = PAST + ar
        ang = pos.astype(np.float32)[:, None] * inv[None, :]
        rope[:, i, 0:8] = np.cos(ang)
        rope[:, i, 8:16] = np.sin(ang)
    vmask = np.zeros((128, 2), np.float32)
    vmask[:, 0] = 1.0
    vmask[0:4, 1] = 1.0
    iota = np.tile(np.arange(256, dtype=np.float32)[None, :], (128, 1))
    return dict(cst=cst, cmask=cmask, rope=rope, vmask=vmask, iota=iota)


def kernel(x_prompt, x_sample, cache_k_win, cache_v_win, state_wkv, state_shift, meta_tokens, norm1_g, w_in, mu_shift,
           w0, w_lora_w2, a0, w_lora_a2, w_lora_g2, k_k, k_a, r_k, lnx_w, lnx_b, attn_sinks, w_out, norm2_g, w_query,
           sub_keys, expert_u, expert_v, final_norm_g, _nta=NTA_FULL, _ntb=NTB_FULL, _nts=NSEQ_S, _dbg=False):
    f = lambda a: np.ascontiguousarray(np.asarray(a), dtype=np.float32)
    key = (_nta, _ntb, _nts)
    if key not in _CACHE:
        _CACHE[key] = build(_nta, _ntb, _nts, dbg=_dbg)
    nc = _CACHE[key]
    x_prompt, x_sample = f(x_prompt), f(x_sample)
    B = x_prompt.shape[0]
    nseqt = _nta + _ntb - 1
    shared = dict(
        w_in=f(w_in)[0], w_out=f(w_out)[0], w_q=f(w_query)[0], subk=f(sub_keys)[0],
        eu=f(expert_u)[0][:OPTS['nexp']], ev=f(expert_v)[0][:OPTS['nexp']],
        lw2=f(w_lora_w2)[0], la2=f(w_lora_a2)[0], lg2=f(w_lora_g2)[0],
        vec512=np.stack([f(w0)[0], f(a0)[0], f(k_k)[0], f(k_a)[0], f(r_k)[0].reshape(512), f(lnx_w)[0], f(lnx_b)[0]]),
        mu=f(mu_shift), g1=f(norm1_g), g2=f(norm2_g), gF=f(final_norm_g)[None, :], sinks=f(attn_sinks))
    cs = [_consts(_nta, _ntb, hf) for hf in range(2)]
    in_maps = []
    for c in range(NCORES):
        b, hf = c // 2, c % 2
        seq = np.zeros((nseqt * 128, D), np.float32)
        seq[112:128] = f(meta_tokens)
        seq[128:] = x_prompt[b][:(nseqt - 1) * 128]
        xp = np.zeros(((_nta + _ntb) * 128, D), np.float32)
        if hf == 0:
            xp[_nta * 128:(_nta + _ntb) * 128] = seq[:_ntb * 128]
        else:
            xp[:nseqt * 128] = seq
        sl = slice(c * NSEQ_S, (c + 1) * NSEQ_S)
        m = dict(shared)
        m.update(cs[hf])
        m.update(xp=xp, xs=x_sample[sl].reshape(NSEQ_S * 4, D),
                 ck=f(cache_k_win)[0, sl].reshape(NSEQ_S, 128, 128), cv=f(cache_v_win)[0, sl].reshape(NSEQ_S, 128, 128),
                 swkv=f(state_wkv)[0, sl], sshift=f(state_shift)[0, sl])
        in_maps.append(m)
    res = run_bass_kernel_spmd(nc, in_maps, core_ids=list(range(NCORES))).results
    if (_nta, _ntb, _nts) != (NTA_FULL, NTB_FULL, NSEQ_S):
        return res
    y_prompt = np.stack([np.concatenate([res[2 * b]["y_p"][128:_ntb * 128], res[2 * b + 1]["y_p"][:(_ntb - 1) * 128]]) for b in range(B)])
    y_sample = np.concatenate([res[c]["y_s"].reshape(NSEQ_S, 4, D) for c in range(NCORES)])
    od = lambda b: res[2 * b + 1]
    kwp = np.stack([od(b)["kw_p"].reshape(128, 2, 64) for b in range(B)])[None]
    vwp = np.stack([od(b)["vw_p"].reshape(128, 2, 64) for b in range(B)])[None]
    wkvp = np.stack([od(b)["wkv_p"] for b in range(B)])[None]
    shp = np.stack([od(b)["sh_p"][0] for b in range(B)])[None]
    kws = np.concatenate([res[c]["kw_s"].reshape(NSEQ_S, 128, 2, 64) for c in range(NCORES)])[None]
    vws = np.concatenate([res[c]["vw_s"].reshape(NSEQ_S, 128, 2, 64) for c in range(NCORES)])[None]
    wkvs = np.concatenate([res[c]["wkv_s"] for c in range(NCORES)])[None]
    shs = np.concatenate([res[c]["sh_s"] for c in range(NCORES)])[None]
    return (y_prompt, y_sample, kwp, vwp, wkvp, shp, kws, vws, wkvs, shs)
```

```python
import numpy as np
from contextlib import ExitStack
import concourse.bass as bass
import concourse.mybir as mybir
from concourse.alu_op_type import AluOpType as ALU
from concourse.bass_utils import run_bass_kernel_spmd

F32 = mybir.dt.float32
BF16 = mybir.dt.bfloat16
U32 = mybir.dt.uint32
AF = mybir.ActivationFunctionType
AX = mybir.AxisListType

D = 1024
NRW = 1792
NCOL = 2560
NCORES = 8
NSEQ_S = 16
PAST = 8192
NPT = 33
NEXP = 16384
OPTS = {'limit': 10**9, 'dump_only': '', 'dumps': False, 'nexp': 16384, 'dbg_tile': 1, 'stage': 99, 'peer': True, 'win_copy': True, 'samp_state': True}


class Sched:
    def __init__(self, nc):
        self.nc = nc
        self.eng = {'pe': nc.tensor, 'act': nc.scalar, 'dve': nc.vector, 'pool': nc.gpsimd, 'sp': nc.sync}
        self.sem = {e: nc.alloc_semaphore("sem_" + e) for e in ['pe', 'act', 'dve', 'pool']}
        self.cnt = {e: 0 for e in self.sem}
        self.seen = {e: {} for e in self.eng}
        self.last_w = {}
        self.readers = {}
        self.dslots = {}
        for q in ['sp', 'pool', 'act']:
            self.dslots[q] = [[nc.alloc_semaphore("dq_%s_%d" % (q, i)), 0] for i in range(8)]
        self.dnext = {q: 0 for q in self.dslots}
        self.tokens = {}

    def _wait(self, e, toks):
        need = {}
        for t in toks:
            if t is None:
                continue
            sid, val, we = t
            if we == e and e == 'pe':
                continue
            if self.seen[e].get(sid, 0) >= val:
                continue
            if need.get(sid, (None, 0))[1] < val:
                need[sid] = (t, val)
        for sid, (t, val) in need.items():
            self.eng[e].wait_ge(self.tokens[sid], val)
            self.seen[e][sid] = val

    def _deps(self, e, reads, writes):
        toks = []
        for k in reads:
            toks.append(self.last_w.get(k))
        for k in writes:
            toks.append(self.last_w.get(k))
            for t in self.readers.get(k, []):
                toks.append(t[:3])
        return toks

    def _mark(self, tok, reads, writes, is_dma):
        for k in reads:
            self.readers.setdefault(k, []).append(tok + (is_dma,))
        for k in writes:
            self.last_w[k] = tok
            self.readers[k] = []

    def op(self, e, reads, writes, fn):
        self.n = getattr(self, 'n', 0) + 1
        if self.n > OPTS['limit']:
            return None
        self._wait(e, self._deps(e, reads, writes))
        inst = fn(self.eng[e])
        self.cnt[e] += 1
        sem = self.sem[e]
        inst.then_inc(sem, 1)
        sid = id(sem)
        self.tokens[sid] = sem
        self._mark((sid, self.cnt[e], e), reads, writes, False)
        return inst

    def dma(self, q, out, in_, reads, writes, fn=None):
        self.n = getattr(self, 'n', 0) + 1
        if self.n > OPTS['limit']:
            return None
        slots = self.dslots[q]
        i = self.dnext[q]
        self.dnext[q] = (i + 1) % len(slots)
        sem, val = slots[i]
        sid = id(sem)
        self.tokens[sid] = sem
        toks = self._deps(q, reads, writes)
        if val > 0:
            toks.append((sid, val, 'dma'))
        self._wait(q, toks)
        if fn is None:
            inst = self.eng[q].dma_start(out=out, in_=in_)
        else:
            inst = fn(self.eng[q])
        inst.then_inc(sem, 16)
        slots[i][1] = val + 16
        self._mark((sid, val + 16, 'dma'), reads, writes, True)

    def finish(self):
        for q, slots in self.dslots.items():
            toks = [(id(s), v, 'dma') for s, v in slots if v > 0]
            self._wait(q, toks)


def build(NT_A, NT_B, NT_S, dbg=False, dbg_tile=1):
    nc = bass.Bass("TRN2", target_bir_lowering=False)
    S = Sched(nc)
    NT_P = NT_A + NT_B
    NTT = NT_P + NT_S
    NPE = NT_B + (1 if NT_S else 0)
    OUT_T = NT_P - 2 if NT_A > 0 else NT_P - 1

    def din(name, shape, dt=F32):
        return nc.dram_tensor(name, list(shape), dt, kind="ExternalInput").ap()

    def dout(name, shape, dt=F32):
        return nc.dram_tensor(name, list(shape), dt, kind="ExternalOutput").ap()

    xp = din("xp", [max(NT_P, 1) * 128, D])
    xs = din("xs", [NSEQ_S * 4, D])
    ck = din("ck", [NSEQ_S, 128, 128])
    cv = din("cv", [NSEQ_S, 128, 128])
    swkv = din("swkv", [NSEQ_S, 8, 64, 64])
    sshift = din("sshift", [NSEQ_S, D])
    w_in = din("w_in", [D, NCOL])
    w_out = din("w_out", [D, D])
    w_q = din("w_q", [D, 2048])
    subk = din("subk", [2, 128, 128])
    eu = din("eu", [OPTS['nexp'], D])
    ev = din("ev", [OPTS['nexp'], D])
    lw2 = din("lw2", [64, 512])
    la2 = din("la2", [64, 512])
    lg2 = din("lg2", [128, 512])
    vec512 = din("vec512", [7, 512])
    mu = din("mu", [1, NRW])
    g1 = din("g1", [1, D])
    g2 = din("g2", [1, D])
    gF = din("gF", [1, D])
    sinks = din("sinks", [1, 8])
    rope = din("rope", [128, NT_P + 1, 16])
    cmask = din("cmask", [128, 3, 256])
    cst = din("cst", [128, 6, 128])
    vmask = din("vmask", [128, 2])
    iota_in = din("iota", [128, 256])

    y_p = dout("y_p", [max(NT_B, 1) * 128, D])
    y_s = dout("y_s", [NSEQ_S * 4, D])
    kw_p = dout("kw_p", [128, 128])
    vw_p = dout("vw_p", [128, 128])
    wkv_p = dout("wkv_p", [8, 64, 64])
    sh_p = dout("sh_p", [1, D])
    kw_s = dout("kw_s", [NSEQ_S, 128, 128])
    vw_s = dout("vw_s", [NSEQ_S, 128, 128])
    wkv_s = dout("wkv_s", [NSEQ_S, 8, 64, 64])
    sh_s = dout("sh_s", [NSEQ_S, D])
    xmid = nc.dram_tensor("xmid", [max(NPE, 1) * 128, D], F32, kind="Internal").ap()
    dbg_t = {}
    if dbg:
        for nm, shp in [("d_m", [128, NRW]), ("d_y", [128, 512]), ("d_at", [128, 512]), ("d_S", [64, 512]),
                        ("d_pre", [128, 128]), ("d_idx", [128, 128]), ("d_gate", [128, 128])]:
            dbg_t[nm] = dout(nm, shp)

    stk = {'cur': ExitStack()}
    glob_stack = stk['cur']

    dumped = {}

    def DUMP(name, t, key, ti=None):
        if not OPTS['dumps'] or (ti is not None and ti != OPTS['dbg_tile']) or name in dumped:
            return
        if OPTS['dump_only'] and name not in str(OPTS['dump_only']).split(','):
            return
        src = t if isinstance(t, bass.AP) else t[:]
        o = nc.dram_tensor("z_" + name, list(src.shape), src.dtype, kind="ExternalOutput").ap()
        dumped[name] = 1
        S.dma('sp', o, src, [key], [])

    def sb(name, shape, dt=F32):
        return stk['cur'].enter_context(nc.sbuf_tensor(name, list(shape), dt))

    def ps(name, shape, dt=F32):
        return nc.alloc_psum_tensor(name, list(shape), dt)

    c_cst = sb("c_cst", [128, 6, 128])
    S.dma('sp', c_cst[:], cst, [], ['c_cst'])
    c_cstb = sb("c_cstb", [128, 6, 128], BF16)
    S.op('dve', ['c_cst'], ['c_cstb'], lambda e: e.tensor_copy(out=c_cstb[:], in_=c_cst[:]))
    ident_f = c_cst[:, 0, :]
    tri_f = c_cst[:, 1, :]
    ones_f = c_cst[:, 2, :]
    ident_b = c_cstb[:, 0, :]
    c_m4 = sb("c_m4", [128, 4, 128])
    for i, j in enumerate([3, 4, 3, 4]):
        S.op('dve', ['c_cst'], ['c_m4'], lambda e, i=i, j=j: e.tensor_copy(out=c_m4[:, i, :], in_=c_cst[:, j, :]))
    c_low = c_cst[:, 5, :]
    c_amask = sb("c_amask", [128, 3, 256])
    S.dma('sp', c_amask[:], cmask, [], ['c_amask'])
    c_rope = sb("c_rope", [128, NT_P + 1, 16])
    S.dma('sp', c_rope[:], rope, [], ['c_rope'])
    c_vm = sb("c_vm", [128, 2])
    S.dma('sp', c_vm[:], vmask, [], ['c_vm'])
    c_v512 = sb("c_v512", [128, 7, 512])
    S.dma('sp', c_v512[:], vec512.partition_broadcast(128), [], ['c_v512'])
    W0, A0, KK, KA, RK, LW, LB = range(7)
    c_g1bc = sb("c_g1bc", [128, D])
    S.dma('sp', c_g1bc[:], g1.partition_broadcast(128)[:, 0, :], [], ['c_g1bc'])
    c_sink = sb("c_sink", [128, 8])
    S.dma('sp', c_sink[:], sinks.partition_broadcast(128)[:, 0, :], [], ['c_sink'])
    c_g1col = sb("c_g1col", [128, 8])
    with nc.allow_non_contiguous_dma(reason="tiny param column load"):
        S.dma('sp', c_g1col[:], g1.rearrange("o (kc p) -> p (o kc)", p=128), [], ['c_g1col'])

    def rsq(key, out, in_, c, op0):
        S.op('dve', [key], [key], lambda e: e.tensor_scalar(out=out, in0=in_, scalar1=c, scalar2=None, op0=op0))
        S.op('act', [key], [key], lambda e: e.activation(out=out, in_=out, func=AF.Sqrt))
        S.op('dve', [key], [key], lambda e: e.reciprocal(out=out, in_=out))

    P = [ps("P%d" % i, [128, 512]) for i in range(8)]
    PK = ["P%d" % i for i in range(8)]

    def barrier():
        toks = []
        for e, sem in S.sem.items():
            if S.cnt[e] > 0:
                S.tokens[id(sem)] = sem
                toks.append((id(sem), S.cnt[e], 'x'))
        for q, slots in S.dslots.items():
            for s, v in slots:
                if v > 0:
                    S.tokens[id(s)] = s
                    toks.append((id(s), v, 'dma'))
        for e in ['pe', 'act', 'dve', 'pool', 'sp']:
            S._wait(e, toks)

    ph1 = ExitStack()
    stk['cur'] = ph1
    W1 = sb("W1", [128, 8, NRW], BF16)
    W2 = sb("W2", [128, 8, NRW], BF16)
    Wat = sb("Wat", [128, 8, 768], BF16)
    Wo = sb("Wo", [128, 8, D], BF16)
    L_w2 = sb("L_w2", [128, 512], BF16)
    L_g2 = sb("L_g2", [128, 512], BF16)
    with nc.sbuf_tensor("stg", [128, NCOL], F32) as stg, nc.sbuf_tensor("mub", [128, NRW], F32) as mub, \
            nc.sbuf_tensor("omu", [128, NRW], F32) as omu:
        S.dma('sp', mub[:], mu.partition_broadcast(128)[:, 0, :], [], ['mub'])
        S.op('dve', ['mub'], ['omu'], lambda e: e.tensor_scalar(out=omu[:], in0=mub[:], scalar1=-1.0, scalar2=1.0,
                                                                op0=ALU.mult, op1=ALU.add))
        for kc in range(8):
            S.dma('sp', stg[:], w_in[kc * 128:(kc + 1) * 128, :], [], ['stg'])
            S.op('dve', ['stg', 'omu'], ['W1'], lambda e, kc=kc: e.tensor_tensor(out=W1[:, kc, :], in0=stg[:, 0:NRW], in1=omu[:], op=ALU.mult))
            S.op('pool', ['stg', 'mub'], ['W2'], lambda e, kc=kc: e.tensor_tensor(out=W2[:, kc, :], in0=stg[:, 0:NRW], in1=mub[:], op=ALU.mult))
            S.op('act', ['stg'], ['Wat'], lambda e, kc=kc: e.copy(out=Wat[:, kc, :], in_=stg[:, NRW:NCOL]))
        for kc in range(8):
            S.dma('sp', stg[:, 0:D], w_out[kc * 128:(kc + 1) * 128, :], [], ['stg'])
            S.op('act', ['stg'], ['Wo'], lambda e, kc=kc: e.copy(out=Wo[:, kc, :], in_=stg[:, 0:D]))
        S.dma('sp', stg[0:64, 0:512], lw2, [], ['stg'])
        S.dma('sp', stg[64:128, 0:512], la2, [], ['stg'])
        S.dma('sp', stg[:, 512:1024], lg2, [], ['stg'])
        S.op('act', ['stg'], ['L_w2'], lambda e: e.copy(out=L_w2[:], in_=stg[:, 0:512]))
        S.op('act', ['stg'], ['L_g2'], lambda e: e.copy(out=L_g2[:], in_=stg[:, 512:1024]))
        barrier()

    xt0 = sb("xt0", [128, D])
    xt = [xt0, xt0]
    xts = xt0
    xn = sb("xn", [128, D], BF16)
    ssq = sb("ssq", [128, 4])
    hT = sb("hT", [128, 8, 128], BF16)
    hTs = sb("hTs", [128, 8, 128], BF16)
    S.op('pool', [], ['hT'], lambda e: e.memset(hT[:], 0.0))
    S.op('pool', [], ['hTs'], lambda e: e.memset(hTs[:], 0.0))
    A = {}
    for nm in ["r", "k", "v", "ld", "asig", "g", "kk", "b", "kmod", "c", "t1", "t2", "t3"]:
        A[nm] = sb("a_" + nm, [128, 512])
    Bt = {}
    for nm in ["rt", "at", "bt", "kt", "bbar", "kbar", "vb", "W0T", "UT"]:
        Bt[nm] = sb("b_" + nm, [128, 512], BF16)
    lin = sb("lin", [128, 2, 128], BF16)
    st8 = sb("st8", [128, 8, 4])
    AR_fm = sb("AR_fm", [64, 8, 256], BF16)
    B_fm = sb("B_fm", [64, 8, 128], BF16)
    K_fm = sb("K_fm", [64, 8, 128], BF16)
    MATS = sb("MATS", [128, 8, 512], BF16)
    _m = sb("Mb", [128, 8, 128], BF16)
    _mt = sb("MTb", [128, 8, 128], BF16)
    _t = sb("Tb", [128, 8, 128], BF16)
    Mb, MTb, Tb = [_m, _m], [_mt, _mt], [_t, _t]
    S32 = sb("S32", [64, 8, 64])
    Sb = sb("Sb", [64, 8, 64], BF16)
    ecl_fm = sb("ecl_fm", [64, 8])
    sti = sb("sti", [64, 8, 64])
    qkv = sb("qkv", [128, 768])
    rtmp = sb("rtmp", [128, 10, 8])
    qb = sb("qb", [128, 768], BF16)
    QT = sb("QT", [64, 8, 128], BF16)
    KT = [sb("KT%d" % i, [64, 2, 128], BF16) for i in range(2)]
    Vb = [sb("Vb%d" % i, [128, 128], BF16) for i in range(2)]
    sm = sb("sm", [128, 256])
    for i in range(2):
        S.op('pool', [], ['KT%d' % i], lambda e, i=i: e.memset(KT[i][:], 0.0))
        S.op('pool', [], ['Vb%d' % i], lambda e, i=i: e.memset(Vb[i][:], 0.0))
    eb = sb("eb", [128, 256], BF16)
    eT = sb("eT", [128, 2, 128], BF16)
    ast = sb("ast", [128, 8])
    ycat = sb("ycat", [128, D], BF16)
    ycatT = sb("ycatT", [128, 8, 128], BF16)
    junk = ycatT[:].rearrange("p k t -> p (k t)")
    xm = sb("xm", [128, D])
    hrow = xm
    ckb = sm

    def v3(ap, h=8):
        return ap.rearrange("p (h j) -> p h j", h=h)

    def bc_last(ap, n):
        return ap.broadcast_to([ap.shape[0], ap.shape[1], n])

    def V512(i):
        return c_v512[:, i, :]

    def load_x(ti):
        if ti < NT_P:
            b = xt[ti % 2]
            S.dma('sp', b[:], xp[ti * 128:(ti + 1) * 128, :], [], ['xt0'])
        else:
            s = ti - NT_P
            if s == 0:
                S.op('pool', [], ['xt0'], lambda e: e.memset(xt0[:], 0.0))
                S.dma('sp', xmid[NT_B * 128:(NT_B + 1) * 128, :], xt0[:], ['xt0'], ['xmid'])
            S.dma('sp', xts[0:4, :], xs[s * 4:(s + 1) * 4, :], [], ['xt0'])

    def mix_tile(ti):
        samp = ti >= NT_P
        sq = ti - NT_P
        xk = 'xt0'
        xb = xts if samp else xt[ti % 2]
        par = ti % 2
        vm = c_vm[:, 1:2] if samp else c_vm[:, 0:1]
        if OPTS['stage'] <= 0:
            return
        S.op('act', [xk], ['ycatT', 'ssq'], lambda e: e.activation(out=junk[:], in_=xb[:], func=AF.Square, accum_out=ssq[:, 0:1]))
        rsq('ssq', ssq[:, 1:2], ssq[:, 0:1], D * 1e-5, ALU.add)
        S.op('dve', [xk, 'ssq'], ['xn'], lambda e: e.tensor_scalar(out=xn[:], in0=xb[:], scalar1=ssq[:, 1:2], scalar2=32.0,
                                                                   op0=ALU.mult, op1=ALU.mult))
        if samp:
            S.dma('sp', hrow[0:1, :], sshift[sq:sq + 1, :], [], ['xm'])
            S.op('act', ['xm'], ['ycatT'], lambda e: e.copy(out=junk[0:1, :], in_=hrow[0:1, :]))
        pT = P[0][:].bitcast(BF16)
        if samp:
            for kc in range(8):
                S.op('pe', ['ycatT', 'c_cstb'], ['P1'], lambda e, kc=kc: e.transpose(out=P[1][:].bitcast(BF16)[:, 2 * kc:2 * kc + 1], in_=junk[0:1, kc * 128:(kc + 1) * 128], identity=ident_b[0:1, 0:1]))
            S.op('dve', ['P1'], ['hTs'], lambda e: e.tensor_copy(out=hTs[:, :, 0], in_=P[1][:].bitcast(BF16)[:, 0:16].rearrange("p (k two) -> p k two", two=2)[:, :, 0]))
        else:
            S.op('dve', ['hT'], ['hTs'], lambda e: e.tensor_copy(out=hTs[:, :, 0], in_=hT[:, :, 127]))
        for kc in range(8):
            S.op('pe', ['xn', 'c_cstb'], ['P0'], lambda e, kc=kc: e.transpose(out=pT[:, kc * 128:(kc + 1) * 128], in_=xn[:, kc * 128:(kc + 1) * 128], identity=ident_b))
        S.op('dve', ['P0', 'c_g1col'], ['hT'], lambda e: e.tensor_tensor(
            out=hT[:], in0=pT.rearrange("p (k t) -> p k t", k=8),
            in1=c_g1col[:].unsqueeze(2).broadcast_to([128, 8, 128]), op=ALU.mult))
        S.op('pool', ['hT'], ['hTs'], lambda e: e.tensor_copy(out=hTs[:, :, 1:128], in_=hT[:, :, 0:127]))
        if samp or ti == OUT_T:
            S.op('pool', ['xn', 'c_g1bc'], ['xm'], lambda e: e.tensor_tensor(out=hrow[:], in0=xn[:], in1=c_g1bc[:], op=ALU.mult))
            if samp:
                S.dma('sp', sh_s[sq:sq + 1, :], hrow[3:4, :], ['xm'], [])
            else:
                S.dma('sp', sh_p[0:1, :], hrow[127:128, :], ['xm'], [])
        DUMP("xn", xn, 'xn', ti); DUMP("hT", hT, 'hT', ti); DUMP("hTs", hTs, 'hTs', ti)
        if OPTS['stage'] <= 1:
            return
        def proj(bank, c0, n, dst_reads=()):
            for kc in range(8):
                S.op('pe', ['hT', 'W1'], [PK[bank]], lambda e, kc=kc: e.matmul(out=P[bank][:, 0:n], lhsT=hT[:, kc, :], rhs=W1[:, kc, c0:c0 + n], start=(kc == 0), stop=False))
            for kc in range(8):
                S.op('pe', ['hTs', 'W2'], [PK[bank]], lambda e, kc=kc: e.matmul(out=P[bank][:, 0:n], lhsT=hTs[:, kc, :], rhs=W2[:, kc, c0:c0 + n], start=False, stop=(kc == 7)))
        proj(1, 0, 512)
        S.op('act', ['P1'], ['a_r'], lambda e: e.copy(out=A["r"][:], in_=P[1][:]))
        proj(2, 512, 512)
        S.op('act', ['P2'], ['a_k'], lambda e: e.copy(out=A["k"][:], in_=P[2][:]))
        proj(3, 1024, 512)
        S.op('act', ['P3'], ['a_v'], lambda e: e.copy(out=A["v"][:], in_=P[3][:]))
        S.op('dve', ['a_v', 'c_vm'], ['b_vb'], lambda e: e.tensor_scalar(out=Bt["vb"][:], in0=A["v"][:], scalar1=vm, scalar2=None, op0=ALU.mult))
        for j, c0 in enumerate([1536, 1664]):
            for kc in range(8):
                S.op('pe', ['hT', 'W1'], ['P4'], lambda e, kc=kc, j=j, c0=c0: e.matmul(out=P[4][:, j * 128:(j + 1) * 128], lhsT=W1[:, kc, c0:c0 + 128], rhs=hT[:, kc, :], start=(kc == 0), stop=False))
            for kc in range(8):
                S.op('pe', ['hTs', 'W2'], ['P4'], lambda e, kc=kc, j=j, c0=c0: e.matmul(out=P[4][:, j * 128:(j + 1) * 128], lhsT=W2[:, kc, c0:c0 + 128], rhs=hTs[:, kc, :], start=False, stop=(kc == 7)))
        S.op('act', ['P4'], ['lin'], lambda e: e.activation(out=lin[0:64, 0, :], in_=P[4][0:64, 0:128], func=AF.Tanh))
        S.op('act', ['P4'], ['lin'], lambda e: e.copy(out=lin[64:128, 0, :], in_=P[4][64:128, 0:128]))
        S.op('act', ['P4'], ['lin'], lambda e: e.activation(out=lin[:, 1, :], in_=P[4][:, 128:256], func=AF.Sigmoid))
        for kc in range(8):
            S.op('pe', ['hT', 'Wat'], ['P5'], lambda e, kc=kc: e.matmul(out=P[5][:], lhsT=hT[:, kc, :], rhs=Wat[:, kc, 0:512], start=(kc == 0), stop=(kc == 7)))
        for kc in range(8):
            S.op('pe', ['hT', 'Wat'], ['P6'], lambda e, kc=kc: e.matmul(out=P[6][:, 0:256], lhsT=hT[:, kc, :], rhs=Wat[:, kc, 512:768], start=(kc == 0), stop=(kc == 7)))
        S.op('act', ['P5'], ['qkv'], lambda e: e.copy(out=qkv[:, 0:512], in_=P[5][:]))
        S.op('act', ['P6'], ['qkv'], lambda e: e.copy(out=qkv[:, 512:768], in_=P[6][:, 0:256]))
        DUMP("a_r", A["r"], 'a_r', ti); DUMP("a_v", A["v"], 'a_v', ti); DUMP("lin", lin, 'lin', ti); DUMP("qkv0", qkv, 'qkv', ti)
        if OPTS['stage'] <= 2:
            return
        S.op('pe', ['lin', 'L_w2'], ['P1'], lambda e: e.matmul(out=P[1][:], lhsT=lin[0:64, 0, :], rhs=L_w2[0:64, :], start=True, stop=True))
        S.op('pe', ['lin', 'L_w2'], ['P2'], lambda e: e.matmul(out=P[2][:], lhsT=lin[64:128, 0, :], rhs=L_w2[64:128, :], start=True, stop=True))
        S.op('pe', ['lin', 'L_g2'], ['P3'], lambda e: e.matmul(out=P[3][:], lhsT=lin[:, 1, :], rhs=L_g2[:], start=True, stop=True))
        S.op('dve', ['P1', 'c_v512'], ['a_t1'], lambda e: e.tensor_tensor(out=A["t1"][:], in0=P[1][:], in1=V512(W0), op=ALU.add))
        S.op('act', ['a_t1'], ['a_t1'], lambda e: e.activation(out=A["t1"][:], in_=A["t1"][:], func=AF.Sigmoid))
        S.op('dve', ['a_t1', 'c_vm'], ['a_ld'], lambda e: e.tensor_scalar(out=A["ld"][:], in0=A["t1"][:], scalar1=vm, scalar2=-0.6065306597,
                                                                           op0=ALU.mult, op1=ALU.mult))
        S.op('dve', ['P2', 'c_v512'], ['a_t2'], lambda e: e.tensor_tensor(out=A["t2"][:], in0=P[2][:], in1=V512(A0), op=ALU.add))
        S.op('act', ['a_t2'], ['a_asig'], lambda e: e.activation(out=A["asig"][:], in_=A["t2"][:], func=AF.Sigmoid))
        S.op('act', ['P3'], ['a_g'], lambda e: e.copy(out=A["g"][:], in_=P[3][:]))
        S.op('pool', ['a_k', 'c_v512'], ['a_kk'], lambda e: e.tensor_tensor(out=A["kk"][:], in0=A["k"][:], in1=V512(KK), op=ALU.mult))
        S.op('pool', ['a_kk'], ['a_t3'], lambda e: e.tensor_tensor(out=A["t3"][:], in0=A["kk"][:], in1=A["kk"][:], op=ALU.mult))
        S.op('dve', ['a_t3'], ['st8'], lambda e: e.tensor_reduce(out=st8[:, :, 0], in_=v3(A["t3"][:]), axis=AX.X, op=ALU.add))
        rsq('st8', st8[:, :, 1], st8[:, :, 0], 1e-24, ALU.max)
        S.op('dve', ['a_kk', 'st8'], ['a_kk'], lambda e: e.tensor_tensor(out=v3(A["kk"][:]), in0=v3(A["kk"][:]), in1=bc_last(st8[:, :, 1:2], 64), op=ALU.mult))
        S.op('dve', ['a_kk', 'a_asig', 'c_vm'], ['a_b'], lambda e: e.scalar_tensor_tensor(out=A["b"][:], in0=A["kk"][:], scalar=vm, in1=A["asig"][:], op0=ALU.mult, op1=ALU.mult))
        S.op('dve', ['a_asig', 'c_v512'], ['a_t2'], lambda e: e.scalar_tensor_tensor(out=A["t2"][:], in0=A["asig"][:], scalar=-1.0, in1=V512(KA), op0=ALU.add, op1=ALU.mult))
        S.op('dve', ['a_t2', 'a_k'], ['a_kmod'], lambda e: e.scalar_tensor_tensor(out=A["kmod"][:], in0=A["t2"][:], scalar=1.0, in1=A["k"][:], op0=ALU.add, op1=ALU.mult))
        S.op('pe', ['a_ld', 'c_cst'], ['P1'], lambda e: e.matmul(out=P[1][:], lhsT=tri_f, rhs=A["ld"][:], start=True, stop=True))
        S.op('pe', ['a_ld', 'c_cst'], ['P2'], lambda e: e.matmul(out=P[2][:], lhsT=ones_f, rhs=A["ld"][:], start=True, stop=True))
        for h in range(8):
            S.op('pe', ['a_ld', 'c_cst'], ['P3'], lambda e, h=h: e.matmul(out=P[3][0:64, h:h + 1], lhsT=A["ld"][:, h * 64:(h + 1) * 64], rhs=ones_f[:, 0:1], start=True, stop=True))
        S.op('act', ['P3'], ['ecl_fm'], lambda e: e.activation(out=ecl_fm[:], in_=P[3][0:64, 0:8], func=AF.Exp))
        S.op('act', ['P1'], ['a_c'], lambda e: e.copy(out=A["c"][:], in_=P[1][:]))
        S.op('act', ['a_c'], ['a_t1'], lambda e: e.activation(out=A["t1"][:], in_=A["c"][:], func=AF.Exp))
        S.op('dve', ['a_t1', 'a_r'], ['b_rt'], lambda e: e.tensor_tensor(out=Bt["rt"][:], in0=A["r"][:], in1=A["t1"][:], op=ALU.mult))
        S.op('pool', ['a_c', 'a_ld'], ['a_t2'], lambda e: e.tensor_tensor(out=A["t2"][:], in0=A["c"][:], in1=A["ld"][:], op=ALU.subtract))
        S.op('act', ['a_t2'], ['a_t2'], lambda e: e.activation(out=A["t2"][:], in_=A["t2"][:], func=AF.Exp))
        S.op('dve', ['a_t2', 'a_kk'], ['b_at'], lambda e: e.scalar_tensor_tensor(out=Bt["at"][:], in0=A["kk"][:], scalar=-1.0, in1=A["t2"][:], op0=ALU.mult, op1=ALU.mult))
        S.op('act', ['a_c'], ['a_t3'], lambda e: e.activation(out=A["t3"][:], in_=A["c"][:], func=AF.Exp, scale=-1.0))
        S.op('dve', ['a_t3', 'a_b'], ['b_bt'], lambda e: e.tensor_tensor(out=Bt["bt"][:], in0=A["b"][:], in1=A["t3"][:], op=ALU.mult))
        S.op('pool', ['a_t3', 'a_kmod'], ['b_kt'], lambda e: e.tensor_tensor(out=Bt["kt"][:], in0=A["kmod"][:], in1=A["t3"][:], op=ALU.mult))
        S.op('dve', ['P2', 'a_c'], ['a_t1'], lambda e: e.tensor_tensor(out=A["t1"][:], in0=P[2][:], in1=A["c"][:], op=ALU.subtract))
        S.op('act', ['a_t1'], ['a_t1'], lambda e: e.activation(out=A["t1"][:], in_=A["t1"][:], func=AF.Exp))
        S.op('dve', ['a_t1', 'a_b'], ['b_bbar'], lambda e: e.tensor_tensor(out=Bt["bbar"][:], in0=A["b"][:], in1=A["t1"][:], op=ALU.mult))
        S.op('pool', ['a_t1', 'a_kmod'], ['b_kbar'], lambda e: e.tensor_tensor(out=Bt["kbar"][:], in0=A["kmod"][:], in1=A["t1"][:], op=ALU.mult))
        S.op('pool', ['a_r', 'c_v512'], ['a_t2'], lambda e: e.tensor_tensor(out=A["t2"][:], in0=A["r"][:], in1=V512(RK), op=ALU.mult))
        S.op('pool', ['a_t2', 'a_kmod'], ['a_t2'], lambda e: e.tensor_tensor(out=A["t2"][:], in0=A["t2"][:], in1=A["kmod"][:], op=ALU.mult))
        S.op('dve', ['a_t2'], ['st8'], lambda e: e.tensor_reduce(out=st8[:, :, 2], in_=v3(A["t2"][:]), axis=AX.X, op=ALU.add))
        DUMP("a_ld", A["ld"], 'a_ld', ti); DUMP("a_c", A["c"], 'a_c', ti); DUMP("a_kk", A["kk"], 'a_kk', ti); DUMP("b_rt", Bt["rt"], 'b_rt', ti); DUMP("b_at", Bt["at"], 'b_at', ti); DUMP("b_bt", Bt["bt"], 'b_bt', ti); DUMP("b_kbar", Bt["kbar"], 'b_kbar', ti); DUMP("ecl_fm", ecl_fm, 'ecl_fm', ti)
        if OPTS['stage'] <= 3:
            return
        pTb = [P[i][:].bitcast(BF16) for i in range(8)]
        for qi, (nm, bank) in enumerate([("at", 4), ("rt", 5), ("bt", 6), ("kt", 7)]):
            for h in range(8):
                S.op('pe', ['b_' + nm, 'c_cstb'], [PK[bank]], lambda e, h=h, nm=nm, bank=bank: e.transpose(
                    out=pTb[bank][0:64, h * 128:(h + 1) * 128], in_=Bt[nm][:, h * 64:(h + 1) * 64], identity=ident_b))
        S.op('act', ['P4'], ['AR_fm'], lambda e: e.copy(out=AR_fm[:, :, 0:128], in_=pTb[4][0:64, 0:1024].rearrange("p (h t) -> p h t", h=8)))
        S.op('dve', ['P5'], ['AR_fm'], lambda e: e.tensor_copy(out=AR_fm[:, :, 128:256], in_=pTb[5][0:64, 0:1024].rearrange("p (h t) -> p h t", h=8)))
        S.op('act', ['P6'], ['B_fm'], lambda e: e.copy(out=B_fm[:], in_=pTb[6][0:64, 0:1024].rearrange("p (h t) -> p h t", h=8)))
        S.op('dve', ['P7'], ['K_fm'], lambda e: e.tensor_copy(out=K_fm[:], in_=pTb[7][0:64, 0:1024].rearrange("p (h t) -> p h t", h=8)))
        for h in range(8):
            bank = h % 2
            S.op('pe', ['B_fm', 'AR_fm'], [PK[bank]], lambda e, h=h, bank=bank: e.matmul(out=P[bank][:, 0:256], lhsT=B_fm[:, h, :], rhs=AR_fm[:, h, :], start=True, stop=True))
            S.op('pe', ['K_fm', 'AR_fm'], [PK[bank]], lambda e, h=h, bank=bank: e.matmul(out=P[bank][:, 256:512], lhsT=K_fm[:, h, :], rhs=AR_fm[:, h, :], start=True, stop=True))
            S.op('dve', [PK[bank], 'c_m4'], ['MATS'], lambda e, h=h, bank=bank: e.tensor_tensor(out=MATS[:, h, :], in0=P[bank][:], in1=c_m4[:].rearrange("p a b -> p (a b)"), op=ALU.mult))
        for hh in range(2):
            bank = 2 + hh
            for h4 in range(4):
                h = hh * 4 + h4
                S.op('pe', ['B_fm', 'AR_fm'], [PK[bank]], lambda e, h=h, h4=h4, bank=bank: e.matmul(out=P[bank][:, h4 * 128:(h4 + 1) * 128], lhsT=AR_fm[:, h, 0:128], rhs=B_fm[:, h, :], start=True, stop=True))
            S.op('dve', [PK[bank], 'c_cst'], ['MTb'], lambda e, hh=hh, bank=bank: e.tensor_tensor(
                out=MTb[0][:, hh * 4:(hh + 1) * 4, :], in0=P[bank][:].rearrange("p (h t) -> p h t", h=4),
                in1=c_low.unsqueeze(1).broadcast_to([128, 4, 128]), op=ALU.mult))
        S.op('act', ['MATS'], ['Mb'], lambda e: e.copy(out=Mb[0][:], in_=MATS[:, :, 0:128]))
        S.op('pool', ['MATS', 'c_cstb'], ['Tb'], lambda e: e.tensor_tensor(out=Tb[0][:], in0=MATS[:, :, 0:128], in1=ident_b.unsqueeze(1).broadcast_to([128, 8, 128]), op=ALU.add))
        cur = 0
        for lvl in range(1, 7):
            nxt = 1 - cur
            for hh in range(2):
                hs = slice(hh * 4, hh * 4 + 4)
                bM, bMT, bT = 2 + hh * 3, 3 + hh * 3, 4 + hh * 3
                for h4 in range(4):
                    h = hh * 4 + h4
                    cs = slice(h4 * 128, (h4 + 1) * 128)
                    if lvl < 6:
                        S.op('pe', ['Mb', 'MTb'], [PK[bM]], lambda e, h=h, cs=cs, bM=bM, cur=cur: e.matmul(out=P[bM][:, cs], lhsT=MTb[cur][:, h, :], rhs=Mb[cur][:, h, :], start=True, stop=True))
                    S.op('pe', ['Mb', 'MTb'], [PK[bMT]], lambda e, h=h, cs=cs, bMT=bMT, cur=cur: e.matmul(out=P[bMT][:, cs], lhsT=Mb[cur][:, h, :], rhs=MTb[cur][:, h, :], start=True, stop=True))
                if lvl < 6:
                    S.op('act', [PK[bM]], ['Mb'], lambda e, hs=hs, bM=bM, nxt=nxt: e.copy(out=Mb[nxt][:, hs, :], in_=P[bM][:].rearrange("p (h t) -> p h t", h=4)))
                S.op('dve', [PK[bMT]], ['MTb'], lambda e, hs=hs, bMT=bMT, nxt=nxt: e.tensor_copy(out=MTb[nxt][:, hs, :], in_=P[bMT][:].rearrange("p (h t) -> p h t", h=4)))
                for h4 in range(4):
                    h = hh * 4 + h4
                    cs = slice(h4 * 128, (h4 + 1) * 128)
                    S.op('pe', ['MTb', 'Tb'], [PK[bT]], lambda e, h=h, cs=cs, bT=bT, cur=cur, nxt=nxt: e.matmul(out=P[bT][:, cs], lhsT=MTb[nxt][:, h, :], rhs=Tb[cur][:, h, :], start=True, stop=True))
                S.op('dve', [PK[bT], 'Tb'], ['Tb'], lambda e, hs=hs, bT=bT, cur=cur, nxt=nxt: e.tensor_tensor(
                    out=Tb[nxt][:, hs, :], in0=P[bT][:].rearrange("p (h t) -> p h t", h=4), in1=Tb[cur][:, hs, :], op=ALU.add))
            cur = nxt
        Tf = Tb[cur]
        TfK = 'Tb'
        DUMP("AR_fm", AR_fm, 'AR_fm', ti); DUMP("K_fm", K_fm, 'K_fm', ti); DUMP("MATS", MATS, 'MATS', ti); DUMP("Tb", Tb[0], 'Tb', ti); DUMP("MTb", MTb[0], 'MTb', ti)
        if OPTS['stage'] <= 4:
            return
        if samp or ti == 0:
            if samp:
                S.dma('sp', sti[:], swkv[sq].rearrange("h i j -> i h j"), [], ['sti'])
                for h in range(8):
                    S.op('pe', ['sti', 'c_cst'], ['P0'], lambda e, h=h: e.transpose(out=P[0][0:64, h * 64:(h + 1) * 64], in_=sti[:, h, :], identity=ident_f[0:64, 0:64]))
                S.op('dve', ['P0'], ['S32'], lambda e: e.tensor_copy(out=S32[:], in_=P[0][0:64, :].rearrange("p (h i) -> p h i", h=8)))
            else:
                S.op('dve', [], ['S32'], lambda e: e.memset(S32[:], 0.0))
            S.op('act', ['S32'], ['Sb'], lambda e: e.copy(out=Sb[:], in_=S32[:]))
        for h in range(8):
            cs = slice(h * 64, (h + 1) * 64)
            S.op('pe', ['AR_fm', 'Sb'], ['P0'], lambda e, h=h, cs=cs: e.matmul(out=P[0][:, cs], lhsT=AR_fm[:, h, 0:128], rhs=Sb[:, h, :], start=True, stop=False))
            S.op('pe', ['MATS', 'b_vb'], ['P0'], lambda e, h=h, cs=cs: e.matmul(out=P[0][:, cs], lhsT=MATS[:, h, 256:384], rhs=Bt["vb"][:, cs], start=False, stop=True))
        S.op('act', ['P0'], ['b_W0T'], lambda e: e.copy(out=Bt["W0T"][:], in_=P[0][:]))
        for h in range(8):
            cs = slice(h * 64, (h + 1) * 64)
            S.op('pe', [TfK, 'b_W0T'], ['P1'], lambda e, h=h, cs=cs: e.matmul(out=P[1][:, cs], lhsT=Tf[:, h, :], rhs=Bt["W0T"][:, cs], start=True, stop=True))
        S.op('act', ['P1'], ['b_UT'], lambda e: e.copy(out=Bt["UT"][:], in_=P[1][:]))
        for h in range(8):
            cs = slice(h * 64, (h + 1) * 64)
            S.op('pe', ['AR_fm', 'Sb'], ['P0'], lambda e, h=h, cs=cs: e.matmul(out=P[0][:, cs], lhsT=AR_fm[:, h, 128:256], rhs=Sb[:, h, :], start=True, stop=False))
            S.op('pe', ['MATS', 'b_UT'], ['P0'], lambda e, h=h, cs=cs: e.matmul(out=P[0][:, cs], lhsT=MATS[:, h, 128:256], rhs=Bt["UT"][:, cs], start=False, stop=False))
            S.op('pe', ['MATS', 'b_vb'], ['P0'], lambda e, h=h, cs=cs: e.matmul(out=P[0][:, cs], lhsT=MATS[:, h, 384:512], rhs=Bt["vb"][:, cs], start=False, stop=True))
        for h in range(8):
            cs = slice(h * 64, (h + 1) * 64)
            S.op('pe', ['b_bbar', 'b_UT'], ['P1'], lambda e, h=h, cs=cs: e.matmul(out=P[1][0:64, cs], lhsT=Bt["bbar"][:, cs], rhs=Bt["UT"][:, cs], start=True, stop=False))
            S.op('pe', ['b_kbar', 'b_vb'], ['P1'], lambda e, h=h, cs=cs: e.matmul(out=P[1][0:64, cs], lhsT=Bt["kbar"][:, cs], rhs=Bt["vb"][:, cs], start=False, stop=True))
        S.op('dve', ['S32', 'ecl_fm'], ['S32'], lambda e: e.tensor_tensor(out=S32[:], in0=S32[:], in1=ecl_fm[:].unsqueeze(2).broadcast_to([64, 8, 64]), op=ALU.mult))
        S.op('dve', ['S32', 'P1'], ['S32'], lambda e: e.tensor_tensor(out=S32[:], in0=S32[:], in1=P[1][0:64, :].rearrange("p (h i) -> p h i", h=8), op=ALU.add))
        S.op('act', ['S32'], ['Sb'], lambda e: e.copy(out=Sb[:], in_=S32[:]))
        if samp or ti == OUT_T:
            for h in range(8):
                S.op('pe', ['S32', 'c_cst'], ['P2'], lambda e, h=h: e.transpose(out=P[2][0:64, h * 64:(h + 1) * 64], in_=S32[:, h, :], identity=ident_f[0:64, 0:64]))
            S.op('act', ['P2'], ['sti'], lambda e: e.copy(out=sti[:], in_=P[2][0:64, :].rearrange("p (h j) -> p h j", h=8)))
            dst = wkv_s[sq] if samp else wkv_p
            S.dma('sp', dst.rearrange("h i j -> i h j"), sti[:], ['sti'], [])
        DUMP("S32", S32, 'S32', ti); DUMP("b_UT", Bt["UT"], 'b_UT', ti); DUMP("b_W0T", Bt["W0T"], 'b_W0T', ti)
        if OPTS['stage'] <= 5:
            return
        state_only = ti < NT_A
        if state_only and ti != NT_A - 1:
            return
        if not state_only:
            rwkv_post(ti)
        attn_and_out(ti, samp, sq, xk, xb, par, state_only)

    def rwkv_post(ti):
        if True:
            pass
        Y3 = v3(P[0][:])
        S.op('dve', ['P0'], ['st8'], lambda e: e.tensor_reduce(out=st8[:, :, 0], in_=Y3, axis=AX.X, op=ALU.add))
        S.op('dve', ['st8'], ['st8'], lambda e: e.tensor_scalar(out=st8[:, :, 0], in0=st8[:, :, 0], scalar1=1.0 / 64, scalar2=None, op0=ALU.mult))
        S.op('dve', ['P0', 'st8'], ['a_t1'], lambda e: e.tensor_tensor(out=v3(A["t1"][:]), in0=Y3, in1=bc_last(st8[:, :, 0:1], 64), op=ALU.subtract))
        S.op('pool', ['a_t1'], ['a_t2'], lambda e: e.tensor_tensor(out=A["t2"][:], in0=A["t1"][:], in1=A["t1"][:], op=ALU.mult))
        S.op('dve', ['a_t2'], ['st8'], lambda e: e.tensor_reduce(out=st8[:, :, 1], in_=v3(A["t2"][:]), axis=AX.X, op=ALU.add))
        S.op('dve', ['st8'], ['st8'], lambda e: e.tensor_scalar(out=st8[:, :, 1], in0=st8[:, :, 1], scalar1=1.0 / 64, scalar2=64e-5, op0=ALU.mult, op1=ALU.add))
        rsq('st8', st8[:, :, 1], st8[:, :, 1], 0.0, ALU.add)
        S.op('dve', ['a_t1', 'st8'], ['a_t1'], lambda e: e.tensor_tensor(out=v3(A["t1"][:]), in0=v3(A["t1"][:]), in1=bc_last(st8[:, :, 1:2], 64), op=ALU.mult))
        S.op('pool', ['a_t1', 'c_v512'], ['a_t1'], lambda e: e.tensor_tensor(out=A["t1"][:], in0=A["t1"][:], in1=V512(LW), op=ALU.mult))
        S.op('pool', ['a_t1', 'c_v512'], ['a_t1'], lambda e: e.tensor_tensor(out=A["t1"][:], in0=A["t1"][:], in1=V512(LB), op=ALU.add))
        S.op('dve', ['a_v', 'st8'], ['a_t2'], lambda e: e.tensor_tensor(out=v3(A["t2"][:]), in0=v3(A["v"][:]), in1=bc_last(st8[:, :, 2:3], 64), op=ALU.mult))
        S.op('pool', ['a_t1', 'a_t2'], ['a_t1'], lambda e: e.tensor_tensor(out=A["t1"][:], in0=A["t1"][:], in1=A["t2"][:], op=ALU.add))
        S.op('dve', ['a_t1', 'a_g'], ['ycat'], lambda e: e.tensor_tensor(out=ycat[:, 0:512], in0=A["t1"][:], in1=A["g"][:], op=ALU.mult))
        DUMP("ycat_rw", ycat[:, 0:512], 'ycat', ti)

    def attn_and_out(ti, samp, sq, xk, xb, par, state_only):
        pTb = [P[i][:].bitcast(BF16) for i in range(8)]
        if OPTS['stage'] <= 6:
            return
        ri = NT_P if samp else ti
        cosb = c_rope[:, ri, 0:8].unsqueeze(1)
        sinb = c_rope[:, ri, 8:16].unsqueeze(1)
        for (c0, nh) in [(0, 8), (512, 2)]:
            X = qkv[:, c0:c0 + nh * 64].rearrange("p (h j) -> p h j", h=nh)
            x1, x2 = X[:, :, 0:8], X[:, :, 8:16]
            cb = cosb.broadcast_to([128, nh, 8])
            sbb = sinb.broadcast_to([128, nh, 8])
            R = rtmp[:, 0:nh, :]
            T1 = rtmp[:, 0:nh, :]
            ra = rtmp[:].rearrange("p a b -> p (a b)")
            t_a = ra[:, 0:nh * 8].rearrange("p (h j) -> p h j", h=nh)
            t_b = ra[:, 80 - 0:80].rearrange("p (h j) -> p h j", h=1) if False else None
            S.op('dve', ['qkv', 'c_rope'], ['rtmp'], lambda e, x1=x1, cb=cb, t_a=t_a: e.tensor_tensor(out=t_a, in0=x1, in1=cb, op=ALU.mult))
            S.op('dve', ['qkv', 'c_rope'], ['sm'], lambda e, x2=x2, sbb=sbb, nh=nh: e.tensor_tensor(out=sm[:, 0:nh * 8].rearrange("p (h j) -> p h j", h=nh), in0=x2, in1=sbb, op=ALU.mult))
            S.op('dve', ['qkv', 'c_rope'], ['sm'], lambda e, x2=x2, cb=cb, nh=nh: e.tensor_tensor(out=sm[:, 64:64 + nh * 8].rearrange("p (h j) -> p h j", h=nh), in0=x2, in1=cb, op=ALU.mult))
            S.op('dve', ['qkv', 'c_rope'], ['sm'], lambda e, x1=x1, sbb=sbb, nh=nh: e.tensor_tensor(out=sm[:, 128:128 + nh * 8].rearrange("p (h j) -> p h j", h=nh), in0=x1, in1=sbb, op=ALU.mult))
            S.op('dve', ['rtmp', 'sm'], ['qkv'], lambda e, x1=x1, t_a=t_a, nh=nh: e.tensor_tensor(out=x1, in0=t_a, in1=sm[:, 0:nh * 8].rearrange("p (h j) -> p h j", h=nh), op=ALU.subtract))
            S.op('dve', ['sm'], ['qkv'], lambda e, x2=x2, nh=nh: e.tensor_tensor(out=x2, in0=sm[:, 64:64 + nh * 8].rearrange("p (h j) -> p h j", h=nh), in1=sm[:, 128:128 + nh * 8].rearrange("p (h j) -> p h j", h=nh), op=ALU.add))
        S.op('act', ['qkv'], ['qb'], lambda e: e.copy(out=qb[:], in_=qkv[:]))
        S.op('pool', ['qkv'], ['Vb%d' % par], lambda e: e.tensor_copy(out=Vb[par][:], in_=qkv[:, 640:768]))
        for h in range(8):
            S.op('pe', ['qb', 'c_cstb'], ['P2'], lambda e, h=h: e.transpose(out=pTb[2][0:64, h * 128:(h + 1) * 128], in_=qb[:, h * 64:(h + 1) * 64], identity=ident_b))
        for kv in range(2):
            S.op('pe', ['qb', 'c_cstb'], ['P3'], lambda e, kv=kv: e.transpose(out=pTb[3][0:64, kv * 128:(kv + 1) * 128], in_=qb[:, 512 + kv * 64:512 + (kv + 1) * 64], identity=ident_b))
        S.op('act', ['P2'], ['QT'], lambda e: e.copy(out=QT[:], in_=pTb[2][0:64, 0:1024].rearrange("p (h t) -> p h t", h=8)))
        S.op('dve', ['P3'], ['KT%d' % par], lambda e: e.tensor_copy(out=KT[par][:], in_=pTb[3][0:64, 0:256].rearrange("p (h t) -> p h t", h=2)))
        pp = 1 - par
        if samp:
            S.dma('sp', ckb[:, 0:128], ck[sq], [], ['sm'])
            S.dma('sp', ckb[:, 128:256], cv[sq], [], ['sm'])
            S.op('act', ['sm'], ['ycatT'], lambda e: e.copy(out=junk[:, 0:256], in_=ckb[:]))
            for kv in range(2):
                S.op('pe', ['ycatT', 'c_cstb'], ['P3'], lambda e, kv=kv: e.transpose(out=pTb[3][0:64, 256 + kv * 128:256 + (kv + 1) * 128], in_=junk[:, kv * 64:(kv + 1) * 64], identity=ident_b))
            S.op('dve', ['P3'], ['KT%d' % pp], lambda e: e.tensor_copy(out=KT[pp][:], in_=pTb[3][0:64, 256:512].rearrange("p (h t) -> p h t", h=2)))
            S.op('pool', ['ycatT'], ['Vb%d' % pp], lambda e: e.tensor_copy(out=Vb[pp][:], in_=junk[:, 128:256]))
            S.dma('sp', kw_s[sq, 0:124, :], ck[sq, 4:128, :], [], [])
            S.dma('sp', vw_s[sq, 0:124, :], cv[sq, 4:128, :], [], [])
            S.dma('sp', kw_s[sq, 124:128, :], qkv[0:4, 512:640], ['qkv'], [])
            S.dma('sp', vw_s[sq, 124:128, :], qkv[0:4, 640:768], ['qkv'], [])
        elif ti == OUT_T:
            S.dma('sp', kw_p, qkv[:, 512:640], ['qkv'], [])
            S.dma('sp', vw_p, qkv[:, 640:768], ['qkv'], [])
        if state_only:
            return
        mi = 2 if (samp or ti - NT_A > 1) else (ti - NT_A)
        for h in range(8):
            kv = h // 4
            bank = 4 + (h % 2)
            S.op('pe', ['QT', 'KT%d' % pp], [PK[bank]], lambda e, h=h, kv=kv, bank=bank: e.matmul(out=P[bank][:, 0:128], lhsT=QT[:, h, :], rhs=KT[pp][:, kv, :], start=True, stop=True))
            S.op('pe', ['QT', 'KT%d' % par], [PK[bank]], lambda e, h=h, kv=kv, bank=bank: e.matmul(out=P[bank][:, 128:256], lhsT=QT[:, h, :], rhs=KT[par][:, kv, :], start=True, stop=True))
            S.op('dve', [PK[bank], 'c_amask'], ['sm'], lambda e, bank=bank: e.scalar_tensor_tensor(out=sm[:], in0=P[bank][:, 0:256], scalar=0.125, in1=c_amask[:, mi, :], op0=ALU.mult, op1=ALU.add))
            S.op('dve', ['sm'], ['ast'], lambda e: e.tensor_reduce(out=ast[:, 0:1], in_=sm[:], axis=AX.X, op=ALU.max))
            S.op('dve', ['ast', 'c_sink'], ['ast'], lambda e, h=h: e.tensor_scalar(out=ast[:, 1:2], in0=ast[:, 0:1], scalar1=c_sink[:, h:h + 1], scalar2=-1.0, op0=ALU.max, op1=ALU.mult))
            S.op('act', ['sm', 'ast'], ['eb', 'ast'], lambda e: e.activation(out=eb[:], in_=sm[:], func=AF.Exp, bias=ast[:, 1:2], scale=1.0, accum_out=ast[:, 2:3]))
            S.op('act', ['ast', 'c_sink'], ['ast'], lambda e, h=h: e.activation(out=ast[:, 3:4], in_=c_sink[:, h:h + 1], func=AF.Exp, bias=ast[:, 1:2], scale=1.0))
            S.op('dve', ['ast'], ['ast'], lambda e: e.tensor_tensor(out=ast[:, 4:5], in0=ast[:, 2:3], in1=ast[:, 3:4], op=ALU.add))
            S.op('dve', ['ast'], ['ast'], lambda e: e.reciprocal(out=ast[:, 5:6], in_=ast[:, 4:5]))
            for half in range(2):
                S.op('pe', ['eb', 'c_cstb'], ['P6'], lambda e, half=half: e.transpose(out=pTb[6][:, half * 128:(half + 1) * 128], in_=eb[:, half * 128:(half + 1) * 128], identity=ident_b))
            S.op('act', ['P6'], ['eT'], lambda e: e.copy(out=eT[:], in_=pTb[6][:, 0:256].rearrange("p (a t) -> p a t", a=2)))
            S.op('pe', ['eT', 'Vb%d' % pp], ['P7'], lambda e, kv=kv: e.matmul(out=P[7][:, 0:64], lhsT=eT[:, 0, :], rhs=Vb[pp][:, kv * 64:(kv + 1) * 64], start=True, stop=False))
            S.op('pe', ['eT', 'Vb%d' % par], ['P7'], lambda e, kv=kv: e.matmul(out=P[7][:, 0:64], lhsT=eT[:, 1, :], rhs=Vb[par][:, kv * 64:(kv + 1) * 64], start=False, stop=True))
            S.op('dve', ['P7', 'ast'], ['ycat'], lambda e, h=h: e.tensor_scalar(out=ycat[:, 512 + h * 64:512 + (h + 1) * 64], in0=P[7][:, 0:64], scalar1=ast[:, 5:6], scalar2=None, op0=ALU.mult))
        DUMP("ycat", ycat, 'ycat', ti); DUMP("qkv", qkv, 'qkv', ti); DUMP("QT", QT, 'QT', ti)
        if OPTS['stage'] <= 7:
            return
        for kc in range(8):
            S.op('pe', ['ycat', 'c_cstb'], ['P2'], lambda e, kc=kc: e.transpose(out=pTb[2][:, kc * 128:(kc + 1) * 128], in_=ycat[:, kc * 128:(kc + 1) * 128], identity=ident_b))
        S.op('act', ['P2'], ['ycatT'], lambda e: e.copy(out=ycatT[:], in_=pTb[2][:, 0:1024].rearrange("p (k t) -> p k t", k=8)))
        for half in range(2):
            bank = 3 + half
            for kc in range(8):
                S.op('pe', ['ycatT', 'Wo'], [PK[bank]], lambda e, kc=kc, half=half, bank=bank: e.matmul(out=P[bank][:], lhsT=ycatT[:, kc, :], rhs=Wo[:, kc, half * 512:(half + 1) * 512], start=(kc == 0), stop=(kc == 7)))
            S.op('dve', [PK[bank], xk], ['xm'], lambda e, half=half, bank=bank: e.tensor_tensor(out=xm[:, half * 512:(half + 1) * 512], in0=P[bank][:], in1=xb[:, half * 512:(half + 1) * 512], op=ALU.add))
        if dbg and ti == OPTS['dbg_tile']:
            S.op('dve', ['ycat'], ['a_t1'], lambda e: e.tensor_copy(out=A["t1"][:], in_=ycat[:, 0:512]))
            S.op('dve', ['ycat'], ['a_t2'], lambda e: e.tensor_copy(out=A["t2"][:], in_=ycat[:, 512:1024]))
            S.dma('sp', dbg_t["d_y"], A["t1"][:], ['a_t1'], [])
            S.dma('sp', dbg_t["d_at"], A["t2"][:], ['a_t2'], [])
            S.dma('sp', dbg_t["d_S"], S32[:].rearrange("p h i -> p (h i)"), ['S32'], [])
        DUMP("xm", xm, 'xm', ti)
        if samp:
            S.dma('sp', xmid[NT_B * 128 + sq * 4:NT_B * 128 + sq * 4 + 4, :], xm[0:4, :], ['xm'], ['xmid'])
        else:
            S.dma('sp', xmid[(ti - NT_A) * 128:(ti - NT_A + 1) * 128, :], xm[:], ['xm'], ['xmid'])


    def peer_phase():
        c_g2bc = sb("c_g2bc", [128, D])
        S.dma('sp', c_g2bc[:], g2.partition_broadcast(128)[:, 0, :], [], ['c_g2bc'])
        c_gFbc = sb("c_gFbc", [128, D])
        S.dma('sp', c_gFbc[:], gF.partition_broadcast(128)[:, 0, :], [], ['c_gFbc'])
        c_iota = sb("c_iota", [128, 256])
        S.dma('sp', c_iota[:], iota_in, [], ['c_iota'])
        Wq = sb("Wq", [128, 8, 2048], BF16)
        skT = sb("skT", [128, 2, 128], BF16)
        xm2 = sb("xm2", [128, D])
        hn32 = sb("hn32", [128, D])
        hnb = sb("hnb", [128, D], BF16)
        hn2T = sb("hn2T", [128, 8, 128], BF16)
        qT = sb("qT", [128, 16, 128], BF16)
        s_sb = sb("s_sb", [128, 16, 128])
        s2 = sb("s2", [128, 16, 128])
        tv = sb("tv", [128, 16, 16])
        tiu = sb("tiu", [128, 16, 16], U32)
        tif = sb("tif", [128, 16, 16])
        cand = sb("cand", [128, 8, 256])
        cand2 = sb("cand2", [128, 8, 256])
        cidx = sb("cidx", [128, 8, 256])
        top = sb("top", [128, 8, 16])
        selu = sb("selu", [128, 8, 16], U32)
        self_ = sb("self", [128, 8, 16])
        idxf = sb("idxf", [128, 8, 16])
        idxu = sb("idxu", [128, 128], U32)
        gate = sb("gate", [128, 8, 16])
        gst = sb("gst", [128, 8, 2])
        pre = sb("pre", [128, 128])
        wgt = sb("wgt", [128, 128])
        acc = sb("acc", [128, D])
        ss2 = sb("ss2", [128, 4])
        NG = 4
        gb = [sb("gb%d" % i, [128, D]) for i in range(NG)]
        with nc.sbuf_tensor("stg2", [128, 2048], F32) as stg2:
            for kc in range(8):
                S.dma('sp', stg2[:], w_q[kc * 128:(kc + 1) * 128, :], [], ['stg2'])
                S.op('act', ['stg2'], ['Wq'], lambda e, kc=kc: e.copy(out=Wq[:, kc, :], in_=stg2[:]))
            S.dma('sp', stg2[:, 0:256].rearrange("p (c d) -> p c d", c=2), subk.rearrange("c n d -> n c d"), [], ['stg2'])
            S.op('act', ['stg2'], ['hnb'], lambda e: e.copy(out=hnb[:, 0:256], in_=stg2[:, 0:256]))
            for c in range(2):
                S.op('pe', ['hnb', 'c_cstb'], ['P0'], lambda e, c=c: e.transpose(out=P[0][:].bitcast(BF16)[:, c * 128:(c + 1) * 128], in_=hnb[:, c * 128:(c + 1) * 128], identity=ident_b))
            S.op('act', ['P0'], ['skT'], lambda e: e.copy(out=skT[:], in_=P[0][:].bitcast(BF16)[:, 0:256].rearrange("p (c n) -> p c n", c=2)))
            barrier()
        pTb = [P[i][:].bitcast(BF16) for i in range(8)]
        for pt in range(NPE):
            samp = pt >= NT_B
            S.dma('sp', xm2[:], xmid[pt * 128:(pt + 1) * 128, :], ['xmid'], ['xm2'])
            S.op('act', ['xm2'], ['hnb', 'ss2'], lambda e: e.activation(out=hnb[:], in_=xm2[:], func=AF.Square, accum_out=ss2[:, 0:1]))
            rsq('ss2', ss2[:, 1:2], ss2[:, 0:1], D * 1e-5, ALU.add)
            S.op('dve', ['xm2', 'ss2'], ['hn32'], lambda e: e.tensor_scalar(out=hn32[:], in0=xm2[:], scalar1=ss2[:, 1:2], scalar2=32.0, op0=ALU.mult, op1=ALU.mult))
            S.op('pool', ['hn32', 'c_g2bc'], ['hn32'], lambda e: e.tensor_tensor(out=hn32[:], in0=hn32[:], in1=c_g2bc[:], op=ALU.mult))
            S.op('act', ['hn32'], ['hnb'], lambda e: e.copy(out=hnb[:], in_=hn32[:]))
            for kc in range(8):
                S.op('pe', ['hnb', 'c_cstb'], ['P0'], lambda e, kc=kc: e.transpose(out=pTb[0][:, kc * 128:(kc + 1) * 128], in_=hnb[:, kc * 128:(kc + 1) * 128], identity=ident_b))
            S.op('act', ['P0'], ['hn2T'], lambda e: e.copy(out=hn2T[:], in_=pTb[0][:, 0:1024].rearrange("p (k t) -> p k t", k=8)))
            for hc in range(16):
                bank = 1 + hc // 4
                cs = slice((hc % 4) * 128, (hc % 4 + 1) * 128)
                for kc in range(8):
                    S.op('pe', ['Wq', 'hn2T'], [PK[bank]], lambda e, hc=hc, kc=kc, bank=bank, cs=cs: e.matmul(out=P[bank][:, cs], lhsT=Wq[:, kc, hc * 128:(hc + 1) * 128], rhs=hn2T[:, kc, :], start=(kc == 0), stop=(kc == 7)))
            for b in range(4):
                S.op('act' if b % 2 else 'dve', [PK[1 + b]], ['qT'], lambda e, b=b: (e.copy if b % 2 else e.tensor_copy)(out=qT[:, b * 4:(b + 1) * 4, :], in_=P[1 + b][:].rearrange("p (a t) -> p a t", a=4)))
            sbanks = [5, 6, 7, 0]
            for hc in range(16):
                bank = sbanks[hc // 4]
                cs = slice((hc % 4) * 128, (hc % 4 + 1) * 128)
                S.op('pe', ['qT', 'skT'], [PK[bank]], lambda e, hc=hc, bank=bank, cs=cs: e.matmul(out=P[bank][:, cs], lhsT=qT[:, hc, :], rhs=skT[:, hc % 2, :], start=True, stop=True))
            for b in range(4):
                S.op('act' if b % 2 else 'dve', [PK[sbanks[b]]], ['s_sb'], lambda e, b=b: (e.copy if b % 2 else e.tensor_copy)(out=s_sb[:, b * 4:(b + 1) * 4, :], in_=P[sbanks[b]][:].rearrange("p (a t) -> p a t", a=4)))
            for hc in range(16):
                S.op('dve', ['s_sb'], ['tv'], lambda e, hc=hc: e.max(out=tv[:, hc, 0:8], in_=s_sb[:, hc, :]))
                S.op('dve', ['s_sb', 'tv'], ['tiu'], lambda e, hc=hc: e.max_index(out=tiu[:, hc, 0:8], in_max=tv[:, hc, 0:8], in_values=s_sb[:, hc, :]))
                S.op('dve', ['s_sb', 'tv'], ['s2'], lambda e, hc=hc: e.match_replace(out=s2[:, hc, :], in_to_replace=tv[:, hc, 0:8], in_values=s_sb[:, hc, :], imm_value=-1e30))
                S.op('dve', ['s2'], ['tv'], lambda e, hc=hc: e.max(out=tv[:, hc, 8:16], in_=s2[:, hc, :]))
                S.op('dve', ['s2', 'tv'], ['tiu'], lambda e, hc=hc: e.max_index(out=tiu[:, hc, 8:16], in_max=tv[:, hc, 8:16], in_values=s2[:, hc, :]))
            S.op('dve', ['tiu'], ['tif'], lambda e: e.tensor_copy(out=tif[:], in_=tiu[:]))
            tv4 = tv[:].rearrange("p (h c) k -> p h c k", c=2)
            tf4 = tif[:].rearrange("p (h c) k -> p h c k", c=2)
            c4 = lambda t: t[:].rearrange("p h (a b) -> p h a b", a=16)
            S.op('dve', ['tv'], ['cand'], lambda e: e.tensor_tensor(out=c4(cand), in0=tv4[:, :, 0, :].unsqueeze(3).broadcast_to([128, 8, 16, 16]),
                                                                    in1=tv4[:, :, 1, :].unsqueeze(2).broadcast_to([128, 8, 16, 16]), op=ALU.add))
            S.op('dve', ['tif'], ['tif'], lambda e: e.tensor_scalar(out=tf4[:, :, 0, :], in0=tf4[:, :, 0, :], scalar1=128.0, scalar2=None, op0=ALU.mult))
            S.op('dve', ['tif'], ['cidx'], lambda e: e.tensor_tensor(out=c4(cidx), in0=tf4[:, :, 0, :].unsqueeze(3).broadcast_to([128, 8, 16, 16]),
                                                                     in1=tf4[:, :, 1, :].unsqueeze(2).broadcast_to([128, 8, 16, 16]), op=ALU.add))
            for h in range(8):
                S.op('dve', ['cand'], ['top'], lambda e, h=h: e.max(out=top[:, h, 0:8], in_=cand[:, h, :]))
                S.op('dve', ['cand', 'top'], ['selu'], lambda e, h=h: e.max_index(out=selu[:, h, 0:8], in_max=top[:, h, 0:8], in_values=cand[:, h, :]))
                S.op('dve', ['cand', 'top'], ['cand2'], lambda e, h=h: e.match_replace(out=cand2[:, h, :], in_to_replace=top[:, h, 0:8], in_values=cand[:, h, :], imm_value=-1e30))
                S.op('dve', ['cand2'], ['top'], lambda e, h=h: e.max(out=top[:, h, 8:16], in_=cand2[:, h, :]))
                S.op('dve', ['cand2', 'top'], ['selu'], lambda e, h=h: e.max_index(out=selu[:, h, 8:16], in_max=top[:, h, 8:16], in_values=cand2[:, h, :]))
            S.op('dve', ['selu'], ['self'], lambda e: e.tensor_copy(out=self_[:], in_=selu[:]))
            for k in range(16):
                S.op('dve', ['self', 'c_iota'], ['cand2'], lambda e, k=k: e.tensor_tensor(out=cand2[:], in0=c_iota[:].unsqueeze(1).broadcast_to([128, 8, 256]),
                                                                                          in1=self_[:, :, k:k + 1].broadcast_to([128, 8, 256]), op=ALU.is_equal))
                S.op('pool', ['cand2', 'cidx'], ['cand2'], lambda e: e.tensor_tensor(out=cand2[:], in0=cand2[:], in1=cidx[:], op=ALU.mult))
                S.op('dve', ['cand2'], ['idxf'], lambda e, k=k: e.tensor_reduce(out=idxf[:, :, k], in_=cand2[:], axis=AX.X, op=ALU.add))
            S.op('dve', ['idxf'], ['idxf'], lambda e: e.tensor_scalar(out=idxf[:], in0=idxf[:], scalar1=0.0, scalar2=float(NEXP - 1), op0=ALU.max, op1=ALU.min))
            S.op('dve', ['idxf'], ['idxu'], lambda e: e.tensor_copy(out=idxu[:], in_=idxf[:].rearrange("p h k -> p (h k)")))
            S.op('dve', ['top'], ['gate'], lambda e: e.tensor_tensor(out=gate[:], in0=top[:], in1=top[:, :, 0:1].broadcast_to([128, 8, 16]), op=ALU.subtract))
            S.op('act', ['gate'], ['gate'], lambda e: e.activation(out=gate[:], in_=gate[:], func=AF.Exp))
            S.op('dve', ['gate'], ['gst'], lambda e: e.tensor_reduce(out=gst[:, :, 0], in_=gate[:], axis=AX.X, op=ALU.add))
            S.op('dve', ['gst'], ['gst'], lambda e: e.reciprocal(out=gst[:, :, 1], in_=gst[:, :, 0]))
            S.op('dve', ['gate', 'gst'], ['gate'], lambda e: e.tensor_tensor(out=gate[:], in0=gate[:], in1=gst[:, :, 1:2].broadcast_to([128, 8, 16]), op=ALU.mult))
            for sl in range(128):
                g = sl % NG
                S.dma('pool', None, None, ['idxu'], ['gb%d' % g], fn=lambda e, sl=sl, g=g: e.indirect_dma_start(
                    out=gb[g][:], out_offset=None, in_=eu, in_offset=bass.IndirectOffsetOnAxis(ap=idxu[:, sl:sl + 1], axis=0)))
                S.op('dve', ['gb%d' % g, 'hn32'], ['gb%d' % g, 'pre'], lambda e, sl=sl, g=g: e.scalar_tensor_tensor(
                    out=gb[g][:], in0=gb[g][:], scalar=1.0, in1=hn32[:], op0=ALU.mult, op1=ALU.mult, accum_out=pre[:, sl:sl + 1]))
            S.op('act', ['pre'], ['wgt'], lambda e: e.activation(out=wgt[:], in_=pre[:], func=AF.Gelu))
            S.op('dve', ['wgt', 'gate'], ['wgt'], lambda e: e.tensor_tensor(out=wgt[:], in0=wgt[:], in1=gate[:].rearrange("p h k -> p (h k)"), op=ALU.mult))
            S.op('dve', ['xm2'], ['acc'], lambda e: e.tensor_copy(out=acc[:], in_=xm2[:]))
            for sl in range(128):
                g = sl % NG
                S.dma('pool', None, None, ['idxu'], ['gb%d' % g], fn=lambda e, sl=sl, g=g: e.indirect_dma_start(
                    out=gb[g][:], out_offset=None, in_=ev, in_offset=bass.IndirectOffsetOnAxis(ap=idxu[:, sl:sl + 1], axis=0)))
                S.op('dve', ['gb%d' % g, 'wgt', 'acc'], ['acc'], lambda e, sl=sl, g=g: e.scalar_tensor_tensor(
                    out=acc[:], in0=gb[g][:], scalar=wgt[:, sl:sl + 1], in1=acc[:], op0=ALU.mult, op1=ALU.add))
            S.op('act', ['acc'], ['hnb', 'ss2'], lambda e: e.activation(out=hnb[:], in_=acc[:], func=AF.Square, accum_out=ss2[:, 2:3]))
            rsq('ss2', ss2[:, 3:4], ss2[:, 2:3], D * 1e-5, ALU.add)
            S.op('dve', ['acc', 'ss2'], ['acc'], lambda e: e.tensor_scalar(out=acc[:], in0=acc[:], scalar1=ss2[:, 3:4], scalar2=32.0, op0=ALU.mult, op1=ALU.mult))
            S.op('pool', ['acc', 'c_gFbc'], ['acc'], lambda e: e.tensor_tensor(out=acc[:], in0=acc[:], in1=c_gFbc[:], op=ALU.mult))
            if samp:
                S.dma('sp', y_s[:, :], acc[0:NSEQ_S * 4, :], ['acc'], [])
            else:
                S.dma('sp', y_p[pt * 128:(pt + 1) * 128, :], acc[:], ['acc'], [])

    for ti in range(NTT):
        load_x(ti)
        mix_tile(ti)
    print("sbuf left", nc.sbuf_bytes_remaining() if callable(nc.sbuf_bytes_remaining) else nc.sbuf_bytes_remaining)
    print("total ops", getattr(S, 'n', 0))
    barrier()
    ph1.close()
    stk['cur'] = glob_stack
    if OPTS['peer']:
        peer_phase()
    barrier()
    S.finish()
    return nc


_CACHE = {}
NTA_FULL, NTB_FULL = 17, 17


def _consts(nta, ntb, hf):
    ar = np.arange(128)
    ident = np.eye(128, dtype=np.float32)
    tri = (ar[:, None] <= ar[None, :]).astype(np.float32)
    ones = np.ones((128, 128), np.float32)
    su = (ar[:, None] < ar[None, :]).astype(np.float32)
    lo = (ar[:, None] > ar[None, :]).astype(np.float32)
    cst = np.stack([ident, tri, ones, su, tri, lo], axis=1).astype(np.float32)
    q = ar[:, None]
    c = np.arange(256)[None, :]
    ok = (c > q) & (c <= q + 128)
    m_std = np.where(ok, 0.0, -30000.0).astype(np.float32)
    m_t0 = np.where(ok & (c >= 240), 0.0, -30000.0).astype(np.float32)
    m_t1 = np.where(ok & (c >= 112), 0.0, -30000.0).astype(np.float32)
    first = (hf == 0) or (nta == 0)
    cmask = np.stack([m_t0 if first else m_std, m_t1 if first else m_std, m_std], axis=1)
    inv = (np.float32(500000.0) ** (-np.arange(0, 16, 2, dtype=np.float32) / np.float32(16))).astype(np.float32)
    ntp = nta + ntb
    rope = np.zeros((128, ntp + 1, 16), np.float32)
    for i in range(ntp + 1):
        if i < ntp:
            st = i if (hf == 1 or nta == 0) else (i - nta if i >= nta else i)
            pos = st * 128 - 112 + ar
        else:
            pos = PAST + ar
        ang = pos.astype(np.float32)[:, None] * inv[None, :]
        rope[:, i, 0:8] = np.cos(ang)
        rope[:, i, 8:16] = np.sin(ang)
    vmask = np.zeros((128, 2), np.float32)
    vmask[:, 0] = 1.0
    vmask[0:4, 1] = 1.0
    iota = np.tile(np.arange(256, dtype=np.float32)[None, :], (128, 1))
    return dict(cst=cst, cmask=cmask, rope=rope, vmask=vmask, iota=iota)


def kernel(x_prompt, x_sample, cache_k_win, cache_v_win, state_wkv, state_shift, meta_tokens, norm1_g, w_in, mu_shift,
           w0, w_lora_w2, a0, w_lora_a2, w_lora_g2, k_k, k_a, r_k, lnx_w, lnx_b, attn_sinks, w_out, norm2_g, w_query,
           sub_keys, expert_u, expert_v, final_norm_g, _nta=NTA_FULL, _ntb=NTB_FULL, _nts=NSEQ_S, _dbg=False):
    f = lambda a: np.ascontiguousarray(np.asarray(a), dtype=np.float32)
    key = (_nta, _ntb, _nts)
    if key not in _CACHE:
        _CACHE[key] = build(_nta, _ntb, _nts, dbg=_dbg)
    nc = _CACHE[key]
    x_prompt, x_sample = f(x_prompt), f(x_sample)
    B = x_prompt.shape[0]
    nseqt = _nta + _ntb - 1
    shared = dict(
        w_in=f(w_in)[0], w_out=f(w_out)[0], w_q=f(w_query)[0], subk=f(sub_keys)[0],
        eu=f(expert_u)[0][:OPTS['nexp']], ev=f(expert_v)[0][:OPTS['nexp']],
        lw2=f(w_lora_w2)[0], la2=f(w_lora_a2)[0], lg2=f(w_lora_g2)[0],
        vec512=np.stack([f(w0)[0], f(a0)[0], f(k_k)[0], f(k_a)[0], f(r_k)[0].reshape(512), f(lnx_w)[0], f(lnx_b)[0]]),
        mu=f(mu_shift), g1=f(norm1_g), g2=f(norm2_g), gF=f(final_norm_g)[None, :], sinks=f(attn_sinks))
    cs = [_consts(_nta, _ntb, hf) for hf in range(2)]
    in_maps = []
    for c in range(NCORES):
        b, hf = c // 2, c % 2
        seq = np.zeros((nseqt * 128, D), np.float32)
        seq[112:128] = f(meta_tokens)
        seq[128:] = x_prompt[b][:(nseqt - 1) * 128]
        xp = np.zeros(((_nta + _ntb) * 128, D), np.float32)
        if hf == 0:
            xp[_nta * 128:(_nta + _ntb) * 128] = seq[:_ntb * 128]
        else:
            xp[:nseqt * 128] = seq
        sl = slice(c * NSEQ_S, (c + 1) * NSEQ_S)
        m = dict(shared)
        m.update(cs[hf])
        m.update(xp=xp, xs=x_sample[sl].reshape(NSEQ_S * 4, D),
                 ck=f(cache_k_win)[0, sl].reshape(NSEQ_S, 128, 128), cv=f(cache_v_win)[0, sl].reshape(NSEQ_S, 128, 128),
                 swkv=f(state_wkv)[0, sl], sshift=f(state_shift)[0, sl])
        in_maps.append(m)
    res = run_bass_kernel_spmd(nc, in_maps, core_ids=list(range(NCORES))).results
    if (_nta, _ntb, _nts) != (NTA_FULL, NTB_FULL, NSEQ_S):
        return res
    y_prompt = np.stack([np.concatenate([res[2 * b]["y_p"][128:_ntb * 128], res[2 * b + 1]["y_p"][:(_ntb - 1) * 128]]) for b in range(B)])
    y_sample = np.concatenate([res[c]["y_s"].reshape(NSEQ_S, 4, D) for c in range(NCORES)])
    od = lambda b: res[2 * b + 1]
    kwp = np.stack([od(b)["kw_p"].reshape(128, 2, 64) for b in range(B)])[None]
    vwp = np.stack([od(b)["vw_p"].reshape(128, 2, 64) for b in range(B)])[None]
    wkvp = np.stack([od(b)["wkv_p"] for b in range(B)])[None]
    shp = np.stack([od(b)["sh_p"][0] for b in range(B)])[None]
    kws = np.concatenate([res[c]["kw_s"].reshape(NSEQ_S, 128, 2, 64) for c in range(NCORES)])[None]
    vws = np.concatenate([res[c]["vw_s"].reshape(NSEQ_S, 128, 2, 64) for c in range(NCORES)])[None]
    wkvs = np.concatenate([res[c]["wkv_s"] for c in range(NCORES)])[None]
    shs = np.concatenate([res[c]["sh_s"] for c in range(NCORES)])[None]
    return (y_prompt, y_sample, kwp, vwp, wkvp, shp, kws, vws, wkvs, shs)
```

```python
import numpy as np
from contextlib import ExitStack
import concourse.bass as bass
import concourse.mybir as mybir
from concourse.alu_op_type import AluOpType as ALU
from concourse.bass_utils import run_bass_kernel_spmd

F32 = mybir.dt.float32
BF16 = mybir.dt.bfloat16
U32 = mybir.dt.uint32
AF = mybir.ActivationFunctionType
AX = mybir.AxisListType

D = 1024
NRW = 1792
NCOL = 2560
NCORES = 8
NSEQ_S = 16
PAST = 8192
NPT = 33
NEXP = 16384
OPTS = {'ng': 12, 'half': 0, 'limit': 10**9, 'dump_only': '', 'dumps': False, 'nexp': 16384, 'dbg_tile': 1, 'stage': 99, 'peer': True, 'win_copy': True, 'samp_state': True}


class Sched:
    def __init__(self, nc):
        self.nc = nc
        self.eng = {'pe': nc.tensor, 'act': nc.scalar, 'dve': nc.vector, 'pool': nc.gpsimd, 'sp': nc.sync}
        self.sem = {e: nc.alloc_semaphore("sem_" + e) for e in ['pe', 'act', 'dve', 'pool']}
        self.cnt = {e: 0 for e in self.sem}
        self.seen = {e: {} for e in self.eng}
        self.last_w = {}
        self.readers = {}
        self.dslots = {}
        for q in ['sp', 'pool', 'act']:
            self.dslots[q] = [[nc.alloc_semaphore("dq_%s_%d" % (q, i)), 0] for i in range(OPTS['ng'] if q == 'pool' else 8)]
        self.dnext = {q: 0 for q in self.dslots}
        self.tokens = {}

    def _wait(self, e, toks):
        need = {}
        for t in toks:
            if t is None:
                continue
            sid, val, we = t
            if we == e and e == 'pe':
                continue
            if self.seen[e].get(sid, 0) >= val:
                continue
            if need.get(sid, (None, 0))[1] < val:
                need[sid] = (t, val)
        for sid, (t, val) in need.items():
            self.eng[e].wait_ge(self.tokens[sid], val)
            self.seen[e][sid] = val

    def _deps(self, e, reads, writes):
        toks = []
        for k in reads:
            toks.append(self.last_w.get(k))
        for k in writes:
            toks.append(self.last_w.get(k))
            for t in self.readers.get(k, []):
                toks.append(t[:3])
        return toks

    def _mark(self, tok, reads, writes, is_dma):
        for k in reads:
            self.readers.setdefault(k, []).append(tok + (is_dma,))
        for k in writes:
            self.last_w[k] = tok
            self.readers[k] = []

    def op(self, e, reads, writes, fn):
        self.n = getattr(self, 'n', 0) + 1
        if self.n > OPTS['limit']:
            return None
        self._wait(e, self._deps(e, reads, writes))
        inst = fn(self.eng[e])
        self.cnt[e] += 1
        sem = self.sem[e]
        inst.then_inc(sem, 1)
        sid = id(sem)
        self.tokens[sid] = sem
        self._mark((sid, self.cnt[e], e), reads, writes, False)
        return inst

    def dma(self, q, out, in_, reads, writes, fn=None):
        self.n = getattr(self, 'n', 0) + 1
        if self.n > OPTS['limit']:
            return None
        slots = self.dslots[q]
        i = self.dnext[q]
        self.dnext[q] = (i + 1) % len(slots)
        sem, val = slots[i]
        sid = id(sem)
        self.tokens[sid] = sem
        toks = self._deps(q, reads, writes)
        if val > 0:
            toks.append((sid, val, 'dma'))
        self._wait(q, toks)
        if fn is None:
            inst = self.eng[q].dma_start(out=out, in_=in_)
        else:
            inst = fn(self.eng[q])
        inst.then_inc(sem, 16)
        slots[i][1] = val + 16
        self._mark((sid, val + 16, 'dma'), reads, writes, True)

    def finish(self):
        for q, slots in self.dslots.items():
            toks = [(id(s), v, 'dma') for s, v in slots if v > 0]
            self._wait(q, toks)


def build(NT_A, NT_B, NT_S, dbg=False, dbg_tile=1):
    nc = bass.Bass("TRN2", target_bir_lowering=False)
    S = Sched(nc)
    NT_P = NT_A + NT_B
    NTT = NT_P + NT_S
    NPE = NT_B + (1 if NT_S else 0)
    OUT_T = NT_P - 2 if NT_A > 0 else NT_P - 1

    def din(name, shape, dt=F32):
        return nc.dram_tensor(name, list(shape), dt, kind="ExternalInput").ap()

    def dout(name, shape, dt=F32):
        return nc.dram_tensor(name, list(shape), dt, kind="ExternalOutput").ap()

    xp = din("xp", [max(NT_P, 1) * 128, D])
    xs = din("xs", [NSEQ_S * 4, D])
    ck = din("ck", [NSEQ_S, 128, 128])
    cv = din("cv", [NSEQ_S, 128, 128])
    swkv = din("swkv", [NSEQ_S, 8, 64, 64])
    sshift = din("sshift", [NSEQ_S, D])
    w_in = din("w_in", [D, NCOL])
    w_out = din("w_out", [D, D])
    w_q = din("w_q", [D, 2048])
    subk = din("subk", [2, 128, 128])
    eu = din("eu", [OPTS['nexp'], D])
    ev = din("ev", [OPTS['nexp'], D])
    lw2 = din("lw2", [64, 512])
    la2 = din("la2", [64, 512])
    lg2 = din("lg2", [128, 512])
    vec512 = din("vec512", [7, 512])
    mu = din("mu", [1, NRW])
    g1 = din("g1", [1, D])
    g2 = din("g2", [1, D])
    gF = din("gF", [1, D])
    sinks = din("sinks", [1, 8])
    rope = din("rope", [128, NT_P + 1, 16])
    cmask = din("cmask", [128, 3, 256])
    cst = din("cst", [128, 6, 128])
    vmask = din("vmask", [128, 2])
    iota_in = din("iota", [128, 256])

    y_p = dout("y_p", [max(NT_B, 1) * 128, D])
    y_s = dout("y_s", [NSEQ_S * 4, D])
    kw_p = dout("kw_p", [128, 128])
    vw_p = dout("vw_p", [128, 128])
    wkv_p = dout("wkv_p", [8, 64, 64])
    sh_p = dout("sh_p", [1, D])
    kw_s = dout("kw_s", [NSEQ_S, 128, 128])
    vw_s = dout("vw_s", [NSEQ_S, 128, 128])
    wkv_s = dout("wkv_s", [NSEQ_S, 8, 64, 64])
    sh_s = dout("sh_s", [NSEQ_S, D])
    xmid = nc.dram_tensor("xmid", [max(NPE, 1) * 128, D], F32, kind="Internal").ap()
    dbg_t = {}
    if dbg:
        for nm, shp in [("d_m", [128, NRW]), ("d_y", [128, 512]), ("d_at", [128, 512]), ("d_S", [64, 512]),
                        ("d_pre", [128, 128]), ("d_idx", [128, 128]), ("d_gate", [128, 128])]:
            dbg_t[nm] = dout(nm, shp)

    stk = {'cur': ExitStack()}
    glob_stack = stk['cur']

    dumped = {}

    def DUMP(name, t, key, ti=None):
        if not OPTS['dumps'] or (ti is not None and ti != OPTS['dbg_tile']) or name in dumped:
            return
        if OPTS['dump_only'] and name not in str(OPTS['dump_only']).split(','):
            return
        src = t if isinstance(t, bass.AP) else t[:]
        o = nc.dram_tensor("z_" + name, list(src.shape), src.dtype, kind="ExternalOutput").ap()
        dumped[name] = 1
        S.dma('sp', o, src, [key], [])

    def sb(name, shape, dt=F32):
        return stk['cur'].enter_context(nc.sbuf_tensor(name, list(shape), dt))

    def ps(name, shape, dt=F32):
        return nc.alloc_psum_tensor(name, list(shape), dt)

    c_cst = sb("c_cst", [128, 6, 128])
    S.dma('sp', c_cst[:], cst, [], ['c_cst'])
    c_cstb = sb("c_cstb", [128, 6, 128], BF16)
    S.op('dve', ['c_cst'], ['c_cstb'], lambda e: e.tensor_copy(out=c_cstb[:], in_=c_cst[:]))
    ident_f = c_cst[:, 0, :]
    tri_f = c_cst[:, 1, :]
    ones_f = c_cst[:, 2, :]
    ident_b = c_cstb[:, 0, :]
    c_m4 = sb("c_m4", [128, 4, 128])
    for i, j in enumerate([3, 4, 3, 4]):
        S.op('dve', ['c_cst'], ['c_m4'], lambda e, i=i, j=j: e.tensor_copy(out=c_m4[:, i, :], in_=c_cst[:, j, :]))
    c_low = c_cst[:, 5, :]
    c_amask = sb("c_amask", [128, 3, 256])
    S.dma('sp', c_amask[:], cmask, [], ['c_amask'])
    c_rope = sb("c_rope", [128, NT_P + 1, 16])
    S.dma('sp', c_rope[:], rope, [], ['c_rope'])
    c_vm = sb("c_vm", [128, 2])
    S.dma('sp', c_vm[:], vmask, [], ['c_vm'])
    c_v512 = sb("c_v512", [128, 7, 512])
    S.dma('sp', c_v512[:], vec512.partition_broadcast(128), [], ['c_v512'])
    W0, A0, KK, KA, RK, LW, LB = range(7)
    c_g1bc = sb("c_g1bc", [128, D])
    S.dma('sp', c_g1bc[:], g1.partition_broadcast(128)[:, 0, :], [], ['c_g1bc'])
    c_sink = sb("c_sink", [128, 8])
    S.dma('sp', c_sink[:], sinks.partition_broadcast(128)[:, 0, :], [], ['c_sink'])
    c_g1col = sb("c_g1col", [128, 8])
    with nc.allow_non_contiguous_dma(reason="tiny param column load"):
        S.dma('sp', c_g1col[:], g1.rearrange("o (kc p) -> p (o kc)", p=128), [], ['c_g1col'])

    def rsq(key, out, in_, c, op0):
        S.op('dve', [key], [key], lambda e: e.tensor_scalar(out=out, in0=in_, scalar1=c, scalar2=None, op0=op0))
        S.op('act', [key], [key], lambda e: e.activation(out=out, in_=out, func=AF.Sqrt))
        S.op('dve', [key], [key], lambda e: e.reciprocal(out=out, in_=out))

    P = [ps("P%d" % i, [128, 512]) for i in range(8)]
    PK = ["P%d" % i for i in range(8)]

    def barrier():
        toks = []
        for e, sem in S.sem.items():
            if S.cnt[e] > 0:
                S.tokens[id(sem)] = sem
                toks.append((id(sem), S.cnt[e], 'x'))
        for q, slots in S.dslots.items():
            for s, v in slots:
                if v > 0:
                    S.tokens[id(s)] = s
                    toks.append((id(s), v, 'dma'))
        for e in ['pe', 'act', 'dve', 'pool', 'sp']:
            S._wait(e, toks)

    ph1 = ExitStack()
    stk['cur'] = ph1
    W1 = sb("W1", [128, 8, NRW], BF16)
    W2 = sb("W2", [128, 8, NRW], BF16)
    Wat = sb("Wat", [128, 8, 768], BF16)
    Wo = sb("Wo", [128, 8, D], BF16)
    L_w2 = sb("L_w2", [128, 512], BF16)
    L_g2 = sb("L_g2", [128, 512], BF16)
    with nc.sbuf_tensor("stg", [128, NCOL], F32) as stg, nc.sbuf_tensor("mub", [128, NRW], F32) as mub, \
            nc.sbuf_tensor("omu", [128, NRW], F32) as omu:
        S.dma('sp', mub[:], mu.partition_broadcast(128)[:, 0, :], [], ['mub'])
        S.op('dve', ['mub'], ['omu'], lambda e: e.tensor_scalar(out=omu[:], in0=mub[:], scalar1=-1.0, scalar2=1.0,
                                                                op0=ALU.mult, op1=ALU.add))
        for kc in range(8):
            S.dma('sp', stg[:], w_in[kc * 128:(kc + 1) * 128, :], [], ['stg'])
            S.op('dve', ['stg', 'omu'], ['W1'], lambda e, kc=kc: e.tensor_tensor(out=W1[:, kc, :], in0=stg[:, 0:NRW], in1=omu[:], op=ALU.mult))
            S.op('pool', ['stg', 'mub'], ['W2'], lambda e, kc=kc: e.tensor_tensor(out=W2[:, kc, :], in0=stg[:, 0:NRW], in1=mub[:], op=ALU.mult))
            S.op('act', ['stg'], ['Wat'], lambda e, kc=kc: e.copy(out=Wat[:, kc, :], in_=stg[:, NRW:NCOL]))
        for kc in range(8):
            S.dma('sp', stg[:, 0:D], w_out[kc * 128:(kc + 1) * 128, :], [], ['stg'])
            S.op('act', ['stg'], ['Wo'], lambda e, kc=kc: e.copy(out=Wo[:, kc, :], in_=stg[:, 0:D]))
        S.dma('sp', stg[0:64, 0:512], lw2, [], ['stg'])
        S.dma('sp', stg[64:128, 0:512], la2, [], ['stg'])
        S.dma('sp', stg[:, 512:1024], lg2, [], ['stg'])
        S.op('act', ['stg'], ['L_w2'], lambda e: e.copy(out=L_w2[:], in_=stg[:, 0:512]))
        S.op('act', ['stg'], ['L_g2'], lambda e: e.copy(out=L_g2[:], in_=stg[:, 512:1024]))
        barrier()

    xt0 = sb("xt0", [128, D])
    xt = [xt0, xt0]
    xts = xt0
    xn = sb("xn", [128, D], BF16)
    ssq = sb("ssq", [128, 4])
    hT = sb("hT", [128, 8, 128], BF16)
    hTs = sb("hTs", [128, 8, 128], BF16)
    S.op('pool', [], ['hT'], lambda e: e.memset(hT[:], 0.0))
    S.op('pool', [], ['hTs'], lambda e: e.memset(hTs[:], 0.0))
    A = {}
    for nm in ["r", "k", "v", "ld", "asig", "g", "kk", "b", "kmod", "c", "t1", "t2", "t3"]:
        A[nm] = sb("a_" + nm, [128, 512])
    Bt = {}
    for nm in ["rt", "at", "bt", "kt", "bbar", "kbar", "vb", "W0T", "UT"]:
        Bt[nm] = sb("b_" + nm, [128, 512], BF16)
    lin = sb("lin", [128, 2, 128], BF16)
    st8 = sb("st8", [128, 8, 4])
    AR_fm = sb("AR_fm", [64, 8, 256], BF16)
    B_fm = sb("B_fm", [64, 8, 128], BF16)
    K_fm = sb("K_fm", [64, 8, 128], BF16)
    MATS = sb("MATS", [128, 8, 512], BF16)
    _m = sb("Mb", [128, 8, 128], BF16)
    _mt = sb("MTb", [128, 8, 128], BF16)
    _t = sb("Tb", [128, 8, 128], BF16)
    Mb, MTb, Tb = [_m, _m], [_mt, _mt], [_t, _t]
    S32 = sb("S32", [64, 8, 64])
    Sb = sb("Sb", [64, 8, 64], BF16)
    ecl_fm = sb("ecl_fm", [64, 8])
    sti = sb("sti", [64, 8, 64])
    qkv = sb("qkv", [128, 768])
    rtmp = sb("rtmp", [128, 10, 8])
    qb = sb("qb", [128, 768], BF16)
    QT = sb("QT", [64, 8, 128], BF16)
    KT = [sb("KT%d" % i, [64, 2, 128], BF16) for i in range(2)]
    Vb = [sb("Vb%d" % i, [128, 128], BF16) for i in range(2)]
    sm = sb("sm", [128, 256])
    for i in range(2):
        S.op('pool', [], ['KT%d' % i], lambda e, i=i: e.memset(KT[i][:], 0.0))
        S.op('pool', [], ['Vb%d' % i], lambda e, i=i: e.memset(Vb[i][:], 0.0))
    eb = sb("eb", [128, 256], BF16)
    eT = sb("eT", [128, 2, 128], BF16)
    ast = sb("ast", [128, 8])
    ycat = sb("ycat", [128, D], BF16)
    ycatT = sb("ycatT", [128, 8, 128], BF16)
    junk = ycatT[:].rearrange("p k t -> p (k t)")
    xm = sb("xm", [128, D])
    hrow = xm
    ckb = sm

    def v3(ap, h=8):
        return ap.rearrange("p (h j) -> p h j", h=h)

    def bc_last(ap, n):
        return ap.broadcast_to([ap.shape[0], ap.shape[1], n])

    def V512(i):
        return c_v512[:, i, :]

    def load_x(ti):
        if ti < NT_P:
            b = xt[ti % 2]
            S.dma('sp', b[:], xp[ti * 128:(ti + 1) * 128, :], [], ['xt0'])
        else:
            s = ti - NT_P
            if s == 0:
                S.op('pool', [], ['xt0'], lambda e: e.memset(xt0[:], 0.0))
                S.dma('sp', xmid[NT_B * 128:(NT_B + 1) * 128, :], xt0[:], ['xt0'], ['xmid'])
            S.dma('sp', xts[0:4, :], xs[s * 4:(s + 1) * 4, :], [], ['xt0'])

    def mix_tile(ti):
        samp = ti >= NT_P
        sq = ti - NT_P
        xk = 'xt0'
        xb = xts if samp else xt[ti % 2]
        par = ti % 2
        vm = c_vm[:, 1:2] if samp else c_vm[:, 0:1]
        if OPTS['stage'] <= 0:
            return
        S.op('act', [xk], ['ycatT', 'ssq'], lambda e: e.activation(out=junk[:], in_=xb[:], func=AF.Square, accum_out=ssq[:, 0:1]))
        rsq('ssq', ssq[:, 1:2], ssq[:, 0:1], D * 1e-5, ALU.add)
        S.op('dve', [xk, 'ssq'], ['xn'], lambda e: e.tensor_scalar(out=xn[:], in0=xb[:], scalar1=ssq[:, 1:2], scalar2=32.0,
                                                                   op0=ALU.mult, op1=ALU.mult))
        if samp:
            S.dma('sp', hrow[0:1, :], sshift[sq:sq + 1, :], [], ['xm'])
            S.op('act', ['xm'], ['ycatT'], lambda e: e.copy(out=junk[0:1, :], in_=hrow[0:1, :]))
        pT = P[0][:].bitcast(BF16)
        if samp:
            for kc in range(8):
                S.op('pe', ['ycatT', 'c_cstb'], ['P1'], lambda e, kc=kc: e.transpose(out=P[1][:].bitcast(BF16)[:, 2 * kc:2 * kc + 1], in_=junk[0:1, kc * 128:(kc + 1) * 128], identity=ident_b[0:1, 0:1]))
            S.op('dve', ['P1'], ['hTs'], lambda e: e.tensor_copy(out=hTs[:, :, 0], in_=P[1][:].bitcast(BF16)[:, 0:16].rearrange("p (k two) -> p k two", two=2)[:, :, 0]))
        else:
            S.op('dve', ['hT'], ['hTs'], lambda e: e.tensor_copy(out=hTs[:, :, 0], in_=hT[:, :, 127]))
        for kc in range(8):
            S.op('pe', ['xn', 'c_cstb'], ['P0'], lambda e, kc=kc: e.transpose(out=pT[:, kc * 128:(kc + 1) * 128], in_=xn[:, kc * 128:(kc + 1) * 128], identity=ident_b))
        S.op('dve', ['P0', 'c_g1col'], ['hT'], lambda e: e.tensor_tensor(
            out=hT[:], in0=pT.rearrange("p (k t) -> p k t", k=8),
            in1=c_g1col[:].unsqueeze(2).broadcast_to([128, 8, 128]), op=ALU.mult))
        S.op('pool', ['hT'], ['hTs'], lambda e: e.tensor_copy(out=hTs[:, :, 1:128], in_=hT[:, :, 0:127]))
        if samp or ti == OUT_T:
            S.op('pool', ['xn', 'c_g1bc'], ['xm'], lambda e: e.tensor_tensor(out=hrow[:], in0=xn[:], in1=c_g1bc[:], op=ALU.mult))
            if samp:
                S.dma('sp', sh_s[sq:sq + 1, :], hrow[3:4, :], ['xm'], [])
            else:
                S.dma('sp', sh_p[0:1, :], hrow[127:128, :], ['xm'], [])
        DUMP("xn", xn, 'xn', ti); DUMP("hT", hT, 'hT', ti); DUMP("hTs", hTs, 'hTs', ti)
        if OPTS['stage'] <= 1:
            return
        def proj(bank, c0, n, dst_reads=()):
            for kc in range(8):
                S.op('pe', ['hT', 'W1'], [PK[bank]], lambda e, kc=kc: e.matmul(out=P[bank][:, 0:n], lhsT=hT[:, kc, :], rhs=W1[:, kc, c0:c0 + n], start=(kc == 0), stop=False))
            for kc in range(8):
                S.op('pe', ['hTs', 'W2'], [PK[bank]], lambda e, kc=kc: e.matmul(out=P[bank][:, 0:n], lhsT=hTs[:, kc, :], rhs=W2[:, kc, c0:c0 + n], start=False, stop=(kc == 7)))
        proj(1, 0, 512)
        S.op('act', ['P1'], ['a_r'], lambda e: e.copy(out=A["r"][:], in_=P[1][:]))
        proj(2, 512, 512)
        S.op('act', ['P2'], ['a_k'], lambda e: e.copy(out=A["k"][:], in_=P[2][:]))
        proj(3, 1024, 512)
        S.op('act', ['P3'], ['a_v'], lambda e: e.copy(out=A["v"][:], in_=P[3][:]))
        S.op('dve', ['a_v', 'c_vm'], ['b_vb'], lambda e: e.tensor_scalar(out=Bt["vb"][:], in0=A["v"][:], scalar1=vm, scalar2=None, op0=ALU.mult))
        for j, c0 in enumerate([1536, 1664]):
            for kc in range(8):
                S.op('pe', ['hT', 'W1'], ['P4'], lambda e, kc=kc, j=j, c0=c0: e.matmul(out=P[4][:, j * 128:(j + 1) * 128], lhsT=W1[:, kc, c0:c0 + 128], rhs=hT[:, kc, :], start=(kc == 0), stop=False))
            for kc in range(8):
                S.op('pe', ['hTs', 'W2'], ['P4'], lambda e, kc=kc, j=j, c0=c0: e.matmul(out=P[4][:, j * 128:(j + 1) * 128], lhsT=W2[:, kc, c0:c0 + 128], rhs=hTs[:, kc, :], start=False, stop=(kc == 7)))
        S.op('act', ['P4'], ['lin'], lambda e: e.activation(out=lin[0:64, 0, :], in_=P[4][0:64, 0:128], func=AF.Tanh))
        S.op('act', ['P4'], ['lin'], lambda e: e.copy(out=lin[64:128, 0, :], in_=P[4][64:128, 0:128]))
        S.op('act', ['P4'], ['lin'], lambda e: e.activation(out=lin[:, 1, :], in_=P[4][:, 128:256], func=AF.Sigmoid))
        for kc in range(8):
            S.op('pe', ['hT', 'Wat'], ['P5'], lambda e, kc=kc: e.matmul(out=P[5][:], lhsT=hT[:, kc, :], rhs=Wat[:, kc, 0:512], start=(kc == 0), stop=(kc == 7)))
        for kc in range(8):
            S.op('pe', ['hT', 'Wat'], ['P6'], lambda e, kc=kc: e.matmul(out=P[6][:, 0:256], lhsT=hT[:, kc, :], rhs=Wat[:, kc, 512:768], start=(kc == 0), stop=(kc == 7)))
        S.op('act', ['P5'], ['qkv'], lambda e: e.copy(out=qkv[:, 0:512], in_=P[5][:]))
        S.op('act', ['P6'], ['qkv'], lambda e: e.copy(out=qkv[:, 512:768], in_=P[6][:, 0:256]))
        DUMP("a_r", A["r"], 'a_r', ti); DUMP("a_v", A["v"], 'a_v', ti); DUMP("lin", lin, 'lin', ti); DUMP("qkv0", qkv, 'qkv', ti)
        if OPTS['stage'] <= 2:
            return
        S.op('pe', ['lin', 'L_w2'], ['P1'], lambda e: e.matmul(out=P[1][:], lhsT=lin[0:64, 0, :], rhs=L_w2[0:64, :], start=True, stop=True))
        S.op('pe', ['lin', 'L_w2'], ['P2'], lambda e: e.matmul(out=P[2][:], lhsT=lin[64:128, 0, :], rhs=L_w2[64:128, :], start=True, stop=True))
        S.op('pe', ['lin', 'L_g2'], ['P3'], lambda e: e.matmul(out=P[3][:], lhsT=lin[:, 1, :], rhs=L_g2[:], start=True, stop=True))
        S.op('dve', ['P1', 'c_v512'], ['a_t1'], lambda e: e.tensor_tensor(out=A["t1"][:], in0=P[1][:], in1=V512(W0), op=ALU.add))
        S.op('act', ['a_t1'], ['a_t1'], lambda e: e.activation(out=A["t1"][:], in_=A["t1"][:], func=AF.Sigmoid))
        S.op('dve', ['a_t1', 'c_vm'], ['a_ld'], lambda e: e.tensor_scalar(out=A["ld"][:], in0=A["t1"][:], scalar1=vm, scalar2=-0.6065306597,
                                                                           op0=ALU.mult, op1=ALU.mult))
        S.op('dve', ['P2', 'c_v512'], ['a_t2'], lambda e: e.tensor_tensor(out=A["t2"][:], in0=P[2][:], in1=V512(A0), op=ALU.add))
        S.op('act', ['a_t2'], ['a_asig'], lambda e: e.activation(out=A["asig"][:], in_=A["t2"][:], func=AF.Sigmoid))
        S.op('act', ['P3'], ['a_g'], lambda e: e.copy(out=A["g"][:], in_=P[3][:]))
        S.op('pool', ['a_k', 'c_v512'], ['a_kk'], lambda e: e.tensor_tensor(out=A["kk"][:], in0=A["k"][:], in1=V512(KK), op=ALU.mult))
        S.op('pool', ['a_kk'], ['a_t3'], lambda e: e.tensor_tensor(out=A["t3"][:], in0=A["kk"][:], in1=A["kk"][:], op=ALU.mult))
        S.op('dve', ['a_t3'], ['st8'], lambda e: e.tensor_reduce(out=st8[:, :, 0], in_=v3(A["t3"][:]), axis=AX.X, op=ALU.add))
        rsq('st8', st8[:, :, 1], st8[:, :, 0], 1e-24, ALU.max)
        S.op('dve', ['a_kk', 'st8'], ['a_kk'], lambda e: e.tensor_tensor(out=v3(A["kk"][:]), in0=v3(A["kk"][:]), in1=bc_last(st8[:, :, 1:2], 64), op=ALU.mult))
        S.op('dve', ['a_kk', 'a_asig', 'c_vm'], ['a_b'], lambda e: e.scalar_tensor_tensor(out=A["b"][:], in0=A["kk"][:], scalar=vm, in1=A["asig"][:], op0=ALU.mult, op1=ALU.mult))
        S.op('dve', ['a_asig', 'c_v512'], ['a_t2'], lambda e: e.scalar_tensor_tensor(out=A["t2"][:], in0=A["asig"][:], scalar=-1.0, in1=V512(KA), op0=ALU.add, op1=ALU.mult))
        S.op('dve', ['a_t2', 'a_k'], ['a_kmod'], lambda e: e.scalar_tensor_tensor(out=A["kmod"][:], in0=A["t2"][:], scalar=1.0, in1=A["k"][:], op0=ALU.add, op1=ALU.mult))
        S.op('pe', ['a_ld', 'c_cst'], ['P1'], lambda e: e.matmul(out=P[1][:], lhsT=tri_f, rhs=A["ld"][:], start=True, stop=True))
        S.op('pe', ['a_ld', 'c_cst'], ['P2'], lambda e: e.matmul(out=P[2][:], lhsT=ones_f, rhs=A["ld"][:], start=True, stop=True))
        for h in range(8):
            S.op('pe', ['a_ld', 'c_cst'], ['P3'], lambda e, h=h: e.matmul(out=P[3][0:64, h:h + 1], lhsT=A["ld"][:, h * 64:(h + 1) * 64], rhs=ones_f[:, 0:1], start=True, stop=True))
        S.op('act', ['P3'], ['ecl_fm'], lambda e: e.activation(out=ecl_fm[:], in_=P[3][0:64, 0:8], func=AF.Exp))
        S.op('act', ['P1'], ['a_c'], lambda e: e.copy(out=A["c"][:], in_=P[1][:]))
        S.op('act', ['a_c'], ['a_t1'], lambda e: e.activation(out=A["t1"][:], in_=A["c"][:], func=AF.Exp))
        S.op('dve', ['a_t1', 'a_r'], ['b_rt'], lambda e: e.tensor_tensor(out=Bt["rt"][:], in0=A["r"][:], in1=A["t1"][:], op=ALU.mult))
        S.op('pool', ['a_c', 'a_ld'], ['a_t2'], lambda e: e.tensor_tensor(out=A["t2"][:], in0=A["c"][:], in1=A["ld"][:], op=ALU.subtract))
        S.op('act', ['a_t2'], ['a_t2'], lambda e: e.activation(out=A["t2"][:], in_=A["t2"][:], func=AF.Exp))
        S.op('dve', ['a_t2', 'a_kk'], ['b_at'], lambda e: e.scalar_tensor_tensor(out=Bt["at"][:], in0=A["kk"][:], scalar=-1.0, in1=A["t2"][:], op0=ALU.mult, op1=ALU.mult))
        S.op('act', ['a_c'], ['a_t3'], lambda e: e.activation(out=A["t3"][:], in_=A["c"][:], func=AF.Exp, scale=-1.0))
        S.op('dve', ['a_t3', 'a_b'], ['b_bt'], lambda e: e.tensor_tensor(out=Bt["bt"][:], in0=A["b"][:], in1=A["t3"][:], op=ALU.mult))
        S.op('pool', ['a_t3', 'a_kmod'], ['b_kt'], lambda e: e.tensor_tensor(out=Bt["kt"][:], in0=A["kmod"][:], in1=A["t3"][:], op=ALU.mult))
        S.op('dve', ['P2', 'a_c'], ['a_t1'], lambda e: e.tensor_tensor(out=A["t1"][:], in0=P[2][:], in1=A["c"][:], op=ALU.subtract))
        S.op('act', ['a_t1'], ['a_t1'], lambda e: e.activation(out=A["t1"][:], in_=A["t1"][:], func=AF.Exp))
        S.op('dve', ['a_t1', 'a_b'], ['b_bbar'], lambda e: e.tensor_tensor(out=Bt["bbar"][:], in0=A["b"][:], in1=A["t1"][:], op=ALU.mult))
        S.op('pool', ['a_t1', 'a_kmod'], ['b_kbar'], lambda e: e.tensor_tensor(out=Bt["kbar"][:], in0=A["kmod"][:], in1=A["t1"][:], op=ALU.mult))
        S.op('pool', ['a_r', 'c_v512'], ['a_t2'], lambda e: e.tensor_tensor(out=A["t2"][:], in0=A["r"][:], in1=V512(RK), op=ALU.mult))
        S.op('pool', ['a_t2', 'a_kmod'], ['a_t2'], lambda e: e.tensor_tensor(out=A["t2"][:], in0=A["t2"][:], in1=A["kmod"][:], op=ALU.mult))
        S.op('dve', ['a_t2'], ['st8'], lambda e: e.tensor_reduce(out=st8[:, :, 2], in_=v3(A["t2"][:]), axis=AX.X, op=ALU.add))
        DUMP("a_ld", A["ld"], 'a_ld', ti); DUMP("a_c", A["c"], 'a_c', ti); DUMP("a_kk", A["kk"], 'a_kk', ti); DUMP("b_rt", Bt["rt"], 'b_rt', ti); DUMP("b_at", Bt["at"], 'b_at', ti); DUMP("b_bt", Bt["bt"], 'b_bt', ti); DUMP("b_kbar", Bt["kbar"], 'b_kbar', ti); DUMP("ecl_fm", ecl_fm, 'ecl_fm', ti)
        if OPTS['stage'] <= 3:
            return
        pTb = [P[i][:].bitcast(BF16) for i in range(8)]
        for qi, (nm, bank) in enumerate([("at", 4), ("rt", 5), ("bt", 6), ("kt", 7)]):
            for h in range(8):
                S.op('pe', ['b_' + nm, 'c_cstb'], [PK[bank]], lambda e, h=h, nm=nm, bank=bank: e.transpose(
                    out=pTb[bank][0:64, h * 128:(h + 1) * 128], in_=Bt[nm][:, h * 64:(h + 1) * 64], identity=ident_b))
        S.op('act', ['P4'], ['AR_fm'], lambda e: e.copy(out=AR_fm[:, :, 0:128], in_=pTb[4][0:64, 0:1024].rearrange("p (h t) -> p h t", h=8)))
        S.op('dve', ['P5'], ['AR_fm'], lambda e: e.tensor_copy(out=AR_fm[:, :, 128:256], in_=pTb[5][0:64, 0:1024].rearrange("p (h t) -> p h t", h=8)))
        S.op('act', ['P6'], ['B_fm'], lambda e: e.copy(out=B_fm[:], in_=pTb[6][0:64, 0:1024].rearrange("p (h t) -> p h t", h=8)))
        S.op('dve', ['P7'], ['K_fm'], lambda e: e.tensor_copy(out=K_fm[:], in_=pTb[7][0:64, 0:1024].rearrange("p (h t) -> p h t", h=8)))
        for h in range(8):
            bank = h % 2
            S.op('pe', ['B_fm', 'AR_fm'], [PK[bank]], lambda e, h=h, bank=bank: e.matmul(out=P[bank][:, 0:256], lhsT=B_fm[:, h, :], rhs=AR_fm[:, h, :], start=True, stop=True))
            S.op('pe', ['K_fm', 'AR_fm'], [PK[bank]], lambda e, h=h, bank=bank: e.matmul(out=P[bank][:, 256:512], lhsT=K_fm[:, h, :], rhs=AR_fm[:, h, :], start=True, stop=True))
            S.op('dve', [PK[bank], 'c_m4'], ['MATS'], lambda e, h=h, bank=bank: e.tensor_tensor(out=MATS[:, h, :], in0=P[bank][:], in1=c_m4[:].rearrange("p a b -> p (a b)"), op=ALU.mult))
        for hh in range(2):
            bank = 2 + hh
            for h4 in range(4):
                h = hh * 4 + h4
                S.op('pe', ['B_fm', 'AR_fm'], [PK[bank]], lambda e, h=h, h4=h4, bank=bank: e.matmul(out=P[bank][:, h4 * 128:(h4 + 1) * 128], lhsT=AR_fm[:, h, 0:128], rhs=B_fm[:, h, :], start=True, stop=True))
            S.op('dve', [PK[bank], 'c_cst'], ['MTb'], lambda e, hh=hh, bank=bank: e.tensor_tensor(
                out=MTb[0][:, hh * 4:(hh + 1) * 4, :], in0=P[bank][:].rearrange("p (h t) -> p h t", h=4),
                in1=c_low.unsqueeze(1).broadcast_to([128, 4, 128]), op=ALU.mult))
        S.op('act', ['MATS'], ['Mb'], lambda e: e.copy(out=Mb[0][:], in_=MATS[:, :, 0:128]))
        S.op('pool', ['MATS', 'c_cstb'], ['Tb'], lambda e: e.tensor_tensor(out=Tb[0][:], in0=MATS[:, :, 0:128], in1=ident_b.unsqueeze(1).broadcast_to([128, 8, 128]), op=ALU.add))
        cur = 0
        for lvl in range(1, 7):
            nxt = 1 - cur
            for hh in range(2):
                hs = slice(hh * 4, hh * 4 + 4)
                bM, bMT, bT = 2 + hh * 3, 3 + hh * 3, 4 + hh * 3
                for h4 in range(4):
                    h = hh * 4 + h4
                    cs = slice(h4 * 128, (h4 + 1) * 128)
                    if lvl < 6:
                        S.op('pe', ['Mb', 'MTb'], [PK[bM]], lambda e, h=h, cs=cs, bM=bM, cur=cur: e.matmul(out=P[bM][:, cs], lhsT=MTb[cur][:, h, :], rhs=Mb[cur][:, h, :], start=True, stop=True))
                    S.op('pe', ['Mb', 'MTb'], [PK[bMT]], lambda e, h=h, cs=cs, bMT=bMT, cur=cur: e.matmul(out=P[bMT][:, cs], lhsT=Mb[cur][:, h, :], rhs=MTb[cur][:, h, :], start=True, stop=True))
                if lvl < 6:
                    S.op('act', [PK[bM]], ['Mb'], lambda e, hs=hs, bM=bM, nxt=nxt: e.copy(out=Mb[nxt][:, hs, :], in_=P[bM][:].rearrange("p (h t) -> p h t", h=4)))
                S.op('dve', [PK[bMT]], ['MTb'], lambda e, hs=hs, bMT=bMT, nxt=nxt: e.tensor_copy(out=MTb[nxt][:, hs, :], in_=P[bMT][:].rearrange("p (h t) -> p h t", h=4)))
                for h4 in range(4):
                    h = hh * 4 + h4
                    cs = slice(h4 * 128, (h4 + 1) * 128)
                    S.op('pe', ['MTb', 'Tb'], [PK[bT]], lambda e, h=h, cs=cs, bT=bT, cur=cur, nxt=nxt: e.matmul(out=P[bT][:, cs], lhsT=MTb[nxt][:, h, :], rhs=Tb[cur][:, h, :], start=True, stop=True))
                S.op('dve', [PK[bT], 'Tb'], ['Tb'], lambda e, hs=hs, bT=bT, cur=cur, nxt=nxt: e.tensor_tensor(
                    out=Tb[nxt][:, hs, :], in0=P[bT][:].rearrange("p (h t) -> p h t", h=4), in1=Tb[cur][:, hs, :], op=ALU.add))
            cur = nxt
        Tf = Tb[cur]
        TfK = 'Tb'
        DUMP("AR_fm", AR_fm, 'AR_fm', ti); DUMP("K_fm", K_fm, 'K_fm', ti); DUMP("MATS", MATS, 'MATS', ti); DUMP("Tb", Tb[0], 'Tb', ti); DUMP("MTb", MTb[0], 'MTb', ti)
        if OPTS['stage'] <= 4:
            return
        if samp or ti == 0:
            if samp:
                S.dma('sp', sti[:], swkv[sq].rearrange("h i j -> i h j"), [], ['sti'])
                for h in range(8):
                    S.op('pe', ['sti', 'c_cst'], ['P0'], lambda e, h=h: e.transpose(out=P[0][0:64, h * 64:(h + 1) * 64], in_=sti[:, h, :], identity=ident_f[0:64, 0:64]))
                S.op('dve', ['P0'], ['S32'], lambda e: e.tensor_copy(out=S32[:], in_=P[0][0:64, :].rearrange("p (h i) -> p h i", h=8)))
            else:
                S.op('dve', [], ['S32'], lambda e: e.memset(S32[:], 0.0))
            S.op('act', ['S32'], ['Sb'], lambda e: e.copy(out=Sb[:], in_=S32[:]))
        for h in range(8):
            cs = slice(h * 64, (h + 1) * 64)
            S.op('pe', ['AR_fm', 'Sb'], ['P0'], lambda e, h=h, cs=cs: e.matmul(out=P[0][:, cs], lhsT=AR_fm[:, h, 0:128], rhs=Sb[:, h, :], start=True, stop=False))
            S.op('pe', ['MATS', 'b_vb'], ['P0'], lambda e, h=h, cs=cs: e.matmul(out=P[0][:, cs], lhsT=MATS[:, h, 256:384], rhs=Bt["vb"][:, cs], start=False, stop=True))
        S.op('act', ['P0'], ['b_W0T'], lambda e: e.copy(out=Bt["W0T"][:], in_=P[0][:]))
        for h in range(8):
            cs = slice(h * 64, (h + 1) * 64)
            S.op('pe', [TfK, 'b_W0T'], ['P1'], lambda e, h=h, cs=cs: e.matmul(out=P[1][:, cs], lhsT=Tf[:, h, :], rhs=Bt["W0T"][:, cs], start=True, stop=True))
        S.op('act', ['P1'], ['b_UT'], lambda e: e.copy(out=Bt["UT"][:], in_=P[1][:]))
        for h in range(8):
            cs = slice(h * 64, (h + 1) * 64)
            S.op('pe', ['AR_fm', 'Sb'], ['P0'], lambda e, h=h, cs=cs: e.matmul(out=P[0][:, cs], lhsT=AR_fm[:, h, 128:256], rhs=Sb[:, h, :], start=True, stop=False))
            S.op('pe', ['MATS', 'b_UT'], ['P0'], lambda e, h=h, cs=cs: e.matmul(out=P[0][:, cs], lhsT=MATS[:, h, 128:256], rhs=Bt["UT"][:, cs], start=False, stop=False))
            S.op('pe', ['MATS', 'b_vb'], ['P0'], lambda e, h=h, cs=cs: e.matmul(out=P[0][:, cs], lhsT=MATS[:, h, 384:512], rhs=Bt["vb"][:, cs], start=False, stop=True))
        for h in range(8):
            cs = slice(h * 64, (h + 1) * 64)
            S.op('pe', ['b_bbar', 'b_UT'], ['P1'], lambda e, h=h, cs=cs: e.matmul(out=P[1][0:64, cs], lhsT=Bt["bbar"][:, cs], rhs=Bt["UT"][:, cs], start=True, stop=False))
            S.op('pe', ['b_kbar', 'b_vb'], ['P1'], lambda e, h=h, cs=cs: e.matmul(out=P[1][0:64, cs], lhsT=Bt["kbar"][:, cs], rhs=Bt["vb"][:, cs], start=False, stop=True))
        S.op('dve', ['S32', 'ecl_fm'], ['S32'], lambda e: e.tensor_tensor(out=S32[:], in0=S32[:], in1=ecl_fm[:].unsqueeze(2).broadcast_to([64, 8, 64]), op=ALU.mult))
        S.op('dve', ['S32', 'P1'], ['S32'], lambda e: e.tensor_tensor(out=S32[:], in0=S32[:], in1=P[1][0:64, :].rearrange("p (h i) -> p h i", h=8), op=ALU.add))
        S.op('act', ['S32'], ['Sb'], lambda e: e.copy(out=Sb[:], in_=S32[:]))
        if samp or ti == OUT_T:
            for h in range(8):
                S.op('pe', ['S32', 'c_cst'], ['P2'], lambda e, h=h: e.transpose(out=P[2][0:64, h * 64:(h + 1) * 64], in_=S32[:, h, :], identity=ident_f[0:64, 0:64]))
            S.op('act', ['P2'], ['sti'], lambda e: e.copy(out=sti[:], in_=P[2][0:64, :].rearrange("p (h j) -> p h j", h=8)))
            dst = wkv_s[sq] if samp else wkv_p
            S.dma('sp', dst.rearrange("h i j -> i h j"), sti[:], ['sti'], [])
        DUMP("S32", S32, 'S32', ti); DUMP("b_UT", Bt["UT"], 'b_UT', ti); DUMP("b_W0T", Bt["W0T"], 'b_W0T', ti)
        if OPTS['stage'] <= 5:
            return
        state_only = ti < NT_A
        if state_only and ti != NT_A - 1:
            return
        if not state_only:
            rwkv_post(ti)
        attn_and_out(ti, samp, sq, xk, xb, par, state_only)

    def rwkv_post(ti):
        if True:
            pass
        Y3 = v3(P[0][:])
        S.op('dve', ['P0'], ['st8'], lambda e: e.tensor_reduce(out=st8[:, :, 0], in_=Y3, axis=AX.X, op=ALU.add))
        S.op('dve', ['st8'], ['st8'], lambda e: e.tensor_scalar(out=st8[:, :, 0], in0=st8[:, :, 0], scalar1=1.0 / 64, scalar2=None, op0=ALU.mult))
        S.op('dve', ['P0', 'st8'], ['a_t1'], lambda e: e.tensor_tensor(out=v3(A["t1"][:]), in0=Y3, in1=bc_last(st8[:, :, 0:1], 64), op=ALU.subtract))
        S.op('pool', ['a_t1'], ['a_t2'], lambda e: e.tensor_tensor(out=A["t2"][:], in0=A["t1"][:], in1=A["t1"][:], op=ALU.mult))
        S.op('dve', ['a_t2'], ['st8'], lambda e: e.tensor_reduce(out=st8[:, :, 1], in_=v3(A["t2"][:]), axis=AX.X, op=ALU.add))
        S.op('dve', ['st8'], ['st8'], lambda e: e.tensor_scalar(out=st8[:, :, 1], in0=st8[:, :, 1], scalar1=1.0 / 64, scalar2=64e-5, op0=ALU.mult, op1=ALU.add))
        rsq('st8', st8[:, :, 1], st8[:, :, 1], 0.0, ALU.add)
        S.op('dve', ['a_t1', 'st8'], ['a_t1'], lambda e: e.tensor_tensor(out=v3(A["t1"][:]), in0=v3(A["t1"][:]), in1=bc_last(st8[:, :, 1:2], 64), op=ALU.mult))
        S.op('pool', ['a_t1', 'c_v512'], ['a_t1'], lambda e: e.tensor_tensor(out=A["t1"][:], in0=A["t1"][:], in1=V512(LW), op=ALU.mult))
        S.op('pool', ['a_t1', 'c_v512'], ['a_t1'], lambda e: e.tensor_tensor(out=A["t1"][:], in0=A["t1"][:], in1=V512(LB), op=ALU.add))
        S.op('dve', ['a_v', 'st8'], ['a_t2'], lambda e: e.tensor_tensor(out=v3(A["t2"][:]), in0=v3(A["v"][:]), in1=bc_last(st8[:, :, 2:3], 64), op=ALU.mult))
        S.op('pool', ['a_t1', 'a_t2'], ['a_t1'], lambda e: e.tensor_tensor(out=A["t1"][:], in0=A["t1"][:], in1=A["t2"][:], op=ALU.add))
        S.op('dve', ['a_t1', 'a_g'], ['ycat'], lambda e: e.tensor_tensor(out=ycat[:, 0:512], in0=A["t1"][:], in1=A["g"][:], op=ALU.mult))
        DUMP("ycat_rw", ycat[:, 0:512], 'ycat', ti)

    def attn_and_out(ti, samp, sq, xk, xb, par, state_only):
        pTb = [P[i][:].bitcast(BF16) for i in range(8)]
        if OPTS['stage'] <= 6:
            return
        ri = NT_P if samp else ti
        cosb = c_rope[:, ri, 0:8].unsqueeze(1)
        sinb = c_rope[:, ri, 8:16].unsqueeze(1)
        for (c0, nh) in [(0, 8), (512, 2)]:
            X = qkv[:, c0:c0 + nh * 64].rearrange("p (h j) -> p h j", h=nh)
            x1, x2 = X[:, :, 0:8], X[:, :, 8:16]
            cb = cosb.broadcast_to([128, nh, 8])
            sbb = sinb.broadcast_to([128, nh, 8])
            R = rtmp[:, 0:nh, :]
            T1 = rtmp[:, 0:nh, :]
            ra = rtmp[:].rearrange("p a b -> p (a b)")
            t_a = ra[:, 0:nh * 8].rearrange("p (h j) -> p h j", h=nh)
            t_b = ra[:, 80 - 0:80].rearrange("p (h j) -> p h j", h=1) if False else None
            S.op('dve', ['qkv', 'c_rope'], ['rtmp'], lambda e, x1=x1, cb=cb, t_a=t_a: e.tensor_tensor(out=t_a, in0=x1, in1=cb, op=ALU.mult))
            S.op('dve', ['qkv', 'c_rope'], ['sm'], lambda e, x2=x2, sbb=sbb, nh=nh: e.tensor_tensor(out=sm[:, 0:nh * 8].rearrange("p (h j) -> p h j", h=nh), in0=x2, in1=sbb, op=ALU.mult))
            S.op('dve', ['qkv', 'c_rope'], ['sm'], lambda e, x2=x2, cb=cb, nh=nh: e.tensor_tensor(out=sm[:, 64:64 + nh * 8].rearrange("p (h j) -> p h j", h=nh), in0=x2, in1=cb, op=ALU.mult))
            S.op('dve', ['qkv', 'c_rope'], ['sm'], lambda e, x1=x1, sbb=sbb, nh=nh: e.tensor_tensor(out=sm[:, 128:128 + nh * 8].rearrange("p (h j) -> p h j", h=nh), in0=x1, in1=sbb, op=ALU.mult))
            S.op('dve', ['rtmp', 'sm'], ['qkv'], lambda e, x1=x1, t_a=t_a, nh=nh: e.tensor_tensor(out=x1, in0=t_a, in1=sm[:, 0:nh * 8].rearrange("p (h j) -> p h j", h=nh), op=ALU.subtract))
            S.op('dve', ['sm'], ['qkv'], lambda e, x2=x2, nh=nh: e.tensor_tensor(out=x2, in0=sm[:, 64:64 + nh * 8].rearrange("p (h j) -> p h j", h=nh), in1=sm[:, 128:128 + nh * 8].rearrange("p (h j) -> p h j", h=nh), op=ALU.add))
        S.op('act', ['qkv'], ['qb'], lambda e: e.copy(out=qb[:], in_=qkv[:]))
        S.op('pool', ['qkv'], ['Vb%d' % par], lambda e: e.tensor_copy(out=Vb[par][:], in_=qkv[:, 640:768]))
        for h in range(8):
            S.op('pe', ['qb', 'c_cstb'], ['P2'], lambda e, h=h: e.transpose(out=pTb[2][0:64, h * 128:(h + 1) * 128], in_=qb[:, h * 64:(h + 1) * 64], identity=ident_b))
        for kv in range(2):
            S.op('pe', ['qb', 'c_cstb'], ['P3'], lambda e, kv=kv: e.transpose(out=pTb[3][0:64, kv * 128:(kv + 1) * 128], in_=qb[:, 512 + kv * 64:512 + (kv + 1) * 64], identity=ident_b))
        S.op('act', ['P2'], ['QT'], lambda e: e.copy(out=QT[:], in_=pTb[2][0:64, 0:1024].rearrange("p (h t) -> p h t", h=8)))
        S.op('dve', ['P3'], ['KT%d' % par], lambda e: e.tensor_copy(out=KT[par][:], in_=pTb[3][0:64, 0:256].rearrange("p (h t) -> p h t", h=2)))
        pp = 1 - par
        if samp:
            S.dma('sp', ckb[:, 0:128], ck[sq], [], ['sm'])
            S.dma('sp', ckb[:, 128:256], cv[sq], [], ['sm'])
            S.op('act', ['sm'], ['ycatT'], lambda e: e.copy(out=junk[:, 0:256], in_=ckb[:]))
            for kv in range(2):
                S.op('pe', ['ycatT', 'c_cstb'], ['P3'], lambda e, kv=kv: e.transpose(out=pTb[3][0:64, 256 + kv * 128:256 + (kv + 1) * 128], in_=junk[:, kv * 64:(kv + 1) * 64], identity=ident_b))
            S.op('dve', ['P3'], ['KT%d' % pp], lambda e: e.tensor_copy(out=KT[pp][:], in_=pTb[3][0:64, 256:512].rearrange("p (h t) -> p h t", h=2)))
            S.op('pool', ['ycatT'], ['Vb%d' % pp], lambda e: e.tensor_copy(out=Vb[pp][:], in_=junk[:, 128:256]))
            S.dma('sp', kw_s[sq, 0:124, :], ck[sq, 4:128, :], [], [])
            S.dma('sp', vw_s[sq, 0:124, :], cv[sq, 4:128, :], [], [])
            S.dma('sp', kw_s[sq, 124:128, :], qkv[0:4, 512:640], ['qkv'], [])
            S.dma('sp', vw_s[sq, 124:128, :], qkv[0:4, 640:768], ['qkv'], [])
        elif ti == OUT_T:
            S.dma('sp', kw_p, qkv[:, 512:640], ['qkv'], [])
            S.dma('sp', vw_p, qkv[:, 640:768], ['qkv'], [])
        if state_only:
            return
        mi = 2 if (samp or ti - NT_A > 1) else (ti - NT_A)
        for h in range(8):
            kv = h // 4
            bank = 4 + (h % 2)
            S.op('pe', ['QT', 'KT%d' % pp], [PK[bank]], lambda e, h=h, kv=kv, bank=bank: e.matmul(out=P[bank][:, 0:128], lhsT=QT[:, h, :], rhs=KT[pp][:, kv, :], start=True, stop=True))
            S.op('pe', ['QT', 'KT%d' % par], [PK[bank]], lambda e, h=h, kv=kv, bank=bank: e.matmul(out=P[bank][:, 128:256], lhsT=QT[:, h, :], rhs=KT[par][:, kv, :], start=True, stop=True))
            S.op('dve', [PK[bank], 'c_amask'], ['sm'], lambda e, bank=bank: e.scalar_tensor_tensor(out=sm[:], in0=P[bank][:, 0:256], scalar=0.125, in1=c_amask[:, mi, :], op0=ALU.mult, op1=ALU.add))
            S.op('dve', ['sm'], ['ast'], lambda e: e.tensor_reduce(out=ast[:, 0:1], in_=sm[:], axis=AX.X, op=ALU.max))
            S.op('dve', ['ast', 'c_sink'], ['ast'], lambda e, h=h: e.tensor_scalar(out=ast[:, 1:2], in0=ast[:, 0:1], scalar1=c_sink[:, h:h + 1], scalar2=-1.0, op0=ALU.max, op1=ALU.mult))
            S.op('act', ['sm', 'ast'], ['eb', 'ast'], lambda e: e.activation(out=eb[:], in_=sm[:], func=AF.Exp, bias=ast[:, 1:2], scale=1.0, accum_out=ast[:, 2:3]))
            S.op('act', ['ast', 'c_sink'], ['ast'], lambda e, h=h: e.activation(out=ast[:, 3:4], in_=c_sink[:, h:h + 1], func=AF.Exp, bias=ast[:, 1:2], scale=1.0))
            S.op('dve', ['ast'], ['ast'], lambda e: e.tensor_tensor(out=ast[:, 4:5], in0=ast[:, 2:3], in1=ast[:, 3:4], op=ALU.add))
            S.op('dve', ['ast'], ['ast'], lambda e: e.reciprocal(out=ast[:, 5:6], in_=ast[:, 4:5]))
            for half in range(2):
                S.op('pe', ['eb', 'c_cstb'], ['P6'], lambda e, half=half: e.transpose(out=pTb[6][:, half * 128:(half + 1) * 128], in_=eb[:, half * 128:(half + 1) * 128], identity=ident_b))
            S.op('act', ['P6'], ['eT'], lambda e: e.copy(out=eT[:], in_=pTb[6][:, 0:256].rearrange("p (a t) -> p a t", a=2)))
            S.op('pe', ['eT', 'Vb%d' % pp], ['P7'], lambda e, kv=kv: e.matmul(out=P[7][:, 0:64], lhsT=eT[:, 0, :], rhs=Vb[pp][:, kv * 64:(kv + 1) * 64], start=True, stop=False))
            S.op('pe', ['eT', 'Vb%d' % par], ['P7'], lambda e, kv=kv: e.matmul(out=P[7][:, 0:64], lhsT=eT[:, 1, :], rhs=Vb[par][:, kv * 64:(kv + 1) * 64], start=False, stop=True))
            S.op('dve', ['P7', 'ast'], ['ycat'], lambda e, h=h: e.tensor_scalar(out=ycat[:, 512 + h * 64:512 + (h + 1) * 64], in0=P[7][:, 0:64], scalar1=ast[:, 5:6], scalar2=None, op0=ALU.mult))
        DUMP("ycat", ycat, 'ycat', ti); DUMP("qkv", qkv, 'qkv', ti); DUMP("QT", QT, 'QT', ti)
        if OPTS['stage'] <= 7:
            return
        for kc in range(8):
            S.op('pe', ['ycat', 'c_cstb'], ['P2'], lambda e, kc=kc: e.transpose(out=pTb[2][:, kc * 128:(kc + 1) * 128], in_=ycat[:, kc * 128:(kc + 1) * 128], identity=ident_b))
        S.op('act', ['P2'], ['ycatT'], lambda e: e.copy(out=ycatT[:], in_=pTb[2][:, 0:1024].rearrange("p (k t) -> p k t", k=8)))
        for half in range(2):
            bank = 3 + half
            for kc in range(8):
                S.op('pe', ['ycatT', 'Wo'], [PK[bank]], lambda e, kc=kc, half=half, bank=bank: e.matmul(out=P[bank][:], lhsT=ycatT[:, kc, :], rhs=Wo[:, kc, half * 512:(half + 1) * 512], start=(kc == 0), stop=(kc == 7)))
            S.op('dve', [PK[bank], xk], ['xm'], lambda e, half=half, bank=bank: e.tensor_tensor(out=xm[:, half * 512:(half + 1) * 512], in0=P[bank][:], in1=xb[:, half * 512:(half + 1) * 512], op=ALU.add))
        if dbg and ti == OPTS['dbg_tile']:
            S.op('dve', ['ycat'], ['a_t1'], lambda e: e.tensor_copy(out=A["t1"][:], in_=ycat[:, 0:512]))
            S.op('dve', ['ycat'], ['a_t2'], lambda e: e.tensor_copy(out=A["t2"][:], in_=ycat[:, 512:1024]))
            S.dma('sp', dbg_t["d_y"], A["t1"][:], ['a_t1'], [])
            S.dma('sp', dbg_t["d_at"], A["t2"][:], ['a_t2'], [])
            S.dma('sp', dbg_t["d_S"], S32[:].rearrange("p h i -> p (h i)"), ['S32'], [])
        DUMP("xm", xm, 'xm', ti)
        if samp:
            S.dma('sp', xmid[NT_B * 128 + sq * 4:NT_B * 128 + sq * 4 + 4, :], xm[0:4, :], ['xm'], ['xmid'])
        else:
            S.dma('sp', xmid[(ti - NT_A) * 128:(ti - NT_A + 1) * 128, :], xm[:], ['xm'], ['xmid'])


    def peer_phase():
        c_g2bc = sb("c_g2bc", [128, D])
        S.dma('sp', c_g2bc[:], g2.partition_broadcast(128)[:, 0, :], [], ['c_g2bc'])
        c_gFbc = sb("c_gFbc", [128, D])
        S.dma('sp', c_gFbc[:], gF.partition_broadcast(128)[:, 0, :], [], ['c_gFbc'])
        c_iota = sb("c_iota", [128, 256])
        S.dma('sp', c_iota[:], iota_in, [], ['c_iota'])
        Wq = sb("Wq", [128, 8, 2048], BF16)
        skT = sb("skT", [128, 2, 128], BF16)
        xm2 = sb("xm2", [128, D])
        hn32 = sb("hn32", [128, D])
        hnb = sb("hnb", [128, D], BF16)
        hn2T = sb("hn2T", [128, 8, 128], BF16)
        qT = sb("qT", [128, 16, 128], BF16)
        s_sb = sb("s_sb", [128, 16, 128])
        s2 = sb("s2", [128, 16, 128])
        tv = sb("tv", [128, 16, 16])
        tiu = sb("tiu", [128, 16, 16], U32)
        tif = sb("tif", [128, 16, 16])
        cand = sb("cand", [128, 8, 256])
        cand2 = sb("cand2", [128, 8, 256])
        cidx = sb("cidx", [128, 8, 256])
        top = sb("top", [128, 8, 16])
        selu = sb("selu", [128, 8, 16], U32)
        self_ = sb("self", [128, 8, 16])
        idxf = sb("idxf", [128, 8, 16])
        idx2 = sb("idx2", [128, 8, 16])
        selu2 = sb("selu2", [128, 2, 8, 16], U32)
        sela = sb("sela", [128, 8, 16])
        selb = sb("selb", [128, 8, 16])
        idxu = sb("idxu", [128, 128], U32)
        gate = sb("gate", [128, 8, 16])
        gst = sb("gst", [128, 8, 2])
        pre = sb("pre", [128, 128])
        wgt = sb("wgt", [128, 128])
        acc = sb("acc", [128, D])
        ss2 = sb("ss2", [128, 4])
        NG = OPTS['ng']
        gb = [sb("gb%d" % i, [128, D]) for i in range(NG)]
        with nc.sbuf_tensor("stg2", [128, 2048], F32) as stg2:
            for kc in range(8):
                S.dma('sp', stg2[:], w_q[kc * 128:(kc + 1) * 128, :], [], ['stg2'])
                S.op('act', ['stg2'], ['Wq'], lambda e, kc=kc: e.copy(out=Wq[:, kc, :], in_=stg2[:]))
            S.dma('sp', stg2[:, 0:256].rearrange("p (c d) -> p c d", c=2), subk.rearrange("c n d -> n c d"), [], ['stg2'])
            S.op('act', ['stg2'], ['hnb'], lambda e: e.copy(out=hnb[:, 0:256], in_=stg2[:, 0:256]))
            for c in range(2):
                S.op('pe', ['hnb', 'c_cstb'], ['P0'], lambda e, c=c: e.transpose(out=P[0][:].bitcast(BF16)[:, c * 128:(c + 1) * 128], in_=hnb[:, c * 128:(c + 1) * 128], identity=ident_b))
            S.op('act', ['P0'], ['skT'], lambda e: e.copy(out=skT[:], in_=P[0][:].bitcast(BF16)[:, 0:256].rearrange("p (c n) -> p c n", c=2)))
            barrier()
        pTb = [P[i][:].bitcast(BF16) for i in range(8)]
        for pt in range(NPE):
            samp = pt >= NT_B
            S.dma('sp', xm2[:], xmid[pt * 128:(pt + 1) * 128, :], ['xmid'], ['xm2'])
            S.op('act', ['xm2'], ['hnb', 'ss2'], lambda e: e.activation(out=hnb[:], in_=xm2[:], func=AF.Square, accum_out=ss2[:, 0:1]))
            rsq('ss2', ss2[:, 1:2], ss2[:, 0:1], D * 1e-5, ALU.add)
            S.op('dve', ['xm2', 'ss2'], ['hn32'], lambda e: e.tensor_scalar(out=hn32[:], in0=xm2[:], scalar1=ss2[:, 1:2], scalar2=32.0, op0=ALU.mult, op1=ALU.mult))
            S.op('pool', ['hn32', 'c_g2bc'], ['hn32'], lambda e: e.tensor_tensor(out=hn32[:], in0=hn32[:], in1=c_g2bc[:], op=ALU.mult))
            S.op('act', ['hn32'], ['hnb'], lambda e: e.copy(out=hnb[:], in_=hn32[:]))
            for kc in range(8):
                S.op('pe', ['hnb', 'c_cstb'], ['P0'], lambda e, kc=kc: e.transpose(out=pTb[0][:, kc * 128:(kc + 1) * 128], in_=hnb[:, kc * 128:(kc + 1) * 128], identity=ident_b))
            S.op('act', ['P0'], ['hn2T'], lambda e: e.copy(out=hn2T[:], in_=pTb[0][:, 0:1024].rearrange("p (k t) -> p k t", k=8)))
            for hc in range(16):
                bank = 1 + hc // 4
                cs = slice((hc % 4) * 128, (hc % 4 + 1) * 128)
                for kc in range(8):
                    S.op('pe', ['Wq', 'hn2T'], [PK[bank]], lambda e, hc=hc, kc=kc, bank=bank, cs=cs: e.matmul(out=P[bank][:, cs], lhsT=Wq[:, kc, hc * 128:(hc + 1) * 128], rhs=hn2T[:, kc, :], start=(kc == 0), stop=(kc == 7)))
            for b in range(4):
                S.op('act' if b % 2 else 'dve', [PK[1 + b]], ['qT'], lambda e, b=b: (e.copy if b % 2 else e.tensor_copy)(out=qT[:, b * 4:(b + 1) * 4, :], in_=P[1 + b][:].rearrange("p (a t) -> p a t", a=4)))
            sbanks = [5, 6, 7, 0]
            for hc in range(16):
                bank = sbanks[hc // 4]
                cs = slice((hc % 4) * 128, (hc % 4 + 1) * 128)
                S.op('pe', ['qT', 'skT'], [PK[bank]], lambda e, hc=hc, bank=bank, cs=cs: e.matmul(out=P[bank][:, cs], lhsT=qT[:, hc, :], rhs=skT[:, hc % 2, :], start=True, stop=True))
            for b in range(4):
                S.op('act' if b % 2 else 'dve', [PK[sbanks[b]]], ['s_sb'], lambda e, b=b: (e.copy if b % 2 else e.tensor_copy)(out=s_sb[:, b * 4:(b + 1) * 4, :], in_=P[sbanks[b]][:].rearrange("p (a t) -> p a t", a=4)))
            for hc in range(16):
                S.op('dve', ['s_sb'], ['tv'], lambda e, hc=hc: e.max(out=tv[:, hc, 0:8], in_=s_sb[:, hc, :]))
                S.op('dve', ['s_sb', 'tv'], ['tiu'], lambda e, hc=hc: e.max_index(out=tiu[:, hc, 0:8], in_max=tv[:, hc, 0:8], in_values=s_sb[:, hc, :]))
                S.op('dve', ['s_sb', 'tv'], ['s2'], lambda e, hc=hc: e.match_replace(out=s2[:, hc, :], in_to_replace=tv[:, hc, 0:8], in_values=s_sb[:, hc, :], imm_value=-1e30))
                S.op('dve', ['s2'], ['tv'], lambda e, hc=hc: e.max(out=tv[:, hc, 8:16], in_=s2[:, hc, :]))
                S.op('dve', ['s2', 'tv'], ['tiu'], lambda e, hc=hc: e.max_index(out=tiu[:, hc, 8:16], in_max=tv[:, hc, 8:16], in_values=s2[:, hc, :]))
            S.op('dve', ['tiu'], ['tif'], lambda e: e.tensor_copy(out=tif[:], in_=tiu[:]))
            tv4 = tv[:].rearrange("p (h c) k -> p h c k", c=2)
            tf4 = tif[:].rearrange("p (h c) k -> p h c k", c=2)
            c4 = lambda t: t[:].rearrange("p h (a b) -> p h a b", a=16)
            S.op('dve', ['tv'], ['cand'], lambda e: e.tensor_tensor(out=c4(cand), in0=tv4[:, :, 0, :].unsqueeze(3).broadcast_to([128, 8, 16, 16]),
                                                                    in1=tv4[:, :, 1, :].unsqueeze(2).broadcast_to([128, 8, 16, 16]), op=ALU.add))
            S.op('dve', ['tif'], ['tif'], lambda e: e.tensor_scalar(out=tf4[:, :, 0, :], in0=tf4[:, :, 0, :], scalar1=128.0, scalar2=None, op0=ALU.mult))
            S.op('dve', ['tif'], ['cidx'], lambda e: e.tensor_tensor(out=c4(cidx), in0=tf4[:, :, 0, :].unsqueeze(3).broadcast_to([128, 8, 16, 16]),
                                                                     in1=tf4[:, :, 1, :].unsqueeze(2).broadcast_to([128, 8, 16, 16]), op=ALU.add))
            for h in range(8):
                S.op('dve', ['cand'], ['top'], lambda e, h=h: e.max(out=top[:, h, 0:8], in_=cand[:, h, :]))
                S.op('dve', ['cand', 'top'], ['selu'], lambda e, h=h: e.max_index(out=selu[:, h, 0:8], in_max=top[:, h, 0:8], in_values=cand[:, h, :]))
                S.op('dve', ['cand', 'top'], ['cand2'], lambda e, h=h: e.match_replace(out=cand2[:, h, :], in_to_replace=top[:, h, 0:8], in_values=cand[:, h, :], imm_value=-1e30))
                S.op('dve', ['cand2'], ['top'], lambda e, h=h: e.max(out=top[:, h, 8:16], in_=cand2[:, h, :]))
                S.op('dve', ['cand2', 'top'], ['selu'], lambda e, h=h: e.max_index(out=selu[:, h, 8:16], in_max=top[:, h, 8:16], in_values=cand2[:, h, :]))
            S.op('dve', ['selu'], ['self'], lambda e: e.tensor_copy(out=self_[:], in_=selu[:]))
            S.op('dve', ['selu'], ['selu2'], lambda e: e.tensor_scalar(out=selu2[:, 0], in0=selu[:], scalar1=4, scalar2=None, op0=ALU.logical_shift_right))
            S.op('dve', ['selu'], ['selu2'], lambda e: e.tensor_scalar(out=selu2[:, 1], in0=selu[:], scalar1=15, scalar2=None, op0=ALU.bitwise_and))
            S.op('dve', ['selu2'], ['sela'], lambda e: e.tensor_copy(out=sela[:], in_=selu2[:, 0]))
            S.op('dve', ['selu2'], ['selb'], lambda e: e.tensor_copy(out=selb[:], in_=selu2[:, 1]))
            io16 = c_iota[:, 0:16].unsqueeze(1).unsqueeze(1).broadcast_to([128, 8, 16, 16])
            for which, (selx, dst) in enumerate([(sela, idxf), (selb, idx2)]):
                S.op('dve', ['sela', 'selb', 'c_iota'], ['cand2'], lambda e, selx=selx: e.tensor_tensor(out=c4(cand2), in0=io16, in1=selx[:].unsqueeze(3).broadcast_to([128, 8, 16, 16]), op=ALU.is_equal))
                S.op('dve', ['cand2', 'tif'], ['cand2'], lambda e, which=which: e.tensor_tensor(out=c4(cand2), in0=c4(cand2), in1=tf4[:, :, which, :].unsqueeze(2).broadcast_to([128, 8, 16, 16]), op=ALU.mult))
                S.op('dve', ['cand2'], ['idxf' if which == 0 else 'idx2'], lambda e, dst=dst: e.tensor_reduce(out=dst[:], in_=c4(cand2), axis=AX.X, op=ALU.add))
            S.op('dve', ['idxf', 'idx2'], ['idxf'], lambda e: e.tensor_tensor(out=idxf[:], in0=idxf[:], in1=idx2[:], op=ALU.add))
            S.op('dve', ['idxf'], ['idxf'], lambda e: e.tensor_scalar(out=idxf[:], in0=idxf[:], scalar1=0.0, scalar2=float(NEXP - 1), op0=ALU.max, op1=ALU.min))
            S.op('dve', ['idxf'], ['idxu'], lambda e: e.tensor_copy(out=idxu[:], in_=idxf[:].rearrange("p h k -> p (h k)")))
            S.op('dve', ['top'], ['gate'], lambda e: e.tensor_tensor(out=gate[:], in0=top[:], in1=top[:, :, 0:1].broadcast_to([128, 8, 16]), op=ALU.subtract))
            S.op('act', ['gate'], ['gate'], lambda e: e.activation(out=gate[:], in_=gate[:], func=AF.Exp))
            S.op('dve', ['gate'], ['gst'], lambda e: e.tensor_reduce(out=gst[:, :, 0], in_=gate[:], axis=AX.X, op=ALU.add))
            S.op('dve', ['gst'], ['gst'], lambda e: e.reciprocal(out=gst[:, :, 1], in_=gst[:, :, 0]))
            S.op('dve', ['gate', 'gst'], ['gate'], lambda e: e.tensor_tensor(out=gate[:], in0=gate[:], in1=gst[:, :, 1:2].broadcast_to([128, 8, 16]), op=ALU.mult))
            for sl in range(128):
                g = sl % NG
                S.dma('pool', None, None, ['idxu'], ['gb%d' % g], fn=lambda e, sl=sl, g=g: e.indirect_dma_start(
                    out=(gb[g][:, 0:512] if OPTS['half'] else gb[g][:]), out_offset=None, in_=(eu[:, 0:512] if OPTS['half'] else eu), in_offset=bass.IndirectOffsetOnAxis(ap=idxu[:, sl:sl + 1], axis=0)))
                S.op('dve', ['gb%d' % g, 'hn32'], ['gb%d' % g, 'pre'], lambda e, sl=sl, g=g: e.scalar_tensor_tensor(
                    out=gb[g][:], in0=gb[g][:], scalar=1.0, in1=hn32[:], op0=ALU.mult, op1=ALU.mult, accum_out=pre[:, sl:sl + 1]))
            S.op('act', ['pre'], ['wgt'], lambda e: e.activation(out=wgt[:], in_=pre[:], func=AF.Gelu))
            S.op('dve', ['wgt', 'gate'], ['wgt'], lambda e: e.tensor_tensor(out=wgt[:], in0=wgt[:], in1=gate[:].rearrange("p h k -> p (h k)"), op=ALU.mult))
            S.op('dve', ['xm2'], ['acc'], lambda e: e.tensor_copy(out=acc[:], in_=xm2[:]))
            for sl in range(128):
                g = sl % NG
                S.dma('pool', None, None, ['idxu'], ['gb%d' % g], fn=lambda e, sl=sl, g=g: e.indirect_dma_start(
                    out=(gb[g][:, 0:512] if OPTS['half'] else gb[g][:]), out_offset=None, in_=(ev[:, 0:512] if OPTS['half'] else ev), in_offset=bass.IndirectOffsetOnAxis(ap=idxu[:, sl:sl + 1], axis=0)))
                S.op('dve', ['gb%d' % g, 'wgt', 'acc'], ['acc'], lambda e, sl=sl, g=g: e.scalar_tensor_tensor(
                    out=acc[:], in0=gb[g][:], scalar=wgt[:, sl:sl + 1], in1=acc[:], op0=ALU.mult, op1=ALU.add))
            S.op('act', ['acc'], ['hnb', 'ss2'], lambda e: e.activation(out=hnb[:], in_=acc[:], func=AF.Square, accum_out=ss2[:, 2:3]))
            rsq('ss2', ss2[:, 3:4], ss2[:, 2:3], D * 1e-5, ALU.add)
            S.op('dve', ['acc', 'ss2'], ['acc'], lambda e: e.tensor_scalar(out=acc[:], in0=acc[:], scalar1=ss2[:, 3:4], scalar2=32.0, op0=ALU.mult, op1=ALU.mult))
            S.op('pool', ['acc', 'c_gFbc'], ['acc'], lambda e: e.tensor_tensor(out=acc[:], in0=acc[:], in1=c_gFbc[:], op=ALU.mult))
            if samp:
                S.dma('sp', y_s[:, :], acc[0:NSEQ_S * 4, :], ['acc'], [])
            else:
                S.dma('sp', y_p[pt * 128:(pt + 1) * 128, :], acc[:], ['acc'], [])

    for ti in range(NTT):
        load_x(ti)
        mix_tile(ti)
    print("sbuf left", nc.sbuf_bytes_remaining() if callable(nc.sbuf_bytes_remaining) else nc.sbuf_bytes_remaining)
    print("total ops", getattr(S, 'n', 0))
    barrier()
    ph1.close()
    stk['cur'] = glob_stack
    if OPTS['peer']:
        peer_phase()
    barrier()
    S.finish()
    return nc


_CACHE = {}
NTA_FULL, NTB_FULL = 17, 17


def _consts(nta, ntb, hf):
    ar = np.arange(128)
    ident = np.eye(128, dtype=np.float32)
    tri = (ar[:, None] <= ar[None, :]).astype(np.float32)
    ones = np.ones((128, 128), np.float32)
    su = (ar[:, None] < ar[None, :]).astype(np.float32)
    lo = (ar[:, None] > ar[None, :]).astype(np.float32)
    cst = np.stack([ident, tri, ones, su, tri, lo], axis=1).astype(np.float32)
    q = ar[:, None]
    c = np.arange(256)[None, :]
    ok = (c > q) & (c <= q + 128)
    m_std = np.where(ok, 0.0, -30000.0).astype(np.float32)
    m_t0 = np.where(ok & (c >= 240), 0.0, -30000.0).astype(np.float32)
    m_t1 = np.where(ok & (c >= 112), 0.0, -30000.0).astype(np.float32)
    first = (hf == 0) or (nta == 0)
    cmask = np.stack([m_t0 if first else m_std, m_t1 if first else m_std, m_std], axis=1)
    inv = (np.float32(500000.0) ** (-np.arange(0, 16, 2, dtype=np.float32) / np.float32(16))).astype(np.float32)
    ntp = nta + ntb
    rope = np.zeros((128, ntp + 1, 16), np.float32)
    for i in range(ntp + 1):
        if i < ntp:
            st = i if (hf == 1 or nta == 0) else (i - nta if i >= nta else i)
            pos = st * 128 - 112 + ar
        else:
            pos = PAST + ar
        ang = pos.astype(np.float32)[:, None] * inv[None, :]
        rope[:, i, 0:8] = np.cos(ang)
        rope[:, i, 8:16] = np.sin(ang)
    vmask = np.zeros((128, 2), np.float32)
    vmask[:, 0] = 1.0
    vmask[0:4, 1] = 1.0
    iota = np.tile(np.arange(256, dtype=np.float32)[None, :], (128, 1))
    return dict(cst=cst, cmask=cmask, rope=rope, vmask=vmask, iota=iota)


def kernel(x_prompt, x_sample, cache_k_win, cache_v_win, state_wkv, state_shift, meta_tokens, norm1_g, w_in, mu_shift,
           w0, w_lora_w2, a0, w_lora_a2, w_lora_g2, k_k, k_a, r_k, lnx_w, lnx_b, attn_sinks, w_out, norm2_g, w_query,
           sub_keys, expert_u, expert_v, final_norm_g, _nta=NTA_FULL, _ntb=NTB_FULL, _nts=NSEQ_S, _dbg=False):
    f = lambda a: np.ascontiguousarray(np.asarray(a), dtype=np.float32)
    key = (_nta, _ntb, _nts)
    if key not in _CACHE:
        _CACHE[key] = build(_nta, _ntb, _nts, dbg=_dbg)
    nc = _CACHE[key]
    x_prompt, x_sample = f(x_prompt), f(x_sample)
    B = x_prompt.shape[0]
    nseqt = _nta + _ntb - 1
    shared = dict(
        w_in=f(w_in)[0], w_out=f(w_out)[0], w_q=f(w_query)[0], subk=f(sub_keys)[0],
        eu=f(expert_u)[0][:OPTS['nexp']], ev=f(expert_v)[0][:OPTS['nexp']],
        lw2=f(w_lora_w2)[0], la2=f(w_lora_a2)[0], lg2=f(w_lora_g2)[0],
        vec512=np.stack([f(w0)[0], f(a0)[0], f(k_k)[0], f(k_a)[0], f(r_k)[0].reshape(512), f(lnx_w)[0], f(lnx_b)[0]]),
        mu=f(mu_shift), g1=f(norm1_g), g2=f(norm2_g), gF=f(final_norm_g)[None, :], sinks=f(attn_sinks))
    cs = [_consts(_nta, _ntb, hf) for hf in range(2)]
    in_maps = []
    for c in range(NCORES):
        b, hf = c // 2, c % 2
        seq = np.zeros((nseqt * 128, D), np.float32)
        seq[112:128] = f(meta_tokens)
        seq[128:] = x_prompt[b][:(nseqt - 1) * 128]
        xp = np.zeros(((_nta + _ntb) * 128, D), np.float32)
        if hf == 0:
            xp[_nta * 128:(_nta + _ntb) * 128] = seq[:_ntb * 128]
        else:
            xp[:nseqt * 128] = seq
        sl = slice(c * NSEQ_S, (c + 1) * NSEQ_S)
        m = dict(shared)
        m.update(cs[hf])
        m.update(xp=xp, xs=x_sample[sl].reshape(NSEQ_S * 4, D),
                 ck=f(cache_k_win)[0, sl].reshape(NSEQ_S, 128, 128), cv=f(cache_v_win)[0, sl].reshape(NSEQ_S, 128, 128),
                 swkv=f(state_wkv)[0, sl], sshift=f(state_shift)[0, sl])
        in_maps.append(m)
    res = run_bass_kernel_spmd(nc, in_maps, core_ids=list(range(NCORES))).results
    if (_nta, _ntb, _nts) != (NTA_FULL, NTB_FULL, NSEQ_S):
        return res
    y_prompt = np.stack([np.concatenate([res[2 * b]["y_p"][128:_ntb * 128], res[2 * b + 1]["y_p"][:(_ntb - 1) * 128]]) for b in range(B)])
    y_sample = np.concatenate([res[c]["y_s"].reshape(NSEQ_S, 4, D) for c in range(NCORES)])
    od = lambda b: res[2 * b + 1]
    kwp = np.stack([od(b)["kw_p"].reshape(128, 2, 64) for b in range(B)])[None]
    vwp = np.stack([od(b)["vw_p"].reshape(128, 2, 64) for b in range(B)])[None]
    wkvp = np.stack([od(b)["wkv_p"] for b in range(B)])[None]
    shp = np.stack([od(b)["sh_p"][0] for b in range(B)])[None]
    kws = np.concatenate([res[c]["kw_s"].reshape(NSEQ_S, 128, 2, 64) for c in range(NCORES)])[None]
    vws = np.concatenate([res[c]["vw_s"].reshape(NSEQ_S, 128, 2, 64) for c in range(NCORES)])[None]
    wkvs = np.concatenate([res[c]["wkv_s"] for c in range(NCORES)])[None]
    shs = np.concatenate([res[c]["sh_s"] for c in range(NCORES)])[None]
    return (y_prompt, y_sample, kwp, vwp, wkvp, shp, kws, vws, wkvs, shs)
```

```python
import numpy as np
from contextlib import ExitStack
import concourse.bass as bass
import concourse.mybir as mybir
from concourse.alu_op_type import AluOpType as ALU
from concourse.bass_utils import run_bass_kernel_spmd

F32 = mybir.dt.float32
BF16 = mybir.dt.bfloat16
U32 = mybir.dt.uint32
AF = mybir.ActivationFunctionType
AX = mybir.AxisListType

D = 1024
NRW = 1792
NCOL = 2560
NCORES = 8
NSEQ_S = 16
PAST = 8192
NPT = 33
NEXP = 16384
OPTS = {'ng': 12, 'half': 0, 'limit': 10**9, 'dump_only': '', 'dumps': False, 'nexp': 16384, 'dbg_tile': 1, 'stage': 99, 'peer': True, 'win_copy': True, 'samp_state': True}


class Sched:
    def __init__(self, nc):
        self.nc = nc
        self.eng = {'pe': nc.tensor, 'act': nc.scalar, 'dve': nc.vector, 'pool': nc.gpsimd, 'sp': nc.sync}
        self.sem = {e: nc.alloc_semaphore("sem_" + e) for e in ['pe', 'act', 'dve', 'pool']}
        self.cnt = {e: 0 for e in self.sem}
        self.seen = {e: {} for e in self.eng}
        self.last_w = {}
        self.readers = {}
        self.dslots = {}
        for q in ['sp', 'pool', 'act']:
            self.dslots[q] = [[nc.alloc_semaphore("dq_%s_%d" % (q, i)), 0] for i in range(OPTS['ng'] if q == 'pool' else 8)]
        self.dnext = {q: 0 for q in self.dslots}
        self.tokens = {}

    def _wait(self, e, toks):
        need = {}
        for t in toks:
            if t is None:
                continue
            sid, val, we = t
            if we == e and e == 'pe':
                continue
            if self.seen[e].get(sid, 0) >= val:
                continue
            if need.get(sid, (None, 0))[1] < val:
                need[sid] = (t, val)
        for sid, (t, val) in need.items():
            self.eng[e].wait_ge(self.tokens[sid], val)
            self.seen[e][sid] = val

    def _deps(self, e, reads, writes):
        toks = []
        for k in reads:
            toks.append(self.last_w.get(k))
        for k in writes:
            toks.append(self.last_w.get(k))
            for t in self.readers.get(k, []):
                toks.append(t[:3])
        return toks

    def _mark(self, tok, reads, writes, is_dma):
        for k in reads:
            self.readers.setdefault(k, []).append(tok + (is_dma,))
        for k in writes:
            self.last_w[k] = tok
            self.readers[k] = []

    def op(self, e, reads, writes, fn):
        self.n = getattr(self, 'n', 0) + 1
        if self.n > OPTS['limit']:
            return None
        self._wait(e, self._deps(e, reads, writes))
        inst = fn(self.eng[e])
        self.cnt[e] += 1
        sem = self.sem[e]
        inst.then_inc(sem, 1)
        sid = id(sem)
        self.tokens[sid] = sem
        self._mark((sid, self.cnt[e], e), reads, writes, False)
        return inst

    def dma(self, q, out, in_, reads, writes, fn=None):
        self.n = getattr(self, 'n', 0) + 1
        if self.n > OPTS['limit']:
            return None
        slots = self.dslots[q]
        i = self.dnext[q]
        self.dnext[q] = (i + 1) % len(slots)
        sem, val = slots[i]
        sid = id(sem)
        self.tokens[sid] = sem
        toks = self._deps(q, reads, writes)
        if val > 0:
            toks.append((sid, val, 'dma'))
        self._wait(q, toks)
        if fn is None:
            inst = self.eng[q].dma_start(out=out, in_=in_)
        else:
            inst = fn(self.eng[q])
        inst.then_inc(sem, 16)
        slots[i][1] = val + 16
        self._mark((sid, val + 16, 'dma'), reads, writes, True)

    def finish(self):
        for q, slots in self.dslots.items():
            toks = [(id(s), v, 'dma') for s, v in slots if v > 0]
            self._wait(q, toks)


def build(NT_A, NT_B, NT_S, dbg=False, dbg_tile=1):
    nc = bass.Bass("TRN2", target_bir_lowering=False)
    S = Sched(nc)
    NT_P = NT_A + NT_B
    NTT = NT_P + NT_S
    NPE = NT_B + (1 if NT_S else 0)
    OUT_T = NT_P - 2 if NT_A > 0 else NT_P - 1

    def din(name, shape, dt=F32):
        return nc.dram_tensor(name, list(shape), dt, kind="ExternalInput").ap()

    def dout(name, shape, dt=F32):
        return nc.dram_tensor(name, list(shape), dt, kind="ExternalOutput").ap()

    xp = din("xp", [max(NT_P, 1) * 128, D])
    xs = din("xs", [NSEQ_S * 4, D])
    ck = din("ck", [NSEQ_S, 128, 128])
    cv = din("cv", [NSEQ_S, 128, 128])
    swkv = din("swkv", [NSEQ_S, 8, 64, 64])
    sshift = din("sshift", [NSEQ_S, D])
    w_in = din("w_in", [D, NCOL])
    w_out = din("w_out", [D, D])
    w_q = din("w_q", [D, 2048])
    subk = din("subk", [2, 128, 128])
    eu = din("eu", [OPTS['nexp'], D])
    ev = din("ev", [OPTS['nexp'], D])
    lw2 = din("lw2", [64, 512])
    la2 = din("la2", [64, 512])
    lg2 = din("lg2", [128, 512])
    vec512 = din("vec512", [7, 512])
    mu = din("mu", [1, NRW])
    g1 = din("g1", [1, D])
    g2 = din("g2", [1, D])
    gF = din("gF", [1, D])
    sinks = din("sinks", [1, 8])
    rope = din("rope", [128, NT_P + 1, 16])
    cmask = din("cmask", [128, 3, 256])
    cst = din("cst", [128, 6, 128])
    vmask = din("vmask", [128, 2])
    iota_in = din("iota", [128, 256])

    y_p = dout("y_p", [max(NT_B, 1) * 128, D])
    y_s = dout("y_s", [NSEQ_S * 4, D])
    kw_p = dout("kw_p", [128, 128])
    vw_p = dout("vw_p", [128, 128])
    wkv_p = dout("wkv_p", [8, 64, 64])
    sh_p = dout("sh_p", [1, D])
    kw_s = dout("kw_s", [NSEQ_S, 128, 128])
    vw_s = dout("vw_s", [NSEQ_S, 128, 128])
    wkv_s = dout("wkv_s", [NSEQ_S, 8, 64, 64])
    sh_s = dout("sh_s", [NSEQ_S, D])
    xmid = nc.dram_tensor("xmid", [max(NPE, 1) * 128, D], F32, kind="Internal").ap()
    dbg_t = {}
    if dbg:
        for nm, shp in [("d_m", [128, NRW]), ("d_y", [128, 512]), ("d_at", [128, 512]), ("d_S", [64, 512]),
                        ("d_pre", [128, 128]), ("d_idx", [128, 128]), ("d_gate", [128, 128])]:
            dbg_t[nm] = dout(nm, shp)

    stk = {'cur': ExitStack()}
    glob_stack = stk['cur']

    dumped = {}

    def DUMP(name, t, key, ti=None):
        if not OPTS['dumps'] or (ti is not None and ti != OPTS['dbg_tile']) or name in dumped:
            return
        if OPTS['dump_only'] and name not in str(OPTS['dump_only']).split(','):
            return
        src = t if isinstance(t, bass.AP) else t[:]
        o = nc.dram_tensor("z_" + name, list(src.shape), src.dtype, kind="ExternalOutput").ap()
        dumped[name] = 1
        S.dma('sp', o, src, [key], [])

    def sb(name, shape, dt=F32):
        return stk['cur'].enter_context(nc.sbuf_tensor(name, list(shape), dt))

    def ps(name, shape, dt=F32):
        return nc.alloc_psum_tensor(name, list(shape), dt)

    c_cst = sb("c_cst", [128, 6, 128])
    S.dma('sp', c_cst[:], cst, [], ['c_cst'])
    c_cstb = sb("c_cstb", [128, 6, 128], BF16)
    S.op('dve', ['c_cst'], ['c_cstb'], lambda e: e.tensor_copy(out=c_cstb[:], in_=c_cst[:]))
    ident_f = c_cst[:, 0, :]
    tri_f = c_cst[:, 1, :]
    ones_f = c_cst[:, 2, :]
    ident_b = c_cstb[:, 0, :]
    c_m4 = sb("c_m4", [128, 4, 128])
    for i, j in enumerate([3, 4, 3, 4]):
        S.op('dve', ['c_cst'], ['c_m4'], lambda e, i=i, j=j: e.tensor_copy(out=c_m4[:, i, :], in_=c_cst[:, j, :]))
    c_low = c_cst[:, 5, :]
    c_amask = sb("c_amask", [128, 3, 256])
    S.dma('sp', c_amask[:], cmask, [], ['c_amask'])
    c_rope = sb("c_rope", [128, NT_P + 1, 16])
    S.dma('sp', c_rope[:], rope, [], ['c_rope'])
    c_vm = sb("c_vm", [128, 2])
    S.dma('sp', c_vm[:], vmask, [], ['c_vm'])
    c_v512 = sb("c_v512", [128, 7, 512])
    S.dma('sp', c_v512[:], vec512.partition_broadcast(128), [], ['c_v512'])
    W0, A0, KK, KA, RK, LW, LB = range(7)
    c_g1bc = sb("c_g1bc", [128, D])
    S.dma('sp', c_g1bc[:], g1.partition_broadcast(128)[:, 0, :], [], ['c_g1bc'])
    c_sink = sb("c_sink", [128, 8])
    S.dma('sp', c_sink[:], sinks.partition_broadcast(128)[:, 0, :], [], ['c_sink'])
    c_g1col = sb("c_g1col", [128, 8])
    with nc.allow_non_contiguous_dma(reason="tiny param column load"):
        S.dma('sp', c_g1col[:], g1.rearrange("o (kc p) -> p (o kc)", p=128), [], ['c_g1col'])

    def rsq(key, out, in_, c, op0):
        S.op('dve', [key], [key], lambda e: e.tensor_scalar(out=out, in0=in_, scalar1=c, scalar2=None, op0=op0))
        S.op('act', [key], [key], lambda e: e.activation(out=out, in_=out, func=AF.Sqrt))
        S.op('dve', [key], [key], lambda e: e.reciprocal(out=out, in_=out))

    P = [ps("P%d" % i, [128, 512]) for i in range(8)]
    PK = ["P%d" % i for i in range(8)]

    def barrier():
        toks = []
        for e, sem in S.sem.items():
            if S.cnt[e] > 0:
                S.tokens[id(sem)] = sem
                toks.append((id(sem), S.cnt[e], 'x'))
        for q, slots in S.dslots.items():
            for s, v in slots:
                if v > 0:
                    S.tokens[id(s)] = s
                    toks.append((id(s), v, 'dma'))
        for e in ['pe', 'act', 'dve', 'pool', 'sp']:
            S._wait(e, toks)

    ph1 = ExitStack()
    stk['cur'] = ph1
    W1 = sb("W1", [128, 8, NRW], BF16)
    W2 = sb("W2", [128, 8, NRW], BF16)
    Wat = sb("Wat", [128, 8, 768], BF16)
    Wo = sb("Wo", [128, 8, D], BF16)
    L_w2 = sb("L_w2", [128, 512], BF16)
    L_g2 = sb("L_g2", [128, 512], BF16)
    with nc.sbuf_tensor("stg", [128, NCOL], F32) as stg, nc.sbuf_tensor("mub", [128, NRW], F32) as mub, \
            nc.sbuf_tensor("omu", [128, NRW], F32) as omu:
        S.dma('sp', mub[:], mu.partition_broadcast(128)[:, 0, :], [], ['mub'])
        S.op('dve', ['mub'], ['omu'], lambda e: e.tensor_scalar(out=omu[:], in0=mub[:], scalar1=-1.0, scalar2=1.0,
                                                                op0=ALU.mult, op1=ALU.add))
        for kc in range(8):
            S.dma('sp', stg[:], w_in[kc * 128:(kc + 1) * 128, :], [], ['stg'])
            S.op('dve', ['stg', 'omu'], ['W1'], lambda e, kc=kc: e.tensor_tensor(out=W1[:, kc, :], in0=stg[:, 0:NRW], in1=omu[:], op=ALU.mult))
            S.op('pool', ['stg', 'mub'], ['W2'], lambda e, kc=kc: e.tensor_tensor(out=W2[:, kc, :], in0=stg[:, 0:NRW], in1=mub[:], op=ALU.mult))
            S.op('act', ['stg'], ['Wat'], lambda e, kc=kc: e.copy(out=Wat[:, kc, :], in_=stg[:, NRW:NCOL]))
        for kc in range(8):
            S.dma('sp', stg[:, 0:D], w_out[kc * 128:(kc + 1) * 128, :], [], ['stg'])
            S.op('act', ['stg'], ['Wo'], lambda e, kc=kc: e.copy(out=Wo[:, kc, :], in_=stg[:, 0:D]))
        S.dma('sp', stg[0:64, 0:512], lw2, [], ['stg'])
        S.dma('sp', stg[64:128, 0:512], la2, [], ['stg'])
        S.dma('sp', stg[:, 512:1024], lg2, [], ['stg'])
        S.op('act', ['stg'], ['L_w2'], lambda e: e.copy(out=L_w2[:], in_=stg[:, 0:512]))
        S.op('act', ['stg'], ['L_g2'], lambda e: e.copy(out=L_g2[:], in_=stg[:, 512:1024]))
        barrier()

    xt0 = sb("xt0", [128, D])
    xt = [xt0, xt0]
    xts = xt0
    xn = sb("xn", [128, D], BF16)
    ssq = sb("ssq", [128, 4])
    hT = sb("hT", [128, 8, 128], BF16)
    hTs = sb("hTs", [128, 8, 128], BF16)
    S.op('pool', [], ['hT'], lambda e: e.memset(hT[:], 0.0))
    S.op('pool', [], ['hTs'], lambda e: e.memset(hTs[:], 0.0))
    A = {}
    for nm in ["r", "k", "v", "ld", "asig", "g", "kk", "b", "kmod", "c", "t1", "t2", "t3"]:
        A[nm] = sb("a_" + nm, [128, 512])
    Bt = {}
    for nm in ["rt", "at", "bt", "kt", "bbar", "kbar", "vb", "W0T", "UT"]:
        Bt[nm] = sb("b_" + nm, [128, 512], BF16)
    lin = sb("lin", [128, 2, 128], BF16)
    st8 = sb("st8", [128, 8, 4])
    AR_fm = sb("AR_fm", [64, 8, 256], BF16)
    B_fm = sb("B_fm", [64, 8, 128], BF16)
    K_fm = sb("K_fm", [64, 8, 128], BF16)
    MATS = sb("MATS", [128, 8, 512], BF16)
    _m = sb("Mb", [128, 8, 128], BF16)
    _mt = sb("MTb", [128, 8, 128], BF16)
    _t = sb("Tb", [128, 8, 128], BF16)
    Mb, MTb, Tb = [_m, _m], [_mt, _mt], [_t, _t]
    S32 = sb("S32", [64, 8, 64])
    Sb = sb("Sb", [64, 8, 64], BF16)
    ecl_fm = sb("ecl_fm", [64, 8])
    sti = sb("sti", [64, 8, 64])
    qkv = sb("qkv", [128, 768])
    rtmp = sb("rtmp", [128, 10, 8])
    qb = sb("qb", [128, 768], BF16)
    QT = sb("QT", [64, 8, 128], BF16)
    KT = [sb("KT%d" % i, [64, 2, 128], BF16) for i in range(2)]
    Vb = [sb("Vb%d" % i, [128, 128], BF16) for i in range(2)]
    sm = sb("sm", [128, 256])
    for i in range(2):
        S.op('pool', [], ['KT%d' % i], lambda e, i=i: e.memset(KT[i][:], 0.0))
        S.op('pool', [], ['Vb%d' % i], lambda e, i=i: e.memset(Vb[i][:], 0.0))
    eb = sb("eb", [128, 256], BF16)
    eT = sb("eT", [128, 2, 128], BF16)
    ast = sb("ast", [128, 8])
    ast8 = sb("ast8", [128, 4, 8])
    ycat = sb("ycat", [128, D], BF16)
    ycatT = sb("ycatT", [128, 8, 128], BF16)
    junk = ycatT[:].rearrange("p k t -> p (k t)")
    xm = sb("xm", [128, D])
    hrow = xm
    ckb = sm

    def v3(ap, h=8):
        return ap.rearrange("p (h j) -> p h j", h=h)

    def bc_last(ap, n):
        return ap.broadcast_to([ap.shape[0], ap.shape[1], n])

    def V512(i):
        return c_v512[:, i, :]

    def load_x(ti):
        if ti < NT_P:
            b = xt[ti % 2]
            S.dma('sp', b[:], xp[ti * 128:(ti + 1) * 128, :], [], ['xt0'])
        else:
            s = ti - NT_P
            if s == 0:
                S.op('pool', [], ['xt0'], lambda e: e.memset(xt0[:], 0.0))
                S.dma('sp', xmid[NT_B * 128:(NT_B + 1) * 128, :], xt0[:], ['xt0'], ['xmid'])
            S.dma('sp', xts[0:4, :], xs[s * 4:(s + 1) * 4, :], [], ['xt0'])

    def mix_tile(ti):
        samp = ti >= NT_P
        sq = ti - NT_P
        xk = 'xt0'
        xb = xts if samp else xt[ti % 2]
        par = ti % 2
        vm = c_vm[:, 1:2] if samp else c_vm[:, 0:1]
        if OPTS['stage'] <= 0:
            return
        S.op('act', [xk], ['ycatT', 'ssq'], lambda e: e.activation(out=junk[:], in_=xb[:], func=AF.Square, accum_out=ssq[:, 0:1]))
        rsq('ssq', ssq[:, 1:2], ssq[:, 0:1], D * 1e-5, ALU.add)
        S.op('dve', [xk, 'ssq'], ['xn'], lambda e: e.tensor_scalar(out=xn[:], in0=xb[:], scalar1=ssq[:, 1:2], scalar2=32.0,
                                                                   op0=ALU.mult, op1=ALU.mult))
        if samp:
            S.dma('sp', hrow[0:1, :], sshift[sq:sq + 1, :], [], ['xm'])
            S.op('act', ['xm'], ['ycatT'], lambda e: e.copy(out=junk[0:1, :], in_=hrow[0:1, :]))
        pT = P[0][:].bitcast(BF16)
        if samp:
            for kc in range(8):
                S.op('pe', ['ycatT', 'c_cstb'], ['P1'], lambda e, kc=kc: e.transpose(out=P[1][:].bitcast(BF16)[:, 2 * kc:2 * kc + 1], in_=junk[0:1, kc * 128:(kc + 1) * 128], identity=ident_b[0:1, 0:1]))
            S.op('dve', ['P1'], ['hTs'], lambda e: e.tensor_copy(out=hTs[:, :, 0], in_=P[1][:].bitcast(BF16)[:, 0:16].rearrange("p (k two) -> p k two", two=2)[:, :, 0]))
        else:
            S.op('dve', ['hT'], ['hTs'], lambda e: e.tensor_copy(out=hTs[:, :, 0], in_=hT[:, :, 127]))
        for kc in range(8):
            S.op('pe', ['xn', 'c_cstb'], ['P0'], lambda e, kc=kc: e.transpose(out=pT[:, kc * 128:(kc + 1) * 128], in_=xn[:, kc * 128:(kc + 1) * 128], identity=ident_b))
        S.op('dve', ['P0', 'c_g1col'], ['hT'], lambda e: e.tensor_tensor(
            out=hT[:], in0=pT.rearrange("p (k t) -> p k t", k=8),
            in1=c_g1col[:].unsqueeze(2).broadcast_to([128, 8, 128]), op=ALU.mult))
        S.op('pool', ['hT'], ['hTs'], lambda e: e.tensor_copy(out=hTs[:, :, 1:128], in_=hT[:, :, 0:127]))
        if samp or ti == OUT_T:
            S.op('pool', ['xn', 'c_g1bc'], ['xm'], lambda e: e.tensor_tensor(out=hrow[:], in0=xn[:], in1=c_g1bc[:], op=ALU.mult))
            if samp:
                S.dma('sp', sh_s[sq:sq + 1, :], hrow[3:4, :], ['xm'], [])
            else:
                S.dma('sp', sh_p[0:1, :], hrow[127:128, :], ['xm'], [])
        DUMP("xn", xn, 'xn', ti); DUMP("hT", hT, 'hT', ti); DUMP("hTs", hTs, 'hTs', ti)
        if OPTS['stage'] <= 1:
            return
        def proj(bank, c0, n, dst_reads=()):
            for kc in range(8):
                S.op('pe', ['hT', 'W1'], [PK[bank]], lambda e, kc=kc: e.matmul(out=P[bank][:, 0:n], lhsT=hT[:, kc, :], rhs=W1[:, kc, c0:c0 + n], start=(kc == 0), stop=False))
            for kc in range(8):
                S.op('pe', ['hTs', 'W2'], [PK[bank]], lambda e, kc=kc: e.matmul(out=P[bank][:, 0:n], lhsT=hTs[:, kc, :], rhs=W2[:, kc, c0:c0 + n], start=False, stop=(kc == 7)))
        proj(1, 0, 512)
        S.op('act', ['P1'], ['a_r'], lambda e: e.copy(out=A["r"][:], in_=P[1][:]))
        proj(2, 512, 512)
        S.op('act', ['P2'], ['a_k'], lambda e: e.copy(out=A["k"][:], in_=P[2][:]))
        proj(3, 1024, 512)
        S.op('act', ['P3'], ['a_v'], lambda e: e.copy(out=A["v"][:], in_=P[3][:]))
        S.op('dve', ['a_v', 'c_vm'], ['b_vb'], lambda e: e.tensor_scalar(out=Bt["vb"][:], in0=A["v"][:], scalar1=vm, scalar2=None, op0=ALU.mult))
        for j, c0 in enumerate([1536, 1664]):
            for kc in range(8):
                S.op('pe', ['hT', 'W1'], ['P4'], lambda e, kc=kc, j=j, c0=c0: e.matmul(out=P[4][:, j * 128:(j + 1) * 128], lhsT=W1[:, kc, c0:c0 + 128], rhs=hT[:, kc, :], start=(kc == 0), stop=False))
            for kc in range(8):
                S.op('pe', ['hTs', 'W2'], ['P4'], lambda e, kc=kc, j=j, c0=c0: e.matmul(out=P[4][:, j * 128:(j + 1) * 128], lhsT=W2[:, kc, c0:c0 + 128], rhs=hTs[:, kc, :], start=False, stop=(kc == 7)))
        S.op('act', ['P4'], ['lin'], lambda e: e.activation(out=lin[0:64, 0, :], in_=P[4][0:64, 0:128], func=AF.Tanh))
        S.op('act', ['P4'], ['lin'], lambda e: e.copy(out=lin[64:128, 0, :], in_=P[4][64:128, 0:128]))
        S.op('act', ['P4'], ['lin'], lambda e: e.activation(out=lin[:, 1, :], in_=P[4][:, 128:256], func=AF.Sigmoid))
        for kc in range(8):
            S.op('pe', ['hT', 'Wat'], ['P5'], lambda e, kc=kc: e.matmul(out=P[5][:], lhsT=hT[:, kc, :], rhs=Wat[:, kc, 0:512], start=(kc == 0), stop=(kc == 7)))
        for kc in range(8):
            S.op('pe', ['hT', 'Wat'], ['P6'], lambda e, kc=kc: e.matmul(out=P[6][:, 0:256], lhsT=hT[:, kc, :], rhs=Wat[:, kc, 512:768], start=(kc == 0), stop=(kc == 7)))
        S.op('act', ['P5'], ['qkv'], lambda e: e.copy(out=qkv[:, 0:512], in_=P[5][:]))
        S.op('act', ['P6'], ['qkv'], lambda e: e.copy(out=qkv[:, 512:768], in_=P[6][:, 0:256]))
        DUMP("a_r", A["r"], 'a_r', ti); DUMP("a_v", A["v"], 'a_v', ti); DUMP("lin", lin, 'lin', ti); DUMP("qkv0", qkv, 'qkv', ti)
        if OPTS['stage'] <= 2:
            return
        S.op('pe', ['lin', 'L_w2'], ['P1'], lambda e: e.matmul(out=P[1][:], lhsT=lin[0:64, 0, :], rhs=L_w2[0:64, :], start=True, stop=True))
        S.op('pe', ['lin', 'L_w2'], ['P2'], lambda e: e.matmul(out=P[2][:], lhsT=lin[64:128, 0, :], rhs=L_w2[64:128, :], start=True, stop=True))
        S.op('pe', ['lin', 'L_g2'], ['P3'], lambda e: e.matmul(out=P[3][:], lhsT=lin[:, 1, :], rhs=L_g2[:], start=True, stop=True))
        S.op('dve', ['P1', 'c_v512'], ['a_t1'], lambda e: e.tensor_tensor(out=A["t1"][:], in0=P[1][:], in1=V512(W0), op=ALU.add))
        S.op('act', ['a_t1'], ['a_t1'], lambda e: e.activation(out=A["t1"][:], in_=A["t1"][:], func=AF.Sigmoid))
        S.op('dve', ['a_t1', 'c_vm'], ['a_ld'], lambda e: e.tensor_scalar(out=A["ld"][:], in0=A["t1"][:], scalar1=vm, scalar2=-0.6065306597,
                                                                           op0=ALU.mult, op1=ALU.mult))
        S.op('dve', ['P2', 'c_v512'], ['a_t2'], lambda e: e.tensor_tensor(out=A["t2"][:], in0=P[2][:], in1=V512(A0), op=ALU.add))
        S.op('act', ['a_t2'], ['a_asig'], lambda e: e.activation(out=A["asig"][:], in_=A["t2"][:], func=AF.Sigmoid))
        S.op('act', ['P3'], ['a_g'], lambda e: e.copy(out=A["g"][:], in_=P[3][:]))
        S.op('pool', ['a_k', 'c_v512'], ['a_kk'], lambda e: e.tensor_tensor(out=A["kk"][:], in0=A["k"][:], in1=V512(KK), op=ALU.mult))
        S.op('pool', ['a_kk'], ['a_t3'], lambda e: e.tensor_tensor(out=A["t3"][:], in0=A["kk"][:], in1=A["kk"][:], op=ALU.mult))
        S.op('dve', ['a_t3'], ['st8'], lambda e: e.tensor_reduce(out=st8[:, :, 0], in_=v3(A["t3"][:]), axis=AX.X, op=ALU.add))
        rsq('st8', st8[:, :, 1], st8[:, :, 0], 1e-24, ALU.max)
        S.op('dve', ['a_kk', 'st8'], ['a_kk'], lambda e: e.tensor_tensor(out=v3(A["kk"][:]), in0=v3(A["kk"][:]), in1=bc_last(st8[:, :, 1:2], 64), op=ALU.mult))
        S.op('dve', ['a_kk', 'a_asig', 'c_vm'], ['a_b'], lambda e: e.scalar_tensor_tensor(out=A["b"][:], in0=A["kk"][:], scalar=vm, in1=A["asig"][:], op0=ALU.mult, op1=ALU.mult))
        S.op('dve', ['a_asig', 'c_v512'], ['a_t2'], lambda e: e.scalar_tensor_tensor(out=A["t2"][:], in0=A["asig"][:], scalar=-1.0, in1=V512(KA), op0=ALU.add, op1=ALU.mult))
        S.op('dve', ['a_t2', 'a_k'], ['a_kmod'], lambda e: e.scalar_tensor_tensor(out=A["kmod"][:], in0=A["t2"][:], scalar=1.0, in1=A["k"][:], op0=ALU.add, op1=ALU.mult))
        S.op('pe', ['a_ld', 'c_cst'], ['P1'], lambda e: e.matmul(out=P[1][:], lhsT=tri_f, rhs=A["ld"][:], start=True, stop=True))
        S.op('pe', ['a_ld', 'c_cst'], ['P2'], lambda e: e.matmul(out=P[2][:], lhsT=ones_f, rhs=A["ld"][:], start=True, stop=True))
        for h in range(8):
            S.op('pe', ['a_ld', 'c_cst'], ['P3'], lambda e, h=h: e.matmul(out=P[3][0:64, h:h + 1], lhsT=A["ld"][:, h * 64:(h + 1) * 64], rhs=ones_f[:, 0:1], start=True, stop=True))
        S.op('act', ['P3'], ['ecl_fm'], lambda e: e.activation(out=ecl_fm[:], in_=P[3][0:64, 0:8], func=AF.Exp))
        S.op('act', ['P1'], ['a_c'], lambda e: e.copy(out=A["c"][:], in_=P[1][:]))
        S.op('act', ['a_c'], ['a_t1'], lambda e: e.activation(out=A["t1"][:], in_=A["c"][:], func=AF.Exp))
        S.op('dve', ['a_t1', 'a_r'], ['b_rt'], lambda e: e.tensor_tensor(out=Bt["rt"][:], in0=A["r"][:], in1=A["t1"][:], op=ALU.mult))
        S.op('pool', ['a_c', 'a_ld'], ['a_t2'], lambda e: e.tensor_tensor(out=A["t2"][:], in0=A["c"][:], in1=A["ld"][:], op=ALU.subtract))
        S.op('act', ['a_t2'], ['a_t2'], lambda e: e.activation(out=A["t2"][:], in_=A["t2"][:], func=AF.Exp))
        S.op('dve', ['a_t2', 'a_kk'], ['b_at'], lambda e: e.scalar_tensor_tensor(out=Bt["at"][:], in0=A["kk"][:], scalar=-1.0, in1=A["t2"][:], op0=ALU.mult, op1=ALU.mult))
        S.op('act', ['a_c'], ['a_t3'], lambda e: e.activation(out=A["t3"][:], in_=A["c"][:], func=AF.Exp, scale=-1.0))
        S.op('dve', ['a_t3', 'a_b'], ['b_bt'], lambda e: e.tensor_tensor(out=Bt["bt"][:], in0=A["b"][:], in1=A["t3"][:], op=ALU.mult))
        S.op('pool', ['a_t3', 'a_kmod'], ['b_kt'], lambda e: e.tensor_tensor(out=Bt["kt"][:], in0=A["kmod"][:], in1=A["t3"][:], op=ALU.mult))
        S.op('dve', ['P2', 'a_c'], ['a_t1'], lambda e: e.tensor_tensor(out=A["t1"][:], in0=P[2][:], in1=A["c"][:], op=ALU.subtract))
        S.op('act', ['a_t1'], ['a_t1'], lambda e: e.activation(out=A["t1"][:], in_=A["t1"][:], func=AF.Exp))
        S.op('dve', ['a_t1', 'a_b'], ['b_bbar'], lambda e: e.tensor_tensor(out=Bt["bbar"][:], in0=A["b"][:], in1=A["t1"][:], op=ALU.mult))
        S.op('pool', ['a_t1', 'a_kmod'], ['b_kbar'], lambda e: e.tensor_tensor(out=Bt["kbar"][:], in0=A["kmod"][:], in1=A["t1"][:], op=ALU.mult))
        S.op('pool', ['a_r', 'c_v512'], ['a_t2'], lambda e: e.tensor_tensor(out=A["t2"][:], in0=A["r"][:], in1=V512(RK), op=ALU.mult))
        S.op('pool', ['a_t2', 'a_kmod'], ['a_t2'], lambda e: e.tensor_tensor(out=A["t2"][:], in0=A["t2"][:], in1=A["kmod"][:], op=ALU.mult))
        S.op('dve', ['a_t2'], ['st8'], lambda e: e.tensor_reduce(out=st8[:, :, 2], in_=v3(A["t2"][:]), axis=AX.X, op=ALU.add))
        DUMP("a_ld", A["ld"], 'a_ld', ti); DUMP("a_c", A["c"], 'a_c', ti); DUMP("a_kk", A["kk"], 'a_kk', ti); DUMP("b_rt", Bt["rt"], 'b_rt', ti); DUMP("b_at", Bt["at"], 'b_at', ti); DUMP("b_bt", Bt["bt"], 'b_bt', ti); DUMP("b_kbar", Bt["kbar"], 'b_kbar', ti); DUMP("ecl_fm", ecl_fm, 'ecl_fm', ti)
        if OPTS['stage'] <= 3:
            return
        pTb = [P[i][:].bitcast(BF16) for i in range(8)]
        for qi, (nm, bank) in enumerate([("at", 4), ("rt", 5), ("bt", 6), ("kt", 7)]):
            for h in range(8):
                S.op('pe', ['b_' + nm, 'c_cstb'], [PK[bank]], lambda e, h=h, nm=nm, bank=bank: e.transpose(
                    out=pTb[bank][0:64, h * 128:(h + 1) * 128], in_=Bt[nm][:, h * 64:(h + 1) * 64], identity=ident_b))
        S.op('act', ['P4'], ['AR_fm'], lambda e: e.copy(out=AR_fm[:, :, 0:128], in_=pTb[4][0:64, 0:1024].rearrange("p (h t) -> p h t", h=8)))
        S.op('dve', ['P5'], ['AR_fm'], lambda e: e.tensor_copy(out=AR_fm[:, :, 128:256], in_=pTb[5][0:64, 0:1024].rearrange("p (h t) -> p h t", h=8)))
        S.op('act', ['P6'], ['B_fm'], lambda e: e.copy(out=B_fm[:], in_=pTb[6][0:64, 0:1024].rearrange("p (h t) -> p h t", h=8)))
        S.op('dve', ['P7'], ['K_fm'], lambda e: e.tensor_copy(out=K_fm[:], in_=pTb[7][0:64, 0:1024].rearrange("p (h t) -> p h t", h=8)))
        for h in range(8):
            bank = h % 2
            S.op('pe', ['B_fm', 'AR_fm'], [PK[bank]], lambda e, h=h, bank=bank: e.matmul(out=P[bank][:, 0:256], lhsT=B_fm[:, h, :], rhs=AR_fm[:, h, :], start=True, stop=True))
            S.op('pe', ['K_fm', 'AR_fm'], [PK[bank]], lambda e, h=h, bank=bank: e.matmul(out=P[bank][:, 256:512], lhsT=K_fm[:, h, :], rhs=AR_fm[:, h, :], start=True, stop=True))
            S.op('dve', [PK[bank], 'c_m4'], ['MATS'], lambda e, h=h, bank=bank: e.tensor_tensor(out=MATS[:, h, :], in0=P[bank][:], in1=c_m4[:].rearrange("p a b -> p (a b)"), op=ALU.mult))
        for hh in range(2):
            bank = 2 + hh
            for h4 in range(4):
                h = hh * 4 + h4
                S.op('pe', ['B_fm', 'AR_fm'], [PK[bank]], lambda e, h=h, h4=h4, bank=bank: e.matmul(out=P[bank][:, h4 * 128:(h4 + 1) * 128], lhsT=AR_fm[:, h, 0:128], rhs=B_fm[:, h, :], start=True, stop=True))
            S.op('dve', [PK[bank], 'c_cst'], ['MTb%d' % hh], lambda e, hh=hh, bank=bank: e.tensor_tensor(
                out=MTb[0][:, hh * 4:(hh + 1) * 4, :], in0=P[bank][:].rearrange("p (h t) -> p h t", h=4),
                in1=c_low.unsqueeze(1).broadcast_to([128, 4, 128]), op=ALU.mult))
        S.op('act', ['MATS'], ['Mb0', 'Mb1'], lambda e: e.copy(out=Mb[0][:], in_=MATS[:, :, 0:128]))
        S.op('pool', ['MATS', 'c_cstb'], ['Tb0', 'Tb1'], lambda e: e.tensor_tensor(out=Tb[0][:], in0=MATS[:, :, 0:128], in1=ident_b.unsqueeze(1).broadcast_to([128, 8, 128]), op=ALU.add))
        cur = 0
        LAST = 1 if samp else 6
        for lvl in range(1, LAST + 1):
            nxt = 1 - cur
            for hh in range(2):
                hs = slice(hh * 4, hh * 4 + 4)
                bM, bMT, bT = 2 + hh * 3, 3 + hh * 3, 4 + hh * 3
                for h4 in range(4):
                    h = hh * 4 + h4
                    cs = slice(h4 * 128, (h4 + 1) * 128)
                    if lvl < LAST:
                        S.op('pe', ['Mb%d' % hh, 'MTb%d' % hh], [PK[bM]], lambda e, h=h, cs=cs, bM=bM, cur=cur: e.matmul(out=P[bM][:, cs], lhsT=MTb[cur][:, h, :], rhs=Mb[cur][:, h, :], start=True, stop=True))
                    S.op('pe', ['Mb%d' % hh, 'MTb%d' % hh], [PK[bMT]], lambda e, h=h, cs=cs, bMT=bMT, cur=cur: e.matmul(out=P[bMT][:, cs], lhsT=Mb[cur][:, h, :], rhs=MTb[cur][:, h, :], start=True, stop=True))
                if lvl < LAST:
                    S.op('act', [PK[bM]], ['Mb%d' % hh], lambda e, hs=hs, bM=bM, nxt=nxt: e.copy(out=Mb[nxt][:, hs, :], in_=P[bM][:].rearrange("p (h t) -> p h t", h=4)))
                S.op('dve', [PK[bMT]], ['MTb%d' % hh], lambda e, hs=hs, bMT=bMT, nxt=nxt: e.tensor_copy(out=MTb[nxt][:, hs, :], in_=P[bMT][:].rearrange("p (h t) -> p h t", h=4)))
                for h4 in range(4):
                    h = hh * 4 + h4
                    cs = slice(h4 * 128, (h4 + 1) * 128)
                    S.op('pe', ['MTb%d' % hh, 'Tb%d' % hh], [PK[bT]], lambda e, h=h, cs=cs, bT=bT, cur=cur, nxt=nxt: e.matmul(out=P[bT][:, cs], lhsT=MTb[nxt][:, h, :], rhs=Tb[cur][:, h, :], start=True, stop=True))
                S.op('dve', [PK[bT], 'Tb%d' % hh], ['Tb%d' % hh], lambda e, hs=hs, bT=bT, cur=cur, nxt=nxt: e.tensor_tensor(
                    out=Tb[nxt][:, hs, :], in0=P[bT][:].rearrange("p (h t) -> p h t", h=4), in1=Tb[cur][:, hs, :], op=ALU.add))
            cur = nxt
        Tf = Tb[cur]
        TfK = 'Tb'
        DUMP("AR_fm", AR_fm, 'AR_fm', ti); DUMP("K_fm", K_fm, 'K_fm', ti); DUMP("MATS", MATS, 'MATS', ti); DUMP("Tb", Tb[0], 'Tb0', ti)
        if OPTS['stage'] <= 4:
            return
        if samp or ti == 0:
            if samp:
                S.dma('sp', sti[:], swkv[sq].rearrange("h i j -> i h j"), [], ['sti'])
                for h in range(8):
                    S.op('pe', ['sti', 'c_cst'], ['P0'], lambda e, h=h: e.transpose(out=P[0][0:64, h * 64:(h + 1) * 64], in_=sti[:, h, :], identity=ident_f[0:64, 0:64]))
                S.op('dve', ['P0'], ['S32'], lambda e: e.tensor_copy(out=S32[:], in_=P[0][0:64, :].rearrange("p (h i) -> p h i", h=8)))
            else:
                S.op('dve', [], ['S32'], lambda e: e.memset(S32[:], 0.0))
            S.op('act', ['S32'], ['Sb'], lambda e: e.copy(out=Sb[:], in_=S32[:]))
        for h in range(8):
            cs = slice(h * 64, (h + 1) * 64)
            S.op('pe', ['AR_fm', 'Sb'], ['P0'], lambda e, h=h, cs=cs: e.matmul(out=P[0][:, cs], lhsT=AR_fm[:, h, 0:128], rhs=Sb[:, h, :], start=True, stop=False))
            S.op('pe', ['MATS', 'b_vb'], ['P0'], lambda e, h=h, cs=cs: e.matmul(out=P[0][:, cs], lhsT=MATS[:, h, 256:384], rhs=Bt["vb"][:, cs], start=False, stop=True))
        S.op('act', ['P0'], ['b_W0T'], lambda e: e.copy(out=Bt["W0T"][:], in_=P[0][:]))
        for h in range(8):
            cs = slice(h * 64, (h + 1) * 64)
            S.op('pe', ['Tb%d' % (h // 4), 'b_W0T'], ['P1'], lambda e, h=h, cs=cs: e.matmul(out=P[1][:, cs], lhsT=Tf[:, h, :], rhs=Bt["W0T"][:, cs], start=True, stop=True))
        S.op('act', ['P1'], ['b_UT'], lambda e: e.copy(out=Bt["UT"][:], in_=P[1][:]))
        for h in range(8):
            cs = slice(h * 64, (h + 1) * 64)
            S.op('pe', ['AR_fm', 'Sb'], ['P0'], lambda e, h=h, cs=cs: e.matmul(out=P[0][:, cs], lhsT=AR_fm[:, h, 128:256], rhs=Sb[:, h, :], start=True, stop=False))
            S.op('pe', ['MATS', 'b_UT'], ['P0'], lambda e, h=h, cs=cs: e.matmul(out=P[0][:, cs], lhsT=MATS[:, h, 128:256], rhs=Bt["UT"][:, cs], start=False, stop=False))
            S.op('pe', ['MATS', 'b_vb'], ['P0'], lambda e, h=h, cs=cs: e.matmul(out=P[0][:, cs], lhsT=MATS[:, h, 384:512], rhs=Bt["vb"][:, cs], start=False, stop=True))
        for h in range(8):
            cs = slice(h * 64, (h + 1) * 64)
            S.op('pe', ['b_bbar', 'b_UT'], ['P1'], lambda e, h=h, cs=cs: e.matmul(out=P[1][0:64, cs], lhsT=Bt["bbar"][:, cs], rhs=Bt["UT"][:, cs], start=True, stop=False))
            S.op('pe', ['b_kbar', 'b_vb'], ['P1'], lambda e, h=h, cs=cs: e.matmul(out=P[1][0:64, cs], lhsT=Bt["kbar"][:, cs], rhs=Bt["vb"][:, cs], start=False, stop=True))
        S.op('dve', ['S32', 'ecl_fm'], ['S32'], lambda e: e.tensor_tensor(out=S32[:], in0=S32[:], in1=ecl_fm[:].unsqueeze(2).broadcast_to([64, 8, 64]), op=ALU.mult))
        S.op('dve', ['S32', 'P1'], ['S32'], lambda e: e.tensor_tensor(out=S32[:], in0=S32[:], in1=P[1][0:64, :].rearrange("p (h i) -> p h i", h=8), op=ALU.add))
        S.op('act', ['S32'], ['Sb'], lambda e: e.copy(out=Sb[:], in_=S32[:]))
        if samp or ti == OUT_T:
            for h in range(8):
                S.op('pe', ['S32', 'c_cst'], ['P2'], lambda e, h=h: e.transpose(out=P[2][0:64, h * 64:(h + 1) * 64], in_=S32[:, h, :], identity=ident_f[0:64, 0:64]))
            S.op('act', ['P2'], ['sti'], lambda e: e.copy(out=sti[:], in_=P[2][0:64, :].rearrange("p (h j) -> p h j", h=8)))
            dst = wkv_s[sq] if samp else wkv_p
            S.dma('sp', dst.rearrange("h i j -> i h j"), sti[:], ['sti'], [])
        DUMP("S32", S32, 'S32', ti); DUMP("b_UT", Bt["UT"], 'b_UT', ti); DUMP("b_W0T", Bt["W0T"], 'b_W0T', ti)
        if OPTS['stage'] <= 5:
            return
        state_only = ti < NT_A
        if state_only and ti != NT_A - 1:
            return
        if not state_only:
            rwkv_post(ti)
        attn_and_out(ti, samp, sq, xk, xb, par, state_only)

    def rwkv_post(ti):
        if True:
            pass
        Y3 = v3(P[0][:])
        S.op('dve', ['P0'], ['st8'], lambda e: e.tensor_reduce(out=st8[:, :, 0], in_=Y3, axis=AX.X, op=ALU.add))
        S.op('dve', ['st8'], ['st8'], lambda e: e.tensor_scalar(out=st8[:, :, 0], in0=st8[:, :, 0], scalar1=1.0 / 64, scalar2=None, op0=ALU.mult))
        S.op('dve', ['P0', 'st8'], ['a_t1'], lambda e: e.tensor_tensor(out=v3(A["t1"][:]), in0=Y3, in1=bc_last(st8[:, :, 0:1], 64), op=ALU.subtract))
        S.op('pool', ['a_t1'], ['a_t2'], lambda e: e.tensor_tensor(out=A["t2"][:], in0=A["t1"][:], in1=A["t1"][:], op=ALU.mult))
        S.op('dve', ['a_t2'], ['st8'], lambda e: e.tensor_reduce(out=st8[:, :, 1], in_=v3(A["t2"][:]), axis=AX.X, op=ALU.add))
        S.op('dve', ['st8'], ['st8'], lambda e: e.tensor_scalar(out=st8[:, :, 1], in0=st8[:, :, 1], scalar1=1.0 / 64, scalar2=64e-5, op0=ALU.mult, op1=ALU.add))
        rsq('st8', st8[:, :, 1], st8[:, :, 1], 0.0, ALU.add)
        S.op('dve', ['a_t1', 'st8'], ['a_t1'], lambda e: e.tensor_tensor(out=v3(A["t1"][:]), in0=v3(A["t1"][:]), in1=bc_last(st8[:, :, 1:2], 64), op=ALU.mult))
        S.op('pool', ['a_t1', 'c_v512'], ['a_t1'], lambda e: e.tensor_tensor(out=A["t1"][:], in0=A["t1"][:], in1=V512(LW), op=ALU.mult))
        S.op('pool', ['a_t1', 'c_v512'], ['a_t1'], lambda e: e.tensor_tensor(out=A["t1"][:], in0=A["t1"][:], in1=V512(LB), op=ALU.add))
        S.op('dve', ['a_v', 'st8'], ['a_t2'], lambda e: e.tensor_tensor(out=v3(A["t2"][:]), in0=v3(A["v"][:]), in1=bc_last(st8[:, :, 2:3], 64), op=ALU.mult))
        S.op('pool', ['a_t1', 'a_t2'], ['a_t1'], lambda e: e.tensor_tensor(out=A["t1"][:], in0=A["t1"][:], in1=A["t2"][:], op=ALU.add))
        S.op('dve', ['a_t1', 'a_g'], ['ycat'], lambda e: e.tensor_tensor(out=ycat[:, 0:512], in0=A["t1"][:], in1=A["g"][:], op=ALU.mult))
        DUMP("ycat_rw", ycat[:, 0:512], 'ycat', ti)

    def attn_and_out(ti, samp, sq, xk, xb, par, state_only):
        pTb = [P[i][:].bitcast(BF16) for i in range(8)]
        if OPTS['stage'] <= 6:
            return
        ri = NT_P if samp else ti
        cosb = c_rope[:, ri, 0:8].unsqueeze(1)
        sinb = c_rope[:, ri, 8:16].unsqueeze(1)
        for (c0, nh) in [(0, 8), (512, 2)]:
            X = qkv[:, c0:c0 + nh * 64].rearrange("p (h j) -> p h j", h=nh)
            x1, x2 = X[:, :, 0:8], X[:, :, 8:16]
            cb = cosb.broadcast_to([128, nh, 8])
            sbb = sinb.broadcast_to([128, nh, 8])
            R = rtmp[:, 0:nh, :]
            T1 = rtmp[:, 0:nh, :]
            ra = rtmp[:].rearrange("p a b -> p (a b)")
            t_a = ra[:, 0:nh * 8].rearrange("p (h j) -> p h j", h=nh)
            t_b = ra[:, 80 - 0:80].rearrange("p (h j) -> p h j", h=1) if False else None
            S.op('dve', ['qkv', 'c_rope'], ['rtmp'], lambda e, x1=x1, cb=cb, t_a=t_a: e.tensor_tensor(out=t_a, in0=x1, in1=cb, op=ALU.mult))
            S.op('dve', ['qkv', 'c_rope'], ['sm'], lambda e, x2=x2, sbb=sbb, nh=nh: e.tensor_tensor(out=sm[:, 0:nh * 8].rearrange("p (h j) -> p h j", h=nh), in0=x2, in1=sbb, op=ALU.mult))
            S.op('dve', ['qkv', 'c_rope'], ['sm'], lambda e, x2=x2, cb=cb, nh=nh: e.tensor_tensor(out=sm[:, 64:64 + nh * 8].rearrange("p (h j) -> p h j", h=nh), in0=x2, in1=cb, op=ALU.mult))
            S.op('dve', ['qkv', 'c_rope'], ['sm'], lambda e, x1=x1, sbb=sbb, nh=nh: e.tensor_tensor(out=sm[:, 128:128 + nh * 8].rearrange("p (h j) -> p h j", h=nh), in0=x1, in1=sbb, op=ALU.mult))
            S.op('dve', ['rtmp', 'sm'], ['qkv'], lambda e, x1=x1, t_a=t_a, nh=nh: e.tensor_tensor(out=x1, in0=t_a, in1=sm[:, 0:nh * 8].rearrange("p (h j) -> p h j", h=nh), op=ALU.subtract))
            S.op('dve', ['sm'], ['qkv'], lambda e, x2=x2, nh=nh: e.tensor_tensor(out=x2, in0=sm[:, 64:64 + nh * 8].rearrange("p (h j) -> p h j", h=nh), in1=sm[:, 128:128 + nh * 8].rearrange("p (h j) -> p h j", h=nh), op=ALU.add))
        S.op('act', ['qkv'], ['qb'], lambda e: e.copy(out=qb[:], in_=qkv[:]))
        S.op('pool', ['qkv'], ['Vb%d' % par], lambda e: e.tensor_copy(out=Vb[par][:], in_=qkv[:, 640:768]))
        for h in range(8):
            S.op('pe', ['qb', 'c_cstb'], ['P2'], lambda e, h=h: e.transpose(out=pTb[2][0:64, h * 128:(h + 1) * 128], in_=qb[:, h * 64:(h + 1) * 64], identity=ident_b))
        for kv in range(2):
            S.op('pe', ['qb', 'c_cstb'], ['P3'], lambda e, kv=kv: e.transpose(out=pTb[3][0:64, kv * 128:(kv + 1) * 128], in_=qb[:, 512 + kv * 64:512 + (kv + 1) * 64], identity=ident_b))
        S.op('act', ['P2'], ['QT'], lambda e: e.copy(out=QT[:], in_=pTb[2][0:64, 0:1024].rearrange("p (h t) -> p h t", h=8)))
        S.op('dve', ['P3'], ['KT%d' % par], lambda e: e.tensor_copy(out=KT[par][:], in_=pTb[3][0:64, 0:256].rearrange("p (h t) -> p h t", h=2)))
        pp = 1 - par
        if samp:
            S.dma('sp', ckb[:, 0:128], ck[sq], [], ['sm'])
            S.dma('sp', ckb[:, 128:256], cv[sq], [], ['sm'])
            S.op('act', ['sm'], ['ycatT'], lambda e: e.copy(out=junk[:, 0:256], in_=ckb[:]))
            for kv in range(2):
                S.op('pe', ['ycatT', 'c_cstb'], ['P3'], lambda e, kv=kv: e.transpose(out=pTb[3][0:64, 256 + kv * 128:256 + (kv + 1) * 128], in_=junk[:, kv * 64:(kv + 1) * 64], identity=ident_b))
            S.op('dve', ['P3'], ['KT%d' % pp], lambda e: e.tensor_copy(out=KT[pp][:], in_=pTb[3][0:64, 256:512].rearrange("p (h t) -> p h t", h=2)))
            S.op('pool', ['ycatT'], ['Vb%d' % pp], lambda e: e.tensor_copy(out=Vb[pp][:], in_=junk[:, 128:256]))
            S.dma('sp', kw_s[sq, 0:124, :], ck[sq, 4:128, :], [], [])
            S.dma('sp', vw_s[sq, 0:124, :], cv[sq, 4:128, :], [], [])
            S.dma('sp', kw_s[sq, 124:128, :], qkv[0:4, 512:640], ['qkv'], [])
            S.dma('sp', vw_s[sq, 124:128, :], qkv[0:4, 640:768], ['qkv'], [])
        elif ti == OUT_T:
            S.dma('sp', kw_p, qkv[:, 512:640], ['qkv'], [])
            S.dma('sp', vw_p, qkv[:, 640:768], ['qkv'], [])
        if state_only:
            return
        mi = 2 if (samp or ti - NT_A > 1) else (ti - NT_A)
        SMB = [A["t1"], A["t2"], A["t3"], A["c"]]
        SMK = ['a_t1', 'a_t2', 'a_t3', 'a_c']
        EBB = [A["kk"][:].bitcast(BF16), A["b"][:].bitcast(BF16)]
        EBK = ['a_kk', 'a_b']
        ETB = [A["kmod"][:].bitcast(BF16), A["ld"][:].bitcast(BF16)]
        ETK = ['a_kmod', 'a_ld']
        for h in range(8):
            kv = h // 4
            bank = 4 + h // 2
            c0 = (h % 2) * 256
            S.op('pe', ['QT', 'KT%d' % pp], [PK[bank]], lambda e, h=h, kv=kv, bank=bank, c0=c0: e.matmul(out=P[bank][:, c0:c0 + 128], lhsT=QT[:, h, :], rhs=KT[pp][:, kv, :], start=True, stop=True))
            S.op('pe', ['QT', 'KT%d' % par], [PK[bank]], lambda e, h=h, kv=kv, bank=bank, c0=c0: e.matmul(out=P[bank][:, c0 + 128:c0 + 256], lhsT=QT[:, h, :], rhs=KT[par][:, kv, :], start=True, stop=True))
        for j in range(4):
            S.op('dve', [PK[4 + j], 'c_amask'], [SMK[j]], lambda e, j=j: e.scalar_tensor_tensor(
                out=SMB[j][:].rearrange("p (h c) -> p h c", h=2), in0=P[4 + j][:].rearrange("p (h c) -> p h c", h=2), scalar=0.125,
                in1=c_amask[:, mi, :].unsqueeze(1).broadcast_to([128, 2, 256]), op0=ALU.mult, op1=ALU.add))
            S.op('dve', [SMK[j]], ['ast'], lambda e, j=j: e.tensor_reduce(out=ast8[:, 0, 2 * j:2 * j + 2], in_=SMB[j][:].rearrange("p (h c) -> p h c", h=2), axis=AX.X, op=ALU.max))
        S.op('dve', ['ast', 'c_sink'], ['ast'], lambda e: e.tensor_tensor(out=ast8[:, 1, :], in0=ast8[:, 0, :], in1=c_sink[:], op=ALU.max))
        S.op('dve', ['ast'], ['ast'], lambda e: e.tensor_scalar(out=ast8[:, 1, :], in0=ast8[:, 1, :], scalar1=-1.0, scalar2=None, op0=ALU.mult))
        for h in range(8):
            S.op('act', [SMK[h // 2], 'ast'], [EBK[h // 4], 'ast2'], lambda e, h=h: e.activation(
                out=EBB[h // 4][:, (h % 4) * 256:(h % 4 + 1) * 256], in_=SMB[h // 2][:, (h % 2) * 256:(h % 2 + 1) * 256], func=AF.Exp,
                bias=ast8[:, 1, h:h + 1], scale=1.0, accum_out=ast8[:, 2, h:h + 1]))
        S.op('dve', ['ast', 'c_sink'], ['ast3'], lambda e: e.tensor_tensor(out=ast8[:, 3, :], in0=ast8[:, 1, :], in1=c_sink[:], op=ALU.add))
        S.op('act', ['ast3'], ['ast3'], lambda e: e.activation(out=ast8[:, 3, :], in_=ast8[:, 3, :], func=AF.Exp))
        S.op('dve', ['ast2', 'ast3'], ['ast3'], lambda e: e.tensor_tensor(out=ast8[:, 3, :], in0=ast8[:, 3, :], in1=ast8[:, 2, :], op=ALU.add))
        S.op('dve', ['ast3'], ['ast3'], lambda e: e.reciprocal(out=ast8[:, 3, :], in_=ast8[:, 3, :]))
        for h in range(8):
            for half in range(2):
                S.op('pe', [EBK[h // 4], 'c_cstb'], [PK[h // 4]], lambda e, h=h, half=half: e.transpose(
                    out=pTb[h // 4][:, (h % 4) * 256 + half * 128:(h % 4) * 256 + (half + 1) * 128],
                    in_=EBB[h // 4][:, (h % 4) * 256 + half * 128:(h % 4) * 256 + (half + 1) * 128], identity=ident_b))
        S.op('act', ['P0'], [ETK[0]], lambda e: e.copy(out=ETB[0], in_=pTb[0][:, 0:1024]))
        S.op('dve', ['P1'], [ETK[1]], lambda e: e.tensor_copy(out=ETB[1], in_=pTb[1][:, 0:1024]))
        for h in range(8):
            kv = h // 4
            o0 = (h % 4) * 256
            S.op('pe', [ETK[h // 4], 'Vb%d' % pp], ['P2'], lambda e, h=h, kv=kv, o0=o0: e.matmul(out=P[2][:, h * 64:(h + 1) * 64], lhsT=ETB[h // 4][:, o0:o0 + 128], rhs=Vb[pp][:, kv * 64:(kv + 1) * 64], start=True, stop=False))
            S.op('pe', [ETK[h // 4], 'Vb%d' % par], ['P2'], lambda e, h=h, kv=kv, o0=o0: e.matmul(out=P[2][:, h * 64:(h + 1) * 64], lhsT=ETB[h // 4][:, o0 + 128:o0 + 256], rhs=Vb[par][:, kv * 64:(kv + 1) * 64], start=False, stop=True))
        S.op('dve', ['P2', 'ast3'], ['ycat'], lambda e: e.tensor_tensor(out=ycat[:, 512:1024].rearrange("p (h j) -> p h j", h=8), in0=P[2][:].rearrange("p (h j) -> p h j", h=8),
                                                                       in1=ast8[:, 3, :].unsqueeze(2).broadcast_to([128, 8, 64]), op=ALU.mult))
        DUMP("ycat", ycat, 'ycat', ti); DUMP("qkv", qkv, 'qkv', ti); DUMP("QT", QT, 'QT', ti)
        if OPTS['stage'] <= 7:
            return
        for kc in range(8):
            S.op('pe', ['ycat', 'c_cstb'], ['P2'], lambda e, kc=kc: e.transpose(out=pTb[2][:, kc * 128:(kc + 1) * 128], in_=ycat[:, kc * 128:(kc + 1) * 128], identity=ident_b))
        S.op('act', ['P2'], ['ycatT'], lambda e: e.copy(out=ycatT[:], in_=pTb[2][:, 0:1024].rearrange("p (k t) -> p k t", k=8)))
        for half in range(2):
            bank = 3 + half
            for kc in range(8):
                S.op('pe', ['ycatT', 'Wo'], [PK[bank]], lambda e, kc=kc, half=half, bank=bank: e.matmul(out=P[bank][:], lhsT=ycatT[:, kc, :], rhs=Wo[:, kc, half * 512:(half + 1) * 512], start=(kc == 0), stop=(kc == 7)))
            S.op('dve', [PK[bank], xk], ['xm'], lambda e, half=half, bank=bank: e.tensor_tensor(out=xm[:, half * 512:(half + 1) * 512], in0=P[bank][:], in1=xb[:, half * 512:(half + 1) * 512], op=ALU.add))
        if dbg and ti == OPTS['dbg_tile']:
            S.op('dve', ['ycat'], ['a_t1'], lambda e: e.tensor_copy(out=A["t1"][:], in_=ycat[:, 0:512]))
            S.op('dve', ['ycat'], ['a_t2'], lambda e: e.tensor_copy(out=A["t2"][:], in_=ycat[:, 512:1024]))
            S.dma('sp', dbg_t["d_y"], A["t1"][:], ['a_t1'], [])
            S.dma('sp', dbg_t["d_at"], A["t2"][:], ['a_t2'], [])
            S.dma('sp', dbg_t["d_S"], S32[:].rearrange("p h i -> p (h i)"), ['S32'], [])
        DUMP("xm", xm, 'xm', ti)
        if samp:
            S.dma('sp', xmid[NT_B * 128 + sq * 4:NT_B * 128 + sq * 4 + 4, :], xm[0:4, :], ['xm'], ['xmid'])
        else:
            S.dma('sp', xmid[(ti - NT_A) * 128:(ti - NT_A + 1) * 128, :], xm[:], ['xm'], ['xmid'])


    def peer_phase():
        c_g2bc = sb("c_g2bc", [128, D])
        S.dma('sp', c_g2bc[:], g2.partition_broadcast(128)[:, 0, :], [], ['c_g2bc'])
        c_gFbc = sb("c_gFbc", [128, D])
        S.dma('sp', c_gFbc[:], gF.partition_broadcast(128)[:, 0, :], [], ['c_gFbc'])
        c_iota = sb("c_iota", [128, 256])
        S.dma('sp', c_iota[:], iota_in, [], ['c_iota'])
        Wq = sb("Wq", [128, 8, 2048], BF16)
        skT = sb("skT", [128, 2, 128], BF16)
        xm2 = sb("xm2", [128, D])
        hn32 = sb("hn32", [128, D])
        hnb = sb("hnb", [128, D], BF16)
        hn2T = sb("hn2T", [128, 8, 128], BF16)
        qT = sb("qT", [128, 16, 128], BF16)
        s_sb = sb("s_sb", [128, 16, 128])
        s2 = sb("s2", [128, 16, 128])
        tv = sb("tv", [128, 16, 16])
        tiu = sb("tiu", [128, 16, 16], U32)
        tif = sb("tif", [128, 16, 16])
        cand = sb("cand", [128, 8, 256])
        cand2 = sb("cand2", [128, 8, 256])
        cidx = sb("cidx", [128, 8, 256])
        top = sb("top", [128, 8, 16])
        selu = sb("selu", [128, 8, 16], U32)
        self_ = sb("self", [128, 8, 16])
        idxf = sb("idxf", [128, 8, 16])
        idx2 = sb("idx2", [128, 8, 16])
        selu2 = sb("selu2", [128, 2, 8, 16], U32)
        sela = sb("sela", [128, 8, 16])
        selb = sb("selb", [128, 8, 16])
        idxu = sb("idxu", [128, 128], U32)
        gate = sb("gate", [128, 8, 16])
        gst = sb("gst", [128, 8, 2])
        pre = sb("pre", [128, 128])
        wgt = sb("wgt", [128, 128])
        acc = sb("acc", [128, D])
        ss2 = sb("ss2", [128, 4])
        NG = OPTS['ng']
        gb = [sb("gb%d" % i, [128, D]) for i in range(NG)]
        gbs = [sb("gbs%d" % i, [128, D], BF16) for i in range(3)]
        with nc.sbuf_tensor("stg2", [128, 2048], F32) as stg2:
            for kc in range(8):
                S.dma('sp', stg2[:], w_q[kc * 128:(kc + 1) * 128, :], [], ['stg2'])
                S.op('act', ['stg2'], ['Wq'], lambda e, kc=kc: e.copy(out=Wq[:, kc, :], in_=stg2[:]))
            S.dma('sp', stg2[:, 0:256].rearrange("p (c d) -> p c d", c=2), subk.rearrange("c n d -> n c d"), [], ['stg2'])
            S.op('act', ['stg2'], ['hnb'], lambda e: e.copy(out=hnb[:, 0:256], in_=stg2[:, 0:256]))
            for c in range(2):
                S.op('pe', ['hnb', 'c_cstb'], ['P0'], lambda e, c=c: e.transpose(out=P[0][:].bitcast(BF16)[:, c * 128:(c + 1) * 128], in_=hnb[:, c * 128:(c + 1) * 128], identity=ident_b))
            S.op('act', ['P0'], ['skT'], lambda e: e.copy(out=skT[:], in_=P[0][:].bitcast(BF16)[:, 0:256].rearrange("p (c n) -> p c n", c=2)))
            barrier()
        pTb = [P[i][:].bitcast(BF16) for i in range(8)]
        for pt in range(NPE):
            samp = pt >= NT_B
            S.dma('sp', xm2[:], xmid[pt * 128:(pt + 1) * 128, :], ['xmid'], ['xm2'])
            S.op('act', ['xm2'], ['hnb', 'ss2'], lambda e: e.activation(out=hnb[:], in_=xm2[:], func=AF.Square, accum_out=ss2[:, 0:1]))
            rsq('ss2', ss2[:, 1:2], ss2[:, 0:1], D * 1e-5, ALU.add)
            S.op('dve', ['xm2', 'ss2'], ['hn32'], lambda e: e.tensor_scalar(out=hn32[:], in0=xm2[:], scalar1=ss2[:, 1:2], scalar2=32.0, op0=ALU.mult, op1=ALU.mult))
            S.op('pool', ['hn32', 'c_g2bc'], ['hn32'], lambda e: e.tensor_tensor(out=hn32[:], in0=hn32[:], in1=c_g2bc[:], op=ALU.mult))
            S.op('act', ['hn32'], ['hnb'], lambda e: e.copy(out=hnb[:], in_=hn32[:]))
            for kc in range(8):
                S.op('pe', ['hnb', 'c_cstb'], ['P0'], lambda e, kc=kc: e.transpose(out=pTb[0][:, kc * 128:(kc + 1) * 128], in_=hnb[:, kc * 128:(kc + 1) * 128], identity=ident_b))
            S.op('act', ['P0'], ['hn2T'], lambda e: e.copy(out=hn2T[:], in_=pTb[0][:, 0:1024].rearrange("p (k t) -> p k t", k=8)))
            for hc in range(16):
                bank = 1 + hc // 4
                cs = slice((hc % 4) * 128, (hc % 4 + 1) * 128)
                for kc in range(8):
                    S.op('pe', ['Wq', 'hn2T'], [PK[bank]], lambda e, hc=hc, kc=kc, bank=bank, cs=cs: e.matmul(out=P[bank][:, cs], lhsT=Wq[:, kc, hc * 128:(hc + 1) * 128], rhs=hn2T[:, kc, :], start=(kc == 0), stop=(kc == 7)))
            for b in range(4):
                S.op('act' if b % 2 else 'dve', [PK[1 + b]], ['qT'], lambda e, b=b: (e.copy if b % 2 else e.tensor_copy)(out=qT[:, b * 4:(b + 1) * 4, :], in_=P[1 + b][:].rearrange("p (a t) -> p a t", a=4)))
            sbanks = [5, 6, 7, 0]
            for hc in range(16):
                bank = sbanks[hc // 4]
                cs = slice((hc % 4) * 128, (hc % 4 + 1) * 128)
                S.op('pe', ['qT', 'skT'], [PK[bank]], lambda e, hc=hc, bank=bank, cs=cs: e.matmul(out=P[bank][:, cs], lhsT=qT[:, hc, :], rhs=skT[:, hc % 2, :], start=True, stop=True))
            for b in range(4):
                S.op('act' if b % 2 else 'dve', [PK[sbanks[b]]], ['s_sb'], lambda e, b=b: (e.copy if b % 2 else e.tensor_copy)(out=s_sb[:, b * 4:(b + 1) * 4, :], in_=P[sbanks[b]][:].rearrange("p (a t) -> p a t", a=4)))
            for hc in range(16):
                S.op('dve', ['s_sb'], ['tv'], lambda e, hc=hc: e.max(out=tv[:, hc, 0:8], in_=s_sb[:, hc, :]))
                S.op('dve', ['s_sb', 'tv'], ['tiu'], lambda e, hc=hc: e.max_index(out=tiu[:, hc, 0:8], in_max=tv[:, hc, 0:8], in_values=s_sb[:, hc, :]))
                S.op('dve', ['s_sb', 'tv'], ['s2'], lambda e, hc=hc: e.match_replace(out=s2[:, hc, :], in_to_replace=tv[:, hc, 0:8], in_values=s_sb[:, hc, :], imm_value=-1e30))
                S.op('dve', ['s2'], ['tv'], lambda e, hc=hc: e.max(out=tv[:, hc, 8:16], in_=s2[:, hc, :]))
                S.op('dve', ['s2', 'tv'], ['tiu'], lambda e, hc=hc: e.max_index(out=tiu[:, hc, 8:16], in_max=tv[:, hc, 8:16], in_values=s2[:, hc, :]))
            S.op('dve', ['tiu'], ['tif'], lambda e: e.tensor_copy(out=tif[:], in_=tiu[:]))
            tv4 = tv[:].rearrange("p (h c) k -> p h c k", c=2)
            tf4 = tif[:].rearrange("p (h c) k -> p h c k", c=2)
            c4 = lambda t: t[:].rearrange("p h (a b) -> p h a b", a=16)
            S.op('dve', ['tv'], ['cand'], lambda e: e.tensor_tensor(out=c4(cand), in0=tv4[:, :, 0, :].unsqueeze(3).broadcast_to([128, 8, 16, 16]),
                                                                    in1=tv4[:, :, 1, :].unsqueeze(2).broadcast_to([128, 8, 16, 16]), op=ALU.add))
            S.op('dve', ['tif'], ['tif'], lambda e: e.tensor_scalar(out=tf4[:, :, 0, :], in0=tf4[:, :, 0, :], scalar1=128.0, scalar2=None, op0=ALU.mult))
            S.op('dve', ['tif'], ['cidx'], lambda e: e.tensor_tensor(out=c4(cidx), in0=tf4[:, :, 0, :].unsqueeze(3).broadcast_to([128, 8, 16, 16]),
                                                                     in1=tf4[:, :, 1, :].unsqueeze(2).broadcast_to([128, 8, 16, 16]), op=ALU.add))
            for h in range(8):
                S.op('dve', ['cand'], ['top'], lambda e, h=h: e.max(out=top[:, h, 0:8], in_=cand[:, h, :]))
                S.op('dve', ['cand', 'top'], ['selu'], lambda e, h=h: e.max_index(out=selu[:, h, 0:8], in_max=top[:, h, 0:8], in_values=cand[:, h, :]))
                S.op('dve', ['cand', 'top'], ['cand2'], lambda e, h=h: e.match_replace(out=cand2[:, h, :], in_to_replace=top[:, h, 0:8], in_values=cand[:, h, :], imm_value=-1e30))
                S.op('dve', ['cand2'], ['top'], lambda e, h=h: e.max(out=top[:, h, 8:16], in_=cand2[:, h, :]))
                S.op('dve', ['cand2', 'top'], ['selu'], lambda e, h=h: e.max_index(out=selu[:, h, 8:16], in_max=top[:, h, 8:16], in_values=cand2[:, h, :]))
            S.op('dve', ['selu'], ['self'], lambda e: e.tensor_copy(out=self_[:], in_=selu[:]))
            S.op('dve', ['selu'], ['selu2'], lambda e: e.tensor_scalar(out=selu2[:, 0], in0=selu[:], scalar1=4, scalar2=None, op0=ALU.logical_shift_right))
            S.op('dve', ['selu'], ['selu2'], lambda e: e.tensor_scalar(out=selu2[:, 1], in0=selu[:], scalar1=15, scalar2=None, op0=ALU.bitwise_and))
            S.op('dve', ['selu2'], ['sela'], lambda e: e.tensor_copy(out=sela[:], in_=selu2[:, 0]))
            S.op('dve', ['selu2'], ['selb'], lambda e: e.tensor_copy(out=selb[:], in_=selu2[:, 1]))
            io16 = c_iota[:, 0:16].unsqueeze(1).unsqueeze(1).broadcast_to([128, 8, 16, 16])
            for which, (selx, dst) in enumerate([(sela, idxf), (selb, idx2)]):
                S.op('dve', ['sela', 'selb', 'c_iota'], ['cand2'], lambda e, selx=selx: e.tensor_tensor(out=c4(cand2), in0=io16, in1=selx[:].unsqueeze(3).broadcast_to([128, 8, 16, 16]), op=ALU.is_equal))
                S.op('dve', ['cand2', 'tif'], ['cand2'], lambda e, which=which: e.tensor_tensor(out=c4(cand2), in0=c4(cand2), in1=tf4[:, :, which, :].unsqueeze(2).broadcast_to([128, 8, 16, 16]), op=ALU.mult))
                S.op('dve', ['cand2'], ['idxf' if which == 0 else 'idx2'], lambda e, dst=dst: e.tensor_reduce(out=dst[:], in_=c4(cand2), axis=AX.X, op=ALU.add))
            S.op('dve', ['idxf', 'idx2'], ['idxf'], lambda e: e.tensor_tensor(out=idxf[:], in0=idxf[:], in1=idx2[:], op=ALU.add))
            S.op('dve', ['idxf'], ['idxf'], lambda e: e.tensor_scalar(out=idxf[:], in0=idxf[:], scalar1=0.0, scalar2=float(NEXP - 1), op0=ALU.max, op1=ALU.min))
            S.op('dve', ['idxf'], ['idxu'], lambda e: e.tensor_copy(out=idxu[:], in_=idxf[:].rearrange("p h k -> p (h k)")))
            S.op('dve', ['top'], ['gate'], lambda e: e.tensor_tensor(out=gate[:], in0=top[:], in1=top[:, :, 0:1].broadcast_to([128, 8, 16]), op=ALU.subtract))
            S.op('act', ['gate'], ['gate'], lambda e: e.activation(out=gate[:], in_=gate[:], func=AF.Exp))
            S.op('dve', ['gate'], ['gst'], lambda e: e.tensor_reduce(out=gst[:, :, 0], in_=gate[:], axis=AX.X, op=ALU.add))
            S.op('dve', ['gst'], ['gst'], lambda e: e.reciprocal(out=gst[:, :, 1], in_=gst[:, :, 0]))
            S.op('dve', ['gate', 'gst'], ['gate'], lambda e: e.tensor_tensor(out=gate[:], in0=gate[:], in1=gst[:, :, 1:2].broadcast_to([128, 8, 16]), op=ALU.mult))
            for sl in range(128):
                g = sl % NG
                S.dma('pool', None, None, ['idxu'], ['gb%d' % g], fn=lambda e, sl=sl, g=g: e.indirect_dma_start(
                    out=(gb[g][:, 0:512] if OPTS['half'] else gb[g][:]), out_offset=None, in_=(eu[:, 0:512] if OPTS['half'] else eu), in_offset=bass.IndirectOffsetOnAxis(ap=idxu[:, sl:sl + 1], axis=0)))
                S.op('dve', ['gb%d' % g, 'hn32'], ['gb%d' % g, 'pre'], lambda e, sl=sl, g=g: e.scalar_tensor_tensor(
                    out=gb[g][:], in0=gb[g][:], scalar=1.0, in1=hn32[:], op0=ALU.mult, op1=ALU.mult, accum_out=pre[:, sl:sl + 1]))
            S.op('act', ['pre'], ['wgt'], lambda e: e.activation(out=wgt[:], in_=pre[:], func=AF.Gelu))
            S.op('dve', ['wgt', 'gate'], ['wgt'], lambda e: e.tensor_tensor(out=wgt[:], in0=wgt[:], in1=gate[:].rearrange("p h k -> p (h k)"), op=ALU.mult))
            for sl in range(128):
                g = sl % NG
                gi = sl % 3
                S.dma('pool', None, None, ['idxu'], ['gb%d' % g], fn=lambda e, sl=sl, g=g: e.indirect_dma_start(
                    out=(gb[g][:, 0:512] if OPTS['half'] else gb[g][:]), out_offset=None, in_=(ev[:, 0:512] if OPTS['half'] else ev), in_offset=bass.IndirectOffsetOnAxis(ap=idxu[:, sl:sl + 1], axis=0)))
                S.op('act', ['gb%d' % g, 'wgt'], ['gbs%d' % gi], lambda e, sl=sl, g=g, gi=gi: e.activation(
                    out=gbs[gi][:], in_=gb[g][:], func=AF.Copy, scale=wgt[:, sl:sl + 1]))
                for half in range(2):
                    S.op('pe', ['gbs%d' % gi, 'c_cstb'], [PK[half]], lambda e, sl=sl, gi=gi, half=half: e.matmul(
                        out=P[half][:], lhsT=ident_b, rhs=gbs[gi][:, half * 512:(half + 1) * 512], start=(sl == 0), stop=(sl == 127)))
            for half in range(2):
                S.op('dve', [PK[half], 'xm2'], ['acc'], lambda e, half=half: e.tensor_tensor(
                    out=acc[:, half * 512:(half + 1) * 512], in0=P[half][:], in1=xm2[:, half * 512:(half + 1) * 512], op=ALU.add))
            S.op('act', ['acc'], ['hnb', 'ss2'], lambda e: e.activation(out=hnb[:], in_=acc[:], func=AF.Square, accum_out=ss2[:, 2:3]))
            rsq('ss2', ss2[:, 3:4], ss2[:, 2:3], D * 1e-5, ALU.add)
            S.op('dve', ['acc', 'ss2'], ['acc'], lambda e: e.tensor_scalar(out=acc[:], in0=acc[:], scalar1=ss2[:, 3:4], scalar2=32.0, op0=ALU.mult, op1=ALU.mult))
            S.op('pool', ['acc', 'c_gFbc'], ['acc'], lambda e: e.tensor_tensor(out=acc[:], in0=acc[:], in1=c_gFbc[:], op=ALU.mult))
            if samp:
                S.dma('sp', y_s[:, :], acc[0:NSEQ_S * 4, :], ['acc'], [])
            else:
                S.dma('sp', y_p[pt * 128:(pt + 1) * 128, :], acc[:], ['acc'], [])

    for ti in range(NTT):
        load_x(ti)
        mix_tile(ti)
    print("sbuf left", nc.sbuf_bytes_remaining() if callable(nc.sbuf_bytes_remaining) else nc.sbuf_bytes_remaining)
    print("total ops", getattr(S, 'n', 0))
    barrier()
    ph1.close()
    stk['cur'] = glob_stack
    if OPTS['peer']:
        peer_phase()
    barrier()
    S.finish()
    return nc


_CACHE = {}
NTA_FULL, NTB_FULL = 17, 17


def _consts(nta, ntb, hf):
    ar = np.arange(128)
    ident = np.eye(128, dtype=np.float32)
    tri = (ar[:, None] <= ar[None, :]).astype(np.float32)
    ones = np.ones((128, 128), np.float32)
    su = (ar[:, None] < ar[None, :]).astype(np.float32)
    lo = (ar[:, None] > ar[None, :]).astype(np.float32)
    cst = np.stack([ident, tri, ones, su, tri, lo], axis=1).astype(np.float32)
    q = ar[:, None]
    c = np.arange(256)[None, :]
    ok = (c > q) & (c <= q + 128)
    m_std = np.where(ok, 0.0, -30000.0).astype(np.float32)
    m_t0 = np.where(ok & (c >= 240), 0.0, -30000.0).astype(np.float32)
    m_t1 = np.where(ok & (c >= 112), 0.0, -30000.0).astype(np.float32)
    first = (hf == 0) or (nta == 0)
    cmask = np.stack([m_t0 if first else m_std, m_t1 if first else m_std, m_std], axis=1)
    inv = (np.float32(500000.0) ** (-np.arange(0, 16, 2, dtype=np.float32) / np.float32(16))).astype(np.float32)
    ntp = nta + ntb
    rope = np.zeros((128, ntp + 1, 16), np.float32)
    for i in range(ntp + 1):
        if i < ntp:
            st = i if (hf == 1 or nta == 0) else (i - nta if i >= nta else i)
            pos = st * 128 - 112 + ar
        else:
            pos = PAST + ar
        ang = pos.astype(np.float32)[:, None] * inv[None, :]
        rope[:, i, 0:8] = np.cos(ang)
        rope[:, i, 8:16] = np.sin(ang)
    vmask = np.zeros((128, 2), np.float32)
    vmask[:, 0] = 1.0
    vmask[0:4, 1] = 1.0
    iota = np.tile(np.arange(256, dtype=np.float32)[None, :], (128, 1))
    return dict(cst=cst, cmask=cmask, rope=rope, vmask=vmask, iota=iota)


def kernel(x_prompt, x_sample, cache_k_win, cache_v_win, state_wkv, state_shift, meta_tokens, norm1_g, w_in, mu_shift,
           w0, w_lora_w2, a0, w_lora_a2, w_lora_g2, k_k, k_a, r_k, lnx_w, lnx_b, attn_sinks, w_out, norm2_g, w_query,
           sub_keys, expert_u, expert_v, final_norm_g, _nta=NTA_FULL, _ntb=NTB_FULL, _nts=NSEQ_S, _dbg=False):
    f = lambda a: np.ascontiguousarray(np.asarray(a), dtype=np.float32)
    key = (_nta, _ntb, _nts)
    if key not in _CACHE:
        _CACHE[key] = build(_nta, _ntb, _nts, dbg=_dbg)
    nc = _CACHE[key]
    x_prompt, x_sample = f(x_prompt), f(x_sample)
    B = x_prompt.shape[0]
    nseqt = _nta + _ntb - 1
    shared = dict(
        w_in=f(w_in)[0], w_out=f(w_out)[0], w_q=f(w_query)[0], subk=f(sub_keys)[0],
        eu=f(expert_u)[0][:OPTS['nexp']], ev=f(expert_v)[0][:OPTS['nexp']],
        lw2=f(w_lora_w2)[0], la2=f(w_lora_a2)[0], lg2=f(w_lora_g2)[0],
        vec512=np.stack([f(w0)[0], f(a0)[0], f(k_k)[0], f(k_a)[0], f(r_k)[0].reshape(512), f(lnx_w)[0], f(lnx_b)[0]]),
        mu=f(mu_shift), g1=f(norm1_g), g2=f(norm2_g), gF=f(final_norm_g)[None, :], sinks=f(attn_sinks))
    cs = [_consts(_nta, _ntb, hf) for hf in range(2)]
    in_maps = []
    for c in range(NCORES):
        b, hf = c // 2, c % 2
        seq = np.zeros((nseqt * 128, D), np.float32)
        seq[112:128] = f(meta_tokens)
        seq[128:] = x_prompt[b][:(nseqt - 1) * 128]
        xp = np.zeros(((_nta + _ntb) * 128, D), np.float32)
        if hf == 0:
            xp[_nta * 128:(_nta + _ntb) * 128] = seq[:_ntb * 128]
        else:
            xp[:nseqt * 128] = seq
        sl = slice(c * NSEQ_S, (c + 1) * NSEQ_S)
        m = dict(shared)
        m.update(cs[hf])
        m.update(xp=xp, xs=x_sample[sl].reshape(NSEQ_S * 4, D),
                 ck=f(cache_k_win)[0, sl].reshape(NSEQ_S, 128, 128), cv=f(cache_v_win)[0, sl].reshape(NSEQ_S, 128, 128),
                 swkv=f(state_wkv)[0, sl], sshift=f(state_shift)[0, sl])
        in_maps.append(m)
    res = run_bass_kernel_spmd(nc, in_maps, core_ids=list(range(NCORES))).results
    if (_nta, _ntb, _nts) != (NTA_FULL, NTB_FULL, NSEQ_S):
        return res
    y_prompt = np.stack([np.concatenate([res[2 * b]["y_p"][128:_ntb * 128], res[2 * b + 1]["y_p"][:(_ntb - 1) * 128]]) for b in range(B)])
    y_sample = np.concatenate([res[c]["y_s"].reshape(NSEQ_S, 4, D) for c in range(NCORES)])
    od = lambda b: res[2 * b + 1]
    kwp = np.stack([od(b)["kw_p"].reshape(128, 2, 64) for b in range(B)])[None]
    vwp = np.stack([od(b)["vw_p"].reshape(128, 2, 64) for b in range(B)])[None]
    wkvp = np.stack([od(b)["wkv_p"] for b in range(B)])[None]
    shp = np.stack([od(b)["sh_p"][0] for b in range(B)])[None]
    kws = np.concatenate([res[c]["kw_s"].reshape(NSEQ_S, 128, 2, 64) for c in range(NCORES)])[None]
    vws = np.concatenate([res[c]["vw_s"].reshape(NSEQ_S, 128, 2, 64) for c in range(NCORES)])[None]
    wkvs = np.concatenate([res[c]["wkv_s"] for c in range(NCORES)])[None]
    shs = np.concatenate([res[c]["sh_s"] for c in range(NCORES)])[None]
    return (y_prompt, y_sample, kwp, vwp, wkvp, shp, kws, vws, wkvs, shs)
```

```python
import numpy as np
from contextlib import ExitStack
import concourse.bass as bass
import concourse.mybir as mybir
from concourse.alu_op_type import AluOpType as ALU
from concourse.bass_utils import run_bass_kernel_spmd

F32 = mybir.dt.float32
BF16 = mybir.dt.bfloat16
U32 = mybir.dt.uint32
AF = mybir.ActivationFunctionType
AX = mybir.AxisListType

D = 1024
NRW = 1792
NCOL = 2560
NCORES = 8
NSEQ_S = 16
PAST = 8192
NPT = 33
NEXP = 16384
OPTS = {'ng': 10, 'half': 0, 'limit': 10**9, 'dump_only': '', 'dumps': False, 'nexp': 16384, 'dbg_tile': 1, 'stage': 99, 'peer': True, 'win_copy': True, 'samp_state': True}


class Sched:
    def __init__(self, nc):
        self.nc = nc
        self.eng = {'pe': nc.tensor, 'act': nc.scalar, 'dve': nc.vector, 'pool': nc.gpsimd, 'sp': nc.sync}
        self.sem = {e: nc.alloc_semaphore("sem_" + e) for e in ['pe', 'act', 'dve', 'pool']}
        self.cnt = {e: 0 for e in self.sem}
        self.seen = {e: {} for e in self.eng}
        self.last_w = {}
        self.readers = {}
        self.dslots = {}
        for q in ['sp', 'pool', 'act']:
            self.dslots[q] = [[nc.alloc_semaphore("dq_%s_%d" % (q, i)), 0] for i in range(OPTS['ng'] if q == 'pool' else 8)]
        self.dnext = {q: 0 for q in self.dslots}
        self.tokens = {}

    def _wait(self, e, toks):
        need = {}
        for t in toks:
            if t is None:
                continue
            sid, val, we = t
            if we == e and e == 'pe':
                continue
            if self.seen[e].get(sid, 0) >= val:
                continue
            if need.get(sid, (None, 0))[1] < val:
                need[sid] = (t, val)
        for sid, (t, val) in need.items():
            self.eng[e].wait_ge(self.tokens[sid], val)
            self.seen[e][sid] = val

    def _deps(self, e, reads, writes):
        toks = []
        for k in reads:
            toks.append(self.last_w.get(k))
        for k in writes:
            toks.append(self.last_w.get(k))
            for t in self.readers.get(k, []):
                toks.append(t[:3])
        return toks

    def _mark(self, tok, reads, writes, is_dma):
        for k in reads:
            self.readers.setdefault(k, []).append(tok + (is_dma,))
        for k in writes:
            self.last_w[k] = tok
            self.readers[k] = []

    def op(self, e, reads, writes, fn):
        self.n = getattr(self, 'n', 0) + 1
        if self.n > OPTS['limit']:
            return None
        self._wait(e, self._deps(e, reads, writes))
        inst = fn(self.eng[e])
        self.cnt[e] += 1
        sem = self.sem[e]
        inst.then_inc(sem, 1)
        sid = id(sem)
        self.tokens[sid] = sem
        self._mark((sid, self.cnt[e], e), reads, writes, False)
        return inst

    def dma(self, q, out, in_, reads, writes, fn=None):
        self.n = getattr(self, 'n', 0) + 1
        if self.n > OPTS['limit']:
            return None
        slots = self.dslots[q]
        i = self.dnext[q]
        self.dnext[q] = (i + 1) % len(slots)
        sem, val = slots[i]
        sid = id(sem)
        self.tokens[sid] = sem
        toks = self._deps(q, reads, writes)
        if val > 0:
            toks.append((sid, val, 'dma'))
        self._wait(q, toks)
        if fn is None:
            inst = self.eng[q].dma_start(out=out, in_=in_)
        else:
            inst = fn(self.eng[q])
        inst.then_inc(sem, 16)
        slots[i][1] = val + 16
        self._mark((sid, val + 16, 'dma'), reads, writes, True)

    def finish(self):
        for q, slots in self.dslots.items():
            toks = [(id(s), v, 'dma') for s, v in slots if v > 0]
            self._wait(q, toks)


def build(NT_A, NT_B, NT_S, dbg=False, dbg_tile=1):
    nc = bass.Bass("TRN2", target_bir_lowering=False)
    S = Sched(nc)
    NT_P = NT_A + NT_B
    NTT = NT_P + NT_S
    NPE = NT_B + (1 if NT_S else 0)
    OUT_T = NT_P - 2 if NT_A > 0 else NT_P - 1

    def din(name, shape, dt=F32):
        return nc.dram_tensor(name, list(shape), dt, kind="ExternalInput").ap()

    def dout(name, shape, dt=F32):
        return nc.dram_tensor(name, list(shape), dt, kind="ExternalOutput").ap()

    xp = din("xp", [max(NT_P, 1) * 128, D])
    xs = din("xs", [NSEQ_S * 4, D])
    ck = din("ck", [NSEQ_S, 128, 128])
    cv = din("cv", [NSEQ_S, 128, 128])
    swkv = din("swkv", [NSEQ_S, 8, 64, 64])
    sshift = din("sshift", [NSEQ_S, D])
    w_in = din("w_in", [D, NCOL])
    w_out = din("w_out", [D, D])
    w_q = din("w_q", [D, 2048])
    subk = din("subk", [2, 128, 128])
    eu = din("eu", [OPTS['nexp'], D])
    ev = din("ev", [OPTS['nexp'], D])
    lw2 = din("lw2", [64, 512])
    la2 = din("la2", [64, 512])
    lg2 = din("lg2", [128, 512])
    vec512 = din("vec512", [7, 512])
    mu = din("mu", [1, NRW])
    g1 = din("g1", [1, D])
    g2 = din("g2", [1, D])
    gF = din("gF", [1, D])
    sinks = din("sinks", [1, 8])
    rope = din("rope", [128, NT_P + 1, 16])
    cmask = din("cmask", [128, 3, 256])
    cst = din("cst", [128, 6, 128])
    vmask = din("vmask", [128, 2])
    iota_in = din("iota", [128, 256])

    y_p = dout("y_p", [max(NT_B, 1) * 128, D])
    y_s = dout("y_s", [NSEQ_S * 4, D])
    kw_p = dout("kw_p", [128, 128])
    vw_p = dout("vw_p", [128, 128])
    wkv_p = dout("wkv_p", [8, 64, 64])
    sh_p = dout("sh_p", [1, D])
    kw_s = dout("kw_s", [NSEQ_S, 128, 128])
    vw_s = dout("vw_s", [NSEQ_S, 128, 128])
    wkv_s = dout("wkv_s", [NSEQ_S, 8, 64, 64])
    sh_s = dout("sh_s", [NSEQ_S, D])
    xmid = nc.dram_tensor("xmid", [max(NPE, 1) * 128, D], F32, kind="Internal").ap()
    dbg_t = {}
    if dbg:
        for nm, shp in [("d_m", [128, NRW]), ("d_y", [128, 512]), ("d_at", [128, 512]), ("d_S", [64, 512]),
                        ("d_pre", [128, 128]), ("d_idx", [128, 128]), ("d_gate", [128, 128])]:
            dbg_t[nm] = dout(nm, shp)

    stk = {'cur': ExitStack()}
    glob_stack = stk['cur']

    dumped = {}

    def DUMP(name, t, key, ti=None):
        if not OPTS['dumps'] or (ti is not None and ti != OPTS['dbg_tile']) or name in dumped:
            return
        if OPTS['dump_only'] and name not in str(OPTS['dump_only']).split(','):
            return
        src = t if isinstance(t, bass.AP) else t[:]
        o = nc.dram_tensor("z_" + name, list(src.shape), src.dtype, kind="ExternalOutput").ap()
        dumped[name] = 1
        S.dma('sp', o, src, [key], [])

    def sb(name, shape, dt=F32):
        return stk['cur'].enter_context(nc.sbuf_tensor(name, list(shape), dt))

    def ps(name, shape, dt=F32):
        return nc.alloc_psum_tensor(name, list(shape), dt)

    c_cst = sb("c_cst", [128, 6, 128])
    S.dma('sp', c_cst[:], cst, [], ['c_cst'])
    c_cstb = sb("c_cstb", [128, 6, 128], BF16)
    S.op('dve', ['c_cst'], ['c_cstb'], lambda e: e.tensor_copy(out=c_cstb[:], in_=c_cst[:]))
    ident_f = c_cst[:, 0, :]
    tri_f = c_cst[:, 1, :]
    ones_f = c_cst[:, 2, :]
    ident_b = c_cstb[:, 0, :]
    c_m4 = sb("c_m4", [128, 4, 128])
    for i, j in enumerate([3, 4, 3, 4]):
        S.op('dve', ['c_cst'], ['c_m4'], lambda e, i=i, j=j: e.tensor_copy(out=c_m4[:, i, :], in_=c_cst[:, j, :]))
    c_low = c_cst[:, 5, :]
    c_amask = sb("c_amask", [128, 3, 256])
    S.dma('sp', c_amask[:], cmask, [], ['c_amask'])
    c_rope = sb("c_rope", [128, NT_P + 1, 16])
    S.dma('sp', c_rope[:], rope, [], ['c_rope'])
    c_vm = sb("c_vm", [128, 2])
    S.dma('sp', c_vm[:], vmask, [], ['c_vm'])
    c_v512 = sb("c_v512", [128, 7, 512])
    S.dma('sp', c_v512[:], vec512.partition_broadcast(128), [], ['c_v512'])
    W0, A0, KK, KA, RK, LW, LB = range(7)
    c_g1bc = sb("c_g1bc", [128, D])
    S.dma('sp', c_g1bc[:], g1.partition_broadcast(128)[:, 0, :], [], ['c_g1bc'])
    c_sink = sb("c_sink", [128, 8])
    S.dma('sp', c_sink[:], sinks.partition_broadcast(128)[:, 0, :], [], ['c_sink'])
    c_g1col = sb("c_g1col", [128, 8])
    with nc.allow_non_contiguous_dma(reason="tiny param column load"):
        S.dma('sp', c_g1col[:], g1.rearrange("o (kc p) -> p (o kc)", p=128), [], ['c_g1col'])

    def rsq(key, out, in_, c, op0):
        S.op('dve', [key], [key], lambda e: e.tensor_scalar(out=out, in0=in_, scalar1=c, scalar2=None, op0=op0))
        S.op('act', [key], [key], lambda e: e.activation(out=out, in_=out, func=AF.Sqrt))
        S.op('dve', [key], [key], lambda e: e.reciprocal(out=out, in_=out))

    P = [ps("P%d" % i, [128, 512]) for i in range(8)]
    PK = ["P%d" % i for i in range(8)]

    def barrier():
        toks = []
        for e, sem in S.sem.items():
            if S.cnt[e] > 0:
                S.tokens[id(sem)] = sem
                toks.append((id(sem), S.cnt[e], 'x'))
        for q, slots in S.dslots.items():
            for s, v in slots:
                if v > 0:
                    S.tokens[id(s)] = s
                    toks.append((id(s), v, 'dma'))
        for e in ['pe', 'act', 'dve', 'pool', 'sp']:
            S._wait(e, toks)

    ph1 = ExitStack()
    stk['cur'] = ph1
    W1 = sb("W1", [128, 8, NRW], BF16)
    W2 = sb("W2", [128, 8, NRW], BF16)
    Wat = sb("Wat", [128, 8, 768], BF16)
    Wo = sb("Wo", [128, 8, D], BF16)
    L_w2 = sb("L_w2", [128, 512], BF16)
    L_g2 = sb("L_g2", [128, 512], BF16)
    with nc.sbuf_tensor("stg", [128, NCOL], F32) as stg, nc.sbuf_tensor("mub", [128, NRW], F32) as mub, \
            nc.sbuf_tensor("omu", [128, NRW], F32) as omu:
        S.dma('sp', mub[:], mu.partition_broadcast(128)[:, 0, :], [], ['mub'])
        S.op('dve', ['mub'], ['omu'], lambda e: e.tensor_scalar(out=omu[:], in0=mub[:], scalar1=-1.0, scalar2=1.0,
                                                                op0=ALU.mult, op1=ALU.add))
        for kc in range(8):
            S.dma('sp', stg[:], w_in[kc * 128:(kc + 1) * 128, :], [], ['stg'])
            S.op('dve', ['stg', 'omu'], ['W1'], lambda e, kc=kc: e.tensor_tensor(out=W1[:, kc, :], in0=stg[:, 0:NRW], in1=omu[:], op=ALU.mult))
            S.op('pool', ['stg', 'mub'], ['W2'], lambda e, kc=kc: e.tensor_tensor(out=W2[:, kc, :], in0=stg[:, 0:NRW], in1=mub[:], op=ALU.mult))
            S.op('act', ['stg'], ['Wat'], lambda e, kc=kc: e.copy(out=Wat[:, kc, :], in_=stg[:, NRW:NCOL]))
        for kc in range(8):
            S.dma('sp', stg[:, 0:D], w_out[kc * 128:(kc + 1) * 128, :], [], ['stg'])
            S.op('act', ['stg'], ['Wo'], lambda e, kc=kc: e.copy(out=Wo[:, kc, :], in_=stg[:, 0:D]))
        S.dma('sp', stg[0:64, 0:512], lw2, [], ['stg'])
        S.dma('sp', stg[64:128, 0:512], la2, [], ['stg'])
        S.dma('sp', stg[:, 512:1024], lg2, [], ['stg'])
        S.op('act', ['stg'], ['L_w2'], lambda e: e.copy(out=L_w2[:], in_=stg[:, 0:512]))
        S.op('act', ['stg'], ['L_g2'], lambda e: e.copy(out=L_g2[:], in_=stg[:, 512:1024]))
        barrier()

    xt0 = sb("xt0", [128, D])
    xt = [xt0, xt0]
    xts = xt0
    xn = sb("xn", [128, D], BF16)
    ssq = sb("ssq", [128, 4])
    hT = sb("hT", [128, 8, 128], BF16)
    hTs = sb("hTs", [128, 8, 128], BF16)
    S.op('pool', [], ['hT'], lambda e: e.memset(hT[:], 0.0))
    S.op('pool', [], ['hTs'], lambda e: e.memset(hTs[:], 0.0))
    A = {}
    for nm in ["r", "k", "v", "ld", "asig", "g", "kk", "b", "kmod", "c", "t1", "t2", "t3"]:
        A[nm] = sb("a_" + nm, [128, 512])
    Bt = {}
    for nm in ["rt", "at", "bt", "kt", "bbar", "kbar", "vb", "W0T", "UT"]:
        Bt[nm] = sb("b_" + nm, [128, 512], BF16)
    lin = sb("lin", [128, 2, 128], BF16)
    st8 = sb("st8", [128, 8, 4])
    AR_fm = sb("AR_fm", [64, 8, 256], BF16)
    B_fm = sb("B_fm", [64, 8, 128], BF16)
    K_fm = sb("K_fm", [64, 8, 128], BF16)
    MATS = sb("MATS", [128, 8, 512], BF16)
    _m = sb("Mb", [128, 8, 128], BF16)
    _mt = sb("MTb", [128, 8, 128], BF16)
    _t = sb("Tb", [128, 8, 128], BF16)
    Mb, MTb, Tb = [_m, _m], [_mt, _mt], [_t, _t]
    S32 = sb("S32", [64, 8, 64])
    Sb = sb("Sb", [64, 8, 64], BF16)
    ecl_fm = sb("ecl_fm", [64, 8])
    sti = sb("sti", [64, 8, 64])
    qkv = sb("qkv", [128, 768])
    rtmp = sb("rtmp", [128, 10, 8])
    qb = sb("qb", [128, 768], BF16)
    QT = sb("QT", [64, 8, 128], BF16)
    KT = [sb("KT%d" % i, [64, 2, 128], BF16) for i in range(2)]
    Vb = [sb("Vb%d" % i, [128, 128], BF16) for i in range(2)]
    sm = sb("sm", [128, 256])
    for i in range(2):
        S.op('pool', [], ['KT%d' % i], lambda e, i=i: e.memset(KT[i][:], 0.0))
        S.op('pool', [], ['Vb%d' % i], lambda e, i=i: e.memset(Vb[i][:], 0.0))
    eb = sb("eb", [128, 256], BF16)
    eT = sb("eT", [128, 2, 128], BF16)
    ast = sb("ast", [128, 8])
    ast8 = sb("ast8", [128, 4, 8])
    ycat = sb("ycat", [128, D], BF16)
    ycatT = sb("ycatT", [128, 8, 128], BF16)
    junk = ycatT[:].rearrange("p k t -> p (k t)")
    xm = sb("xm", [128, D])
    hrow = xm
    ckb = sm

    def v3(ap, h=8):
        return ap.rearrange("p (h j) -> p h j", h=h)

    def bc_last(ap, n):
        return ap.broadcast_to([ap.shape[0], ap.shape[1], n])

    def V512(i):
        return c_v512[:, i, :]

    def load_x(ti):
        if ti < NT_P:
            b = xt[ti % 2]
            S.dma('sp', b[:], xp[ti * 128:(ti + 1) * 128, :], [], ['xt0'])
        else:
            s = ti - NT_P
            if s == 0:
                S.op('pool', [], ['xt0'], lambda e: e.memset(xt0[:], 0.0))
                S.dma('sp', xmid[NT_B * 128:(NT_B + 1) * 128, :], xt0[:], ['xt0'], ['xmid'])
            S.dma('sp', xts[0:4, :], xs[s * 4:(s + 1) * 4, :], [], ['xt0'])

    def mix_tile(ti):
        samp = ti >= NT_P
        sq = ti - NT_P
        xk = 'xt0'
        xb = xts if samp else xt[ti % 2]
        par = ti % 2
        vm = c_vm[:, 1:2] if samp else c_vm[:, 0:1]
        if OPTS['stage'] <= 0:
            return
        S.op('act', [xk], ['ycatT', 'ssq'], lambda e: e.activation(out=junk[:], in_=xb[:], func=AF.Square, accum_out=ssq[:, 0:1]))
        rsq('ssq', ssq[:, 1:2], ssq[:, 0:1], D * 1e-5, ALU.add)
        S.op('dve', [xk, 'ssq'], ['xn'], lambda e: e.tensor_scalar(out=xn[:], in0=xb[:], scalar1=ssq[:, 1:2], scalar2=32.0,
                                                                   op0=ALU.mult, op1=ALU.mult))
        if samp:
            S.dma('sp', hrow[0:1, :], sshift[sq:sq + 1, :], [], ['xm'])
            S.op('act', ['xm'], ['ycatT'], lambda e: e.copy(out=junk[0:1, :], in_=hrow[0:1, :]))
        pT = P[0][:].bitcast(BF16)
        if samp:
            for kc in range(8):
                S.op('pe', ['ycatT', 'c_cstb'], ['P1'], lambda e, kc=kc: e.transpose(out=P[1][:].bitcast(BF16)[:, 2 * kc:2 * kc + 1], in_=junk[0:1, kc * 128:(kc + 1) * 128], identity=ident_b[0:1, 0:1]))
            S.op('dve', ['P1'], ['hTs'], lambda e: e.tensor_copy(out=hTs[:, :, 0], in_=P[1][:].bitcast(BF16)[:, 0:16].rearrange("p (k two) -> p k two", two=2)[:, :, 0]))
        else:
            S.op('dve', ['hT'], ['hTs'], lambda e: e.tensor_copy(out=hTs[:, :, 0], in_=hT[:, :, 127]))
        for kc in range(8):
            S.op('pe', ['xn', 'c_cstb'], ['P0'], lambda e, kc=kc: e.transpose(out=pT[:, kc * 128:(kc + 1) * 128], in_=xn[:, kc * 128:(kc + 1) * 128], identity=ident_b))
        S.op('dve', ['P0', 'c_g1col'], ['hT'], lambda e: e.tensor_tensor(
            out=hT[:], in0=pT.rearrange("p (k t) -> p k t", k=8),
            in1=c_g1col[:].unsqueeze(2).broadcast_to([128, 8, 128]), op=ALU.mult))
        S.op('pool', ['hT'], ['hTs'], lambda e: e.tensor_copy(out=hTs[:, :, 1:128], in_=hT[:, :, 0:127]))
        if samp or ti == OUT_T:
            S.op('pool', ['xn', 'c_g1bc'], ['xm'], lambda e: e.tensor_tensor(out=hrow[:], in0=xn[:], in1=c_g1bc[:], op=ALU.mult))
            if samp:
                S.dma('sp', sh_s[sq:sq + 1, :], hrow[3:4, :], ['xm'], [])
            else:
                S.dma('sp', sh_p[0:1, :], hrow[127:128, :], ['xm'], [])
        DUMP("xn", xn, 'xn', ti); DUMP("hT", hT, 'hT', ti); DUMP("hTs", hTs, 'hTs', ti)
        if OPTS['stage'] <= 1:
            return
        def proj(bank, c0, n, dst_reads=()):
            for kc in range(8):
                S.op('pe', ['hT', 'W1'], [PK[bank]], lambda e, kc=kc: e.matmul(out=P[bank][:, 0:n], lhsT=hT[:, kc, :], rhs=W1[:, kc, c0:c0 + n], start=(kc == 0), stop=False))
            for kc in range(8):
                S.op('pe', ['hTs', 'W2'], [PK[bank]], lambda e, kc=kc: e.matmul(out=P[bank][:, 0:n], lhsT=hTs[:, kc, :], rhs=W2[:, kc, c0:c0 + n], start=False, stop=(kc == 7)))
        proj(1, 0, 512)
        S.op('act', ['P1'], ['a_r'], lambda e: e.copy(out=A["r"][:], in_=P[1][:]))
        proj(2, 512, 512)
        S.op('act', ['P2'], ['a_k'], lambda e: e.copy(out=A["k"][:], in_=P[2][:]))
        proj(3, 1024, 512)
        S.op('act', ['P3'], ['a_v'], lambda e: e.copy(out=A["v"][:], in_=P[3][:]))
        S.op('dve', ['a_v', 'c_vm'], ['b_vb'], lambda e: e.tensor_scalar(out=Bt["vb"][:], in0=A["v"][:], scalar1=vm, scalar2=None, op0=ALU.mult))
        for j, c0 in enumerate([1536, 1664]):
            for kc in range(8):
                S.op('pe', ['hT', 'W1'], ['P4'], lambda e, kc=kc, j=j, c0=c0: e.matmul(out=P[4][:, j * 128:(j + 1) * 128], lhsT=W1[:, kc, c0:c0 + 128], rhs=hT[:, kc, :], start=(kc == 0), stop=False))
            for kc in range(8):
                S.op('pe', ['hTs', 'W2'], ['P4'], lambda e, kc=kc, j=j, c0=c0: e.matmul(out=P[4][:, j * 128:(j + 1) * 128], lhsT=W2[:, kc, c0:c0 + 128], rhs=hTs[:, kc, :], start=False, stop=(kc == 7)))
        S.op('act', ['P4'], ['lin'], lambda e: e.activation(out=lin[0:64, 0, :], in_=P[4][0:64, 0:128], func=AF.Tanh))
        S.op('act', ['P4'], ['lin'], lambda e: e.copy(out=lin[64:128, 0, :], in_=P[4][64:128, 0:128]))
        S.op('act', ['P4'], ['lin'], lambda e: e.activation(out=lin[:, 1, :], in_=P[4][:, 128:256], func=AF.Sigmoid))
        for kc in range(8):
            S.op('pe', ['hT', 'Wat'], ['P5'], lambda e, kc=kc: e.matmul(out=P[5][:], lhsT=hT[:, kc, :], rhs=Wat[:, kc, 0:512], start=(kc == 0), stop=(kc == 7)))
        for kc in range(8):
            S.op('pe', ['hT', 'Wat'], ['P6'], lambda e, kc=kc: e.matmul(out=P[6][:, 0:256], lhsT=hT[:, kc, :], rhs=Wat[:, kc, 512:768], start=(kc == 0), stop=(kc == 7)))
        S.op('act', ['P5'], ['qkv'], lambda e: e.copy(out=qkv[:, 0:512], in_=P[5][:]))
        S.op('act', ['P6'], ['qkv'], lambda e: e.copy(out=qkv[:, 512:768], in_=P[6][:, 0:256]))
        DUMP("a_r", A["r"], 'a_r', ti); DUMP("a_v", A["v"], 'a_v', ti); DUMP("lin", lin, 'lin', ti); DUMP("qkv0", qkv, 'qkv', ti)
        if OPTS['stage'] <= 2:
            return
        S.op('pe', ['lin', 'L_w2'], ['P1'], lambda e: e.matmul(out=P[1][:], lhsT=lin[0:64, 0, :], rhs=L_w2[0:64, :], start=True, stop=True))
        S.op('pe', ['lin', 'L_w2'], ['P2'], lambda e: e.matmul(out=P[2][:], lhsT=lin[64:128, 0, :], rhs=L_w2[64:128, :], start=True, stop=True))
        S.op('pe', ['lin', 'L_g2'], ['P3'], lambda e: e.matmul(out=P[3][:], lhsT=lin[:, 1, :], rhs=L_g2[:], start=True, stop=True))
        S.op('dve', ['P1', 'c_v512'], ['a_t1'], lambda e: e.tensor_tensor(out=A["t1"][:], in0=P[1][:], in1=V512(W0), op=ALU.add))
        S.op('act', ['a_t1'], ['a_t1'], lambda e: e.activation(out=A["t1"][:], in_=A["t1"][:], func=AF.Sigmoid))
        S.op('dve', ['a_t1', 'c_vm'], ['a_ld'], lambda e: e.tensor_scalar(out=A["ld"][:], in0=A["t1"][:], scalar1=vm, scalar2=-0.6065306597,
                                                                           op0=ALU.mult, op1=ALU.mult))
        S.op('dve', ['P2', 'c_v512'], ['a_t2'], lambda e: e.tensor_tensor(out=A["t2"][:], in0=P[2][:], in1=V512(A0), op=ALU.add))
        S.op('act', ['a_t2'], ['a_asig'], lambda e: e.activation(out=A["asig"][:], in_=A["t2"][:], func=AF.Sigmoid))
        S.op('act', ['P3'], ['a_g'], lambda e: e.copy(out=A["g"][:], in_=P[3][:]))
        S.op('pool', ['a_k', 'c_v512'], ['a_kk'], lambda e: e.tensor_tensor(out=A["kk"][:], in0=A["k"][:], in1=V512(KK), op=ALU.mult))
        S.op('pool', ['a_kk'], ['a_t3'], lambda e: e.tensor_tensor(out=A["t3"][:], in0=A["kk"][:], in1=A["kk"][:], op=ALU.mult))
        S.op('dve', ['a_t3'], ['st8'], lambda e: e.tensor_reduce(out=st8[:, :, 0], in_=v3(A["t3"][:]), axis=AX.X, op=ALU.add))
        rsq('st8', st8[:, :, 1], st8[:, :, 0], 1e-24, ALU.max)
        S.op('dve', ['a_kk', 'st8'], ['a_kk'], lambda e: e.tensor_tensor(out=v3(A["kk"][:]), in0=v3(A["kk"][:]), in1=bc_last(st8[:, :, 1:2], 64), op=ALU.mult))
        S.op('dve', ['a_kk', 'a_asig', 'c_vm'], ['a_b'], lambda e: e.scalar_tensor_tensor(out=A["b"][:], in0=A["kk"][:], scalar=vm, in1=A["asig"][:], op0=ALU.mult, op1=ALU.mult))
        S.op('dve', ['a_asig', 'c_v512'], ['a_t2'], lambda e: e.scalar_tensor_tensor(out=A["t2"][:], in0=A["asig"][:], scalar=-1.0, in1=V512(KA), op0=ALU.add, op1=ALU.mult))
        S.op('dve', ['a_t2', 'a_k'], ['a_kmod'], lambda e: e.scalar_tensor_tensor(out=A["kmod"][:], in0=A["t2"][:], scalar=1.0, in1=A["k"][:], op0=ALU.add, op1=ALU.mult))
        S.op('pe', ['a_ld', 'c_cst'], ['P1'], lambda e: e.matmul(out=P[1][:], lhsT=tri_f, rhs=A["ld"][:], start=True, stop=True))
        S.op('pe', ['a_ld', 'c_cst'], ['P2'], lambda e: e.matmul(out=P[2][:], lhsT=ones_f, rhs=A["ld"][:], start=True, stop=True))
        for h in range(8):
            S.op('pe', ['a_ld', 'c_cst'], ['P3'], lambda e, h=h: e.matmul(out=P[3][0:64, h:h + 1], lhsT=A["ld"][:, h * 64:(h + 1) * 64], rhs=ones_f[:, 0:1], start=True, stop=True))
        S.op('act', ['P3'], ['ecl_fm'], lambda e: e.activation(out=ecl_fm[:], in_=P[3][0:64, 0:8], func=AF.Exp))
        S.op('act', ['P1'], ['a_c'], lambda e: e.copy(out=A["c"][:], in_=P[1][:]))
        S.op('act', ['a_c'], ['a_t1'], lambda e: e.activation(out=A["t1"][:], in_=A["c"][:], func=AF.Exp))
        S.op('dve', ['a_t1', 'a_r'], ['b_rt'], lambda e: e.tensor_tensor(out=Bt["rt"][:], in0=A["r"][:], in1=A["t1"][:], op=ALU.mult))
        S.op('pool', ['a_c', 'a_ld'], ['a_t2'], lambda e: e.tensor_tensor(out=A["t2"][:], in0=A["c"][:], in1=A["ld"][:], op=ALU.subtract))
        S.op('act', ['a_t2'], ['a_t2'], lambda e: e.activation(out=A["t2"][:], in_=A["t2"][:], func=AF.Exp))
        S.op('dve', ['a_t2', 'a_kk'], ['b_at'], lambda e: e.scalar_tensor_tensor(out=Bt["at"][:], in0=A["kk"][:], scalar=-1.0, in1=A["t2"][:], op0=ALU.mult, op1=ALU.mult))
        S.op('act', ['a_c'], ['a_t3'], lambda e: e.activation(out=A["t3"][:], in_=A["c"][:], func=AF.Exp, scale=-1.0))
        S.op('dve', ['a_t3', 'a_b'], ['b_bt'], lambda e: e.tensor_tensor(out=Bt["bt"][:], in0=A["b"][:], in1=A["t3"][:], op=ALU.mult))
        S.op('pool', ['a_t3', 'a_kmod'], ['b_kt'], lambda e: e.tensor_tensor(out=Bt["kt"][:], in0=A["kmod"][:], in1=A["t3"][:], op=ALU.mult))
        S.op('dve', ['P2', 'a_c'], ['a_t1'], lambda e: e.tensor_tensor(out=A["t1"][:], in0=P[2][:], in1=A["c"][:], op=ALU.subtract))
        S.op('act', ['a_t1'], ['a_t1'], lambda e: e.activation(out=A["t1"][:], in_=A["t1"][:], func=AF.Exp))
        S.op('dve', ['a_t1', 'a_b'], ['b_bbar'], lambda e: e.tensor_tensor(out=Bt["bbar"][:], in0=A["b"][:], in1=A["t1"][:], op=ALU.mult))
        S.op('pool', ['a_t1', 'a_kmod'], ['b_kbar'], lambda e: e.tensor_tensor(out=Bt["kbar"][:], in0=A["kmod"][:], in1=A["t1"][:], op=ALU.mult))
        S.op('pool', ['a_r', 'c_v512'], ['a_t2'], lambda e: e.tensor_tensor(out=A["t2"][:], in0=A["r"][:], in1=V512(RK), op=ALU.mult))
        S.op('pool', ['a_t2', 'a_kmod'], ['a_t2'], lambda e: e.tensor_tensor(out=A["t2"][:], in0=A["t2"][:], in1=A["kmod"][:], op=ALU.mult))
        S.op('dve', ['a_t2'], ['st8'], lambda e: e.tensor_reduce(out=st8[:, :, 2], in_=v3(A["t2"][:]), axis=AX.X, op=ALU.add))
        DUMP("a_ld", A["ld"], 'a_ld', ti); DUMP("a_c", A["c"], 'a_c', ti); DUMP("a_kk", A["kk"], 'a_kk', ti); DUMP("b_rt", Bt["rt"], 'b_rt', ti); DUMP("b_at", Bt["at"], 'b_at', ti); DUMP("b_bt", Bt["bt"], 'b_bt', ti); DUMP("b_kbar", Bt["kbar"], 'b_kbar', ti); DUMP("ecl_fm", ecl_fm, 'ecl_fm', ti)
        if OPTS['stage'] <= 3:
            return
        pTb = [P[i][:].bitcast(BF16) for i in range(8)]
        for qi, (nm, bank) in enumerate([("at", 4), ("rt", 5), ("bt", 6), ("kt", 7)]):
            for h in range(8):
                S.op('pe', ['b_' + nm, 'c_cstb'], [PK[bank]], lambda e, h=h, nm=nm, bank=bank: e.transpose(
                    out=pTb[bank][0:64, h * 128:(h + 1) * 128], in_=Bt[nm][:, h * 64:(h + 1) * 64], identity=ident_b))
        S.op('act', ['P4'], ['AR_fm'], lambda e: e.copy(out=AR_fm[:, :, 0:128], in_=pTb[4][0:64, 0:1024].rearrange("p (h t) -> p h t", h=8)))
        S.op('dve', ['P5'], ['AR_fm'], lambda e: e.tensor_copy(out=AR_fm[:, :, 128:256], in_=pTb[5][0:64, 0:1024].rearrange("p (h t) -> p h t", h=8)))
        S.op('act', ['P6'], ['B_fm'], lambda e: e.copy(out=B_fm[:], in_=pTb[6][0:64, 0:1024].rearrange("p (h t) -> p h t", h=8)))
        S.op('dve', ['P7'], ['K_fm'], lambda e: e.tensor_copy(out=K_fm[:], in_=pTb[7][0:64, 0:1024].rearrange("p (h t) -> p h t", h=8)))
        for h in range(8):
            bank = h % 2
            S.op('pe', ['B_fm', 'AR_fm'], [PK[bank]], lambda e, h=h, bank=bank: e.matmul(out=P[bank][:, 0:256], lhsT=B_fm[:, h, :], rhs=AR_fm[:, h, :], start=True, stop=True))
            S.op('pe', ['K_fm', 'AR_fm'], [PK[bank]], lambda e, h=h, bank=bank: e.matmul(out=P[bank][:, 256:512], lhsT=K_fm[:, h, :], rhs=AR_fm[:, h, :], start=True, stop=True))
            S.op('dve', [PK[bank], 'c_m4'], ['MATS'], lambda e, h=h, bank=bank: e.tensor_tensor(out=MATS[:, h, :], in0=P[bank][:], in1=c_m4[:].rearrange("p a b -> p (a b)"), op=ALU.mult))
        for hh in range(2):
            bank = 2 + hh
            for h4 in range(4):
                h = hh * 4 + h4
                S.op('pe', ['B_fm', 'AR_fm'], [PK[bank]], lambda e, h=h, h4=h4, bank=bank: e.matmul(out=P[bank][:, h4 * 128:(h4 + 1) * 128], lhsT=AR_fm[:, h, 0:128], rhs=B_fm[:, h, :], start=True, stop=True))
            S.op('dve', [PK[bank], 'c_cst'], ['MTb%d' % hh], lambda e, hh=hh, bank=bank: e.tensor_tensor(
                out=MTb[0][:, hh * 4:(hh + 1) * 4, :], in0=P[bank][:].rearrange("p (h t) -> p h t", h=4),
                in1=c_low.unsqueeze(1).broadcast_to([128, 4, 128]), op=ALU.mult))
        S.op('act', ['MATS'], ['Mb0', 'Mb1'], lambda e: e.copy(out=Mb[0][:], in_=MATS[:, :, 0:128]))
        S.op('pool', ['MATS', 'c_cstb'], ['Tb0', 'Tb1'], lambda e: e.tensor_tensor(out=Tb[0][:], in0=MATS[:, :, 0:128], in1=ident_b.unsqueeze(1).broadcast_to([128, 8, 128]), op=ALU.add))
        cur = 0
        LAST = 1 if samp else 6
        for lvl in range(1, LAST + 1):
            nxt = 1 - cur
            for hh in range(2):
                hs = slice(hh * 4, hh * 4 + 4)
                bM, bMT, bT = 2 + hh * 3, 3 + hh * 3, 4 + hh * 3
                for h4 in range(4):
                    h = hh * 4 + h4
                    cs = slice(h4 * 128, (h4 + 1) * 128)
                    if lvl < LAST:
                        S.op('pe', ['Mb%d' % hh, 'MTb%d' % hh], [PK[bM]], lambda e, h=h, cs=cs, bM=bM, cur=cur: e.matmul(out=P[bM][:, cs], lhsT=MTb[cur][:, h, :], rhs=Mb[cur][:, h, :], start=True, stop=True))
                    S.op('pe', ['Mb%d' % hh, 'MTb%d' % hh], [PK[bMT]], lambda e, h=h, cs=cs, bMT=bMT, cur=cur: e.matmul(out=P[bMT][:, cs], lhsT=Mb[cur][:, h, :], rhs=MTb[cur][:, h, :], start=True, stop=True))
                if lvl < LAST:
                    S.op('act', [PK[bM]], ['Mb%d' % hh], lambda e, hs=hs, bM=bM, nxt=nxt: e.copy(out=Mb[nxt][:, hs, :], in_=P[bM][:].rearrange("p (h t) -> p h t", h=4)))
                S.op('dve', [PK[bMT]], ['MTb%d' % hh], lambda e, hs=hs, bMT=bMT, nxt=nxt: e.tensor_copy(out=MTb[nxt][:, hs, :], in_=P[bMT][:].rearrange("p (h t) -> p h t", h=4)))
                for h4 in range(4):
                    h = hh * 4 + h4
                    cs = slice(h4 * 128, (h4 + 1) * 128)
                    S.op('pe', ['MTb%d' % hh, 'Tb%d' % hh], [PK[bT]], lambda e, h=h, cs=cs, bT=bT, cur=cur, nxt=nxt: e.matmul(out=P[bT][:, cs], lhsT=MTb[nxt][:, h, :], rhs=Tb[cur][:, h, :], start=True, stop=True))
                S.op('dve', [PK[bT], 'Tb%d' % hh], ['Tb%d' % hh], lambda e, hs=hs, bT=bT, cur=cur, nxt=nxt: e.tensor_tensor(
                    out=Tb[nxt][:, hs, :], in0=P[bT][:].rearrange("p (h t) -> p h t", h=4), in1=Tb[cur][:, hs, :], op=ALU.add))
            cur = nxt
        Tf = Tb[cur]
        TfK = 'Tb'
        DUMP("AR_fm", AR_fm, 'AR_fm', ti); DUMP("K_fm", K_fm, 'K_fm', ti); DUMP("MATS", MATS, 'MATS', ti); DUMP("Tb", Tb[0], 'Tb0', ti)
        if OPTS['stage'] <= 4:
            return
        if samp or ti == 0:
            if samp:
                S.dma('sp', sti[:], swkv[sq].rearrange("h i j -> i h j"), [], ['sti'])
                for h in range(8):
                    S.op('pe', ['sti', 'c_cst'], ['P0'], lambda e, h=h: e.transpose(out=P[0][0:64, h * 64:(h + 1) * 64], in_=sti[:, h, :], identity=ident_f[0:64, 0:64]))
                S.op('dve', ['P0'], ['S32'], lambda e: e.tensor_copy(out=S32[:], in_=P[0][0:64, :].rearrange("p (h i) -> p h i", h=8)))
            else:
                S.op('dve', [], ['S32'], lambda e: e.memset(S32[:], 0.0))
            S.op('act', ['S32'], ['Sb'], lambda e: e.copy(out=Sb[:], in_=S32[:]))
        for h in range(8):
            cs = slice(h * 64, (h + 1) * 64)
            S.op('pe', ['AR_fm', 'Sb'], ['P0'], lambda e, h=h, cs=cs: e.matmul(out=P[0][:, cs], lhsT=AR_fm[:, h, 0:128], rhs=Sb[:, h, :], start=True, stop=False))
            S.op('pe', ['MATS', 'b_vb'], ['P0'], lambda e, h=h, cs=cs: e.matmul(out=P[0][:, cs], lhsT=MATS[:, h, 256:384], rhs=Bt["vb"][:, cs], start=False, stop=True))
        S.op('act', ['P0'], ['b_W0T'], lambda e: e.copy(out=Bt["W0T"][:], in_=P[0][:]))
        for h in range(8):
            cs = slice(h * 64, (h + 1) * 64)
            S.op('pe', ['Tb%d' % (h // 4), 'b_W0T'], ['P1'], lambda e, h=h, cs=cs: e.matmul(out=P[1][:, cs], lhsT=Tf[:, h, :], rhs=Bt["W0T"][:, cs], start=True, stop=True))
        S.op('act', ['P1'], ['b_UT'], lambda e: e.copy(out=Bt["UT"][:], in_=P[1][:]))
        for h in range(8):
            cs = slice(h * 64, (h + 1) * 64)
            S.op('pe', ['AR_fm', 'Sb'], ['P0'], lambda e, h=h, cs=cs: e.matmul(out=P[0][:, cs], lhsT=AR_fm[:, h, 128:256], rhs=Sb[:, h, :], start=True, stop=False))
            S.op('pe', ['MATS', 'b_UT'], ['P0'], lambda e, h=h, cs=cs: e.matmul(out=P[0][:, cs], lhsT=MATS[:, h, 128:256], rhs=Bt["UT"][:, cs], start=False, stop=False))
            S.op('pe', ['MATS', 'b_vb'], ['P0'], lambda e, h=h, cs=cs: e.matmul(out=P[0][:, cs], lhsT=MATS[:, h, 384:512], rhs=Bt["vb"][:, cs], start=False, stop=True))
        for h in range(8):
            cs = slice(h * 64, (h + 1) * 64)
            S.op('pe', ['b_bbar', 'b_UT'], ['P1'], lambda e, h=h, cs=cs: e.matmul(out=P[1][0:64, cs], lhsT=Bt["bbar"][:, cs], rhs=Bt["UT"][:, cs], start=True, stop=False))
            S.op('pe', ['b_kbar', 'b_vb'], ['P1'], lambda e, h=h, cs=cs: e.matmul(out=P[1][0:64, cs], lhsT=Bt["kbar"][:, cs], rhs=Bt["vb"][:, cs], start=False, stop=True))
        S.op('dve', ['S32', 'ecl_fm'], ['S32'], lambda e: e.tensor_tensor(out=S32[:], in0=S32[:], in1=ecl_fm[:].unsqueeze(2).broadcast_to([64, 8, 64]), op=ALU.mult))
        S.op('dve', ['S32', 'P1'], ['S32'], lambda e: e.tensor_tensor(out=S32[:], in0=S32[:], in1=P[1][0:64, :].rearrange("p (h i) -> p h i", h=8), op=ALU.add))
        S.op('act', ['S32'], ['Sb'], lambda e: e.copy(out=Sb[:], in_=S32[:]))
        if samp or ti == OUT_T:
            for h in range(8):
                S.op('pe', ['S32', 'c_cst'], ['P2'], lambda e, h=h: e.transpose(out=P[2][0:64, h * 64:(h + 1) * 64], in_=S32[:, h, :], identity=ident_f[0:64, 0:64]))
            S.op('act', ['P2'], ['sti'], lambda e: e.copy(out=sti[:], in_=P[2][0:64, :].rearrange("p (h j) -> p h j", h=8)))
            dst = wkv_s[sq] if samp else wkv_p
            S.dma('sp', dst.rearrange("h i j -> i h j"), sti[:], ['sti'], [])
        DUMP("S32", S32, 'S32', ti); DUMP("b_UT", Bt["UT"], 'b_UT', ti); DUMP("b_W0T", Bt["W0T"], 'b_W0T', ti)
        if OPTS['stage'] <= 5:
            return
        state_only = ti < NT_A
        if state_only and ti != NT_A - 1:
            return
        if not state_only:
            rwkv_post(ti)
        attn_and_out(ti, samp, sq, xk, xb, par, state_only)

    def rwkv_post(ti):
        if True:
            pass
        Y3 = v3(P[0][:])
        S.op('dve', ['P0'], ['st8'], lambda e: e.tensor_reduce(out=st8[:, :, 0], in_=Y3, axis=AX.X, op=ALU.add))
        S.op('dve', ['st8'], ['st8'], lambda e: e.tensor_scalar(out=st8[:, :, 0], in0=st8[:, :, 0], scalar1=1.0 / 64, scalar2=None, op0=ALU.mult))
        S.op('dve', ['P0', 'st8'], ['a_t1'], lambda e: e.tensor_tensor(out=v3(A["t1"][:]), in0=Y3, in1=bc_last(st8[:, :, 0:1], 64), op=ALU.subtract))
        S.op('pool', ['a_t1'], ['a_t2'], lambda e: e.tensor_tensor(out=A["t2"][:], in0=A["t1"][:], in1=A["t1"][:], op=ALU.mult))
        S.op('dve', ['a_t2'], ['st8'], lambda e: e.tensor_reduce(out=st8[:, :, 1], in_=v3(A["t2"][:]), axis=AX.X, op=ALU.add))
        S.op('dve', ['st8'], ['st8'], lambda e: e.tensor_scalar(out=st8[:, :, 1], in0=st8[:, :, 1], scalar1=1.0 / 64, scalar2=64e-5, op0=ALU.mult, op1=ALU.add))
        rsq('st8', st8[:, :, 1], st8[:, :, 1], 0.0, ALU.add)
        S.op('dve', ['a_t1', 'st8'], ['a_t1'], lambda e: e.tensor_tensor(out=v3(A["t1"][:]), in0=v3(A["t1"][:]), in1=bc_last(st8[:, :, 1:2], 64), op=ALU.mult))
        S.op('pool', ['a_t1', 'c_v512'], ['a_t1'], lambda e: e.tensor_tensor(out=A["t1"][:], in0=A["t1"][:], in1=V512(LW), op=ALU.mult))
        S.op('pool', ['a_t1', 'c_v512'], ['a_t1'], lambda e: e.tensor_tensor(out=A["t1"][:], in0=A["t1"][:], in1=V512(LB), op=ALU.add))
        S.op('dve', ['a_v', 'st8'], ['a_t2'], lambda e: e.tensor_tensor(out=v3(A["t2"][:]), in0=v3(A["v"][:]), in1=bc_last(st8[:, :, 2:3], 64), op=ALU.mult))
        S.op('pool', ['a_t1', 'a_t2'], ['a_t1'], lambda e: e.tensor_tensor(out=A["t1"][:], in0=A["t1"][:], in1=A["t2"][:], op=ALU.add))
        S.op('dve', ['a_t1', 'a_g'], ['ycat'], lambda e: e.tensor_tensor(out=ycat[:, 0:512], in0=A["t1"][:], in1=A["g"][:], op=ALU.mult))
        DUMP("ycat_rw", ycat[:, 0:512], 'ycat', ti)

    def attn_and_out(ti, samp, sq, xk, xb, par, state_only):
        pTb = [P[i][:].bitcast(BF16) for i in range(8)]
        if OPTS['stage'] <= 6:
            return
        ri = NT_P if samp else ti
        cosb = c_rope[:, ri, 0:8].unsqueeze(1)
        sinb = c_rope[:, ri, 8:16].unsqueeze(1)
        for (c0, nh) in [(0, 8), (512, 2)]:
            X = qkv[:, c0:c0 + nh * 64].rearrange("p (h j) -> p h j", h=nh)
            x1, x2 = X[:, :, 0:8], X[:, :, 8:16]
            cb = cosb.broadcast_to([128, nh, 8])
            sbb = sinb.broadcast_to([128, nh, 8])
            R = rtmp[:, 0:nh, :]
            T1 = rtmp[:, 0:nh, :]
            ra = rtmp[:].rearrange("p a b -> p (a b)")
            t_a = ra[:, 0:nh * 8].rearrange("p (h j) -> p h j", h=nh)
            t_b = ra[:, 80 - 0:80].rearrange("p (h j) -> p h j", h=1) if False else None
            S.op('dve', ['qkv', 'c_rope'], ['rtmp'], lambda e, x1=x1, cb=cb, t_a=t_a: e.tensor_tensor(out=t_a, in0=x1, in1=cb, op=ALU.mult))
            S.op('dve', ['qkv', 'c_rope'], ['sm'], lambda e, x2=x2, sbb=sbb, nh=nh: e.tensor_tensor(out=sm[:, 0:nh * 8].rearrange("p (h j) -> p h j", h=nh), in0=x2, in1=sbb, op=ALU.mult))
            S.op('dve', ['qkv', 'c_rope'], ['sm'], lambda e, x2=x2, cb=cb, nh=nh: e.tensor_tensor(out=sm[:, 64:64 + nh * 8].rearrange("p (h j) -> p h j", h=nh), in0=x2, in1=cb, op=ALU.mult))
            S.op('dve', ['qkv', 'c_rope'], ['sm'], lambda e, x1=x1, sbb=sbb, nh=nh: e.tensor_tensor(out=sm[:, 128:128 + nh * 8].rearrange("p (h j) -> p h j", h=nh), in0=x1, in1=sbb, op=ALU.mult))
            S.op('dve', ['rtmp', 'sm'], ['qkv'], lambda e, x1=x1, t_a=t_a, nh=nh: e.tensor_tensor(out=x1, in0=t_a, in1=sm[:, 0:nh * 8].rearrange("p (h j) -> p h j", h=nh), op=ALU.subtract))
            S.op('dve', ['sm'], ['qkv'], lambda e, x2=x2, nh=nh: e.tensor_tensor(out=x2, in0=sm[:, 64:64 + nh * 8].rearrange("p (h j) -> p h j", h=nh), in1=sm[:, 128:128 + nh * 8].rearrange("p (h j) -> p h j", h=nh), op=ALU.add))
        S.op('act', ['qkv'], ['qb'], lambda e: e.copy(out=qb[:], in_=qkv[:]))
        S.op('pool', ['qkv'], ['Vb%d' % par], lambda e: e.tensor_copy(out=Vb[par][:], in_=qkv[:, 640:768]))
        for h in range(8):
            S.op('pe', ['qb', 'c_cstb'], ['P2'], lambda e, h=h: e.transpose(out=pTb[2][0:64, h * 128:(h + 1) * 128], in_=qb[:, h * 64:(h + 1) * 64], identity=ident_b))
        for kv in range(2):
            S.op('pe', ['qb', 'c_cstb'], ['P3'], lambda e, kv=kv: e.transpose(out=pTb[3][0:64, kv * 128:(kv + 1) * 128], in_=qb[:, 512 + kv * 64:512 + (kv + 1) * 64], identity=ident_b))
        S.op('act', ['P2'], ['QT'], lambda e: e.copy(out=QT[:], in_=pTb[2][0:64, 0:1024].rearrange("p (h t) -> p h t", h=8)))
        S.op('dve', ['P3'], ['KT%d' % par], lambda e: e.tensor_copy(out=KT[par][:], in_=pTb[3][0:64, 0:256].rearrange("p (h t) -> p h t", h=2)))
        pp = 1 - par
        if samp:
            S.dma('sp', ckb[:, 0:128], ck[sq], [], ['sm'])
            S.dma('sp', ckb[:, 128:256], cv[sq], [], ['sm'])
            S.op('act', ['sm'], ['ycatT'], lambda e: e.copy(out=junk[:, 0:256], in_=ckb[:]))
            for kv in range(2):
                S.op('pe', ['ycatT', 'c_cstb'], ['P3'], lambda e, kv=kv: e.transpose(out=pTb[3][0:64, 256 + kv * 128:256 + (kv + 1) * 128], in_=junk[:, kv * 64:(kv + 1) * 64], identity=ident_b))
            S.op('dve', ['P3'], ['KT%d' % pp], lambda e: e.tensor_copy(out=KT[pp][:], in_=pTb[3][0:64, 256:512].rearrange("p (h t) -> p h t", h=2)))
            S.op('pool', ['ycatT'], ['Vb%d' % pp], lambda e: e.tensor_copy(out=Vb[pp][:], in_=junk[:, 128:256]))
            S.dma('sp', kw_s[sq, 0:124, :], ck[sq, 4:128, :], [], [])
            S.dma('sp', vw_s[sq, 0:124, :], cv[sq, 4:128, :], [], [])
            S.dma('sp', kw_s[sq, 124:128, :], qkv[0:4, 512:640], ['qkv'], [])
            S.dma('sp', vw_s[sq, 124:128, :], qkv[0:4, 640:768], ['qkv'], [])
        elif ti == OUT_T:
            S.dma('sp', kw_p, qkv[:, 512:640], ['qkv'], [])
            S.dma('sp', vw_p, qkv[:, 640:768], ['qkv'], [])
        if state_only:
            return
        mi = 2 if (samp or ti - NT_A > 1) else (ti - NT_A)
        SMB = [A["t1"], A["t2"], A["t3"], A["c"]]
        SMK = ['a_t1', 'a_t2', 'a_t3', 'a_c']
        EBB = [A["kk"][:].bitcast(BF16), A["b"][:].bitcast(BF16)]
        EBK = ['a_kk', 'a_b']
        ETB = [A["kmod"][:].bitcast(BF16), A["ld"][:].bitcast(BF16)]
        ETK = ['a_kmod', 'a_ld']
        for h in range(8):
            kv = h // 4
            bank = 4 + h // 2
            c0 = (h % 2) * 256
            S.op('pe', ['QT', 'KT%d' % pp], [PK[bank]], lambda e, h=h, kv=kv, bank=bank, c0=c0: e.matmul(out=P[bank][:, c0:c0 + 128], lhsT=QT[:, h, :], rhs=KT[pp][:, kv, :], start=True, stop=True))
            S.op('pe', ['QT', 'KT%d' % par], [PK[bank]], lambda e, h=h, kv=kv, bank=bank, c0=c0: e.matmul(out=P[bank][:, c0 + 128:c0 + 256], lhsT=QT[:, h, :], rhs=KT[par][:, kv, :], start=True, stop=True))
        for j in range(4):
            S.op('dve', [PK[4 + j], 'c_amask'], [SMK[j]], lambda e, j=j: e.scalar_tensor_tensor(
                out=SMB[j][:].rearrange("p (h c) -> p h c", h=2), in0=P[4 + j][:].rearrange("p (h c) -> p h c", h=2), scalar=0.125,
                in1=c_amask[:, mi, :].unsqueeze(1).broadcast_to([128, 2, 256]), op0=ALU.mult, op1=ALU.add))
            S.op('dve', [SMK[j]], ['ast'], lambda e, j=j: e.tensor_reduce(out=ast8[:, 0, 2 * j:2 * j + 2], in_=SMB[j][:].rearrange("p (h c) -> p h c", h=2), axis=AX.X, op=ALU.max))
        S.op('dve', ['ast', 'c_sink'], ['ast'], lambda e: e.tensor_tensor(out=ast8[:, 1, :], in0=ast8[:, 0, :], in1=c_sink[:], op=ALU.max))
        S.op('dve', ['ast'], ['ast'], lambda e: e.tensor_scalar(out=ast8[:, 1, :], in0=ast8[:, 1, :], scalar1=-1.0, scalar2=None, op0=ALU.mult))
        for h in range(8):
            S.op('act', [SMK[h // 2], 'ast'], [EBK[h // 4], 'ast2'], lambda e, h=h: e.activation(
                out=EBB[h // 4][:, (h % 4) * 256:(h % 4 + 1) * 256], in_=SMB[h // 2][:, (h % 2) * 256:(h % 2 + 1) * 256], func=AF.Exp,
                bias=ast8[:, 1, h:h + 1], scale=1.0, accum_out=ast8[:, 2, h:h + 1]))
        S.op('dve', ['ast', 'c_sink'], ['ast3'], lambda e: e.tensor_tensor(out=ast8[:, 3, :], in0=ast8[:, 1, :], in1=c_sink[:], op=ALU.add))
        S.op('act', ['ast3'], ['ast3'], lambda e: e.activation(out=ast8[:, 3, :], in_=ast8[:, 3, :], func=AF.Exp))
        S.op('dve', ['ast2', 'ast3'], ['ast3'], lambda e: e.tensor_tensor(out=ast8[:, 3, :], in0=ast8[:, 3, :], in1=ast8[:, 2, :], op=ALU.add))
        S.op('dve', ['ast3'], ['ast3'], lambda e: e.reciprocal(out=ast8[:, 3, :], in_=ast8[:, 3, :]))
        for h in range(8):
            for half in range(2):
                S.op('pe', [EBK[h // 4], 'c_cstb'], [PK[h // 4]], lambda e, h=h, half=half: e.transpose(
                    out=pTb[h // 4][:, (h % 4) * 256 + half * 128:(h % 4) * 256 + (half + 1) * 128],
                    in_=EBB[h // 4][:, (h % 4) * 256 + half * 128:(h % 4) * 256 + (half + 1) * 128], identity=ident_b))
        S.op('act', ['P0'], [ETK[0]], lambda e: e.copy(out=ETB[0], in_=pTb[0][:, 0:1024]))
        S.op('dve', ['P1'], [ETK[1]], lambda e: e.tensor_copy(out=ETB[1], in_=pTb[1][:, 0:1024]))
        for h in range(8):
            kv = h // 4
            o0 = (h % 4) * 256
            S.op('pe', [ETK[h // 4], 'Vb%d' % pp], ['P2'], lambda e, h=h, kv=kv, o0=o0: e.matmul(out=P[2][:, h * 64:(h + 1) * 64], lhsT=ETB[h // 4][:, o0:o0 + 128], rhs=Vb[pp][:, kv * 64:(kv + 1) * 64], start=True, stop=False))
            S.op('pe', [ETK[h // 4], 'Vb%d' % par], ['P2'], lambda e, h=h, kv=kv, o0=o0: e.matmul(out=P[2][:, h * 64:(h + 1) * 64], lhsT=ETB[h // 4][:, o0 + 128:o0 + 256], rhs=Vb[par][:, kv * 64:(kv + 1) * 64], start=False, stop=True))
        S.op('dve', ['P2', 'ast3'], ['ycat'], lambda e: e.tensor_tensor(out=ycat[:, 512:1024].rearrange("p (h j) -> p h j", h=8), in0=P[2][:].rearrange("p (h j) -> p h j", h=8),
                                                                       in1=ast8[:, 3, :].unsqueeze(2).broadcast_to([128, 8, 64]), op=ALU.mult))
        DUMP("ycat", ycat, 'ycat', ti); DUMP("qkv", qkv, 'qkv', ti); DUMP("QT", QT, 'QT', ti)
        if OPTS['stage'] <= 7:
            return
        for kc in range(8):
            S.op('pe', ['ycat', 'c_cstb'], ['P2'], lambda e, kc=kc: e.transpose(out=pTb[2][:, kc * 128:(kc + 1) * 128], in_=ycat[:, kc * 128:(kc + 1) * 128], identity=ident_b))
        S.op('act', ['P2'], ['ycatT'], lambda e: e.copy(out=ycatT[:], in_=pTb[2][:, 0:1024].rearrange("p (k t) -> p k t", k=8)))
        for half in range(2):
            bank = 3 + half
            for kc in range(8):
                S.op('pe', ['ycatT', 'Wo'], [PK[bank]], lambda e, kc=kc, half=half, bank=bank: e.matmul(out=P[bank][:], lhsT=ycatT[:, kc, :], rhs=Wo[:, kc, half * 512:(half + 1) * 512], start=(kc == 0), stop=(kc == 7)))
            S.op('dve', [PK[bank], xk], ['xm'], lambda e, half=half, bank=bank: e.tensor_tensor(out=xm[:, half * 512:(half + 1) * 512], in0=P[bank][:], in1=xb[:, half * 512:(half + 1) * 512], op=ALU.add))
        if dbg and ti == OPTS['dbg_tile']:
            S.op('dve', ['ycat'], ['a_t1'], lambda e: e.tensor_copy(out=A["t1"][:], in_=ycat[:, 0:512]))
            S.op('dve', ['ycat'], ['a_t2'], lambda e: e.tensor_copy(out=A["t2"][:], in_=ycat[:, 512:1024]))
            S.dma('sp', dbg_t["d_y"], A["t1"][:], ['a_t1'], [])
            S.dma('sp', dbg_t["d_at"], A["t2"][:], ['a_t2'], [])
            S.dma('sp', dbg_t["d_S"], S32[:].rearrange("p h i -> p (h i)"), ['S32'], [])
        DUMP("xm", xm, 'xm', ti)
        if samp:
            S.dma('sp', xmid[NT_B * 128 + sq * 4:NT_B * 128 + sq * 4 + 4, :], xm[0:4, :], ['xm'], ['xmid'])
        else:
            S.dma('sp', xmid[(ti - NT_A) * 128:(ti - NT_A + 1) * 128, :], xm[:], ['xm'], ['xmid'])


    def peer_phase():
        c_g2bc = sb("c_g2bc", [128, D])
        S.dma('sp', c_g2bc[:], g2.partition_broadcast(128)[:, 0, :], [], ['c_g2bc'])
        c_gFbc = sb("c_gFbc", [128, D])
        S.dma('sp', c_gFbc[:], gF.partition_broadcast(128)[:, 0, :], [], ['c_gFbc'])
        c_iota = sb("c_iota", [128, 256])
        S.dma('sp', c_iota[:], iota_in, [], ['c_iota'])
        Wq = sb("Wq", [128, 8, 2048], BF16)
        skT = sb("skT", [128, 2, 128], BF16)
        xm2s = [sb("xm2_%d" % i, [128, D]) for i in range(2)]
        hn32 = sb("hn32", [128, D])
        hnb = sb("hnb", [128, D], BF16)
        hn2T = sb("hn2T", [128, 8, 128], BF16)
        qT = sb("qT", [128, 16, 128], BF16)
        s_sb = sb("s_sb", [128, 16, 128])
        s2 = sb("s2", [128, 16, 128])
        tv = sb("tv", [128, 16, 16])
        tiu = sb("tiu", [128, 16, 16], U32)
        tif = sb("tif", [128, 16, 16])
        cand = sb("cand", [128, 8, 256])
        cand2 = sb("cand2", [128, 8, 256])
        cidx = sb("cidx", [128, 8, 256])
        top = sb("top", [128, 8, 16])
        selu = sb("selu", [128, 8, 16], U32)
        self_ = sb("self", [128, 8, 16])
        idxf = sb("idxf", [128, 8, 16])
        idx2 = sb("idx2", [128, 8, 16])
        selu2 = sb("selu2", [128, 2, 8, 16], U32)
        sela = sb("sela", [128, 8, 16])
        selb = sb("selb", [128, 8, 16])
        idxus = [sb("idxu_%d" % i, [128, 128], U32) for i in range(2)]
        gate = sb("gate", [128, 8, 16])
        gst = sb("gst", [128, 8, 2])
        pre = sb("pre", [128, 128])
        wgt = sb("wgt", [128, 128])
        acc = sb("acc", [128, D])
        ss2 = sb("ss2", [128, 4])
        NG = OPTS['ng']
        gb = [sb("gb%d" % i, [128, D]) for i in range(NG)]
        gbs = [sb("gbs%d" % i, [128, D], BF16) for i in range(3)]
        with nc.sbuf_tensor("stg2", [128, 2048], F32) as stg2:
            for kc in range(8):
                S.dma('sp', stg2[:], w_q[kc * 128:(kc + 1) * 128, :], [], ['stg2'])
                S.op('act', ['stg2'], ['Wq'], lambda e, kc=kc: e.copy(out=Wq[:, kc, :], in_=stg2[:]))
            S.dma('sp', stg2[:, 0:256].rearrange("p (c d) -> p c d", c=2), subk.rearrange("c n d -> n c d"), [], ['stg2'])
            S.op('act', ['stg2'], ['hnb'], lambda e: e.copy(out=hnb[:, 0:256], in_=stg2[:, 0:256]))
            for c in range(2):
                S.op('pe', ['hnb', 'c_cstb'], ['P0'], lambda e, c=c: e.transpose(out=P[0][:].bitcast(BF16)[:, c * 128:(c + 1) * 128], in_=hnb[:, c * 128:(c + 1) * 128], identity=ident_b))
            S.op('act', ['P0'], ['skT'], lambda e: e.copy(out=skT[:], in_=P[0][:].bitcast(BF16)[:, 0:256].rearrange("p (c n) -> p c n", c=2)))
            barrier()
        pTb = [P[i][:].bitcast(BF16) for i in range(8)]
        def A_part(pt, bi):
            xm2 = xm2s[bi]
            idxu = idxus[bi]
            kx = 'xm2_%d' % bi
            ki = 'idxu_%d' % bi
            S.dma('sp', xm2[:], xmid[pt * 128:(pt + 1) * 128, :], ['xmid'], [kx])
            S.op('act', [kx], ['hnb', 'ss2'], lambda e: e.activation(out=hnb[:], in_=xm2[:], func=AF.Square, accum_out=ss2[:, 0:1]))
            rsq('ss2', ss2[:, 1:2], ss2[:, 0:1], D * 1e-5, ALU.add)
            S.op('dve', [kx, 'ss2'], ['hn32'], lambda e: e.tensor_scalar(out=hn32[:], in0=xm2[:], scalar1=ss2[:, 1:2], scalar2=32.0, op0=ALU.mult, op1=ALU.mult))
            S.op('pool', ['hn32', 'c_g2bc'], ['hn32'], lambda e: e.tensor_tensor(out=hn32[:], in0=hn32[:], in1=c_g2bc[:], op=ALU.mult))
            S.op('act', ['hn32'], ['hnb'], lambda e: e.copy(out=hnb[:], in_=hn32[:]))
            yield
            for kc in range(8):
                S.op('pe', ['hnb', 'c_cstb'], ['P0'], lambda e, kc=kc: e.transpose(out=pTb[0][:, kc * 128:(kc + 1) * 128], in_=hnb[:, kc * 128:(kc + 1) * 128], identity=ident_b))
            S.op('act', ['P0'], ['hn2T'], lambda e: e.copy(out=hn2T[:], in_=pTb[0][:, 0:1024].rearrange("p (k t) -> p k t", k=8)))
            yield
            for r in range(2):
                for hc in range(8 * r, 8 * r + 8):
                    bank = 1 + (hc % 8) // 4
                    cs = slice((hc % 4) * 128, (hc % 4 + 1) * 128)
                    for kc in range(8):
                        S.op('pe', ['Wq', 'hn2T'], [PK[bank]], lambda e, hc=hc, kc=kc, bank=bank, cs=cs: e.matmul(out=P[bank][:, cs], lhsT=Wq[:, kc, hc * 128:(hc + 1) * 128], rhs=hn2T[:, kc, :], start=(kc == 0), stop=(kc == 7)))
                    yield
                for b in range(2):
                    S.op('act' if b % 2 else 'dve', [PK[1 + b]], ['qT'], lambda e, b=b, r=r: (e.copy if b % 2 else e.tensor_copy)(out=qT[:, r * 8 + b * 4:r * 8 + (b + 1) * 4, :], in_=P[1 + b][:].rearrange("p (a t) -> p a t", a=4)))
                yield
            for r in range(2):
                for hc in range(8 * r, 8 * r + 8):
                    bank = 5 + (hc % 8) // 4
                    cs = slice((hc % 4) * 128, (hc % 4 + 1) * 128)
                    S.op('pe', ['qT', 'skT'], [PK[bank]], lambda e, hc=hc, bank=bank, cs=cs: e.matmul(out=P[bank][:, cs], lhsT=qT[:, hc, :], rhs=skT[:, hc % 2, :], start=True, stop=True))
                for b in range(2):
                    S.op('act' if b % 2 else 'dve', [PK[5 + b]], ['s_sb'], lambda e, b=b, r=r: (e.copy if b % 2 else e.tensor_copy)(out=s_sb[:, r * 8 + b * 4:r * 8 + (b + 1) * 4, :], in_=P[5 + b][:].rearrange("p (a t) -> p a t", a=4)))
                yield
            for hc in range(16):
                S.op('dve', ['s_sb'], ['tv'], lambda e, hc=hc: e.max(out=tv[:, hc, 0:8], in_=s_sb[:, hc, :]))
                S.op('dve', ['s_sb', 'tv'], ['tiu'], lambda e, hc=hc: e.max_index(out=tiu[:, hc, 0:8], in_max=tv[:, hc, 0:8], in_values=s_sb[:, hc, :]))
                S.op('dve', ['s_sb', 'tv'], ['s2'], lambda e, hc=hc: e.match_replace(out=s2[:, hc, :], in_to_replace=tv[:, hc, 0:8], in_values=s_sb[:, hc, :], imm_value=-1e30))
                S.op('dve', ['s2'], ['tv'], lambda e, hc=hc: e.max(out=tv[:, hc, 8:16], in_=s2[:, hc, :]))
                S.op('dve', ['s2', 'tv'], ['tiu'], lambda e, hc=hc: e.max_index(out=tiu[:, hc, 8:16], in_max=tv[:, hc, 8:16], in_values=s2[:, hc, :]))
                yield
            S.op('dve', ['tiu'], ['tif'], lambda e: e.tensor_copy(out=tif[:], in_=tiu[:]))
            yield
            tv4 = tv[:].rearrange("p (h c) k -> p h c k", c=2)
            tf4 = tif[:].rearrange("p (h c) k -> p h c k", c=2)
            c4 = lambda t: t[:].rearrange("p h (a b) -> p h a b", a=16)
            S.op('dve', ['tv'], ['cand'], lambda e: e.tensor_tensor(out=c4(cand), in0=tv4[:, :, 0, :].unsqueeze(3).broadcast_to([128, 8, 16, 16]),
                                                                    in1=tv4[:, :, 1, :].unsqueeze(2).broadcast_to([128, 8, 16, 16]), op=ALU.add))
            S.op('dve', ['tif'], ['tif'], lambda e: e.tensor_scalar(out=tf4[:, :, 0, :], in0=tf4[:, :, 0, :], scalar1=128.0, scalar2=None, op0=ALU.mult))
            S.op('dve', ['tif'], ['cidx'], lambda e: e.tensor_tensor(out=c4(cidx), in0=tf4[:, :, 0, :].unsqueeze(3).broadcast_to([128, 8, 16, 16]),
                                                                     in1=tf4[:, :, 1, :].unsqueeze(2).broadcast_to([128, 8, 16, 16]), op=ALU.add))
            for h in range(8):
                S.op('dve', ['cand'], ['top'], lambda e, h=h: e.max(out=top[:, h, 0:8], in_=cand[:, h, :]))
                S.op('dve', ['cand', 'top'], ['selu'], lambda e, h=h: e.max_index(out=selu[:, h, 0:8], in_max=top[:, h, 0:8], in_values=cand[:, h, :]))
                S.op('dve', ['cand', 'top'], ['cand2'], lambda e, h=h: e.match_replace(out=cand2[:, h, :], in_to_replace=top[:, h, 0:8], in_values=cand[:, h, :], imm_value=-1e30))
                S.op('dve', ['cand2'], ['top'], lambda e, h=h: e.max(out=top[:, h, 8:16], in_=cand2[:, h, :]))
                S.op('dve', ['cand2', 'top'], ['selu'], lambda e, h=h: e.max_index(out=selu[:, h, 8:16], in_max=top[:, h, 8:16], in_values=cand2[:, h, :]))
                yield
            S.op('dve', ['selu'], ['self'], lambda e: e.tensor_copy(out=self_[:], in_=selu[:]))
            S.op('dve', ['selu'], ['selu2'], lambda e: e.tensor_scalar(out=selu2[:, 0], in0=selu[:], scalar1=4, scalar2=None, op0=ALU.logical_shift_right))
            S.op('dve', ['selu'], ['selu2'], lambda e: e.tensor_scalar(out=selu2[:, 1], in0=selu[:], scalar1=15, scalar2=None, op0=ALU.bitwise_and))
            S.op('dve', ['selu2'], ['sela'], lambda e: e.tensor_copy(out=sela[:], in_=selu2[:, 0]))
            S.op('dve', ['selu2'], ['selb'], lambda e: e.tensor_copy(out=selb[:], in_=selu2[:, 1]))
            io16 = c_iota[:, 0:16].unsqueeze(1).unsqueeze(1).broadcast_to([128, 8, 16, 16])
            for which, (selx, dst) in enumerate([(sela, idxf), (selb, idx2)]):
                S.op('dve', ['sela', 'selb', 'c_iota'], ['cand2'], lambda e, selx=selx: e.tensor_tensor(out=c4(cand2), in0=io16, in1=selx[:].unsqueeze(3).broadcast_to([128, 8, 16, 16]), op=ALU.is_equal))
                S.op('dve', ['cand2', 'tif'], ['cand2'], lambda e, which=which: e.tensor_tensor(out=c4(cand2), in0=c4(cand2), in1=tf4[:, :, which, :].unsqueeze(2).broadcast_to([128, 8, 16, 16]), op=ALU.mult))
                S.op('dve', ['cand2'], ['idxf' if which == 0 else 'idx2'], lambda e, dst=dst: e.tensor_reduce(out=dst[:], in_=c4(cand2), axis=AX.X, op=ALU.add))
                yield
            S.op('dve', ['idxf', 'idx2'], ['idxf'], lambda e: e.tensor_tensor(out=idxf[:], in0=idxf[:], in1=idx2[:], op=ALU.add))
            S.op('dve', ['idxf'], ['idxf'], lambda e: e.tensor_scalar(out=idxf[:], in0=idxf[:], scalar1=0.0, scalar2=float(NEXP - 1), op0=ALU.max, op1=ALU.min))
            S.op('dve', ['idxf'], [ki], lambda e: e.tensor_copy(out=idxu[:], in_=idxf[:].rearrange("p h k -> p (h k)")))
            S.op('dve', ['top'], ['gate'], lambda e: e.tensor_tensor(out=gate[:], in0=top[:], in1=top[:, :, 0:1].broadcast_to([128, 8, 16]), op=ALU.subtract))
            S.op('act', ['gate'], ['gate'], lambda e: e.activation(out=gate[:], in_=gate[:], func=AF.Exp))
            S.op('dve', ['gate'], ['gst'], lambda e: e.tensor_reduce(out=gst[:, :, 0], in_=gate[:], axis=AX.X, op=ALU.add))
            S.op('dve', ['gst'], ['gst'], lambda e: e.reciprocal(out=gst[:, :, 1], in_=gst[:, :, 0]))
            S.op('dve', ['gate', 'gst'], ['gate'], lambda e: e.tensor_tensor(out=gate[:], in0=gate[:], in1=gst[:, :, 1:2].broadcast_to([128, 8, 16]), op=ALU.mult))
        gen = A_part(0, 0)
        for _ in gen:
            pass
        for pt in range(NPE):
            samp = pt >= NT_B
            bi = pt % 2
            xm2 = xm2s[bi]
            idxu = idxus[bi]
            kx = 'xm2_%d' % bi
            ki = 'idxu_%d' % bi
            gen = A_part(pt + 1, 1 - bi) if pt + 1 < NPE else iter(())
            for sl in range(128):
                g = sl % NG
                S.dma('pool', None, None, [ki], ['gb%d' % g], fn=lambda e, sl=sl, g=g: e.indirect_dma_start(
                    out=(gb[g][:, 0:512] if OPTS['half'] else gb[g][:]), out_offset=None, in_=(eu[:, 0:512] if OPTS['half'] else eu), in_offset=bass.IndirectOffsetOnAxis(ap=idxu[:, sl:sl + 1], axis=0)))
                S.op('dve', ['gb%d' % g, 'hn32'], ['gb%d' % g, 'pre'], lambda e, sl=sl, g=g: e.scalar_tensor_tensor(
                    out=gb[g][:], in0=gb[g][:], scalar=1.0, in1=hn32[:], op0=ALU.mult, op1=ALU.mult, accum_out=pre[:, sl:sl + 1]))
            S.op('act', ['pre'], ['wgt'], lambda e: e.activation(out=wgt[:], in_=pre[:], func=AF.Gelu))
            S.op('dve', ['wgt', 'gate'], ['wgt'], lambda e: e.tensor_tensor(out=wgt[:], in0=wgt[:], in1=gate[:].rearrange("p h k -> p (h k)"), op=ALU.mult))
            for sl in range(128):
                g = sl % NG
                gi = sl % 3
                S.dma('pool', None, None, [ki], ['gb%d' % g], fn=lambda e, sl=sl, g=g: e.indirect_dma_start(
                    out=(gb[g][:, 0:512] if OPTS['half'] else gb[g][:]), out_offset=None, in_=(ev[:, 0:512] if OPTS['half'] else ev), in_offset=bass.IndirectOffsetOnAxis(ap=idxu[:, sl:sl + 1], axis=0)))
                S.op('act', ['gb%d' % g, 'wgt'], ['gbs%d' % gi], lambda e, sl=sl, g=g, gi=gi: e.activation(
                    out=gbs[gi][:], in_=gb[g][:], func=AF.Copy, scale=wgt[:, sl:sl + 1]))
                next(gen, None)
                for half in range(2):
                    S.op('pe', ['gbs%d' % gi, 'c_cstb'], [PK[3 + half]], lambda e, sl=sl, gi=gi, half=half: e.matmul(
                        out=P[3 + half][:], lhsT=ident_b, rhs=gbs[gi][:, half * 512:(half + 1) * 512], start=(sl == 0), stop=(sl == 127)))
            for _ in gen:
                pass
            for half in range(2):
                S.op('dve', [PK[3 + half], kx], ['acc'], lambda e, half=half: e.tensor_tensor(
                    out=acc[:, half * 512:(half + 1) * 512], in0=P[3 + half][:], in1=xm2[:, half * 512:(half + 1) * 512], op=ALU.add))
            S.op('act', ['acc'], ['hnb', 'ss2'], lambda e: e.activation(out=hnb[:], in_=acc[:], func=AF.Square, accum_out=ss2[:, 2:3]))
            rsq('ss2', ss2[:, 3:4], ss2[:, 2:3], D * 1e-5, ALU.add)
            S.op('dve', ['acc', 'ss2'], ['acc'], lambda e: e.tensor_scalar(out=acc[:], in0=acc[:], scalar1=ss2[:, 3:4], scalar2=32.0, op0=ALU.mult, op1=ALU.mult))
            S.op('pool', ['acc', 'c_gFbc'], ['acc'], lambda e: e.tensor_tensor(out=acc[:], in0=acc[:], in1=c_gFbc[:], op=ALU.mult))
            if samp:
                S.dma('sp', y_s[:, :], acc[0:NSEQ_S * 4, :], ['acc'], [])
            else:
                S.dma('sp', y_p[pt * 128:(pt + 1) * 128, :], acc[:], ['acc'], [])

    for ti in range(NTT):
        load_x(ti)
        mix_tile(ti)
    print("sbuf left", nc.sbuf_bytes_remaining() if callable(nc.sbuf_bytes_remaining) else nc.sbuf_bytes_remaining)
    print("total ops", getattr(S, 'n', 0))
    barrier()
    ph1.close()
    stk['cur'] = glob_stack
    if OPTS['peer']:
        peer_phase()
    barrier()
    S.finish()
    return nc


_CACHE = {}
NTA_FULL, NTB_FULL = 17, 17


def _consts(nta, ntb, hf):
    ar = np.arange(128)
    ident = np.eye(128, dtype=np.float32)
    tri = (ar[:, None] <= ar[None, :]).astype(np.float32)
    ones = np.ones((128, 128), np.float32)
    su = (ar[:, None] < ar[None, :]).astype(np.float32)
    lo = (ar[:, None] > ar[None, :]).astype(np.float32)
    cst = np.stack([ident, tri, ones, su, tri, lo], axis=1).astype(np.float32)
    q = ar[:, None]
    c = np.arange(256)[None, :]
    ok = (c > q) & (c <= q + 128)
    m_std = np.where(ok, 0.0, -30000.0).astype(np.float32)
    m_t0 = np.where(ok & (c >= 240), 0.0, -30000.0).astype(np.float32)
    m_t1 = np.where(ok & (c >= 112), 0.0, -30000.0).astype(np.float32)
    first = (hf == 0) or (nta == 0)
    cmask = np.stack([m_t0 if first else m_std, m_t1 if first else m_std, m_std], axis=1)
    inv = (np.float32(500000.0) ** (-np.arange(0, 16, 2, dtype=np.float32) / np.float32(16))).astype(np.float32)
    ntp = nta + ntb
    rope = np.zeros((128, ntp + 1, 16), np.float32)
    for i in range(ntp + 1):
        if i < ntp:
            st = i if (hf == 1 or nta == 0) else (i - nta if i >= nta else i)
            pos = st * 128 - 112 + ar
        else:
            pos = PAST + ar
        ang = pos.astype(np.float32)[:, None] * inv[None, :]
        rope[:, i, 0:8] = np.cos(ang)
        rope[:, i, 8:16] = np.sin(ang)
    vmask = np.zeros((128, 2), np.float32)
    vmask[:, 0] = 1.0
    vmask[0:4, 1] = 1.0
    iota = np.tile(np.arange(256, dtype=np.float32)[None, :], (128, 1))
    return dict(cst=cst, cmask=cmask, rope=rope, vmask=vmask, iota=iota)


def kernel(x_prompt, x_sample, cache_k_win, cache_v_win, state_wkv, state_shift, meta_tokens, norm1_g, w_in, mu_shift,
           w0, w_lora_w2, a0, w_lora_a2, w_lora_g2, k_k, k_a, r_k, lnx_w, lnx_b, attn_sinks, w_out, norm2_g, w_query,
           sub_keys, expert_u, expert_v, final_norm_g, _nta=NTA_FULL, _ntb=NTB_FULL, _nts=NSEQ_S, _dbg=False):
    f = lambda a: np.ascontiguousarray(np.asarray(a), dtype=np.float32)
    key = (_nta, _ntb, _nts)
    if key not in _CACHE:
        _CACHE[key] = build(_nta, _ntb, _nts, dbg=_dbg)
    nc = _CACHE[key]
    x_prompt, x_sample = f(x_prompt), f(x_sample)
    B = x_prompt.shape[0]
    nseqt = _nta + _ntb - 1
    shared = dict(
        w_in=f(w_in)[0], w_out=f(w_out)[0], w_q=f(w_query)[0], subk=f(sub_keys)[0],
        eu=f(expert_u)[0][:OPTS['nexp']], ev=f(expert_v)[0][:OPTS['nexp']],
        lw2=f(w_lora_w2)[0], la2=f(w_lora_a2)[0], lg2=f(w_lora_g2)[0],
        vec512=np.stack([f(w0)[0], f(a0)[0], f(k_k)[0], f(k_a)[0], f(r_k)[0].reshape(512), f(lnx_w)[0], f(lnx_b)[0]]),
        mu=f(mu_shift), g1=f(norm1_g), g2=f(norm2_g), gF=f(final_norm_g)[None, :], sinks=f(attn_sinks))
    cs = [_consts(_nta, _ntb, hf) for hf in range(2)]
    in_maps = []
    for c in range(NCORES):
        b, hf = c // 2, c % 2
        seq = np.zeros((nseqt * 128, D), np.float32)
        seq[112:128] = f(meta_tokens)
        seq[128:] = x_prompt[b][:(nseqt - 1) * 128]
        xp = np.zeros(((_nta + _ntb) * 128, D), np.float32)
        if hf == 0:
            xp[_nta * 128:(_nta + _ntb) * 128] = seq[:_ntb * 128]
        else:
            xp[:nseqt * 128] = seq
        sl = slice(c * NSEQ_S, (c + 1) * NSEQ_S)
        m = dict(shared)
        m.update(cs[hf])
        m.update(xp=xp, xs=x_sample[sl].reshape(NSEQ_S * 4, D),
                 ck=f(cache_k_win)[0, sl].reshape(NSEQ_S, 128, 128), cv=f(cache_v_win)[0, sl].reshape(NSEQ_S, 128, 128),
                 swkv=f(state_wkv)[0, sl], sshift=f(state_shift)[0, sl])
        in_maps.append(m)
    res = run_bass_kernel_spmd(nc, in_maps, core_ids=list(range(NCORES))).results
    if (_nta, _ntb, _nts) != (NTA_FULL, NTB_FULL, NSEQ_S):
        return res
    y_prompt = np.stack([np.concatenate([res[2 * b]["y_p"][128:_ntb * 128], res[2 * b + 1]["y_p"][:(_ntb - 1) * 128]]) for b in range(B)])
    y_sample = np.concatenate([res[c]["y_s"].reshape(NSEQ_S, 4, D) for c in range(NCORES)])
    od = lambda b: res[2 * b + 1]
    kwp = np.stack([od(b)["kw_p"].reshape(128, 2, 64) for b in range(B)])[None]
    vwp = np.stack([od(b)["vw_p"].reshape(128, 2, 64) for b in range(B)])[None]
    wkvp = np.stack([od(b)["wkv_p"] for b in range(B)])[None]
    shp = np.stack([od(b)["sh_p"][0] for b in range(B)])[None]
    kws = np.concatenate([res[c]["kw_s"].reshape(NSEQ_S, 128, 2, 64) for c in range(NCORES)])[None]
    vws = np.concatenate([res[c]["vw_s"].reshape(NSEQ_S, 128, 2, 64) for c in range(NCORES)])[None]
    wkvs = np.concatenate([res[c]["wkv_s"] for c in range(NCORES)])[None]
    shs = np.concatenate([res[c]["sh_s"] for c in range(NCORES)])[None]
    return (y_prompt, y_sample, kwp, vwp, wkvp, shp, kws, vws, wkvs, shs)
```

```python
import numpy as np
from contextlib import ExitStack
import concourse.bass as bass
import concourse.mybir as mybir
from concourse.alu_op_type import AluOpType as ALU
from concourse.bass_utils import run_bass_kernel_spmd

F32 = mybir.dt.float32
BF16 = mybir.dt.bfloat16
U32 = mybir.dt.uint32
AF = mybir.ActivationFunctionType
AX = mybir.AxisListType

D = 1024
NRW = 1792
NCOL = 2560
NCORES = 8
NSEQ_S = 16
PAST = 8192
NPT = 33
NEXP = 16384
OPTS = {'ng': 12, 'half': 0, 'limit': 10**9, 'dump_only': '', 'dumps': False, 'nexp': 16384, 'dbg_tile': 1, 'stage': 99, 'peer': True, 'win_copy': True, 'samp_state': True}


class Sched:
    def __init__(self, nc):
        self.nc = nc
        self.eng = {'pe': nc.tensor, 'act': nc.scalar, 'dve': nc.vector, 'pool': nc.gpsimd, 'sp': nc.sync}
        self.sem = {e: nc.alloc_semaphore("sem_" + e) for e in ['pe', 'act', 'dve', 'pool']}
        self.cnt = {e: 0 for e in self.sem}
        self.seen = {e: {} for e in self.eng}
        self.last_w = {}
        self.readers = {}
        self.dslots = {}
        for q in ['sp', 'pool', 'act']:
            self.dslots[q] = [[nc.alloc_semaphore("dq_%s_%d" % (q, i)), 0] for i in range(OPTS['ng'] if q == 'pool' else 8)]
        self.dnext = {q: 0 for q in self.dslots}
        self.tokens = {}

    def _wait(self, e, toks):
        need = {}
        for t in toks:
            if t is None:
                continue
            sid, val, we = t
            if we == e and e == 'pe':
                continue
            if self.seen[e].get(sid, 0) >= val:
                continue
            if need.get(sid, (None, 0))[1] < val:
                need[sid] = (t, val)
        for sid, (t, val) in need.items():
            self.eng[e].wait_ge(self.tokens[sid], val)
            self.seen[e][sid] = val

    def _deps(self, e, reads, writes):
        toks = []
        for k in reads:
            toks.append(self.last_w.get(k))
        for k in writes:
            toks.append(self.last_w.get(k))
            for t in self.readers.get(k, []):
                toks.append(t[:3])
        return toks

    def _mark(self, tok, reads, writes, is_dma):
        for k in reads:
            self.readers.setdefault(k, []).append(tok + (is_dma,))
        for k in writes:
            self.last_w[k] = tok
            self.readers[k] = []

    def op(self, e, reads, writes, fn):
        self.n = getattr(self, 'n', 0) + 1
        if self.n > OPTS['limit']:
            return None
        self._wait(e, self._deps(e, reads, writes))
        inst = fn(self.eng[e])
        self.cnt[e] += 1
        sem = self.sem[e]
        inst.then_inc(sem, 1)
        sid = id(sem)
        self.tokens[sid] = sem
        self._mark((sid, self.cnt[e], e), reads, writes, False)
        return inst

    def dma(self, q, out, in_, reads, writes, fn=None):
        self.n = getattr(self, 'n', 0) + 1
        if self.n > OPTS['limit']:
            return None
        slots = self.dslots[q]
        i = self.dnext[q]
        self.dnext[q] = (i + 1) % len(slots)
        sem, val = slots[i]
        sid = id(sem)
        self.tokens[sid] = sem
        toks = self._deps(q, reads, writes)
        if val > 0:
            toks.append((sid, val, 'dma'))
        self._wait(q, toks)
        if fn is None:
            inst = self.eng[q].dma_start(out=out, in_=in_)
        else:
            inst = fn(self.eng[q])
        inst.then_inc(sem, 16)
        slots[i][1] = val + 16
        self._mark((sid, val + 16, 'dma'), reads, writes, True)

    def finish(self):
        for q, slots in self.dslots.items():
            toks = [(id(s), v, 'dma') for s, v in slots if v > 0]
            self._wait(q, toks)


def build(NT_A, NT_B, NT_S, dbg=False, dbg_tile=1):
    nc = bass.Bass("TRN2", target_bir_lowering=False)
    S = Sched(nc)
    NT_P = NT_A + NT_B
    NTT = NT_P + NT_S
    NPE = NT_B + (1 if NT_S else 0)
    OUT_T = NT_P - 2 if NT_A > 0 else NT_P - 1

    def din(name, shape, dt=F32):
        return nc.dram_tensor(name, list(shape), dt, kind="ExternalInput").ap()

    def dout(name, shape, dt=F32):
        return nc.dram_tensor(name, list(shape), dt, kind="ExternalOutput").ap()

    xp = din("xp", [max(NT_P, 1) * 128, D])
    xs = din("xs", [NSEQ_S * 4, D])
    ck = din("ck", [NSEQ_S, 128, 128])
    cv = din("cv", [NSEQ_S, 128, 128])
    swkv = din("swkv", [NSEQ_S, 8, 64, 64])
    sshift = din("sshift", [NSEQ_S, D])
    w_in = din("w_in", [D, NCOL])
    w_out = din("w_out", [D, D])
    w_q = din("w_q", [D, 2048])
    subk = din("subk", [2, 128, 128])
    eu = din("eu", [OPTS['nexp'], D])
    ev = din("ev", [OPTS['nexp'], D])
    lw2 = din("lw2", [64, 512])
    la2 = din("la2", [64, 512])
    lg2 = din("lg2", [128, 512])
    vec512 = din("vec512", [7, 512])
    mu = din("mu", [1, NRW])
    g1 = din("g1", [1, D])
    g2 = din("g2", [1, D])
    gF = din("gF", [1, D])
    sinks = din("sinks", [1, 8])
    rope = din("rope", [128, NT_P + 1, 16])
    cmask = din("cmask", [128, 3, 256])
    cst = din("cst", [128, 6, 128])
    vmask = din("vmask", [128, 2])
    iota_in = din("iota", [128, 256])

    y_p = dout("y_p", [max(NT_B, 1) * 128, D])
    y_s = dout("y_s", [NSEQ_S * 4, D])
    kw_p = dout("kw_p", [128, 128])
    vw_p = dout("vw_p", [128, 128])
    wkv_p = dout("wkv_p", [8, 64, 64])
    sh_p = dout("sh_p", [1, D])
    kw_s = dout("kw_s", [NSEQ_S, 128, 128])
    vw_s = dout("vw_s", [NSEQ_S, 128, 128])
    wkv_s = dout("wkv_s", [NSEQ_S, 8, 64, 64])
    sh_s = dout("sh_s", [NSEQ_S, D])
    xmid = nc.dram_tensor("xmid", [max(NPE, 1) * 128, D], F32, kind="Internal").ap()
    dbg_t = {}
    if dbg:
        for nm, shp in [("d_m", [128, NRW]), ("d_y", [128, 512]), ("d_at", [128, 512]), ("d_S", [64, 512]),
                        ("d_pre", [128, 128]), ("d_idx", [128, 128]), ("d_gate", [128, 128])]:
            dbg_t[nm] = dout(nm, shp)

    stk = {'cur': ExitStack()}
    glob_stack = stk['cur']

    dumped = {}

    def DUMP(name, t, key, ti=None):
        if not OPTS['dumps'] or (ti is not None and ti != OPTS['dbg_tile']) or name in dumped:
            return
        if OPTS['dump_only'] and name not in str(OPTS['dump_only']).split(','):
            return
        src = t if isinstance(t, bass.AP) else t[:]
        o = nc.dram_tensor("z_" + name, list(src.shape), src.dtype, kind="ExternalOutput").ap()
        dumped[name] = 1
        S.dma('sp', o, src, [key], [])

    def sb(name, shape, dt=F32):
        return stk['cur'].enter_context(nc.sbuf_tensor(name, list(shape), dt))

    def ps(name, shape, dt=F32):
        return nc.alloc_psum_tensor(name, list(shape), dt)

    c_cst = sb("c_cst", [128, 6, 128])
    S.dma('sp', c_cst[:], cst, [], ['c_cst'])
    c_cstb = sb("c_cstb", [128, 6, 128], BF16)
    S.op('dve', ['c_cst'], ['c_cstb'], lambda e: e.tensor_copy(out=c_cstb[:], in_=c_cst[:]))
    ident_f = c_cst[:, 0, :]
    tri_f = c_cst[:, 1, :]
    ones_f = c_cst[:, 2, :]
    ident_b = c_cstb[:, 0, :]
    c_m4 = sb("c_m4", [128, 4, 128])
    for i, j in enumerate([3, 4, 3, 4]):
        S.op('dve', ['c_cst'], ['c_m4'], lambda e, i=i, j=j: e.tensor_copy(out=c_m4[:, i, :], in_=c_cst[:, j, :]))
    c_low = c_cst[:, 5, :]
    c_amask = sb("c_amask", [128, 3, 256])
    S.dma('sp', c_amask[:], cmask, [], ['c_amask'])
    c_rope = sb("c_rope", [128, NT_P + 1, 16])
    S.dma('sp', c_rope[:], rope, [], ['c_rope'])
    c_vm = sb("c_vm", [128, 2])
    S.dma('sp', c_vm[:], vmask, [], ['c_vm'])
    c_v512 = sb("c_v512", [128, 7, 512])
    S.dma('sp', c_v512[:], vec512.partition_broadcast(128), [], ['c_v512'])
    W0, A0, KK, KA, RK, LW, LB = range(7)
    c_g1bc = sb("c_g1bc", [128, D])
    S.dma('sp', c_g1bc[:], g1.partition_broadcast(128)[:, 0, :], [], ['c_g1bc'])
    c_sink = sb("c_sink", [128, 8])
    S.dma('sp', c_sink[:], sinks.partition_broadcast(128)[:, 0, :], [], ['c_sink'])
    c_g1col = sb("c_g1col", [128, 8])
    with nc.allow_non_contiguous_dma(reason="tiny param column load"):
        S.dma('sp', c_g1col[:], g1.rearrange("o (kc p) -> p (o kc)", p=128), [], ['c_g1col'])

    def rsq(key, out, in_, c, op0):
        S.op('dve', [key], [key], lambda e: e.tensor_scalar(out=out, in0=in_, scalar1=c, scalar2=None, op0=op0))
        S.op('act', [key], [key], lambda e: e.activation(out=out, in_=out, func=AF.Sqrt))
        S.op('dve', [key], [key], lambda e: e.reciprocal(out=out, in_=out))

    P = [ps("P%d" % i, [128, 512]) for i in range(8)]
    PK = ["P%d" % i for i in range(8)]

    def barrier():
        toks = []
        for e, sem in S.sem.items():
            if S.cnt[e] > 0:
                S.tokens[id(sem)] = sem
                toks.append((id(sem), S.cnt[e], 'x'))
        for q, slots in S.dslots.items():
            for s, v in slots:
                if v > 0:
                    S.tokens[id(s)] = s
                    toks.append((id(s), v, 'dma'))
        for e in ['pe', 'act', 'dve', 'pool', 'sp']:
            S._wait(e, toks)

    ph1 = ExitStack()
    stk['cur'] = ph1
    W1 = sb("W1", [128, 8, NRW], BF16)
    W2 = sb("W2", [128, 8, NRW], BF16)
    Wat = sb("Wat", [128, 8, 768], BF16)
    Wo = sb("Wo", [128, 8, D], BF16)
    L_w2 = sb("L_w2", [128, 512], BF16)
    L_g2 = sb("L_g2", [128, 512], BF16)
    with nc.sbuf_tensor("stg", [128, NCOL], F32) as stg, nc.sbuf_tensor("mub", [128, NRW], F32) as mub, \
            nc.sbuf_tensor("omu", [128, NRW], F32) as omu:
        S.dma('sp', mub[:], mu.partition_broadcast(128)[:, 0, :], [], ['mub'])
        S.op('dve', ['mub'], ['omu'], lambda e: e.tensor_scalar(out=omu[:], in0=mub[:], scalar1=-1.0, scalar2=1.0,
                                                                op0=ALU.mult, op1=ALU.add))
        for kc in range(8):
            S.dma('sp', stg[:], w_in[kc * 128:(kc + 1) * 128, :], [], ['stg'])
            S.op('dve', ['stg', 'omu'], ['W1'], lambda e, kc=kc: e.tensor_tensor(out=W1[:, kc, :], in0=stg[:, 0:NRW], in1=omu[:], op=ALU.mult))
            S.op('pool', ['stg', 'mub'], ['W2'], lambda e, kc=kc: e.tensor_tensor(out=W2[:, kc, :], in0=stg[:, 0:NRW], in1=mub[:], op=ALU.mult))
            S.op('act', ['stg'], ['Wat'], lambda e, kc=kc: e.copy(out=Wat[:, kc, :], in_=stg[:, NRW:NCOL]))
        for kc in range(8):
            S.dma('sp', stg[:, 0:D], w_out[kc * 128:(kc + 1) * 128, :], [], ['stg'])
            S.op('act', ['stg'], ['Wo'], lambda e, kc=kc: e.copy(out=Wo[:, kc, :], in_=stg[:, 0:D]))
        S.dma('sp', stg[0:64, 0:512], lw2, [], ['stg'])
        S.dma('sp', stg[64:128, 0:512], la2, [], ['stg'])
        S.dma('sp', stg[:, 512:1024], lg2, [], ['stg'])
        S.op('act', ['stg'], ['L_w2'], lambda e: e.copy(out=L_w2[:], in_=stg[:, 0:512]))
        S.op('act', ['stg'], ['L_g2'], lambda e: e.copy(out=L_g2[:], in_=stg[:, 512:1024]))
        barrier()

    xt0 = sb("xt0", [128, D])
    xt = [xt0, xt0]
    xts = xt0
    xn = sb("xn", [128, D], BF16)
    ssq = sb("ssq", [128, 4])
    hT = sb("hT", [128, 8, 128], BF16)
    hTs = sb("hTs", [128, 8, 128], BF16)
    S.op('pool', [], ['hT'], lambda e: e.memset(hT[:], 0.0))
    S.op('pool', [], ['hTs'], lambda e: e.memset(hTs[:], 0.0))
    A = {}
    for nm in ["r", "k", "v", "ld", "asig", "g", "kk", "b", "kmod", "c", "t1", "t2", "t3"]:
        A[nm] = sb("a_" + nm, [128, 512])
    Bt = {}
    for nm in ["rt", "at", "bt", "kt", "bbar", "kbar", "vb", "W0T", "UT"]:
        Bt[nm] = sb("b_" + nm, [128, 512], BF16)
    lin = sb("lin", [128, 2, 128], BF16)
    st8 = sb("st8", [128, 8, 4])
    AR_fm = sb("AR_fm", [64, 8, 256], BF16)
    B_fm = sb("B_fm", [64, 8, 128], BF16)
    K_fm = sb("K_fm", [64, 8, 128], BF16)
    MATS = sb("MATS", [128, 8, 512], BF16)
    _m = sb("Mb", [128, 8, 128], BF16)
    _mt = sb("MTb", [128, 8, 128], BF16)
    _t = sb("Tb", [128, 8, 128], BF16)
    Mb, MTb, Tb = [_m, _m], [_mt, _mt], [_t, _t]
    S32 = sb("S32", [64, 8, 64])
    Sb = sb("Sb", [64, 8, 64], BF16)
    ecl_fm = sb("ecl_fm", [64, 8])
    sti = sb("sti", [64, 8, 64])
    qkv = sb("qkv", [128, 768])
    rtmp = sb("rtmp", [128, 10, 8])
    qb = sb("qb", [128, 768], BF16)
    QT = sb("QT", [64, 8, 128], BF16)
    KT = [sb("KT%d" % i, [64, 2, 128], BF16) for i in range(2)]
    Vb = [sb("Vb%d" % i, [128, 128], BF16) for i in range(2)]
    sm = sb("sm", [128, 256])
    for i in range(2):
        S.op('pool', [], ['KT%d' % i], lambda e, i=i: e.memset(KT[i][:], 0.0))
        S.op('pool', [], ['Vb%d' % i], lambda e, i=i: e.memset(Vb[i][:], 0.0))
    eb = sb("eb", [128, 256], BF16)
    eT = sb("eT", [128, 2, 128], BF16)
    ast = sb("ast", [128, 8])
    ast8 = sb("ast8", [128, 4, 8])
    ycat = sb("ycat", [128, D], BF16)
    ycatT = sb("ycatT", [128, 8, 128], BF16)
    junk = ycatT[:].rearrange("p k t -> p (k t)")
    xm = sb("xm", [128, D])
    hrow = xm
    ckb = sm

    def v3(ap, h=8):
        return ap.rearrange("p (h j) -> p h j", h=h)

    def bc_last(ap, n):
        return ap.broadcast_to([ap.shape[0], ap.shape[1], n])

    def V512(i):
        return c_v512[:, i, :]

    def load_x(ti):
        if ti < NT_P:
            b = xt[ti % 2]
            S.dma('sp', b[:], xp[ti * 128:(ti + 1) * 128, :], [], ['xt0'])
        else:
            s = ti - NT_P
            if s == 0:
                S.op('pool', [], ['xt0'], lambda e: e.memset(xt0[:], 0.0))
                S.dma('sp', xmid[NT_B * 128:(NT_B + 1) * 128, :], xt0[:], ['xt0'], ['xmid'])
            S.dma('sp', xts[0:4, :], xs[s * 4:(s + 1) * 4, :], [], ['xt0'])

    def mix_tile(ti):
        samp = ti >= NT_P
        sq = ti - NT_P
        xk = 'xt0'
        xb = xts if samp else xt[ti % 2]
        par = ti % 2
        vm = c_vm[:, 1:2] if samp else c_vm[:, 0:1]
        if OPTS['stage'] <= 0:
            return
        S.op('act', [xk], ['ycatT', 'ssq'], lambda e: e.activation(out=junk[:], in_=xb[:], func=AF.Square, accum_out=ssq[:, 0:1]))
        rsq('ssq', ssq[:, 1:2], ssq[:, 0:1], D * 1e-5, ALU.add)
        S.op('dve', [xk, 'ssq'], ['xn'], lambda e: e.tensor_scalar(out=xn[:], in0=xb[:], scalar1=ssq[:, 1:2], scalar2=32.0,
                                                                   op0=ALU.mult, op1=ALU.mult))
        if samp:
            S.dma('sp', hrow[0:1, :], sshift[sq:sq + 1, :], [], ['xm'])
            S.op('act', ['xm'], ['ycatT'], lambda e: e.copy(out=junk[0:1, :], in_=hrow[0:1, :]))
        pT = P[0][:].bitcast(BF16)
        if samp:
            for kc in range(8):
                S.op('pe', ['ycatT', 'c_cstb'], ['P1'], lambda e, kc=kc: e.transpose(out=P[1][:].bitcast(BF16)[:, 2 * kc:2 * kc + 1], in_=junk[0:1, kc * 128:(kc + 1) * 128], identity=ident_b[0:1, 0:1]))
            S.op('dve', ['P1'], ['hTs'], lambda e: e.tensor_copy(out=hTs[:, :, 0], in_=P[1][:].bitcast(BF16)[:, 0:16].rearrange("p (k two) -> p k two", two=2)[:, :, 0]))
        else:
            S.op('dve', ['hT'], ['hTs'], lambda e: e.tensor_copy(out=hTs[:, :, 0], in_=hT[:, :, 127]))
        for kc in range(8):
            S.op('pe', ['xn', 'c_cstb'], ['P0'], lambda e, kc=kc: e.transpose(out=pT[:, kc * 128:(kc + 1) * 128], in_=xn[:, kc * 128:(kc + 1) * 128], identity=ident_b))
        S.op('dve', ['P0', 'c_g1col'], ['hT'], lambda e: e.tensor_tensor(
            out=hT[:], in0=pT.rearrange("p (k t) -> p k t", k=8),
            in1=c_g1col[:].unsqueeze(2).broadcast_to([128, 8, 128]), op=ALU.mult))
        S.op('pool', ['hT'], ['hTs'], lambda e: e.tensor_copy(out=hTs[:, :, 1:128], in_=hT[:, :, 0:127]))
        if samp or ti == OUT_T:
            S.op('pool', ['xn', 'c_g1bc'], ['xm'], lambda e: e.tensor_tensor(out=hrow[:], in0=xn[:], in1=c_g1bc[:], op=ALU.mult))
            if samp:
                S.dma('sp', sh_s[sq:sq + 1, :], hrow[3:4, :], ['xm'], [])
            else:
                S.dma('sp', sh_p[0:1, :], hrow[127:128, :], ['xm'], [])
        DUMP("xn", xn, 'xn', ti); DUMP("hT", hT, 'hT', ti); DUMP("hTs", hTs, 'hTs', ti)
        if OPTS['stage'] <= 1:
            return
        def proj(bank, c0, n, dst_reads=()):
            for kc in range(8):
                S.op('pe', ['hT', 'W1'], [PK[bank]], lambda e, kc=kc: e.matmul(out=P[bank][:, 0:n], lhsT=hT[:, kc, :], rhs=W1[:, kc, c0:c0 + n], start=(kc == 0), stop=False))
            for kc in range(8):
                S.op('pe', ['hTs', 'W2'], [PK[bank]], lambda e, kc=kc: e.matmul(out=P[bank][:, 0:n], lhsT=hTs[:, kc, :], rhs=W2[:, kc, c0:c0 + n], start=False, stop=(kc == 7)))
        proj(1, 0, 512)
        S.op('act', ['P1'], ['a_r'], lambda e: e.copy(out=A["r"][:], in_=P[1][:]))
        proj(2, 512, 512)
        S.op('act', ['P2'], ['a_k'], lambda e: e.copy(out=A["k"][:], in_=P[2][:]))
        proj(3, 1024, 512)
        S.op('act', ['P3'], ['a_v'], lambda e: e.copy(out=A["v"][:], in_=P[3][:]))
        S.op('dve', ['a_v', 'c_vm'], ['b_vb'], lambda e: e.tensor_scalar(out=Bt["vb"][:], in0=A["v"][:], scalar1=vm, scalar2=None, op0=ALU.mult))
        for j, c0 in enumerate([1536, 1664]):
            for kc in range(8):
                S.op('pe', ['hT', 'W1'], ['P4'], lambda e, kc=kc, j=j, c0=c0: e.matmul(out=P[4][:, j * 128:(j + 1) * 128], lhsT=W1[:, kc, c0:c0 + 128], rhs=hT[:, kc, :], start=(kc == 0), stop=False))
            for kc in range(8):
                S.op('pe', ['hTs', 'W2'], ['P4'], lambda e, kc=kc, j=j, c0=c0: e.matmul(out=P[4][:, j * 128:(j + 1) * 128], lhsT=W2[:, kc, c0:c0 + 128], rhs=hTs[:, kc, :], start=False, stop=(kc == 7)))
        S.op('act', ['P4'], ['lin'], lambda e: e.activation(out=lin[0:64, 0, :], in_=P[4][0:64, 0:128], func=AF.Tanh))
        S.op('act', ['P4'], ['lin'], lambda e: e.copy(out=lin[64:128, 0, :], in_=P[4][64:128, 0:128]))
        S.op('act', ['P4'], ['lin'], lambda e: e.activation(out=lin[:, 1, :], in_=P[4][:, 128:256], func=AF.Sigmoid))
        for kc in range(8):
            S.op('pe', ['hT', 'Wat'], ['P5'], lambda e, kc=kc: e.matmul(out=P[5][:], lhsT=hT[:, kc, :], rhs=Wat[:, kc, 0:512], start=(kc == 0), stop=(kc == 7)))
        for kc in range(8):
            S.op('pe', ['hT', 'Wat'], ['P6'], lambda e, kc=kc: e.matmul(out=P[6][:, 0:256], lhsT=hT[:, kc, :], rhs=Wat[:, kc, 512:768], start=(kc == 0), stop=(kc == 7)))
        S.op('act', ['P5'], ['qkv'], lambda e: e.copy(out=qkv[:, 0:512], in_=P[5][:]))
        S.op('act', ['P6'], ['qkv'], lambda e: e.copy(out=qkv[:, 512:768], in_=P[6][:, 0:256]))
        DUMP("a_r", A["r"], 'a_r', ti); DUMP("a_v", A["v"], 'a_v', ti); DUMP("lin", lin, 'lin', ti); DUMP("qkv0", qkv, 'qkv', ti)
        if OPTS['stage'] <= 2:
            return
        S.op('pe', ['lin', 'L_w2'], ['P1'], lambda e: e.matmul(out=P[1][:], lhsT=lin[0:64, 0, :], rhs=L_w2[0:64, :], start=True, stop=True))
        S.op('pe', ['lin', 'L_w2'], ['P2'], lambda e: e.matmul(out=P[2][:], lhsT=lin[64:128, 0, :], rhs=L_w2[64:128, :], start=True, stop=True))
        S.op('pe', ['lin', 'L_g2'], ['P3'], lambda e: e.matmul(out=P[3][:], lhsT=lin[:, 1, :], rhs=L_g2[:], start=True, stop=True))
        S.op('dve', ['P1', 'c_v512'], ['a_t1'], lambda e: e.tensor_tensor(out=A["t1"][:], in0=P[1][:], in1=V512(W0), op=ALU.add))
        S.op('act', ['a_t1'], ['a_t1'], lambda e: e.activation(out=A["t1"][:], in_=A["t1"][:], func=AF.Sigmoid))
        S.op('dve', ['a_t1', 'c_vm'], ['a_ld'], lambda e: e.tensor_scalar(out=A["ld"][:], in0=A["t1"][:], scalar1=vm, scalar2=-0.6065306597,
                                                                           op0=ALU.mult, op1=ALU.mult))
        S.op('dve', ['P2', 'c_v512'], ['a_t2'], lambda e: e.tensor_tensor(out=A["t2"][:], in0=P[2][:], in1=V512(A0), op=ALU.add))
        S.op('act', ['a_t2'], ['a_asig'], lambda e: e.activation(out=A["asig"][:], in_=A["t2"][:], func=AF.Sigmoid))
        S.op('act', ['P3'], ['a_g'], lambda e: e.copy(out=A["g"][:], in_=P[3][:]))
        S.op('pool', ['a_k', 'c_v512'], ['a_kk'], lambda e: e.tensor_tensor(out=A["kk"][:], in0=A["k"][:], in1=V512(KK), op=ALU.mult))
        S.op('pool', ['a_kk'], ['a_t3'], lambda e: e.tensor_tensor(out=A["t3"][:], in0=A["kk"][:], in1=A["kk"][:], op=ALU.mult))
        S.op('dve', ['a_t3'], ['st8'], lambda e: e.tensor_reduce(out=st8[:, :, 0], in_=v3(A["t3"][:]), axis=AX.X, op=ALU.add))
        rsq('st8', st8[:, :, 1], st8[:, :, 0], 1e-24, ALU.max)
        S.op('dve', ['a_kk', 'st8'], ['a_kk'], lambda e: e.tensor_tensor(out=v3(A["kk"][:]), in0=v3(A["kk"][:]), in1=bc_last(st8[:, :, 1:2], 64), op=ALU.mult))
        S.op('dve', ['a_kk', 'a_asig', 'c_vm'], ['a_b'], lambda e: e.scalar_tensor_tensor(out=A["b"][:], in0=A["kk"][:], scalar=vm, in1=A["asig"][:], op0=ALU.mult, op1=ALU.mult))
        S.op('dve', ['a_asig', 'c_v512'], ['a_t2'], lambda e: e.scalar_tensor_tensor(out=A["t2"][:], in0=A["asig"][:], scalar=-1.0, in1=V512(KA), op0=ALU.add, op1=ALU.mult))
        S.op('dve', ['a_t2', 'a_k'], ['a_kmod'], lambda e: e.scalar_tensor_tensor(out=A["kmod"][:], in0=A["t2"][:], scalar=1.0, in1=A["k"][:], op0=ALU.add, op1=ALU.mult))
        S.op('pe', ['a_ld', 'c_cst'], ['P1'], lambda e: e.matmul(out=P[1][:], lhsT=tri_f, rhs=A["ld"][:], start=True, stop=True))
        S.op('pe', ['a_ld', 'c_cst'], ['P2'], lambda e: e.matmul(out=P[2][:], lhsT=ones_f, rhs=A["ld"][:], start=True, stop=True))
        for h in range(8):
            S.op('pe', ['a_ld', 'c_cst'], ['P3'], lambda e, h=h: e.matmul(out=P[3][0:64, h:h + 1], lhsT=A["ld"][:, h * 64:(h + 1) * 64], rhs=ones_f[:, 0:1], start=True, stop=True))
        S.op('act', ['P3'], ['ecl_fm'], lambda e: e.activation(out=ecl_fm[:], in_=P[3][0:64, 0:8], func=AF.Exp))
        S.op('act', ['P1'], ['a_c'], lambda e: e.copy(out=A["c"][:], in_=P[1][:]))
        S.op('act', ['a_c'], ['a_t1'], lambda e: e.activation(out=A["t1"][:], in_=A["c"][:], func=AF.Exp))
        S.op('dve', ['a_t1', 'a_r'], ['b_rt'], lambda e: e.tensor_tensor(out=Bt["rt"][:], in0=A["r"][:], in1=A["t1"][:], op=ALU.mult))
        S.op('pool', ['a_c', 'a_ld'], ['a_t2'], lambda e: e.tensor_tensor(out=A["t2"][:], in0=A["c"][:], in1=A["ld"][:], op=ALU.subtract))
        S.op('act', ['a_t2'], ['a_t2'], lambda e: e.activation(out=A["t2"][:], in_=A["t2"][:], func=AF.Exp))
        S.op('dve', ['a_t2', 'a_kk'], ['b_at'], lambda e: e.scalar_tensor_tensor(out=Bt["at"][:], in0=A["kk"][:], scalar=-1.0, in1=A["t2"][:], op0=ALU.mult, op1=ALU.mult))
        S.op('act', ['a_c'], ['a_t3'], lambda e: e.activation(out=A["t3"][:], in_=A["c"][:], func=AF.Exp, scale=-1.0))
        S.op('dve', ['a_t3', 'a_b'], ['b_bt'], lambda e: e.tensor_tensor(out=Bt["bt"][:], in0=A["b"][:], in1=A["t3"][:], op=ALU.mult))
        S.op('pool', ['a_t3', 'a_kmod'], ['b_kt'], lambda e: e.tensor_tensor(out=Bt["kt"][:], in0=A["kmod"][:], in1=A["t3"][:], op=ALU.mult))
        S.op('dve', ['P2', 'a_c'], ['a_t1'], lambda e: e.tensor_tensor(out=A["t1"][:], in0=P[2][:], in1=A["c"][:], op=ALU.subtract))
        S.op('act', ['a_t1'], ['a_t1'], lambda e: e.activation(out=A["t1"][:], in_=A["t1"][:], func=AF.Exp))
        S.op('dve', ['a_t1', 'a_b'], ['b_bbar'], lambda e: e.tensor_tensor(out=Bt["bbar"][:], in0=A["b"][:], in1=A["t1"][:], op=ALU.mult))
        S.op('pool', ['a_t1', 'a_kmod'], ['b_kbar'], lambda e: e.tensor_tensor(out=Bt["kbar"][:], in0=A["kmod"][:], in1=A["t1"][:], op=ALU.mult))
        S.op('pool', ['a_r', 'c_v512'], ['a_t2'], lambda e: e.tensor_tensor(out=A["t2"][:], in0=A["r"][:], in1=V512(RK), op=ALU.mult))
        S.op('pool', ['a_t2', 'a_kmod'], ['a_t2'], lambda e: e.tensor_tensor(out=A["t2"][:], in0=A["t2"][:], in1=A["kmod"][:], op=ALU.mult))
        S.op('dve', ['a_t2'], ['st8'], lambda e: e.tensor_reduce(out=st8[:, :, 2], in_=v3(A["t2"][:]), axis=AX.X, op=ALU.add))
        DUMP("a_ld", A["ld"], 'a_ld', ti); DUMP("a_c", A["c"], 'a_c', ti); DUMP("a_kk", A["kk"], 'a_kk', ti); DUMP("b_rt", Bt["rt"], 'b_rt', ti); DUMP("b_at", Bt["at"], 'b_at', ti); DUMP("b_bt", Bt["bt"], 'b_bt', ti); DUMP("b_kbar", Bt["kbar"], 'b_kbar', ti); DUMP("ecl_fm", ecl_fm, 'ecl_fm', ti)
        if OPTS['stage'] <= 3:
            return
        pTb = [P[i][:].bitcast(BF16) for i in range(8)]
        for qi, (nm, bank) in enumerate([("at", 4), ("rt", 5), ("bt", 6), ("kt", 7)]):
            for h in range(8):
                S.op('pe', ['b_' + nm, 'c_cstb'], [PK[bank]], lambda e, h=h, nm=nm, bank=bank: e.transpose(
                    out=pTb[bank][0:64, h * 128:(h + 1) * 128], in_=Bt[nm][:, h * 64:(h + 1) * 64], identity=ident_b))
        S.op('act', ['P4'], ['AR_fm'], lambda e: e.copy(out=AR_fm[:, :, 0:128], in_=pTb[4][0:64, 0:1024].rearrange("p (h t) -> p h t", h=8)))
        S.op('dve', ['P5'], ['AR_fm'], lambda e: e.tensor_copy(out=AR_fm[:, :, 128:256], in_=pTb[5][0:64, 0:1024].rearrange("p (h t) -> p h t", h=8)))
        S.op('act', ['P6'], ['B_fm'], lambda e: e.copy(out=B_fm[:], in_=pTb[6][0:64, 0:1024].rearrange("p (h t) -> p h t", h=8)))
        S.op('dve', ['P7'], ['K_fm'], lambda e: e.tensor_copy(out=K_fm[:], in_=pTb[7][0:64, 0:1024].rearrange("p (h t) -> p h t", h=8)))
        for h in range(8):
            bank = h % 2
            S.op('pe', ['B_fm', 'AR_fm'], [PK[bank]], lambda e, h=h, bank=bank: e.matmul(out=P[bank][:, 0:256], lhsT=B_fm[:, h, :], rhs=AR_fm[:, h, :], start=True, stop=True))
            S.op('pe', ['K_fm', 'AR_fm'], [PK[bank]], lambda e, h=h, bank=bank: e.matmul(out=P[bank][:, 256:512], lhsT=K_fm[:, h, :], rhs=AR_fm[:, h, :], start=True, stop=True))
            S.op('dve', [PK[bank], 'c_m4'], ['MATS'], lambda e, h=h, bank=bank: e.tensor_tensor(out=MATS[:, h, :], in0=P[bank][:], in1=c_m4[:].rearrange("p a b -> p (a b)"), op=ALU.mult))
        for hh in range(2):
            bank = 2 + hh
            for h4 in range(4):
                h = hh * 4 + h4
                S.op('pe', ['B_fm', 'AR_fm'], [PK[bank]], lambda e, h=h, h4=h4, bank=bank: e.matmul(out=P[bank][:, h4 * 128:(h4 + 1) * 128], lhsT=AR_fm[:, h, 0:128], rhs=B_fm[:, h, :], start=True, stop=True))
            S.op('dve', [PK[bank], 'c_cst'], ['MTb%d' % hh], lambda e, hh=hh, bank=bank: e.tensor_tensor(
                out=MTb[0][:, hh * 4:(hh + 1) * 4, :], in0=P[bank][:].rearrange("p (h t) -> p h t", h=4),
                in1=c_low.unsqueeze(1).broadcast_to([128, 4, 128]), op=ALU.mult))
        S.op('act', ['MATS'], ['Mb0', 'Mb1'], lambda e: e.copy(out=Mb[0][:], in_=MATS[:, :, 0:128]))
        S.op('pool', ['MATS', 'c_cstb'], ['Tb0', 'Tb1'], lambda e: e.tensor_tensor(out=Tb[0][:], in0=MATS[:, :, 0:128], in1=ident_b.unsqueeze(1).broadcast_to([128, 8, 128]), op=ALU.add))
        cur = 0
        LAST = 1 if samp else 6
        for lvl in range(1, LAST + 1):
            nxt = 1 - cur
            for hh in range(2):
                hs = slice(hh * 4, hh * 4 + 4)
                bM, bMT, bT = 2 + hh * 3, 3 + hh * 3, 4 + hh * 3
                for h4 in range(4):
                    h = hh * 4 + h4
                    cs = slice(h4 * 128, (h4 + 1) * 128)
                    if lvl < LAST:
                        S.op('pe', ['Mb%d' % hh, 'MTb%d' % hh], [PK[bM]], lambda e, h=h, cs=cs, bM=bM, cur=cur: e.matmul(out=P[bM][:, cs], lhsT=MTb[cur][:, h, :], rhs=Mb[cur][:, h, :], start=True, stop=True))
                    S.op('pe', ['Mb%d' % hh, 'MTb%d' % hh], [PK[bMT]], lambda e, h=h, cs=cs, bMT=bMT, cur=cur: e.matmul(out=P[bMT][:, cs], lhsT=Mb[cur][:, h, :], rhs=MTb[cur][:, h, :], start=True, stop=True))
                if lvl < LAST:
                    S.op('act', [PK[bM]], ['Mb%d' % hh], lambda e, hs=hs, bM=bM, nxt=nxt: e.copy(out=Mb[nxt][:, hs, :], in_=P[bM][:].rearrange("p (h t) -> p h t", h=4)))
                S.op('dve', [PK[bMT]], ['MTb%d' % hh], lambda e, hs=hs, bMT=bMT, nxt=nxt: e.tensor_copy(out=MTb[nxt][:, hs, :], in_=P[bMT][:].rearrange("p (h t) -> p h t", h=4)))
                for h4 in range(4):
                    h = hh * 4 + h4
                    cs = slice(h4 * 128, (h4 + 1) * 128)
                    S.op('pe', ['MTb%d' % hh, 'Tb%d' % hh], [PK[bT]], lambda e, h=h, cs=cs, bT=bT, cur=cur, nxt=nxt: e.matmul(out=P[bT][:, cs], lhsT=MTb[nxt][:, h, :], rhs=Tb[cur][:, h, :], start=True, stop=True))
                S.op('dve', [PK[bT], 'Tb%d' % hh], ['Tb%d' % hh], lambda e, hs=hs, bT=bT, cur=cur, nxt=nxt: e.tensor_tensor(
                    out=Tb[nxt][:, hs, :], in0=P[bT][:].rearrange("p (h t) -> p h t", h=4), in1=Tb[cur][:, hs, :], op=ALU.add))
            cur = nxt
        Tf = Tb[cur]
        TfK = 'Tb'
        DUMP("AR_fm", AR_fm, 'AR_fm', ti); DUMP("K_fm", K_fm, 'K_fm', ti); DUMP("MATS", MATS, 'MATS', ti); DUMP("Tb", Tb[0], 'Tb0', ti)
        if OPTS['stage'] <= 4:
            return
        if samp or ti == 0:
            if samp:
                S.dma('sp', sti[:], swkv[sq].rearrange("h i j -> i h j"), [], ['sti'])
                for h in range(8):
                    S.op('pe', ['sti', 'c_cst'], ['P0'], lambda e, h=h: e.transpose(out=P[0][0:64, h * 64:(h + 1) * 64], in_=sti[:, h, :], identity=ident_f[0:64, 0:64]))
                S.op('dve', ['P0'], ['S32'], lambda e: e.tensor_copy(out=S32[:], in_=P[0][0:64, :].rearrange("p (h i) -> p h i", h=8)))
            else:
                S.op('dve', [], ['S32'], lambda e: e.memset(S32[:], 0.0))
            S.op('act', ['S32'], ['Sb'], lambda e: e.copy(out=Sb[:], in_=S32[:]))
        for h in range(8):
            cs = slice(h * 64, (h + 1) * 64)
            S.op('pe', ['AR_fm', 'Sb'], ['P0'], lambda e, h=h, cs=cs: e.matmul(out=P[0][:, cs], lhsT=AR_fm[:, h, 0:128], rhs=Sb[:, h, :], start=True, stop=False))
            S.op('pe', ['MATS', 'b_vb'], ['P0'], lambda e, h=h, cs=cs: e.matmul(out=P[0][:, cs], lhsT=MATS[:, h, 256:384], rhs=Bt["vb"][:, cs], start=False, stop=True))
        S.op('act', ['P0'], ['b_W0T'], lambda e: e.copy(out=Bt["W0T"][:], in_=P[0][:]))
        for h in range(8):
            cs = slice(h * 64, (h + 1) * 64)
            S.op('pe', ['Tb%d' % (h // 4), 'b_W0T'], ['P1'], lambda e, h=h, cs=cs: e.matmul(out=P[1][:, cs], lhsT=Tf[:, h, :], rhs=Bt["W0T"][:, cs], start=True, stop=True))
        S.op('act', ['P1'], ['b_UT'], lambda e: e.copy(out=Bt["UT"][:], in_=P[1][:]))
        for h in range(8):
            cs = slice(h * 64, (h + 1) * 64)
            S.op('pe', ['AR_fm', 'Sb'], ['P0'], lambda e, h=h, cs=cs: e.matmul(out=P[0][:, cs], lhsT=AR_fm[:, h, 128:256], rhs=Sb[:, h, :], start=True, stop=False))
            S.op('pe', ['MATS', 'b_UT'], ['P0'], lambda e, h=h, cs=cs: e.matmul(out=P[0][:, cs], lhsT=MATS[:, h, 128:256], rhs=Bt["UT"][:, cs], start=False, stop=False))
            S.op('pe', ['MATS', 'b_vb'], ['P0'], lambda e, h=h, cs=cs: e.matmul(out=P[0][:, cs], lhsT=MATS[:, h, 384:512], rhs=Bt["vb"][:, cs], start=False, stop=True))
        for h in range(8):
            cs = slice(h * 64, (h + 1) * 64)
            S.op('pe', ['b_bbar', 'b_UT'], ['P1'], lambda e, h=h, cs=cs: e.matmul(out=P[1][0:64, cs], lhsT=Bt["bbar"][:, cs], rhs=Bt["UT"][:, cs], start=True, stop=False))
            S.op('pe', ['b_kbar', 'b_vb'], ['P1'], lambda e, h=h, cs=cs: e.matmul(out=P[1][0:64, cs], lhsT=Bt["kbar"][:, cs], rhs=Bt["vb"][:, cs], start=False, stop=True))
        S.op('dve', ['S32', 'ecl_fm'], ['S32'], lambda e: e.tensor_tensor(out=S32[:], in0=S32[:], in1=ecl_fm[:].unsqueeze(2).broadcast_to([64, 8, 64]), op=ALU.mult))
        S.op('dve', ['S32', 'P1'], ['S32'], lambda e: e.tensor_tensor(out=S32[:], in0=S32[:], in1=P[1][0:64, :].rearrange("p (h i) -> p h i", h=8), op=ALU.add))
        S.op('act', ['S32'], ['Sb'], lambda e: e.copy(out=Sb[:], in_=S32[:]))
        if samp or ti == OUT_T:
            for h in range(8):
                S.op('pe', ['S32', 'c_cst'], ['P2'], lambda e, h=h: e.transpose(out=P[2][0:64, h * 64:(h + 1) * 64], in_=S32[:, h, :], identity=ident_f[0:64, 0:64]))
            S.op('act', ['P2'], ['sti'], lambda e: e.copy(out=sti[:], in_=P[2][0:64, :].rearrange("p (h j) -> p h j", h=8)))
            dst = wkv_s[sq] if samp else wkv_p
            S.dma('sp', dst.rearrange("h i j -> i h j"), sti[:], ['sti'], [])
        DUMP("S32", S32, 'S32', ti); DUMP("b_UT", Bt["UT"], 'b_UT', ti); DUMP("b_W0T", Bt["W0T"], 'b_W0T', ti)
        if OPTS['stage'] <= 5:
            return
        state_only = ti < NT_A
        if state_only and ti != NT_A - 1:
            return
        if not state_only:
            rwkv_post(ti)
        attn_and_out(ti, samp, sq, xk, xb, par, state_only)

    def rwkv_post(ti):
        if True:
            pass
        Y3 = v3(P[0][:])
        S.op('dve', ['P0'], ['st8'], lambda e: e.tensor_reduce(out=st8[:, :, 0], in_=Y3, axis=AX.X, op=ALU.add))
        S.op('dve', ['st8'], ['st8'], lambda e: e.tensor_scalar(out=st8[:, :, 0], in0=st8[:, :, 0], scalar1=1.0 / 64, scalar2=None, op0=ALU.mult))
        S.op('dve', ['P0', 'st8'], ['a_t1'], lambda e: e.tensor_tensor(out=v3(A["t1"][:]), in0=Y3, in1=bc_last(st8[:, :, 0:1], 64), op=ALU.subtract))
        S.op('pool', ['a_t1'], ['a_t2'], lambda e: e.tensor_tensor(out=A["t2"][:], in0=A["t1"][:], in1=A["t1"][:], op=ALU.mult))
        S.op('dve', ['a_t2'], ['st8'], lambda e: e.tensor_reduce(out=st8[:, :, 1], in_=v3(A["t2"][:]), axis=AX.X, op=ALU.add))
        S.op('dve', ['st8'], ['st8'], lambda e: e.tensor_scalar(out=st8[:, :, 1], in0=st8[:, :, 1], scalar1=1.0 / 64, scalar2=64e-5, op0=ALU.mult, op1=ALU.add))
        rsq('st8', st8[:, :, 1], st8[:, :, 1], 0.0, ALU.add)
        S.op('dve', ['a_t1', 'st8'], ['a_t1'], lambda e: e.tensor_tensor(out=v3(A["t1"][:]), in0=v3(A["t1"][:]), in1=bc_last(st8[:, :, 1:2], 64), op=ALU.mult))
        S.op('pool', ['a_t1', 'c_v512'], ['a_t1'], lambda e: e.tensor_tensor(out=A["t1"][:], in0=A["t1"][:], in1=V512(LW), op=ALU.mult))
        S.op('pool', ['a_t1', 'c_v512'], ['a_t1'], lambda e: e.tensor_tensor(out=A["t1"][:], in0=A["t1"][:], in1=V512(LB), op=ALU.add))
        S.op('dve', ['a_v', 'st8'], ['a_t2'], lambda e: e.tensor_tensor(out=v3(A["t2"][:]), in0=v3(A["v"][:]), in1=bc_last(st8[:, :, 2:3], 64), op=ALU.mult))
        S.op('pool', ['a_t1', 'a_t2'], ['a_t1'], lambda e: e.tensor_tensor(out=A["t1"][:], in0=A["t1"][:], in1=A["t2"][:], op=ALU.add))
        S.op('dve', ['a_t1', 'a_g'], ['ycat'], lambda e: e.tensor_tensor(out=ycat[:, 0:512], in0=A["t1"][:], in1=A["g"][:], op=ALU.mult))
        DUMP("ycat_rw", ycat[:, 0:512], 'ycat', ti)

    def attn_and_out(ti, samp, sq, xk, xb, par, state_only):
        pTb = [P[i][:].bitcast(BF16) for i in range(8)]
        if OPTS['stage'] <= 6:
            return
        ri = NT_P if samp else ti
        cosb = c_rope[:, ri, 0:8].unsqueeze(1)
        sinb = c_rope[:, ri, 8:16].unsqueeze(1)
        for (c0, nh) in [(0, 8), (512, 2)]:
            X = qkv[:, c0:c0 + nh * 64].rearrange("p (h j) -> p h j", h=nh)
            x1, x2 = X[:, :, 0:8], X[:, :, 8:16]
            cb = cosb.broadcast_to([128, nh, 8])
            sbb = sinb.broadcast_to([128, nh, 8])
            R = rtmp[:, 0:nh, :]
            T1 = rtmp[:, 0:nh, :]
            ra = rtmp[:].rearrange("p a b -> p (a b)")
            t_a = ra[:, 0:nh * 8].rearrange("p (h j) -> p h j", h=nh)
            t_b = ra[:, 80 - 0:80].rearrange("p (h j) -> p h j", h=1) if False else None
            S.op('dve', ['qkv', 'c_rope'], ['rtmp'], lambda e, x1=x1, cb=cb, t_a=t_a: e.tensor_tensor(out=t_a, in0=x1, in1=cb, op=ALU.mult))
            S.op('dve', ['qkv', 'c_rope'], ['sm'], lambda e, x2=x2, sbb=sbb, nh=nh: e.tensor_tensor(out=sm[:, 0:nh * 8].rearrange("p (h j) -> p h j", h=nh), in0=x2, in1=sbb, op=ALU.mult))
            S.op('dve', ['qkv', 'c_rope'], ['sm'], lambda e, x2=x2, cb=cb, nh=nh: e.tensor_tensor(out=sm[:, 64:64 + nh * 8].rearrange("p (h j) -> p h j", h=nh), in0=x2, in1=cb, op=ALU.mult))
            S.op('dve', ['qkv', 'c_rope'], ['sm'], lambda e, x1=x1, sbb=sbb, nh=nh: e.tensor_tensor(out=sm[:, 128:128 + nh * 8].rearrange("p (h j) -> p h j", h=nh), in0=x1, in1=sbb, op=ALU.mult))
            S.op('dve', ['rtmp', 'sm'], ['qkv'], lambda e, x1=x1, t_a=t_a, nh=nh: e.tensor_tensor(out=x1, in0=t_a, in1=sm[:, 0:nh * 8].rearrange("p (h j) -> p h j", h=nh), op=ALU.subtract))
            S.op('dve', ['sm'], ['qkv'], lambda e, x2=x2, nh=nh: e.tensor_tensor(out=x2, in0=sm[:, 64:64 + nh * 8].rearrange("p (h j) -> p h j", h=nh), in1=sm[:, 128:128 + nh * 8].rearrange("p (h j) -> p h j", h=nh), op=ALU.add))
        S.op('act', ['qkv'], ['qb'], lambda e: e.copy(out=qb[:], in_=qkv[:]))
        S.op('pool', ['qkv'], ['Vb%d' % par], lambda e: e.tensor_copy(out=Vb[par][:], in_=qkv[:, 640:768]))
        for h in range(8):
            S.op('pe', ['qb', 'c_cstb'], ['P2'], lambda e, h=h: e.transpose(out=pTb[2][0:64, h * 128:(h + 1) * 128], in_=qb[:, h * 64:(h + 1) * 64], identity=ident_b))
        for kv in range(2):
            S.op('pe', ['qb', 'c_cstb'], ['P3'], lambda e, kv=kv: e.transpose(out=pTb[3][0:64, kv * 128:(kv + 1) * 128], in_=qb[:, 512 + kv * 64:512 + (kv + 1) * 64], identity=ident_b))
        S.op('act', ['P2'], ['QT'], lambda e: e.copy(out=QT[:], in_=pTb[2][0:64, 0:1024].rearrange("p (h t) -> p h t", h=8)))
        S.op('dve', ['P3'], ['KT%d' % par], lambda e: e.tensor_copy(out=KT[par][:], in_=pTb[3][0:64, 0:256].rearrange("p (h t) -> p h t", h=2)))
        pp = 1 - par
        if samp:
            S.dma('sp', ckb[:, 0:128], ck[sq], [], ['sm'])
            S.dma('sp', ckb[:, 128:256], cv[sq], [], ['sm'])
            S.op('act', ['sm'], ['ycatT'], lambda e: e.copy(out=junk[:, 0:256], in_=ckb[:]))
            for kv in range(2):
                S.op('pe', ['ycatT', 'c_cstb'], ['P3'], lambda e, kv=kv: e.transpose(out=pTb[3][0:64, 256 + kv * 128:256 + (kv + 1) * 128], in_=junk[:, kv * 64:(kv + 1) * 64], identity=ident_b))
            S.op('dve', ['P3'], ['KT%d' % pp], lambda e: e.tensor_copy(out=KT[pp][:], in_=pTb[3][0:64, 256:512].rearrange("p (h t) -> p h t", h=2)))
            S.op('pool', ['ycatT'], ['Vb%d' % pp], lambda e: e.tensor_copy(out=Vb[pp][:], in_=junk[:, 128:256]))
            S.dma('sp', kw_s[sq, 0:124, :], ck[sq, 4:128, :], [], [])
            S.dma('sp', vw_s[sq, 0:124, :], cv[sq, 4:128, :], [], [])
            S.dma('sp', kw_s[sq, 124:128, :], qkv[0:4, 512:640], ['qkv'], [])
            S.dma('sp', vw_s[sq, 124:128, :], qkv[0:4, 640:768], ['qkv'], [])
        elif ti == OUT_T:
            S.dma('sp', kw_p, qkv[:, 512:640], ['qkv'], [])
            S.dma('sp', vw_p, qkv[:, 640:768], ['qkv'], [])
        if state_only:
            return
        mi = 2 if (samp or ti - NT_A > 1) else (ti - NT_A)
        SMB = [A["t1"], A["t2"], A["t3"], A["c"]]
        SMK = ['a_t1', 'a_t2', 'a_t3', 'a_c']
        EBB = [A["kk"][:].bitcast(BF16), A["b"][:].bitcast(BF16)]
        EBK = ['a_kk', 'a_b']
        ETB = [A["kmod"][:].bitcast(BF16), A["ld"][:].bitcast(BF16)]
        ETK = ['a_kmod', 'a_ld']
        for h in range(8):
            kv = h // 4
            bank = 4 + h // 2
            c0 = (h % 2) * 256
            S.op('pe', ['QT', 'KT%d' % pp], [PK[bank]], lambda e, h=h, kv=kv, bank=bank, c0=c0: e.matmul(out=P[bank][:, c0:c0 + 128], lhsT=QT[:, h, :], rhs=KT[pp][:, kv, :], start=True, stop=True))
            S.op('pe', ['QT', 'KT%d' % par], [PK[bank]], lambda e, h=h, kv=kv, bank=bank, c0=c0: e.matmul(out=P[bank][:, c0 + 128:c0 + 256], lhsT=QT[:, h, :], rhs=KT[par][:, kv, :], start=True, stop=True))
        for j in range(4):
            S.op('dve', [PK[4 + j], 'c_amask'], [SMK[j]], lambda e, j=j: e.scalar_tensor_tensor(
                out=SMB[j][:].rearrange("p (h c) -> p h c", h=2), in0=P[4 + j][:].rearrange("p (h c) -> p h c", h=2), scalar=0.125,
                in1=c_amask[:, mi, :].unsqueeze(1).broadcast_to([128, 2, 256]), op0=ALU.mult, op1=ALU.add))
            S.op('dve', [SMK[j]], ['ast'], lambda e, j=j: e.tensor_reduce(out=ast8[:, 0, 2 * j:2 * j + 2], in_=SMB[j][:].rearrange("p (h c) -> p h c", h=2), axis=AX.X, op=ALU.max))
        S.op('dve', ['ast', 'c_sink'], ['ast'], lambda e: e.tensor_tensor(out=ast8[:, 1, :], in0=ast8[:, 0, :], in1=c_sink[:], op=ALU.max))
        S.op('dve', ['ast'], ['ast'], lambda e: e.tensor_scalar(out=ast8[:, 1, :], in0=ast8[:, 1, :], scalar1=-1.0, scalar2=None, op0=ALU.mult))
        for h in range(8):
            S.op('act', [SMK[h // 2], 'ast'], [EBK[h // 4], 'ast2'], lambda e, h=h: e.activation(
                out=EBB[h // 4][:, (h % 4) * 256:(h % 4 + 1) * 256], in_=SMB[h // 2][:, (h % 2) * 256:(h % 2 + 1) * 256], func=AF.Exp,
                bias=ast8[:, 1, h:h + 1], scale=1.0, accum_out=ast8[:, 2, h:h + 1]))
        S.op('dve', ['ast', 'c_sink'], ['ast3'], lambda e: e.tensor_tensor(out=ast8[:, 3, :], in0=ast8[:, 1, :], in1=c_sink[:], op=ALU.add))
        S.op('act', ['ast3'], ['ast3'], lambda e: e.activation(out=ast8[:, 3, :], in_=ast8[:, 3, :], func=AF.Exp))
        S.op('dve', ['ast2', 'ast3'], ['ast3'], lambda e: e.tensor_tensor(out=ast8[:, 3, :], in0=ast8[:, 3, :], in1=ast8[:, 2, :], op=ALU.add))
        S.op('dve', ['ast3'], ['ast3'], lambda e: e.reciprocal(out=ast8[:, 3, :], in_=ast8[:, 3, :]))
        for h in range(8):
            for half in range(2):
                S.op('pe', [EBK[h // 4], 'c_cstb'], [PK[h // 4]], lambda e, h=h, half=half: e.transpose(
                    out=pTb[h // 4][:, (h % 4) * 256 + half * 128:(h % 4) * 256 + (half + 1) * 128],
                    in_=EBB[h // 4][:, (h % 4) * 256 + half * 128:(h % 4) * 256 + (half + 1) * 128], identity=ident_b))
        S.op('act', ['P0'], [ETK[0]], lambda e: e.copy(out=ETB[0], in_=pTb[0][:, 0:1024]))
        S.op('dve', ['P1'], [ETK[1]], lambda e: e.tensor_copy(out=ETB[1], in_=pTb[1][:, 0:1024]))
        for h in range(8):
            kv = h // 4
            o0 = (h % 4) * 256
            S.op('pe', [ETK[h // 4], 'Vb%d' % pp], ['P2'], lambda e, h=h, kv=kv, o0=o0: e.matmul(out=P[2][:, h * 64:(h + 1) * 64], lhsT=ETB[h // 4][:, o0:o0 + 128], rhs=Vb[pp][:, kv * 64:(kv + 1) * 64], start=True, stop=False))
            S.op('pe', [ETK[h // 4], 'Vb%d' % par], ['P2'], lambda e, h=h, kv=kv, o0=o0: e.matmul(out=P[2][:, h * 64:(h + 1) * 64], lhsT=ETB[h // 4][:, o0 + 128:o0 + 256], rhs=Vb[par][:, kv * 64:(kv + 1) * 64], start=False, stop=True))
        S.op('dve', ['P2', 'ast3'], ['ycat'], lambda e: e.tensor_tensor(out=ycat[:, 512:1024].rearrange("p (h j) -> p h j", h=8), in0=P[2][:].rearrange("p (h j) -> p h j", h=8),
                                                                       in1=ast8[:, 3, :].unsqueeze(2).broadcast_to([128, 8, 64]), op=ALU.mult))
        DUMP("ycat", ycat, 'ycat', ti); DUMP("qkv", qkv, 'qkv', ti); DUMP("QT", QT, 'QT', ti)
        if OPTS['stage'] <= 7:
            return
        for kc in range(8):
            S.op('pe', ['ycat', 'c_cstb'], ['P2'], lambda e, kc=kc: e.transpose(out=pTb[2][:, kc * 128:(kc + 1) * 128], in_=ycat[:, kc * 128:(kc + 1) * 128], identity=ident_b))
        S.op('act', ['P2'], ['ycatT'], lambda e: e.copy(out=ycatT[:], in_=pTb[2][:, 0:1024].rearrange("p (k t) -> p k t", k=8)))
        for half in range(2):
            bank = 3 + half
            for kc in range(8):
                S.op('pe', ['ycatT', 'Wo'], [PK[bank]], lambda e, kc=kc, half=half, bank=bank: e.matmul(out=P[bank][:], lhsT=ycatT[:, kc, :], rhs=Wo[:, kc, half * 512:(half + 1) * 512], start=(kc == 0), stop=(kc == 7)))
            S.op('dve', [PK[bank], xk], ['xm'], lambda e, half=half, bank=bank: e.tensor_tensor(out=xm[:, half * 512:(half + 1) * 512], in0=P[bank][:], in1=xb[:, half * 512:(half + 1) * 512], op=ALU.add))
        if dbg and ti == OPTS['dbg_tile']:
            S.op('dve', ['ycat'], ['a_t1'], lambda e: e.tensor_copy(out=A["t1"][:], in_=ycat[:, 0:512]))
            S.op('dve', ['ycat'], ['a_t2'], lambda e: e.tensor_copy(out=A["t2"][:], in_=ycat[:, 512:1024]))
            S.dma('sp', dbg_t["d_y"], A["t1"][:], ['a_t1'], [])
            S.dma('sp', dbg_t["d_at"], A["t2"][:], ['a_t2'], [])
            S.dma('sp', dbg_t["d_S"], S32[:].rearrange("p h i -> p (h i)"), ['S32'], [])
        DUMP("xm", xm, 'xm', ti)
        if samp:
            S.dma('sp', xmid[NT_B * 128 + sq * 4:NT_B * 128 + sq * 4 + 4, :], xm[0:4, :], ['xm'], ['xmid'])
        else:
            S.dma('sp', xmid[(ti - NT_A) * 128:(ti - NT_A + 1) * 128, :], xm[:], ['xm'], ['xmid'])


    def peer_phase():
        c_g2bc = sb("c_g2bc", [128, D])
        S.dma('sp', c_g2bc[:], g2.partition_broadcast(128)[:, 0, :], [], ['c_g2bc'])
        c_gFbc = sb("c_gFbc", [128, D])
        S.dma('sp', c_gFbc[:], gF.partition_broadcast(128)[:, 0, :], [], ['c_gFbc'])
        c_iota = sb("c_iota", [128, 256])
        S.dma('sp', c_iota[:], iota_in, [], ['c_iota'])
        Wq = sb("Wq", [128, 8, 2048], BF16)
        skT = sb("skT", [128, 2, 128], BF16)
        xm2s = [sb("xm2_%d" % i, [128, D]) for i in range(2)]
        hn32 = sb("hn32", [128, D])
        hnb = sb("hnb", [128, D], BF16)
        hn2T = sb("hn2T", [128, 8, 128], BF16)
        qT = sb("qT", [128, 16, 128], BF16)
        s_sb = sb("s_sb", [128, 16, 128])
        s2 = sb("s2", [128, 16, 128])
        tv = sb("tv", [128, 16, 16])
        tiu = sb("tiu", [128, 16, 16], U32)
        tif = sb("tif", [128, 16, 16])
        cand = sb("cand", [128, 8, 256])
        cand2 = sb("cand2", [128, 8, 256])
        cidx = sb("cidx", [128, 8, 256])
        top = sb("top", [128, 8, 16])
        selu = sb("selu", [128, 8, 16], U32)
        self_ = sb("self", [128, 8, 16])
        idxf = sb("idxf", [128, 8, 16])
        idx2 = sb("idx2", [128, 8, 16])
        selu2 = sb("selu2", [128, 2, 8, 16], U32)
        sela = sb("sela", [128, 8, 16])
        selb = sb("selb", [128, 8, 16])
        idxus = [sb("idxu_%d" % i, [128, 128], U32) for i in range(2)]
        gate = sb("gate", [128, 8, 16])
        gst = sb("gst", [128, 8, 2])
        pre = sb("pre", [128, 128])
        wgt = sb("wgt", [128, 128])
        acc = sb("acc", [128, D])
        ss2 = sb("ss2", [128, 4])
        NG = OPTS['ng']
        gb = [sb("gb%d" % i, [128, D]) for i in range(NG)]
        gbs = [sb("gbs%d" % i, [128, D], BF16) for i in range(3)]
        for kc in range(8):
            for hf2 in range(2):
                S.dma('sp', gb[hf2][:], w_q[kc * 128:(kc + 1) * 128, hf2 * 1024:(hf2 + 1) * 1024], [], ['gb%d' % hf2])
                S.op('act' if hf2 else 'dve', ['gb%d' % hf2], ['Wq'], lambda e, kc=kc, hf2=hf2: (e.copy if hf2 else e.tensor_copy)(out=Wq[:, kc, hf2 * 1024:(hf2 + 1) * 1024], in_=gb[hf2][:]))
        S.dma('sp', gb[2][:, 0:256].rearrange("p (c d) -> p c d", c=2), subk.rearrange("c n d -> n c d"), [], ['gb2'])
        S.op('act', ['gb2'], ['hnb'], lambda e: e.copy(out=hnb[:, 0:256], in_=gb[2][:, 0:256]))
        for c in range(2):
            S.op('pe', ['hnb', 'c_cstb'], ['P0'], lambda e, c=c: e.transpose(out=P[0][:].bitcast(BF16)[:, c * 128:(c + 1) * 128], in_=hnb[:, c * 128:(c + 1) * 128], identity=ident_b))
        S.op('act', ['P0'], ['skT'], lambda e: e.copy(out=skT[:], in_=P[0][:].bitcast(BF16)[:, 0:256].rearrange("p (c n) -> p c n", c=2)))
        pTb = [P[i][:].bitcast(BF16) for i in range(8)]
        def A_part(pt, bi):
            xm2 = xm2s[bi]
            idxu = idxus[bi]
            kx = 'xm2_%d' % bi
            ki = 'idxu_%d' % bi
            S.dma('sp', xm2[:], xmid[pt * 128:(pt + 1) * 128, :], ['xmid'], [kx])
            S.op('act', [kx], ['hnb', 'ss2'], lambda e: e.activation(out=hnb[:], in_=xm2[:], func=AF.Square, accum_out=ss2[:, 0:1]))
            rsq('ss2', ss2[:, 1:2], ss2[:, 0:1], D * 1e-5, ALU.add)
            S.op('dve', [kx, 'ss2'], ['hn32'], lambda e: e.tensor_scalar(out=hn32[:], in0=xm2[:], scalar1=ss2[:, 1:2], scalar2=32.0, op0=ALU.mult, op1=ALU.mult))
            S.op('pool', ['hn32', 'c_g2bc'], ['hn32'], lambda e: e.tensor_tensor(out=hn32[:], in0=hn32[:], in1=c_g2bc[:], op=ALU.mult))
            S.op('act', ['hn32'], ['hnb'], lambda e: e.copy(out=hnb[:], in_=hn32[:]))
            yield
            for kc in range(8):
                S.op('pe', ['hnb', 'c_cstb'], ['P0'], lambda e, kc=kc: e.transpose(out=pTb[0][:, kc * 128:(kc + 1) * 128], in_=hnb[:, kc * 128:(kc + 1) * 128], identity=ident_b))
            S.op('act', ['P0'], ['hn2T'], lambda e: e.copy(out=hn2T[:], in_=pTb[0][:, 0:1024].rearrange("p (k t) -> p k t", k=8)))
            yield
            for r in range(2):
                for hc in range(8 * r, 8 * r + 8):
                    bank = 1 + (hc % 8) // 4
                    cs = slice((hc % 4) * 128, (hc % 4 + 1) * 128)
                    for kc in range(8):
                        S.op('pe', ['Wq', 'hn2T'], [PK[bank]], lambda e, hc=hc, kc=kc, bank=bank, cs=cs: e.matmul(out=P[bank][:, cs], lhsT=Wq[:, kc, hc * 128:(hc + 1) * 128], rhs=hn2T[:, kc, :], start=(kc == 0), stop=(kc == 7)))
                    yield
                for b in range(2):
                    S.op('act' if b % 2 else 'dve', [PK[1 + b]], ['qT'], lambda e, b=b, r=r: (e.copy if b % 2 else e.tensor_copy)(out=qT[:, r * 8 + b * 4:r * 8 + (b + 1) * 4, :], in_=P[1 + b][:].rearrange("p (a t) -> p a t", a=4)))
                yield
            for r in range(2):
                for hc in range(8 * r, 8 * r + 8):
                    bank = 5 + (hc % 8) // 4
                    cs = slice((hc % 4) * 128, (hc % 4 + 1) * 128)
                    S.op('pe', ['qT', 'skT'], [PK[bank]], lambda e, hc=hc, bank=bank, cs=cs: e.matmul(out=P[bank][:, cs], lhsT=qT[:, hc, :], rhs=skT[:, hc % 2, :], start=True, stop=True))
                for b in range(2):
                    S.op('act' if b % 2 else 'dve', [PK[5 + b]], ['s_sb'], lambda e, b=b, r=r: (e.copy if b % 2 else e.tensor_copy)(out=s_sb[:, r * 8 + b * 4:r * 8 + (b + 1) * 4, :], in_=P[5 + b][:].rearrange("p (a t) -> p a t", a=4)))
                yield
            for hc in range(16):
                S.op('dve', ['s_sb'], ['tv'], lambda e, hc=hc: e.max(out=tv[:, hc, 0:8], in_=s_sb[:, hc, :]))
                S.op('dve', ['s_sb', 'tv'], ['tiu'], lambda e, hc=hc: e.max_index(out=tiu[:, hc, 0:8], in_max=tv[:, hc, 0:8], in_values=s_sb[:, hc, :]))
                S.op('dve', ['s_sb', 'tv'], ['s2'], lambda e, hc=hc: e.match_replace(out=s2[:, hc, :], in_to_replace=tv[:, hc, 0:8], in_values=s_sb[:, hc, :], imm_value=-1e30))
                S.op('dve', ['s2'], ['tv'], lambda e, hc=hc: e.max(out=tv[:, hc, 8:16], in_=s2[:, hc, :]))
                S.op('dve', ['s2', 'tv'], ['tiu'], lambda e, hc=hc: e.max_index(out=tiu[:, hc, 8:16], in_max=tv[:, hc, 8:16], in_values=s2[:, hc, :]))
                yield
            S.op('dve', ['tiu'], ['tif'], lambda e: e.tensor_copy(out=tif[:], in_=tiu[:]))
            yield
            tv4 = tv[:].rearrange("p (h c) k -> p h c k", c=2)
            tf4 = tif[:].rearrange("p (h c) k -> p h c k", c=2)
            c4 = lambda t: t[:].rearrange("p h (a b) -> p h a b", a=16)
            S.op('dve', ['tv'], ['cand'], lambda e: e.tensor_tensor(out=c4(cand), in0=tv4[:, :, 0, :].unsqueeze(3).broadcast_to([128, 8, 16, 16]),
                                                                    in1=tv4[:, :, 1, :].unsqueeze(2).broadcast_to([128, 8, 16, 16]), op=ALU.add))
            S.op('dve', ['tif'], ['tif'], lambda e: e.tensor_scalar(out=tf4[:, :, 0, :], in0=tf4[:, :, 0, :], scalar1=128.0, scalar2=None, op0=ALU.mult))
            S.op('dve', ['tif'], ['cidx'], lambda e: e.tensor_tensor(out=c4(cidx), in0=tf4[:, :, 0, :].unsqueeze(3).broadcast_to([128, 8, 16, 16]),
                                                                     in1=tf4[:, :, 1, :].unsqueeze(2).broadcast_to([128, 8, 16, 16]), op=ALU.add))
            for h in range(8):
                S.op('dve', ['cand'], ['top'], lambda e, h=h: e.max(out=top[:, h, 0:8], in_=cand[:, h, :]))
                S.op('dve', ['cand', 'top'], ['selu'], lambda e, h=h: e.max_index(out=selu[:, h, 0:8], in_max=top[:, h, 0:8], in_values=cand[:, h, :]))
                S.op('dve', ['cand', 'top'], ['cand2'], lambda e, h=h: e.match_replace(out=cand2[:, h, :], in_to_replace=top[:, h, 0:8], in_values=cand[:, h, :], imm_value=-1e30))
                S.op('dve', ['cand2'], ['top'], lambda e, h=h: e.max(out=top[:, h, 8:16], in_=cand2[:, h, :]))
                S.op('dve', ['cand2', 'top'], ['selu'], lambda e, h=h: e.max_index(out=selu[:, h, 8:16], in_max=top[:, h, 8:16], in_values=cand2[:, h, :]))
                yield
            S.op('dve', ['selu'], ['self'], lambda e: e.tensor_copy(out=self_[:], in_=selu[:]))
            S.op('dve', ['selu'], ['selu2'], lambda e: e.tensor_scalar(out=selu2[:, 0], in0=selu[:], scalar1=4, scalar2=None, op0=ALU.logical_shift_right))
            S.op('dve', ['selu'], ['selu2'], lambda e: e.tensor_scalar(out=selu2[:, 1], in0=selu[:], scalar1=15, scalar2=None, op0=ALU.bitwise_and))
            S.op('dve', ['selu2'], ['sela'], lambda e: e.tensor_copy(out=sela[:], in_=selu2[:, 0]))
            S.op('dve', ['selu2'], ['selb'], lambda e: e.tensor_copy(out=selb[:], in_=selu2[:, 1]))
            io16 = c_iota[:, 0:16].unsqueeze(1).unsqueeze(1).broadcast_to([128, 8, 16, 16])
            for which, (selx, dst) in enumerate([(sela, idxf), (selb, idx2)]):
                S.op('dve', ['sela', 'selb', 'c_iota'], ['cand2'], lambda e, selx=selx: e.tensor_tensor(out=c4(cand2), in0=io16, in1=selx[:].unsqueeze(3).broadcast_to([128, 8, 16, 16]), op=ALU.is_equal))
                S.op('dve', ['cand2', 'tif'], ['cand2'], lambda e, which=which: e.tensor_tensor(out=c4(cand2), in0=c4(cand2), in1=tf4[:, :, which, :].unsqueeze(2).broadcast_to([128, 8, 16, 16]), op=ALU.mult))
                S.op('dve', ['cand2'], ['idxf' if which == 0 else 'idx2'], lambda e, dst=dst: e.tensor_reduce(out=dst[:], in_=c4(cand2), axis=AX.X, op=ALU.add))
                yield
            S.op('dve', ['idxf', 'idx2'], ['idxf'], lambda e: e.tensor_tensor(out=idxf[:], in0=idxf[:], in1=idx2[:], op=ALU.add))
            S.op('dve', ['idxf'], ['idxf'], lambda e: e.tensor_scalar(out=idxf[:], in0=idxf[:], scalar1=0.0, scalar2=float(NEXP - 1), op0=ALU.max, op1=ALU.min))
            S.op('dve', ['idxf'], [ki], lambda e: e.tensor_copy(out=idxu[:], in_=idxf[:].rearrange("p h k -> p (h k)")))
            S.op('dve', ['top'], ['gate'], lambda e: e.tensor_tensor(out=gate[:], in0=top[:], in1=top[:, :, 0:1].broadcast_to([128, 8, 16]), op=ALU.subtract))
            S.op('act', ['gate'], ['gate'], lambda e: e.activation(out=gate[:], in_=gate[:], func=AF.Exp))
            S.op('dve', ['gate'], ['gst'], lambda e: e.tensor_reduce(out=gst[:, :, 0], in_=gate[:], axis=AX.X, op=ALU.add))
            S.op('dve', ['gst'], ['gst'], lambda e: e.reciprocal(out=gst[:, :, 1], in_=gst[:, :, 0]))
            S.op('dve', ['gate', 'gst'], ['gate'], lambda e: e.tensor_tensor(out=gate[:], in0=gate[:], in1=gst[:, :, 1:2].broadcast_to([128, 8, 16]), op=ALU.mult))
        gen = A_part(0, 0)
        for _ in gen:
            pass
        for pt in range(NPE):
            samp = pt >= NT_B
            bi = pt % 2
            xm2 = xm2s[bi]
            idxu = idxus[bi]
            kx = 'xm2_%d' % bi
            ki = 'idxu_%d' % bi
            gen = A_part(pt + 1, 1 - bi) if pt + 1 < NPE else iter(())
            for sl in range(128):
                g = sl % NG
                S.dma('pool', None, None, [ki], ['gb%d' % g], fn=lambda e, sl=sl, g=g: e.indirect_dma_start(
                    out=(gb[g][:, 0:512] if OPTS['half'] else gb[g][:]), out_offset=None, in_=(eu[:, 0:512] if OPTS['half'] else eu), in_offset=bass.IndirectOffsetOnAxis(ap=idxu[:, sl:sl + 1], axis=0)))
                S.op('dve', ['gb%d' % g, 'hn32'], ['gb%d' % g, 'pre'], lambda e, sl=sl, g=g: e.scalar_tensor_tensor(
                    out=gb[g][:], in0=gb[g][:], scalar=1.0, in1=hn32[:], op0=ALU.mult, op1=ALU.mult, accum_out=pre[:, sl:sl + 1]))
            S.op('act', ['pre'], ['wgt'], lambda e: e.activation(out=wgt[:], in_=pre[:], func=AF.Gelu))
            S.op('dve', ['wgt', 'gate'], ['wgt'], lambda e: e.tensor_tensor(out=wgt[:], in0=wgt[:], in1=gate[:].rearrange("p h k -> p (h k)"), op=ALU.mult))
            for sl in range(128):
                g = sl % NG
                gi = sl % 3
                S.dma('pool', None, None, [ki], ['gb%d' % g], fn=lambda e, sl=sl, g=g: e.indirect_dma_start(
                    out=(gb[g][:, 0:512] if OPTS['half'] else gb[g][:]), out_offset=None, in_=(ev[:, 0:512] if OPTS['half'] else ev), in_offset=bass.IndirectOffsetOnAxis(ap=idxu[:, sl:sl + 1], axis=0)))
                S.op('act', ['gb%d' % g, 'wgt'], ['gbs%d' % gi], lambda e, sl=sl, g=g, gi=gi: e.activation(
                    out=gbs[gi][:], in_=gb[g][:], func=AF.Copy, scale=wgt[:, sl:sl + 1]))
                next(gen, None)
                for half in range(2):
                    S.op('pe', ['gbs%d' % gi, 'c_cstb'], [PK[3 + half]], lambda e, sl=sl, gi=gi, half=half: e.matmul(
                        out=P[3 + half][:], lhsT=ident_b, rhs=gbs[gi][:, half * 512:(half + 1) * 512], start=(sl == 0), stop=(sl == 127)))
            for _ in gen:
                pass
            for half in range(2):
                S.op('dve', [PK[3 + half], kx], ['acc'], lambda e, half=half: e.tensor_tensor(
                    out=acc[:, half * 512:(half + 1) * 512], in0=P[3 + half][:], in1=xm2[:, half * 512:(half + 1) * 512], op=ALU.add))
            S.op('act', ['acc'], ['hnb', 'ss2'], lambda e: e.activation(out=hnb[:], in_=acc[:], func=AF.Square, accum_out=ss2[:, 2:3]))
            rsq('ss2', ss2[:, 3:4], ss2[:, 2:3], D * 1e-5, ALU.add)
            S.op('dve', ['acc', 'ss2'], ['acc'], lambda e: e.tensor_scalar(out=acc[:], in0=acc[:], scalar1=ss2[:, 3:4], scalar2=32.0, op0=ALU.mult, op1=ALU.mult))
            S.op('pool', ['acc', 'c_gFbc'], ['acc'], lambda e: e.tensor_tensor(out=acc[:], in0=acc[:], in1=c_gFbc[:], op=ALU.mult))
            if samp:
                S.dma('sp', y_s[:, :], acc[0:NSEQ_S * 4, :], ['acc'], [])
            else:
                S.dma('sp', y_p[pt * 128:(pt + 1) * 128, :], acc[:], ['acc'], [])

    for ti in range(NTT):
        load_x(ti)
        mix_tile(ti)
    print("sbuf left", nc.sbuf_bytes_remaining() if callable(nc.sbuf_bytes_remaining) else nc.sbuf_bytes_remaining)
    print("total ops", getattr(S, 'n', 0))
    barrier()
    ph1.close()
    stk['cur'] = glob_stack
    if OPTS['peer']:
        peer_phase()
    barrier()
    S.finish()
    return nc


_CACHE = {}
NTA_FULL, NTB_FULL = 17, 17


def _consts(nta, ntb, hf):
    ar = np.arange(128)
    ident = np.eye(128, dtype=np.float32)
    tri = (ar[:, None] <= ar[None, :]).astype(np.float32)
    ones = np.ones((128, 128), np.float32)
    su = (ar[:, None] < ar[None, :]).astype(np.float32)
    lo = (ar[:, None] > ar[None, :]).astype(np.float32)
    cst = np.stack([ident, tri, ones, su, tri, lo], axis=1).astype(np.float32)
    q = ar[:, None]
    c = np.arange(256)[None, :]
    ok = (c > q) & (c <= q + 128)
    m_std = np.where(ok, 0.0, -30000.0).astype(np.float32)
    m_t0 = np.where(ok & (c >= 240), 0.0, -30000.0).astype(np.float32)
    m_t1 = np.where(ok & (c >= 112), 0.0, -30000.0).astype(np.float32)
    first = (hf == 0) or (nta == 0)
    cmask = np.stack([m_t0 if first else m_std, m_t1 if first else m_std, m_std], axis=1)
    inv = (np.float32(500000.0) ** (-np.arange(0, 16, 2, dtype=np.float32) / np.float32(16))).astype(np.float32)
    ntp = nta + ntb
    rope = np.zeros((128, ntp + 1, 16), np.float32)
    for i in range(ntp + 1):
        if i < ntp:
            st = i if (hf == 1 or nta == 0) else (i - nta if i >= nta else i)
            pos = st * 128 - 112 + ar
        else:
            pos = PAST + ar
        ang = pos.astype(np.float32)[:, None] * inv[None, :]
        rope[:, i, 0:8] = np.cos(ang)
        rope[:, i, 8:16] = np.sin(ang)
    vmask = np.zeros((128, 2), np.float32)
    vmask[:, 0] = 1.0
    vmask[0:4, 1] = 1.0
    iota = np.tile(np.arange(256, dtype=np.float32)[None, :], (128, 1))
    return dict(cst=cst, cmask=cmask, rope=rope, vmask=vmask, iota=iota)


def kernel(x_prompt, x_sample, cache_k_win, cache_v_win, state_wkv, state_shift, meta_tokens, norm1_g, w_in, mu_shift,
           w0, w_lora_w2, a0, w_lora_a2, w_lora_g2, k_k, k_a, r_k, lnx_w, lnx_b, attn_sinks, w_out, norm2_g, w_query,
           sub_keys, expert_u, expert_v, final_norm_g, _nta=NTA_FULL, _ntb=NTB_FULL, _nts=NSEQ_S, _dbg=False):
    f = lambda a: np.ascontiguousarray(np.asarray(a), dtype=np.float32)
    key = (_nta, _ntb, _nts)
    if key not in _CACHE:
        _CACHE[key] = build(_nta, _ntb, _nts, dbg=_dbg)
    nc = _CACHE[key]
    x_prompt, x_sample = f(x_prompt), f(x_sample)
    B = x_prompt.shape[0]
    nseqt = _nta + _ntb - 1
    shared = dict(
        w_in=f(w_in)[0], w_out=f(w_out)[0], w_q=f(w_query)[0], subk=f(sub_keys)[0],
        eu=f(expert_u)[0][:OPTS['nexp']], ev=f(expert_v)[0][:OPTS['nexp']],
        lw2=f(w_lora_w2)[0], la2=f(w_lora_a2)[0], lg2=f(w_lora_g2)[0],
        vec512=np.stack([f(w0)[0], f(a0)[0], f(k_k)[0], f(k_a)[0], f(r_k)[0].reshape(512), f(lnx_w)[0], f(lnx_b)[0]]),
        mu=f(mu_shift), g1=f(norm1_g), g2=f(norm2_g), gF=f(final_norm_g)[None, :], sinks=f(attn_sinks))
    cs = [_consts(_nta, _ntb, hf) for hf in range(2)]
    in_maps = []
    for c in range(NCORES):
        b, hf = c // 2, c % 2
        seq = np.zeros((nseqt * 128, D), np.float32)
        seq[112:128] = f(meta_tokens)
        seq[128:] = x_prompt[b][:(nseqt - 1) * 128]
        xp = np.zeros(((_nta + _ntb) * 128, D), np.float32)
        if hf == 0:
            xp[_nta * 128:(_nta + _ntb) * 128] = seq[:_ntb * 128]
        else:
            xp[:nseqt * 128] = seq
        sl = slice(c * NSEQ_S, (c + 1) * NSEQ_S)
        m = dict(shared)
        m.update(cs[hf])
        m.update(xp=xp, xs=x_sample[sl].reshape(NSEQ_S * 4, D),
                 ck=f(cache_k_win)[0, sl].reshape(NSEQ_S, 128, 128), cv=f(cache_v_win)[0, sl].reshape(NSEQ_S, 128, 128),
                 swkv=f(state_wkv)[0, sl], sshift=f(state_shift)[0, sl])
        in_maps.append(m)
    res = run_bass_kernel_spmd(nc, in_maps, core_ids=list(range(NCORES))).results
    if (_nta, _ntb, _nts) != (NTA_FULL, NTB_FULL, NSEQ_S):
        return res
    y_prompt = np.stack([np.concatenate([res[2 * b]["y_p"][128:_ntb * 128], res[2 * b + 1]["y_p"][:(_ntb - 1) * 128]]) for b in range(B)])
    y_sample = np.concatenate([res[c]["y_s"].reshape(NSEQ_S, 4, D) for c in range(NCORES)])
    od = lambda b: res[2 * b + 1]
    kwp = np.stack([od(b)["kw_p"].reshape(128, 2, 64) for b in range(B)])[None]
    vwp = np.stack([od(b)["vw_p"].reshape(128, 2, 64) for b in range(B)])[None]
    wkvp = np.stack([od(b)["wkv_p"] for b in range(B)])[None]
    shp = np.stack([od(b)["sh_p"][0] for b in range(B)])[None]
    kws = np.concatenate([res[c]["kw_s"].reshape(NSEQ_S, 128, 2, 64) for c in range(NCORES)])[None]
    vws = np.concatenate([res[c]["vw_s"].reshape(NSEQ_S, 128, 2, 64) for c in range(NCORES)])[None]
    wkvs = np.concatenate([res[c]["wkv_s"] for c in range(NCORES)])[None]
    shs = np.concatenate([res[c]["sh_s"] for c in range(NCORES)])[None]
    return (y_prompt, y_sample, kwp, vwp, wkvp, shp, kws, vws, wkvs, shs)
```
